# Optimizing a Trainium2 kernel written in Bass

```python
import jax, jax.numpy as jnp
from jax import lax
import numpy as np

D_MODEL = 2048
BATCH = 4
SEQ = 2048
DEPTH = 4
DEC_BATCH = 128
DEC_SEQ = 4
PAST_LEN = 16384
PAGE_SIZE = 128

D_MIX = 2 * D_MODEL
D_SSD = D_MIX // 2
SSD_HEAD_DIM = 64
SSD_HEADS = D_SSD // SSD_HEAD_DIM
SSD_GROUPS = 4
SSD_STATE = 128
SSD_CHUNK = 128
CONV_W = 4
CONV_DIM = D_SSD + 2 * SSD_GROUPS * SSD_STATE
D_CM = D_MIX - D_SSD
CM_GROUPS = 16
CM_GROUP_DIM = D_CM // CM_GROUPS
CM_CHUNK = 128
D_FF = -(-(8 * D_MODEL) // (3 * 256)) * 256
D_PLE = 256
D_IN = 2 * D_SSD + 2 * SSD_GROUPS * SSD_STATE + SSD_HEADS + 2 * D_CM
EPS = 1e-6

kernel_name = "hybrid_ssd_chunkmlp_decoder_step"


def rmsnorm(x, g):
    xf = x.astype(jnp.float32)
    y = xf * lax.rsqrt(jnp.mean(xf * xf, axis=-1, keepdims=True) + EPS)
    return (y * g.astype(jnp.float32)).astype(x.dtype)


def causal_dwconv(xbc, buf, w, b):
    xp = jnp.concatenate([buf.astype(xbc.dtype), xbc], axis=1)
    y = lax.conv_general_dilated(
        xp, w[:, None, :].astype(xbc.dtype), window_strides=(1,), padding='VALID',
        dimension_numbers=('NWC', 'WIO', 'NWC'), feature_group_count=xbc.shape[-1])
    return y + b.astype(xbc.dtype), xp[:, -(CONV_W - 1):]


def ssd(x, dt, A, Bm, Cm, h0):
    b, L = x.shape[:2]
    Q = min(SSD_CHUNK, L)
    nc = L // Q
    G, Hg, P, N = SSD_GROUPS, SSD_HEADS // SSD_GROUPS, SSD_HEAD_DIM, SSD_STATE
    f32 = jnp.float32
    xr = x.astype(f32).reshape(b, nc, Q, G, Hg, P)
    dtr = dt.astype(f32).reshape(b, nc, Q, G, Hg)
    Br = Bm.astype(f32).reshape(b, nc, Q, G, N)
    Cr = Cm.astype(f32).reshape(b, nc, Q, G, N)
    acum = jnp.cumsum(dtr * A.astype(f32).reshape(G, Hg), axis=2)
    xdt = xr * dtr[..., None]
    mask = jnp.tril(jnp.ones((Q, Q), dtype=bool))[:, :, None, None]
    seg = acum[:, :, :, None] - acum[:, :, None, :]
    decay = jnp.exp(jnp.where(mask, seg, -jnp.inf))
    cb = jnp.einsum('bclgn,bcsgn->bclsg', Cr, Br)
    y_diag = jnp.einsum('bclsgh,bcsghp->bclghp', cb[..., None] * decay, xdt)
    decay_s = jnp.exp(acum[:, :, -1:] - acum)
    states = jnp.einsum('bcsgn,bcsgh,bcsghp->bcghpn', Br, decay_s, xdt)
    chunk_decay = jnp.exp(acum[:, :, -1])

    def step(h, inp):
        st, dc = inp
        return h * dc[..., None, None] + st, h

    hT, prev = lax.scan(step, h0.astype(f32).reshape(b, G, Hg, P, N),
                        (jnp.moveaxis(states, 1, 0), jnp.moveaxis(chunk_decay, 1, 0)))
    prev = jnp.moveaxis(prev, 0, 1)
    y_off = jnp.einsum('bclgn,bcghpn,bclgh->bclghp', Cr, prev, jnp.exp(acum))
    y = (y_diag + y_off).reshape(b, L, SSD_HEADS, P)
    return y, hT.reshape(b, SSD_HEADS, P, N)


def chunk_mlp(u, v, w_s, b_s):
    b, L = u.shape[:2]
    Q = min(CM_CHUNK, L)
    nc = L // Q
    w = w_s[:, :Q, :Q] * jnp.tril(jnp.ones((Q, Q), dtype=w_s.dtype))
    vr = v.reshape(b, nc, Q, CM_GROUPS, CM_GROUP_DIM)
    s = jnp.einsum('gts,bcsgd->bctgd', w, vr) + b_s[:, :Q].T[None, None, :, :, None]
    return u * s.reshape(b, L, D_CM)


def layer(h, p_l, ssm0, conv0, g_mix, w_in, conv_w, conv_b, dt_bias, a_log, d_skip,
          g_ssd, g_cm, w_s, b_s, w_out, g_ffn, w_gate, w_up, w_down, g_pg, w_pg, w_ple):
    b, L, _ = h.shape
    n = rmsnorm(h, g_mix)
    proj = n @ w_in
    i1 = D_SSD
    i2 = i1 + CONV_DIM
    i3 = i2 + SSD_HEADS
    i4 = i3 + D_CM
    z, xbc, dt_raw, u, v = (proj[..., :i1], proj[..., i1:i2], proj[..., i2:i3],
                            proj[..., i3:i4], proj[..., i4:])
    xbc_c, conv_new = causal_dwconv(xbc, conv0, conv_w, conv_b)
    xbc_c = jax.nn.silu(xbc_c)
    gn = SSD_GROUPS * SSD_STATE
    xs = xbc_c[..., :D_SSD].reshape(b, L, SSD_HEADS, SSD_HEAD_DIM)
    Bm = xbc_c[..., D_SSD:D_SSD + gn].reshape(b, L, SSD_GROUPS, SSD_STATE)
    Cm = xbc_c[..., D_SSD + gn:].reshape(b, L, SSD_GROUPS, SSD_STATE)
    dt = jax.nn.softplus(dt_raw.astype(jnp.float32) + dt_bias.astype(jnp.float32))
    A = -jnp.exp(a_log.astype(jnp.float32))
    y, ssm_new = ssd(xs, dt, A, Bm, Cm, ssm0)
    y = y + d_skip.astype(jnp.float32)[:, None] * xs.astype(jnp.float32)
    y = y.reshape(b, L, SSD_GROUPS, D_SSD // SSD_GROUPS) * \
        jax.nn.silu(z.astype(jnp.float32)).reshape(b, L, SSD_GROUPS, D_SSD // SSD_GROUPS)
    y_ssd = rmsnorm(y, g_ssd.reshape(SSD_GROUPS, -1)).reshape(b, L, D_SSD).astype(h.dtype)
    u = jax.nn.gelu(u)
    v = rmsnorm(jax.nn.gelu(v), g_cm)
    y_cm = chunk_mlp(u, v, w_s, b_s)
    h = h + jnp.concatenate([y_ssd, y_cm.astype(h.dtype)], axis=-1) @ w_out
    f = rmsnorm(h, g_ffn)
    h = h + (jax.nn.silu(f @ w_gate) * (f @ w_up)) @ w_down
    gate = jax.nn.sigmoid(rmsnorm(h, g_pg) @ w_pg)
    h = h + (p_l @ w_ple) * gate
    return h, ssm_new, conv_new, v


def setup_inputs(seed: int = 0) -> dict:
    key = jax.random.key(seed)
    ks = iter(jax.random.split(key, 32))
    nrm = lambda shape, s=1.0: s * jax.random.normal(next(ks), shape, jnp.float32)
    gain = lambda shape: 1.0 + 0.02 * jax.random.normal(next(ks), shape, jnp.float32)
    dt0 = jnp.exp(jax.random.uniform(next(ks), (DEPTH, SSD_HEADS), jnp.float32,
                                     np.log(1e-3), np.log(1e-1)))
    return {
        "x_prompt": nrm((BATCH, SEQ, D_MODEL)),
        "x_sample": nrm((DEC_BATCH, DEC_SEQ, D_MODEL)),
        "state_ssm": nrm((DEPTH, DEC_BATCH, SSD_HEADS, SSD_HEAD_DIM, SSD_STATE), 0.1),
        "state_conv": nrm((DEPTH, DEC_BATCH, CONV_W - 1, CONV_DIM)),
        "p_prompt": nrm((DEPTH, BATCH, SEQ, D_PLE)),
        "p_sample": nrm((DEPTH, DEC_BATCH, DEC_SEQ, D_PLE)),
        "g_mix": gain((DEPTH, D_MODEL)),
        "w_in": nrm((DEPTH, D_MODEL, D_IN), D_MODEL ** -0.5),
        "conv_w": nrm((DEPTH, CONV_W, CONV_DIM), CONV_W ** -0.5),
        "conv_b": nrm((DEPTH, CONV_DIM), 0.02),
        "dt_bias": dt0 + jnp.log(-jnp.expm1(-dt0)),
        "a_log": jnp.log(jax.random.uniform(next(ks), (DEPTH, SSD_HEADS), jnp.float32, 1.0, 16.0)),
        "d_skip": gain((DEPTH, SSD_HEADS)),
        "g_ssd": gain((DEPTH, D_SSD)),
        "g_cm": gain((DEPTH, D_CM)),
        "w_s": nrm((DEPTH, CM_GROUPS, CM_CHUNK, CM_CHUNK), 0.05),
        "b_s": 1.0 + nrm((DEPTH, CM_GROUPS, CM_CHUNK), 0.1),
        "w_out": nrm((DEPTH, D_MIX, D_MODEL), D_MIX ** -0.5),
        "g_ffn": gain((DEPTH, D_MODEL)),
        "w_gate": nrm((DEPTH, D_MODEL, D_FF), D_MODEL ** -0.5),
        "w_up": nrm((DEPTH, D_MODEL, D_FF), D_MODEL ** -0.5),
        "w_down": nrm((DEPTH, D_FF, D_MODEL), D_FF ** -0.5),
        "g_pg": gain((DEPTH, D_MODEL)),
        "w_pg": nrm((DEPTH, D_MODEL, D_MODEL), D_MODEL ** -0.5),
        "w_ple": nrm((DEPTH, D_PLE, D_MODEL), D_PLE ** -0.5),
        "g_final": gain((D_MODEL,)),
    }


def reference(x_prompt, x_sample, state_ssm, state_conv, p_prompt, p_sample, g_mix, w_in,
              conv_w, conv_b, dt_bias, a_log, d_skip, g_ssd, g_cm, w_s, b_s, w_out, g_ffn,
              w_gate, w_up, w_down, g_pg, w_pg, w_ple, g_final):
    bp = x_prompt.shape[0]
    ssm_zero = jnp.zeros((bp, SSD_HEADS, SSD_HEAD_DIM, SSD_STATE), x_prompt.dtype)
    conv_zero = jnp.zeros((bp, CONV_W - 1, CONV_DIM), x_prompt.dtype)
    hp, hs = x_prompt, x_sample
    ssm_p, conv_p, ssm_s, conv_s, v_s = [], [], [], [], []
    for i in range(DEPTH):
        lw = (g_mix[i], w_in[i], conv_w[i], conv_b[i], dt_bias[i], a_log[i], d_skip[i],
              g_ssd[i], g_cm[i], w_s[i], b_s[i], w_out[i], g_ffn[i], w_gate[i], w_up[i],
              w_down[i], g_pg[i], w_pg[i], w_ple[i])
        hp, sp, cp, _ = layer(hp, p_prompt[i], ssm_zero, conv_zero, *lw)
        hs, ss, cs, vs = layer(hs, p_sample[i], state_ssm[i], state_conv[i], *lw)
        ssm_p.append(sp); conv_p.append(cp)
        ssm_s.append(ss); conv_s.append(cs); v_s.append(vs)
    y_prompt = rmsnorm(hp, g_final)
    y_sample = rmsnorm(hs, g_final)
    return (y_prompt, y_sample, jnp.stack(ssm_p), jnp.stack(conv_p),
            jnp.stack(ssm_s), jnp.stack(conv_s), jnp.stack(v_s))
```

```python
import numpy as np
from contextlib import ExitStack
import concourse.bass as bass
import concourse.mybir as mybir
from concourse.bass_utils import run_bass_kernel_spmd

F32 = mybir.dt.float32
BF16 = mybir.dt.bfloat16
AF = mybir.ActivationFunctionType
ALU = mybir.AluOpType
AX = mybir.AxisListType

D = 2048
DIN = 9248
DFF = 5632
DPLE = 256
CONV = 3072
I1 = 2048
I2 = I1 + 3072
I3 = I2 + 32
I4 = I3 + 2048
EPS = 1e-6
NSS = 16
TS = 64
SAME_ENGINE_SYNC = True


class Buf:
    __slots__ = ("w", "r", "name")

    def __init__(self, name=""):
        self.w = None
        self.r = {}
        self.name = name


class Sched:
    def __init__(self, nc, es, ndma=24):
        self.nc = nc
        self.eng = {"pe": nc.tensor, "dve": nc.vector, "act": nc.scalar, "pool": nc.gpsimd, "sp": nc.sync}
        self.sem = {}
        for k in ["pe", "dve", "act", "pool"]:
            self.sem[k] = es.enter_context(nc.semaphore("s_" + k))
        self.cnt = {k: 0 for k in self.sem}
        self.seen = {e: {} for e in self.eng}
        self.dsem = [es.enter_context(nc.semaphore("s_dma%d" % i)) for i in range(ndma)]
        self.dcnt = [0] * ndma
        self.drr = 0
        self.ndma = ndma

    def _semof(self, key):
        if isinstance(key, tuple):
            return self.dsem[key[1]]
        return self.sem[key]

    def _wait(self, e, deps):
        best = {}
        for d in deps:
            if d is None:
                continue
            key, val = d
            if key == e and not SAME_ENGINE_SYNC:
                continue
            if val > best.get(key, 0):
                best[key] = val
        for key, val in best.items():
            if self.seen[e].get(key, 0) >= val:
                continue
            self.eng[e].wait_ge(self._semof(key), val)
            self.seen[e][key] = val

    def _deps(self, R, W):
        deps = []
        for b in R:
            deps.append(b.w)
        for b in W:
            deps.append(b.w)
            deps.extend(b.r.values())
        return deps

    def _mark(self, tok, R, W):
        key = tok[0]
        for b in R:
            b.r[key] = tok
        for b in W:
            b.w = tok
            b.r = {}

    def op(self, e, fn, R=(), W=()):
        self._wait(e, self._deps(R, W))
        inst = fn()
        self.cnt[e] += 1
        inst.then_inc(self.sem[e], 1)
        self._mark((e, self.cnt[e]), R, W)

    def group(self, e, fns, R=(), W=()):
        self._wait(e, self._deps(R, W))
        inst = None
        for fn in fns:
            inst = fn()
        self.cnt[e] += 1
        inst.then_inc(self.sem[e], 1)
        self._mark((e, self.cnt[e]), R, W)

    def dma(self, q, out, in_, R=(), W=(), **kw):
        i = self.drr
        self.drr = (i + 1) % self.ndma
        deps = self._deps(R, W)
        if self.dcnt[i] > 0:
            deps.append((("dma", i), self.dcnt[i]))
        self._wait(q, deps)
        inst = self.eng[q].dma_start(out=out, in_=in_, **kw)
        self.dcnt[i] += 16
        inst.then_inc(self.dsem[i], 16)
        self._mark((("dma", i), self.dcnt[i]), R, W)

    def drain(self, e="sp"):
        deps = [(("dma", i), self.dcnt[i]) for i in range(self.ndma) if self.dcnt[i] > 0]
        deps += [(k, self.cnt[k]) for k in self.cnt if self.cnt[k] > 0]
        self._wait(e, deps)


def split_cols(c0, n, mx=512):
    out = []
    while n > 0:
        k = min(mx, n)
        out.append((c0, k))
        c0 += k
        n -= k
    return out


DBG = False


def build(NTP, DEPTH, stop_after=None):
    NCH = NTP // 128
    NT = NCH + 1
    T = NTP + TS
    TC = 3 + T
    nc = bass.Bass("TRN2", target_bir_lowering=False)

    def din(name, shape):
        return nc.dram_tensor(name, list(shape), F32, kind="ExternalInput").ap()

    def dout(name, shape):
        return nc.dram_tensor(name, list(shape), F32, kind="ExternalOutput").ap()

    NSEG = 2
    xin = din("xin", [NSEG, T, D])
    pin = din("pin", [DEPTH, NSEG, T, DPLE])
    sst = din("sst", [DEPTH, NSEG, NSS, 2048, 128])
    scv = din("scv", [DEPTH, NSEG, NSS * 3, CONV])
    prm = din("prm", [DEPTH * 200 + 16, 128])
    smallp = din("smallp", [DEPTH, 96])
    g_cm_d = din("g_cm", [DEPTH, 2048])
    w_s_d = din("w_s", [DEPTH, 16, 128, 128])
    b_s_d = din("b_s", [DEPTH, 16, 128])
    w_in = din("w_in", [DEPTH, D, DIN])
    w_out = din("w_out", [DEPTH, 4096, D])
    w_gate = din("w_gate", [DEPTH, D, DFF])
    w_up = din("w_up", [DEPTH, D, DFF])
    w_down = din("w_down", [DEPTH, DFF, D])
    w_pg = din("w_pg", [DEPTH, D, D])
    w_ple = din("w_ple", [DEPTH, DPLE, D])
    g_final_d = din("g_final", [1, 2048])

    yout = dout("yout", [NSEG, T, D])
    ssm_p = dout("ssm_p", [DEPTH, 2048, 128])
    conv_o = dout("conv_o", [DEPTH, NSEG, 3 + NSS * 3, CONV])
    ssm_s = dout("ssm_s", [DEPTH, NSEG, NSS, 2048, 128])
    v_s = dout("v_s", [DEPTH, NSEG, TS, 2048])
    hbuf = nc.dram_tensor("hbuf", [NSEG, T, D], F32, kind="Internal").ap()
    hstate = nc.dram_tensor("hstate", [DEPTH, 4, 128, 512], F32, kind="Internal").ap()
    hbuf_b = [[Buf() for _ in range(NTP // 128 + 1)] for _ in range(NSEG)]
    hstate_b = [[Buf() for _ in range(4)] for _ in range(DEPTH)]

    es = ExitStack()
    S = Sched(nc, es)

    class TL:
        def __init__(self, name, shape, dt, nb=1, psum=False, sc=None):
            f = nc.psum_tensor if psum else nc.sbuf_tensor
            self.t = (sc or es).enter_context(f(name, list(shape), dt))
            self.bufs = [Buf(name + str(i)) for i in range(nb)]
            self.b = self.bufs[0]

    def rows(i):
        return 128 if i < NCH else TS

    def ncol(i):
        return 3 + i * 128

    def tcol(i):
        return i * 128

    h = TL("h", [128, NT, D], F32, nb=NT)
    nT = TL("nT", [128, 16, TC], BF16)
    NSLOT = 4
    ring = [TL("ring%d" % i, [128, 2048], BF16) for i in range(NSLOT)]
    ring_i = [0]
    scr = TL("scr", [128, 512], F32)
    PT = TL("PT", [128, DEPTH * 200 + 16], F32)
    ident = TL("ident", [128, 128], F32)
    tri = TL("tri", [128, 128], F32)
    ustr = TL("ustr", [128, 128], F32)
    ones = TL("ones", [128, 128], F32)
    maskS = TL("maskS", [64, 64], F32)
    blk1 = TL("blk1", [64, 64], F32)
    rmask = TL("rmask", [64, 16], F32)
    maskB3 = TL("maskB3", [128, 16, 64], BF16)
    smb = TL("smb", [128, 96], F32)
    CONSTS = [ident.b, tri.b, ustr.b, ones.b, maskS.b, blk1.b, rmask.b, maskB3.b]

    banks = [TL("bank%d" % i, [128, 512], F32, psum=True) for i in range(8)]
    NROT = 6
    rot = [0]

    def ps():
        b = banks[rot[0]]
        rot[0] = (rot[0] + 1) % NROT
        return b

    accA = banks[6]
    accB = banks[7]

    V = nc.vector
    A = nc.scalar
    PE = nc.tensor

    def const_tri(tl, n, cmp, base=0, mult=1, pat=-1):
        def f():
            nc.gpsimd.memset(tl.t[:], 1.0)
            return nc.gpsimd.affine_select(out=tl.t[:], in_=tl.t[:], pattern=[[pat, n]], compare_op=cmp,
                                           fill=0.0, base=base, channel_multiplier=mult)
        S.group("pool", [f], W=[tl.b])

    const_tri(ident, 128, ALU.is_equal)
    const_tri(tri, 128, ALU.is_ge, pat=1, mult=-1)
    const_tri(ustr, 128, ALU.is_gt, pat=-1, mult=1)
    S.op("pool", lambda: nc.gpsimd.memset(ones.t[:], 1.0), W=[ones.b])
    Rm = TL("Rm", [4, 64], F32)

    def f_rm():
        nc.gpsimd.memset(Rm.t[:], 1.0)
        return nc.gpsimd.affine_select(out=Rm.t[:].rearrange("p (b l) -> p b l", l=4),
                                       in_=Rm.t[:].rearrange("p (b l) -> p b l", l=4), pattern=[[0, 16], [1, 4]],
                                       compare_op=ALU.is_equal, fill=0.0, base=0, channel_multiplier=-1)
    S.group("pool", [f_rm], W=[Rm.b])
    def f_rmask():
        nc.gpsimd.memset(rmask.t[:], 1.0)
        nc.gpsimd.affine_select(out=rmask.t[:], in_=rmask.t[:], pattern=[[-4, 16]], compare_op=ALU.is_ge,
                                fill=0.0, base=0, channel_multiplier=1)
        return nc.gpsimd.affine_select(out=rmask.t[:], in_=rmask.t[:], pattern=[[4, 16]], compare_op=ALU.is_ge,
                                       fill=0.0, base=3, channel_multiplier=-1)
    S.group("pool", [f_rmask], W=[rmask.b])
    with ExitStack() as sc0:
        mb3f = TL("mb3f", [128, 16, 64], F32, sc=sc0)

        def f_mb3():
            nc.gpsimd.memset(mb3f.t[:], 1.0)
            nc.gpsimd.affine_select(out=mb3f.t[:], in_=mb3f.t[:], pattern=[[-4, 16], [1, 64]], compare_op=ALU.is_ge,
                                    fill=0.0, base=0, channel_multiplier=0)
            return nc.gpsimd.affine_select(out=mb3f.t[:], in_=mb3f.t[:], pattern=[[4, 16], [-1, 64]],
                                           compare_op=ALU.is_ge, fill=0.0, base=3, channel_multiplier=0)
        S.group("pool", [f_mb3], W=[mb3f.b])
        S.op("dve", lambda: V.tensor_copy(out=maskB3.t[:, :, :], in_=mb3f.t[:, :, :]), R=[mb3f.b], W=[maskB3.b])
        for e_ in ["pe", "dve", "act", "pool", "sp"]:
            S.drain(e_)
    bk = ps()
    rmT = TL("rmT", [16, 64], F32)
    S.op("pe", lambda: PE.transpose(bk.t[0:16, 0:64], rmask.t[:], ident.t[0:64, 0:64]), R=[rmask.b, ident.b], W=[bk.b])
    S.op("dve", lambda: V.tensor_copy(out=rmT.t[:], in_=bk.t[0:16, 0:64]), R=[bk.b], W=[rmT.b])
    bk2 = ps()
    S.op("pe", lambda: PE.matmul(bk2.t[0:64, 0:64], lhsT=rmT.t[:], rhs=rmT.t[:], start=True, stop=True),
         R=[rmT.b], W=[bk2.b])
    S.op("dve", lambda: V.tensor_copy(out=blk1.t[:], in_=bk2.t[0:64, 0:64]), R=[bk2.b], W=[blk1.b])
    S.op("dve", lambda: V.tensor_tensor(out=maskS.t[:], in0=blk1.t[:], in1=tri.t[0:64, 0:64], op=ALU.mult),
         R=[blk1.b, tri.b], W=[maskS.b])

    NPR = DEPTH * 200 + 16
    O_GMIX, O_GFFN, O_GPG, O_GSSD, O_GCM, O_CW, O_CB = 0, 16, 32, 48, 64, 80, 176
    O_GFIN = DEPTH * 200
    pstg = TL("pstg", [128, 128], F32)
    for r0 in range(0, NPR, 128):
        nr = min(128, NPR - r0)
        S.dma("sp", pstg.t[0:nr, :], prm[r0:r0 + nr, :], W=[pstg.b])
        bkp = ps()
        S.op("pe", lambda: PE.transpose(bkp.t[:, 0:nr], pstg.t[0:nr, :], ident.t[0:nr, 0:nr]),
             R=[pstg.b, ident.b], W=[bkp.b])
        S.op("dve", lambda: V.tensor_copy(out=PT.t[:, r0:r0 + nr], in_=bkp.t[:, 0:nr]), R=[bkp.b], W=[PT.b])

    ntail = TL("ntail", [128, 16, 3], BF16)

    yb = TL("yb", [128, 4, T], BF16)
    dbg_t = {}

    def dump_yb(name):
        if not DBG:
            return
        dd = dout(name, [128, 4 * T])
        for ct in range(4):
            for (c0, n) in split_cols(0, T):
                S.op("dve", lambda: V.tensor_copy(out=scr.t[:, 0:n], in_=yb.t[:, ct, c0:c0 + n]), R=[yb.b], W=[scr.b])
                S.dma("sp", dd[:, ct * T + c0:ct * T + c0 + n], scr.t[:, 0:n], R=[scr.b])
    epsb = TL("epsb", [128, 1], F32)
    S.op("dve", lambda: V.memset(epsb.t[:, :], EPS), W=[epsb.b])
    stat = TL("stat", [128, NT * 8], F32, nb=NT)

    def barrier():
        for e in ["pe", "dve", "act", "pool", "sp"]:
            S.drain(e)

    def wload(src_ap):
        sl = ring[ring_i[0]]
        ring_i[0] = (ring_i[0] + 1) % NSLOT
        a, b = src_ap.shape[1], src_ap.shape[2]
        view = sl.t[:, 0:a * b].rearrange("p (a b) -> p a b", a=a)
        S.dma("pool", view, src_ap, W=[sl.b])
        return view, sl.b

    def wblock(w2d, c0, ncols=512, K=2048, r0=0):
        out = []
        for s in range(K // 512):
            src = w2d[r0 + s * 512:r0 + (s + 1) * 512, c0:c0 + ncols].rearrange("(kc p) n -> p kc n", p=128)
            out.append(wload(src))
        return out

    def mm_fm(slots, ct, c0, n):
        bk = ps()
        KC = 4 * len(slots)
        fns = []
        for kc in range(KC):
            v = slots[kc // 4][0]
            fns.append(lambda kc=kc, v=v: PE.matmul(bk.t[:, 0:n], lhsT=v[:, kc % 4, ct * 128:(ct + 1) * 128],
                                                    rhs=nT.t[:, kc, c0:c0 + n], start=(kc == 0), stop=(kc == KC - 1)))
        S.group("pe", fns, R=[nT.b] + [b for _, b in slots], W=[bk.b])
        return bk

    def gemm_tm(slots, src, src_buf, colof, consumer, ncols=512):
        KC = 4 * len(slots)
        for i in range(NT):
            bk = ps()
            r = rows(i)
            c = colof(i)
            fns = []
            for kc in range(KC):
                v = slots[kc // 4][0]
                fns.append(lambda kc=kc, v=v: PE.matmul(bk.t[0:r, 0:ncols], lhsT=src[:, kc, c:c + r],
                                                        rhs=v[:, kc % 4, 0:ncols], start=(kc == 0),
                                                        stop=(kc == KC - 1)))
            S.group("pe", fns, R=[src_buf] + [b for _, b in slots], W=[bk.b])
            consumer(i, bk)

    def rstd_of(i, n_el):
        r = rows(i)
        o = 8 * i
        S.op("act", lambda: A.activation(out=stat.t[0:r, o + 5:o + 6], in_=stat.t[0:r, o + 4:o + 5], func=AF.Sqrt,
                                         scale=1.0 / n_el, bias=epsb.t[0:r, 0:1]), R=[stat.bufs[i], epsb.b],
             W=[stat.bufs[i]])
        S.op("dve", lambda: V.reciprocal(out=stat.t[0:r, o + 6:o + 7], in_=stat.t[0:r, o + 5:o + 6]),
             R=[stat.bufs[i]], W=[stat.bufs[i]])

    def sumsq_h(i):
        r = rows(i)
        o = 8 * i
        for q in range(4):
            S.op("act", lambda: A.activation(out=scr.t[0:r, :], in_=h.t[0:r, i, q * 512:(q + 1) * 512], func=AF.Square,
                                             accum_out=stat.t[0:r, o + q:o + q + 1]),
                 R=[h.bufs[i]], W=[scr.b, stat.bufs[i]])
        S.op("dve", lambda: V.tensor_reduce(out=stat.t[0:r, o + 4:o + 5], in_=stat.t[0:r, o:o + 4], axis=AX.X,
                                            op=ALU.add), R=[stat.bufs[i]], W=[stat.bufs[i]])
        rstd_of(i, D)

    def norm_to_nT(gcol):
        for i in range(NT):
            r = rows(i)
            o = 8 * i
            sumsq_h(i)
            for q in range(4):
                S.op("dve", lambda: V.tensor_scalar(out=scr.t[0:r, :], in0=h.t[0:r, i, q * 512:(q + 1) * 512],
                                                    scalar1=stat.t[0:r, o + 6:o + 7], scalar2=None, op0=ALU.mult),
                     R=[h.bufs[i], stat.bufs[i]], W=[scr.b])
                bk = ps()
                fns = [lambda kk=kk: PE.transpose(bk.t[:, kk * 128:kk * 128 + r], scr.t[0:r, kk * 128:(kk + 1) * 128],
                                                  ident.t[0:r, 0:r]) for kk in range(4)]
                S.group("pe", fns, R=[scr.b, ident.b], W=[bk.b])
                c = ncol(i)
                S.op("dve", lambda: V.tensor_tensor(
                    out=nT.t[:, q * 4:(q + 1) * 4, c:c + r],
                    in0=bk.t[:, :].rearrange("p (a b) -> p a b", a=4)[:, :, 0:r],
                    in1=PT.t[:, gcol + q * 4:gcol + q * 4 + 4].unsqueeze(2).to_broadcast([128, 4, r]),
                    op=ALU.mult), R=[bk.b, PT.b], W=[nT.b])

    def gelu_psum(bk, r, n, out_ap, out_buf, tmpx, tmpa):
        S.op("act", lambda: A.copy(out=tmpx.t[0:r, 0:n], in_=bk.t[0:r, 0:n]), R=[bk.b], W=[tmpx.b])
        S.op("dve", lambda: V.scalar_tensor_tensor(out=tmpa.t[0:r, 0:n], in0=tmpx.t[0:r, 0:n], scalar=0.044715,
                                                   in1=tmpx.t[0:r, 0:n], op0=ALU.mult, op1=ALU.mult),
             R=[tmpx.b], W=[tmpa.b])
        S.op("dve", lambda: V.scalar_tensor_tensor(out=tmpa.t[0:r, 0:n], in0=tmpa.t[0:r, 0:n], scalar=1.0,
                                                   in1=tmpx.t[0:r, 0:n], op0=ALU.add, op1=ALU.mult),
             R=[tmpx.b, tmpa.b], W=[tmpa.b])
        S.op("act", lambda: A.activation(out=tmpa.t[0:r, 0:n], in_=tmpa.t[0:r, 0:n], func=AF.Sigmoid,
                                         scale=1.5957691216), R=[tmpa.b], W=[tmpa.b])
        S.op("dve", lambda: V.tensor_tensor(out=out_ap, in0=tmpx.t[0:r, 0:n], in1=tmpa.t[0:r, 0:n], op=ALU.mult),
             R=[tmpx.b, tmpa.b], W=[out_buf])

    FMB_H = split_cols(0, TC)
    FMB = split_cols(3, T)

    def outproj(wsrc2d, r0):
        for cb in range(4):
            src = wsrc2d[r0:r0 + 512, cb * 512:(cb + 1) * 512].rearrange("(kc p) n -> p kc n", p=128)
            sl = wload(src)

            def cons(i, bk, cb=cb):
                r = rows(i)
                S.op("dve", lambda: V.tensor_tensor(out=h.t[0:r, i, cb * 512:(cb + 1) * 512],
                                                    in0=h.t[0:r, i, cb * 512:(cb + 1) * 512], in1=bk.t[0:r, :],
                                                    op=ALU.add), R=[bk.b, h.bufs[i]], W=[h.bufs[i]])
            gemm_tm([sl], yb.t, yb.b, tcol, cons)

    def ssd_phase(d, seg, sc):
        pc = d * 200
        L = lambda name, shape, dt, nb=1: TL(name + "_%d_%d" % (d, seg), shape, dt, nb=nb, sc=sc)
        Wdt = L("Wdt", [128, 16, 32], BF16)
        dtv = L("dtv", [128, NT, 32], F32)
        dtA = L("dtA", [128, NT, 32], F32)
        eac = L("eac", [128, NT, 32], F32)
        dtdec = L("dtdec", [128, NT, 32], F32)
        cdec = L("cdec", [128, NT, 32], F32)
        tmp32 = L("tmp32", [128, 64], F32)
        xp = L("xp", [128, TC], F32)
        xps = L("xps", [128, NSS, 7], F32)
        hsq = L("hsq", [128, 4, NSS * 3], F32)
        cvst = L("cvst", [48, 512], F32)
        cvo1 = L("cvo1", [128, 3 + NSS * 3], F32)
        cvt1 = L("cvt1", [64, 128], F32)
        cacc = L("cacc", [128, T], F32)
        BTg = L("BTg", [128, T], BF16)
        CTg = L("CTg", [128, T], BF16)
        Btok = L("Btok", [128, NT, 128], BF16)
        xtok = L("xtok", [128, NT, 512], BF16)
        cbm = L("cbm", [128, 128], F32)
        rh = L("rh", [128, 4, 128], F32)
        Mh = L("Mh", [128, 8, 128], BF16)
        xdt = L("xdt", [128, 512], BF16)
        xdts = L("xdts", [128, 512], BF16)
        prev = L("prev", [128, 512], F32)
        prevb = L("prevb", [128, 512], BF16)
        t1 = L("t1", [128, 512], F32)
        sqf = L("sqf", [128, 512], F32)
        Hs = [L("Hs%d" % k, [128, 4, 128], F32) for k in range(2)]
        HTb = L("HTb", [128, 512], BF16)
        Cmb = L("Cmb", [128, 64], BF16)
        Bmb = L("Bmb", [64, 128], BF16)
        dtArep = L("dtArep", [64, 512], F32)
        dAT = L("dAT", [128, 4, NSS], F32)

        norm_to_nT(pc + O_GMIX)
        if seg == 0:
            S.op("dve", lambda: V.memset(nT.t[:, :, 0:3], 0.0), W=[nT.b])
        else:
            S.op("dve", lambda: V.tensor_copy(out=nT.t[:, :, 0:3], in_=ntail.t[:, :, :]), R=[ntail.b], W=[nT.b])
        if seg + 1 < NSEG:
            S.op("dve", lambda: V.tensor_copy(out=ntail.t[:, :, :], in_=nT.t[:, :, NTP:NTP + 3]), R=[nT.b],
                 W=[ntail.b])
        S.dma("sp", smb.t[:, :], smallp[d].partition_broadcast(128), W=[smb.b])
        S.op("act", lambda: A.activation(out=smb.t[:, 32:64], in_=smb.t[:, 32:64], func=AF.Exp), R=[smb.b], W=[smb.b])
        S.op("dve", lambda: V.tensor_scalar(out=smb.t[:, 32:64], in0=smb.t[:, 32:64], scalar1=-1.0, scalar2=None,
                                            op0=ALU.mult), R=[smb.b], W=[smb.b])
        S.dma("pool", Wdt.t[:, :, :], w_in[d][:, I2:I3].rearrange("(kc p) n -> p kc n", p=128), W=[Wdt.b])
        for i in range(NT):
            r = rows(i)
            c = ncol(i)
            bk = ps()
            fns = [lambda kc=kc: PE.matmul(bk.t[0:r, 0:32], lhsT=nT.t[:, kc, c:c + r], rhs=Wdt.t[:, kc, :],
                                           start=(kc == 0), stop=(kc == 15)) for kc in range(16)]
            S.group("pe", fns, R=[nT.b, Wdt.b], W=[bk.b])
            S.op("dve", lambda: V.tensor_tensor(out=tmp32.t[0:r, 0:32], in0=bk.t[0:r, 0:32], in1=smb.t[0:r, 0:32],
                                                op=ALU.add), R=[bk.b, smb.b], W=[tmp32.b])
            S.op("act", lambda: A.activation(out=tmp32.t[0:r, 0:32], in_=tmp32.t[0:r, 0:32], func=AF.Exp),
                 R=[tmp32.b], W=[tmp32.b])
            S.op("act", lambda: A.activation(out=dtv.t[0:r, i, :], in_=tmp32.t[0:r, 0:32], func=AF.Ln, bias=1.0),
                 R=[tmp32.b], W=[dtv.b])
            S.op("dve", lambda: V.tensor_tensor(out=dtA.t[0:r, i, :], in0=dtv.t[0:r, i, :], in1=smb.t[0:r, 32:64],
                                                op=ALU.mult), R=[dtv.b, smb.b], W=[dtA.b])
            bk2 = ps()
            mk = tri if i < NCH else maskS
            on = ones if i < NCH else blk1
            S.group("pe", [lambda: PE.matmul(bk2.t[0:r, 0:32], lhsT=mk.t[0:r, 0:r], rhs=dtA.t[0:r, i, :], start=True,
                                             stop=True),
                           lambda: PE.matmul(bk2.t[0:r, 32:64], lhsT=on.t[0:r, 0:r], rhs=dtA.t[0:r, i, :], start=True,
                                             stop=True)], R=[dtA.b] + CONSTS, W=[bk2.b])
            S.op("act", lambda: A.activation(out=eac.t[0:r, i, :], in_=bk2.t[0:r, 0:32], func=AF.Exp),
                 R=[bk2.b], W=[eac.b])
            S.op("act", lambda: A.activation(out=cdec.t[0:r, i, :], in_=bk2.t[0:r, 32:64], func=AF.Exp),
                 R=[bk2.b], W=[cdec.b])
            S.op("act", lambda: A.copy(out=tmp32.t[0:r, 32:64], in_=bk2.t[0:r, 32:64]), R=[bk2.b], W=[tmp32.b])
            S.op("dve", lambda: V.tensor_tensor(out=tmp32.t[0:r, 32:64], in0=tmp32.t[0:r, 32:64], in1=bk2.t[0:r, 0:32],
                                                op=ALU.subtract), R=[bk2.b, tmp32.b], W=[tmp32.b])
            S.op("act", lambda: A.activation(out=tmp32.t[0:r, 32:64], in_=tmp32.t[0:r, 32:64], func=AF.Exp),
                 R=[tmp32.b], W=[tmp32.b])
            S.op("dve", lambda: V.tensor_tensor(out=dtdec.t[0:r, i, :], in0=tmp32.t[0:r, 32:64], in1=dtv.t[0:r, i, :],
                                                op=ALU.mult), R=[tmp32.b, dtv.b], W=[dtdec.b])

        def prep_hist(ct0, nct):
            S.dma("sp", cvst.t[0:48, 0:nct * 128], scv[d, seg][:, ct0 * 128:(ct0 + nct) * 128], W=[cvst.b])
            for k in range(nct):
                bk = ps()
                S.op("pe", lambda: PE.transpose(bk.t[:, 0:48], cvst.t[0:48, k * 128:(k + 1) * 128], ident.t[0:48, 0:48]),
                     R=[cvst.b, ident.b], W=[bk.b])
                S.op("act", lambda: A.copy(out=hsq.t[:, k, :], in_=bk.t[:, 0:48]), R=[bk.b], W=[hsq.b])

        def conv_tile(ctg, k, kind):
            cw = pc + O_CW
            cb_ = pc + O_CB + ctg
            S.op("dve", lambda: V.tensor_copy(out=xps.t[:, :, 0:3],
                                              in_=hsq.t[:, k, :].rearrange("p (b j) -> p b j", j=3)),
                 R=[hsq.b], W=[xps.b])
            S.op("dve", lambda: V.tensor_copy(out=xps.t[:, :, 3:7],
                                              in_=xp.t[:, 3 + NTP:3 + NTP + TS].rearrange("p (b l) -> p b l", l=4)),
                 R=[xp.b, xps.b], W=[xps.b])
            S.op("act", lambda: A.copy(out=cvo1.t[:, 0:3], in_=xp.t[:, NTP:NTP + 3]), R=[xp.b], W=[cvo1.b])
            S.op("act", lambda: A.copy(out=cvo1.t[:, 3:3 + NSS * 3].rearrange("p (b j) -> p b j", j=3),
                                       in_=xps.t[:, :, 4:7]), R=[xps.b, cvo1.b], W=[cvo1.b])
            bk = ps()
            S.op("pe", lambda: PE.transpose(bk.t[0:51, 0:128], cvo1.t[:, :], ident.t[:, :]), R=[cvo1.b, ident.b],
                 W=[bk.b])
            S.op("act", lambda: A.copy(out=cvt1.t[0:51, :], in_=bk.t[0:51, 0:128]), R=[bk.b], W=[cvt1.b])
            S.dma("sp", conv_o[d, seg][:, ctg * 128:(ctg + 1) * 128], cvt1.t[0:51, :], R=[cvt1.b])
            S.op("dve", lambda: V.tensor_scalar(out=cacc.t[:, 0:NTP], in0=xp.t[:, 0:NTP],
                                                scalar1=PT.t[:, cw + ctg:cw + ctg + 1], scalar2=None, op0=ALU.mult),
                 R=[xp.b, PT.b], W=[cacc.b])
            for j in range(1, 4):
                S.op("dve", lambda: V.scalar_tensor_tensor(out=cacc.t[:, 0:NTP], in0=xp.t[:, j:j + NTP],
                                                           scalar=PT.t[:, cw + j * 24 + ctg:cw + j * 24 + ctg + 1],
                                                           in1=cacc.t[:, 0:NTP], op0=ALU.mult, op1=ALU.add),
                     R=[xp.b, PT.b, cacc.b], W=[cacc.b])
            cs = cacc.t[:, NTP:T].rearrange("p (b l) -> p b l", l=4)
            S.op("dve", lambda: V.tensor_scalar(out=cs, in0=xps.t[:, :, 0:4], scalar1=PT.t[:, cw + ctg:cw + ctg + 1],
                                                scalar2=None, op0=ALU.mult), R=[xps.b, PT.b, cacc.b], W=[cacc.b])
            for j in range(1, 4):
                S.op("dve", lambda: V.scalar_tensor_tensor(out=cs, in0=xps.t[:, :, j:j + 4],
                                                           scalar=PT.t[:, cw + j * 24 + ctg:cw + j * 24 + ctg + 1],
                                                           in1=cs, op0=ALU.mult, op1=ALU.add),
                     R=[xps.b, PT.b, cacc.b], W=[cacc.b])
            if kind == "C":
                S.op("act", lambda: A.activation(out=CTg.t[:, :], in_=cacc.t[:, :], func=AF.Silu,
                                                 bias=PT.t[:, cb_:cb_ + 1]), R=[cacc.b, PT.b], W=[CTg.b])
            else:
                S.op("act", lambda: A.activation(out=cacc.t[:, :], in_=cacc.t[:, :], func=AF.Silu,
                                                 bias=PT.t[:, cb_:cb_ + 1]), R=[cacc.b, PT.b], W=[cacc.b])
            if kind == "B":
                S.op("dve", lambda: V.tensor_copy(out=BTg.t[:, :], in_=cacc.t[:, :]), R=[cacc.b], W=[BTg.b])

        def to_tok(dst, width_off):
            for i0 in range(0, NT, 4):
                bk = ps()
                tiles = list(range(i0, min(NT, i0 + 4)))
                fns = [lambda i=i: PE.transpose(bk.t[0:rows(i), (i - i0) * 128:(i - i0 + 1) * 128],
                                                cacc.t[:, tcol(i):tcol(i) + rows(i)], ident.t[:, :]) for i in tiles]
                S.group("pe", fns, R=[cacc.b, ident.b], W=[bk.b])
                full = [i for i in tiles if rows(i) == 128]
                if full:
                    nf = len(full)
                    S.op("act", lambda: A.copy(out=dst.t[:, full[0]:full[0] + nf, width_off:width_off + 128],
                                               in_=bk.t[:, 0:nf * 128].rearrange("p (a b) -> p a b", a=nf)),
                         R=[bk.b], W=[dst.b])
                for i in tiles:
                    if rows(i) != 128:
                        S.op("act", lambda: A.copy(out=dst.t[0:TS, i, width_off:width_off + 128],
                                                   in_=bk.t[0:TS, (i - i0) * 128:(i - i0 + 1) * 128]),
                             R=[bk.b], W=[dst.b])

        def sample_states(g, bko):
            iS = NCH
            cS = tcol(iS)
            hsl = slice(g * 8, (g + 1) * 8)
            S.op("dve", lambda: V.tensor_copy(out=dtArep.t[:, :].rearrange("p (a b) -> p a b", a=8),
                                              in_=dtA.t[0:TS, iS, hsl].unsqueeze(2).to_broadcast([TS, 8, 64])),
                 R=[dtA.b], W=[dtArep.b])
            bkd = ps()
            fns = [lambda rt=rt: PE.matmul(bkd.t[:, rt * NSS:(rt + 1) * NSS], lhsT=dtArep.t[:, rt * 128:(rt + 1) * 128],
                                           rhs=rmask.t[:, :], start=True, stop=True) for rt in range(4)]
            S.group("pe", fns, R=[dtArep.b] + CONSTS, W=[bkd.b])
            S.op("act", lambda: A.activation(out=dAT.t[:, :, :],
                                             in_=bkd.t[:, 0:4 * NSS].rearrange("p (a b) -> p a b", a=4),
                                             func=AF.Exp), R=[bkd.b], W=[dAT.b])
            for b in range(NSS):
                H = Hs[b % 2]
                S.dma("sp", H.t[:, :, :], sst[d, seg, b][g * 512:(g + 1) * 512, :].rearrange("(rt p) n -> p rt n", p=128),
                      W=[H.b])
                S.op("dve", lambda: V.tensor_tensor(out=Cmb.t[:, :], in0=CTg.t[:, cS:cS + TS], in1=maskB3.t[:, b, :],
                                                    op=ALU.mult), R=[CTg.b] + CONSTS, W=[Cmb.b])
                S.op("dve", lambda: V.tensor_scalar(out=Bmb.t[:, :], in0=Btok.t[0:TS, iS, :],
                                                    scalar1=rmask.t[:, b:b + 1], scalar2=None, op0=ALU.mult),
                     R=[Btok.b] + CONSTS, W=[Bmb.b])
                bkt = ps()
                fns = [lambda rt=rt: PE.transpose(bkt.t[:, rt * 128:(rt + 1) * 128], H.t[:, rt, :], ident.t[:, :])
                       for rt in range(4)]
                S.group("pe", fns, R=[H.b, ident.b], W=[bkt.b])
                S.op("act", lambda: A.copy(out=HTb.t[:, :], in_=bkt.t[:, :]), R=[bkt.b], W=[HTb.b])
                S.op("pe", lambda: PE.matmul(bko.t[0:TS, :], lhsT=Cmb.t[:, :], rhs=HTb.t[:, :], start=(b == 0),
                                             stop=(b == NSS - 1)), R=[Cmb.b, HTb.b], W=[bko.b])
                bku = ps()
                fns = [lambda rt=rt: PE.matmul(bku.t[:, rt * 128:(rt + 1) * 128],
                                               lhsT=xdts.t[0:TS, rt * 128:(rt + 1) * 128], rhs=Bmb.t[:, :], start=True,
                                               stop=True) for rt in range(4)]
                S.group("pe", fns, R=[xdts.b, Bmb.b], W=[bku.b])
                S.op("dve", lambda: V.tensor_tensor(out=H.t[:, :, :], in0=H.t[:, :, :],
                                                    in1=dAT.t[:, :, b:b + 1].to_broadcast([128, 4, 128]), op=ALU.mult),
                     R=[H.b, dAT.b], W=[H.b])
                S.op("dve", lambda: V.tensor_tensor(out=H.t[:, :, :], in0=H.t[:, :, :],
                                                    in1=bku.t[:, :].rearrange("p (a b) -> p a b", a=4), op=ALU.add),
                     R=[H.b, bku.b], W=[H.b])
                S.dma("sp", ssm_s[d, seg, b][g * 512:(g + 1) * 512, :].rearrange("(rt p) n -> p rt n", p=128), H.t[:, :, :],
                      R=[H.b])

        for g in range(4):
            for kind, c0w, ctg in (("B", I1 + 2048 + g * 128, 16 + g), ("C", I1 + 2560 + g * 128, 20 + g)):
                slv, slb = wload(w_in[d][:, c0w:c0w + 128].rearrange("(kc p) n -> p kc n", p=128))
                prep_hist(ctg, 1)
                for (c0, n) in FMB_H:
                    bk = ps()
                    fns = [lambda kc=kc: PE.matmul(bk.t[:, 0:n], lhsT=slv[:, kc, :], rhs=nT.t[:, kc, c0:c0 + n],
                                                   start=(kc == 0), stop=(kc == 15)) for kc in range(16)]
                    S.group("pe", fns, R=[nT.b, slb], W=[bk.b])
                    S.op("act", lambda: A.copy(out=xp.t[:, c0:c0 + n], in_=bk.t[:, 0:n]), R=[bk.b], W=[xp.b])
                conv_tile(ctg, 0, kind)
                if kind == "B":
                    to_tok(Btok, 0)
            slots = wblock(w_in[d], I1 + g * 512)
            prep_hist(g * 4, 4)
            for ct in range(4):
                for (c0, n) in FMB_H:
                    bk = mm_fm(slots, ct, c0, n)
                    S.op("act", lambda: A.copy(out=xp.t[:, c0:c0 + n], in_=bk.t[:, 0:n]), R=[bk.b], W=[xp.b])
                conv_tile(g * 4 + ct, ct, "x")
                to_tok(xtok, ct * 128)

            if seg == 0:
                S.op("dve", lambda: V.memset(prev.t[:, :], 0.0), W=[prev.b])
            else:
                S.dma("sp", prev.t[:, :], hstate[d, g], R=[hstate_b[d][g]], W=[prev.b])
            S.op("act", lambda: A.copy(out=prevb.t[:, :], in_=prev.t[:, :]), R=[prev.b], W=[prevb.b])
            hsl = slice(g * 8, (g + 1) * 8)
            for i in range(NT):
                r = rows(i)
                c = tcol(i)
                mk = tri if i < NCH else maskS
                bkc = ps()
                S.op("pe", lambda: PE.matmul(bkc.t[0:r, 0:r], lhsT=BTg.t[:, c:c + r], rhs=CTg.t[:, c:c + r], start=True,
                                             stop=True), R=[BTg.b, CTg.b], W=[bkc.b])
                S.op("dve", lambda: V.tensor_tensor(out=cbm.t[0:r, 0:r], in0=bkc.t[0:r, 0:r], in1=mk.t[0:r, 0:r],
                                                    op=ALU.mult), R=[bkc.b] + CONSTS, W=[cbm.b])
                for h4 in range(2):
                    bks = ps()
                    for hh in range(4):
                        hd = g * 8 + h4 * 4 + hh
                        S.op("dve", lambda: V.tensor_scalar(out=rh.t[0:r, hh, 0:r], in0=tri.t[0:r, 0:r],
                                                            scalar1=dtA.t[0:r, i, hd:hd + 1], scalar2=None,
                                                            op0=ALU.mult), R=[dtA.b] + CONSTS, W=[rh.b])
                    fns = [lambda hh=hh: PE.matmul(bks.t[0:r, hh * 128:hh * 128 + r], lhsT=ustr.t[0:r, 0:r],
                                                   rhs=rh.t[0:r, hh, 0:r], start=True, stop=True) for hh in range(4)]
                    S.group("pe", fns, R=[rh.b] + CONSTS, W=[bks.b])
                    S.op("act", lambda: A.activation(out=rh.t[0:r, :, 0:r],
                                                     in_=bks.t[0:r, :].rearrange("p (a b) -> p a b", a=4)[:, :, 0:r],
                                                     func=AF.Exp), R=[bks.b, rh.b], W=[rh.b])
                    S.op("dve", lambda: V.tensor_tensor(out=Mh.t[0:r, h4 * 4:h4 * 4 + 4, 0:r], in0=rh.t[0:r, :, 0:r],
                                                        in1=cbm.t[0:r, 0:r].unsqueeze(1).to_broadcast([r, 4, r]),
                                                        op=ALU.mult), R=[rh.b, cbm.b], W=[Mh.b])
                xv = xtok.t[0:r, i, :].rearrange("p (a b) -> p a b", a=8)
                S.op("dve", lambda: V.tensor_tensor(out=xdt.t[0:r, :].rearrange("p (a b) -> p a b", a=8), in0=xv,
                                                    in1=dtv.t[0:r, i, hsl].unsqueeze(2).to_broadcast([r, 8, 64]),
                                                    op=ALU.mult), R=[xtok.b, dtv.b], W=[xdt.b])
                S.op("dve", lambda: V.tensor_tensor(out=xdts.t[0:r, :].rearrange("p (a b) -> p a b", a=8), in0=xv,
                                                    in1=dtdec.t[0:r, i, hsl].unsqueeze(2).to_broadcast([r, 8, 64]),
                                                    op=ALU.mult), R=[xtok.b, dtdec.b], W=[xdts.b])
                bky = ps() if i < NCH else accA
                fns = [lambda hh=hh: PE.matmul(bky.t[0:r, hh * 64:(hh + 1) * 64], lhsT=Mh.t[0:r, hh, 0:r],
                                               rhs=xdt.t[0:r, hh * 64:(hh + 1) * 64], start=True, stop=True)
                       for hh in range(8)]
                S.group("pe", fns, R=[Mh.b, xdt.b], W=[bky.b])
                if i < NCH:
                    bko = ps()
                    S.op("pe", lambda: PE.matmul(bko.t[0:r, :], lhsT=CTg.t[:, c:c + r], rhs=prevb.t[:, :], start=True,
                                                 stop=True), R=[CTg.b, prevb.b], W=[bko.b])
                else:
                    bko = accB
                    sample_states(g, bko)
                S.op("dve", lambda: V.tensor_tensor(out=t1.t[0:r, :].rearrange("p (a b) -> p a b", a=8),
                                                    in0=bko.t[0:r, :].rearrange("p (a b) -> p a b", a=8),
                                                    in1=eac.t[0:r, i, hsl].unsqueeze(2).to_broadcast([r, 8, 64]),
                                                    op=ALU.mult), R=[bko.b, eac.b], W=[t1.b])
                S.op("dve", lambda: V.tensor_tensor(out=t1.t[0:r, :], in0=t1.t[0:r, :], in1=bky.t[0:r, :], op=ALU.add),
                     R=[bky.b, t1.b], W=[t1.b])
                S.op("dve", lambda: V.tensor_tensor(out=sqf.t[0:r, :].rearrange("p (a b) -> p a b", a=8), in0=xv,
                                                    in1=smb.t[0:r, 64 + g * 8:64 + g * 8 + 8].unsqueeze(2).to_broadcast(
                                                        [r, 8, 64]), op=ALU.mult), R=[xtok.b, smb.b], W=[sqf.b])
                S.op("dve", lambda: V.tensor_tensor(out=t1.t[0:r, :], in0=t1.t[0:r, :], in1=sqf.t[0:r, :], op=ALU.add),
                     R=[t1.b, sqf.b], W=[t1.b])
                bkt = ps()
                fns = [lambda k4=k4: PE.transpose(bkt.t[:, k4 * 128:k4 * 128 + r], t1.t[0:r, k4 * 128:(k4 + 1) * 128],
                                                  ident.t[0:r, 0:r]) for k4 in range(4)]
                S.group("pe", fns, R=[t1.b, ident.b], W=[bkt.b])
                S.op("act", lambda: A.copy(out=yb.t[:, :, c:c + r],
                                           in_=bkt.t[:, :].rearrange("p (a b) -> p a b", a=4)[:, :, 0:r]),
                     R=[bkt.b], W=[yb.b])
                if i < NCH:
                    bkS = ps()
                    S.op("pe", lambda: PE.matmul(bkS.t[:, :], lhsT=Btok.t[0:r, i, :], rhs=xdts.t[0:r, :], start=True,
                                                 stop=True), R=[Btok.b, xdts.b], W=[bkS.b])
                    S.op("dve", lambda: V.tensor_tensor(out=prev.t[:, :].rearrange("p (a b) -> p a b", a=8),
                                                        in0=prev.t[:, :].rearrange("p (a b) -> p a b", a=8),
                                                        in1=cdec.t[:, i, hsl].unsqueeze(2).to_broadcast([128, 8, 64]),
                                                        op=ALU.mult), R=[prev.b, cdec.b], W=[prev.b])
                    S.op("dve", lambda: V.tensor_tensor(out=prev.t[:, :], in0=prev.t[:, :], in1=bkS.t[:, :], op=ALU.add),
                         R=[prev.b, bkS.b], W=[prev.b])
                    S.op("act", lambda: A.copy(out=prevb.t[:, :], in_=prev.t[:, :]), R=[prev.b], W=[prevb.b])
            if seg + 1 < NSEG:
                S.dma("sp", hstate[d, g], prev.t[:, :], R=[prev.b], W=[hstate_b[d][g]])
            else:
                bkf = ps()
                fns = [lambda k4=k4: PE.transpose(bkf.t[:, k4 * 128:(k4 + 1) * 128],
                                                  prev.t[:, k4 * 128:(k4 + 1) * 128], ident.t[:, :])
                       for k4 in range(4)]
                S.group("pe", fns, R=[prev.b, ident.b], W=[bkf.b])
                S.op("act", lambda: A.copy(out=t1.t[:, :], in_=bkf.t[:, :]), R=[bkf.b], W=[t1.b])
                S.dma("sp", ssm_p[d][g * 512:(g + 1) * 512, :].rearrange("(rt p) n -> p rt n", p=128),
                      t1.t[:, :].rearrange("p (a b) -> p a b", a=4), R=[t1.b])

            slots = wblock(w_in[d], g * 512)
            for (c0, n) in FMB:
                tc0 = c0 - 3
                for ct in range(4):
                    bk = mm_fm(slots, ct, c0, n)
                    S.op("act", lambda: A.activation(out=sqf.t[:, 0:n], in_=bk.t[:, 0:n], func=AF.Silu),
                         R=[bk.b], W=[sqf.b])
                    S.op("dve", lambda: V.tensor_tensor(out=yb.t[:, ct, tc0:tc0 + n], in0=yb.t[:, ct, tc0:tc0 + n],
                                                        in1=sqf.t[:, 0:n], op=ALU.mult), R=[yb.b, sqf.b], W=[yb.b])
                    S.op("act", lambda: A.activation(out=sqf.t[:, 0:n], in_=yb.t[:, ct, tc0:tc0 + n], func=AF.Square),
                         R=[yb.b, sqf.b], W=[sqf.b])
                    S.op("pe", lambda: PE.matmul(accA.t[:, 0:n], lhsT=ones.t[:, :], rhs=sqf.t[:, 0:n], start=(ct == 0),
                                                 stop=(ct == 3)), R=[sqf.b, ones.b], W=[accA.b])
                S.op("act", lambda: A.activation(out=t1.t[:, 0:n], in_=accA.t[:, 0:n], func=AF.Sqrt, scale=1.0 / 512,
                                                 bias=epsb.t[:, 0:1]), R=[accA.b, t1.b, epsb.b], W=[t1.b])
                S.op("dve", lambda: V.reciprocal(out=t1.t[:, 0:n], in_=t1.t[:, 0:n]), R=[t1.b], W=[t1.b])
                for ct in range(4):
                    gc = pc + O_GSSD + g * 4 + ct
                    S.op("dve", lambda: V.scalar_tensor_tensor(out=yb.t[:, ct, tc0:tc0 + n], in0=yb.t[:, ct, tc0:tc0 + n],
                                                               scalar=PT.t[:, gc:gc + 1], in1=t1.t[:, 0:n],
                                                               op0=ALU.mult, op1=ALU.mult),
                         R=[yb.b, PT.b, t1.b], W=[yb.b])
            if d == 0 and g == 0 and seg == 0:
                dump_yb("dbg_yssd0")
            outproj(w_out[d], g * 512)

    def cm_phase(d, seg, sc):
        pc = d * 200
        L = lambda name, shape, dt, nb=1: TL(name + "_%d_%d" % (d, seg), shape, dt, nb=nb, sc=sc)
        vg = L("vg", [128, NT, 2048], BF16)
        gx = L("gx", [128, 512], F32)
        ga = L("ga", [128, 512], F32)
        ssq = L("ssq", [128, NT, 4], F32)
        Wn = L("Wn", [128, 4, 128], F32)
        WsT = L("WsT", [128, 4, 128], F32)
        Wr = L("Wr", [128, 128], BF16)
        WsS = L("WsS", [64, 16, 64], F32)
        bsb = L("bsb", [128, 4, 192], F32)
        tmpc = L("tmpc", [128, 128], F32)
        gcb = L("gcb", [64, 512], F32)
        vso = L("vso", [64, 512], F32)
        W4n = L("W4n", [4, 16, 4], F32)
        bs4 = L("bs4", [128, 4, 4], F32)
        o1s = L("o1s", [4, 512], F32)
        for j in range(4):
            slots = wblock(w_in[d], I4 + j * 512)

            def cons(i, bk, j=j):
                r = rows(i)
                gelu_psum(bk, r, 512, vg.t[0:r, i, j * 512:(j + 1) * 512], vg.b, gx, ga)
                S.op("act", lambda: A.activation(out=ga.t[0:r, :], in_=vg.t[0:r, i, j * 512:(j + 1) * 512],
                                                 func=AF.Square, accum_out=ssq.t[0:r, i, j:j + 1]),
                     R=[vg.b, ga.b], W=[ga.b, ssq.b])
            gemm_tm(slots, nT.t, nT.b, ncol, cons)
        for i in range(NT):
            r = rows(i)
            S.op("dve", lambda: V.tensor_reduce(out=stat.t[0:r, 8 * i + 4:8 * i + 5], in_=ssq.t[0:r, i, :], axis=AX.X,
                                                op=ALU.add), R=[ssq.b, stat.bufs[i]], W=[stat.bufs[i]])
            rstd_of(i, 2048)
        iS = NCH
        for q in range(4):
            S.dma("sp", gcb.t[:, :], g_cm_d[d][q * 512:(q + 1) * 512].partition_broadcast(64), W=[gcb.b])
            S.op("dve", lambda: V.scalar_tensor_tensor(out=vso.t[:, :], in0=vg.t[0:TS, iS, q * 512:(q + 1) * 512],
                                                       scalar=stat.t[0:TS, 8 * iS + 6:8 * iS + 7], in1=gcb.t[:, :],
                                                       op0=ALU.mult, op1=ALU.mult),
                 R=[vg.b, stat.bufs[iS], gcb.b], W=[vso.b])
            S.dma("sp", v_s[d, seg][:, q * 512:(q + 1) * 512], vso.t[:, :], R=[vso.b])
        S.dma("sp", W4n.t[0:4, :, :], w_s_d[d][:, 0:4, 0:4].rearrange("g t s -> t g s"), W=[W4n.b])
        for hf in range(2):
            bk = ps()
            fns = [lambda g8=g8: PE.matmul(bk.t[0:4, g8 * 64:(g8 + 1) * 64], lhsT=W4n.t[0:4, hf * 8 + g8, :],
                                           rhs=Rm.t[0:4, :], start=True, stop=True) for g8 in range(8)]
            S.group("pe", fns, R=[W4n.b, Rm.b], W=[bk.b])
            S.op("act", lambda: A.copy(out=o1s.t[0:4, :], in_=bk.t[0:4, :]), R=[bk.b], W=[o1s.b])
            bk2 = ps()
            S.op("pe", lambda: PE.matmul(bk2.t[0:64, :], lhsT=Rm.t[0:4, :], rhs=o1s.t[0:4, :], start=True, stop=True),
                 R=[Rm.b, o1s.b], W=[bk2.b])
            S.op("dve", lambda: V.tensor_tensor(out=WsS.t[:, hf * 8:(hf + 1) * 8, :],
                                                in0=bk2.t[0:64, :].rearrange("p (a b) -> p a b", a=8),
                                                in1=maskS.t[:, :].unsqueeze(1).to_broadcast([64, 8, 64]), op=ALU.mult),
                 R=[bk2.b] + CONSTS, W=[WsS.b])
        for j in range(4):
            S.dma("sp", Wn.t[:, :, :], w_s_d[d][j * 4:(j + 1) * 4].rearrange("g t s -> t g s"), W=[Wn.b])
            bk = ps()
            fns = [lambda k=k: PE.transpose(bk.t[:, k * 128:(k + 1) * 128], Wn.t[:, k, :], ident.t[:, :])
                   for k in range(4)]
            S.group("pe", fns, R=[Wn.b, ident.b], W=[bk.b])
            S.op("dve", lambda: V.tensor_tensor(out=WsT.t[:, :, :], in0=bk.t[:, :].rearrange("p (a b) -> p a b", a=4),
                                                in1=tri.t[:, :].unsqueeze(1).to_broadcast([128, 4, 128]), op=ALU.mult),
                 R=[bk.b] + CONSTS, W=[WsT.b])
            S.dma("sp", bsb.t[:, :, 0:128], b_s_d[d][j * 4:(j + 1) * 4, :].partition_broadcast(128), W=[bsb.b])
            S.dma("sp", bs4.t[:, :, :], b_s_d[d][j * 4:(j + 1) * 4, 0:4].partition_broadcast(128), W=[bs4.b])
            S.op("dve", lambda: V.tensor_copy(out=bsb.t[:, :, 128:192].rearrange("p g (b l) -> p g b l", l=4),
                                              in_=bs4.t[:, :, :].unsqueeze(2).to_broadcast([128, 4, NSS, 4])),
                 R=[bs4.b, bsb.b], W=[bsb.b])
            slots = wblock(w_in[d], I3 + j * 512)
            for ct in range(4):
                for (c0, n) in FMB:
                    bk = mm_fm(slots, ct, c0, n)
                    gelu_psum(bk, 128, n, yb.t[:, ct, c0 - 3:c0 - 3 + n], yb.b, gx, ga)
            for ct in range(4):
                gg = j * 4 + ct
                for i in range(NT):
                    r = rows(i)
                    c = tcol(i)
                    rs = stat.t[0:r, 8 * i + 6:8 * i + 7]
                    if i < NCH:
                        S.op("dve", lambda: V.tensor_scalar(out=Wr.t[0:r, 0:r], in0=WsT.t[0:r, ct, 0:r], scalar1=rs,
                                                            scalar2=None, op0=ALU.mult),
                             R=[WsT.b, stat.bufs[i]], W=[Wr.b])
                        boff = 0
                    else:
                        S.op("dve", lambda: V.tensor_scalar(out=Wr.t[0:r, 0:r], in0=WsS.t[0:r, gg, 0:r], scalar1=rs,
                                                            scalar2=None, op0=ALU.mult),
                             R=[WsS.b, stat.bufs[i]], W=[Wr.b])
                        boff = 128
                    bk = ps()
                    S.op("pe", lambda: PE.matmul(bk.t[:, 0:r], lhsT=vg.t[0:r, i, gg * 128:(gg + 1) * 128],
                                                 rhs=Wr.t[0:r, 0:r], start=True, stop=True), R=[vg.b, Wr.b], W=[bk.b])
                    gcol = pc + O_GCM + gg
                    S.op("dve", lambda: V.scalar_tensor_tensor(out=tmpc.t[:, 0:r], in0=bk.t[:, 0:r],
                                                               scalar=PT.t[:, gcol:gcol + 1],
                                                               in1=bsb.t[:, ct, boff:boff + r], op0=ALU.mult,
                                                               op1=ALU.add), R=[bk.b, PT.b, bsb.b], W=[tmpc.b])
                    S.op("dve", lambda: V.tensor_tensor(out=yb.t[:, ct, c:c + r], in0=yb.t[:, ct, c:c + r],
                                                        in1=tmpc.t[:, 0:r], op=ALU.mult), R=[yb.b, tmpc.b], W=[yb.b])
            if d == 0 and j == 0 and seg == 0:
                dump_yb("dbg_ycm0")
            outproj(w_out[d], 2048 + j * 512)

    def ffn_phase(d, seg, sc):
        pc = d * 200
        sg = TL("sg_%d_%d" % (d, seg), [128, 512], F32, sc=sc)
        norm_to_nT(pc + O_GFFN)
        for blk in range(DFF // 512):
            slots = wblock(w_gate[d], blk * 512)
            for ct in range(4):
                for (c0, n) in FMB:
                    bk = mm_fm(slots, ct, c0, n)
                    S.op("act", lambda: A.activation(out=yb.t[:, ct, c0 - 3:c0 - 3 + n], in_=bk.t[:, 0:n], func=AF.Silu),
                         R=[bk.b], W=[yb.b])
            slots = wblock(w_up[d], blk * 512)
            for ct in range(4):
                for (c0, n) in FMB:
                    bk = mm_fm(slots, ct, c0, n)
                    S.op("dve", lambda: V.tensor_tensor(out=yb.t[:, ct, c0 - 3:c0 - 3 + n],
                                                        in0=yb.t[:, ct, c0 - 3:c0 - 3 + n], in1=bk.t[:, 0:n],
                                                        op=ALU.mult), R=[bk.b, yb.b], W=[yb.b])
            outproj(w_down[d], blk * 512)

    def ple_phase(d, seg, sc):
        pc = d * 200
        L = lambda name, shape, dt, nb=1: TL(name + "_%d_%d" % (d, seg), shape, dt, nb=nb, sc=sc)
        pT = L("pT", [128, 2, T], BF16)
        ptok = L("ptok", [128, DPLE], F32)
        gsig = L("gsig", [128, 512], F32)
        for i in range(NT):
            r = rows(i)
            c = tcol(i)
            S.dma("sp", ptok.t[0:r, :], pin[d, seg, c:c + r, :], W=[ptok.b])
            bk = ps()
            fns = [lambda k=k: PE.transpose(bk.t[:, k * 128:k * 128 + r], ptok.t[0:r, k * 128:(k + 1) * 128],
                                            ident.t[0:r, 0:r]) for k in range(2)]
            S.group("pe", fns, R=[ptok.b, ident.b], W=[bk.b])
            S.op("act", lambda: A.copy(out=pT.t[:, :, c:c + r],
                                       in_=bk.t[:, 0:256].rearrange("p (a b) -> p a b", a=2)[:, :, 0:r]),
                 R=[bk.b], W=[pT.b])
        norm_to_nT(pc + O_GPG)
        wpl = L("wpl", [128, 2, D], BF16)
        S.dma("pool", wpl.t[:, :, :], w_ple[d].rearrange("(kc p) n -> p kc n", p=128), W=[wpl.b])
        for cb in range(4):
            slots = wblock(w_pg[d], cb * 512)

            def cons(i, bk, cb=cb):
                r = rows(i)
                c = tcol(i)
                S.op("act", lambda: A.activation(out=gsig.t[0:r, :], in_=bk.t[0:r, :], func=AF.Sigmoid),
                     R=[bk.b], W=[gsig.b])
                bkp = ps()
                fns = [lambda k=k: PE.matmul(bkp.t[0:r, :], lhsT=pT.t[:, k, c:c + r],
                                             rhs=wpl.t[:, k, cb * 512:(cb + 1) * 512], start=(k == 0), stop=(k == 1))
                       for k in range(2)]
                S.group("pe", fns, R=[pT.b, wpl.b], W=[bkp.b])
                S.op("dve", lambda: V.tensor_tensor(out=gsig.t[0:r, :], in0=gsig.t[0:r, :], in1=bkp.t[0:r, :],
                                                    op=ALU.mult), R=[gsig.b, bkp.b], W=[gsig.b])
                S.op("dve", lambda: V.tensor_tensor(out=h.t[0:r, i, cb * 512:(cb + 1) * 512],
                                                    in0=h.t[0:r, i, cb * 512:(cb + 1) * 512], in1=gsig.t[0:r, :],
                                                    op=ALU.add), R=[gsig.b, h.bufs[i]], W=[h.bufs[i]])
            gemm_tm(slots, nT.t, nT.b, ncol, cons)

    gfb = TL("gfb", [128, 512], F32)
    for d in range(DEPTH):
        for seg in range(NSEG):
            for i in range(NT):
                r = rows(i)
                if d == 0:
                    S.dma("sp", h.t[0:r, i, :], xin[seg, tcol(i):tcol(i) + r, :], W=[h.bufs[i]])
                else:
                    S.dma("sp", h.t[0:r, i, :], hbuf[seg, tcol(i):tcol(i) + r, :], R=[hbuf_b[seg][i]], W=[h.bufs[i]])
            for phase in (ssd_phase, cm_phase, ffn_phase, ple_phase):
                with ExitStack() as sc:
                    phase(d, seg, sc)
                    barrier()
            if d + 1 < DEPTH:
                for i in range(NT):
                    r = rows(i)
                    S.dma("sp", hbuf[seg, tcol(i):tcol(i) + r, :], h.t[0:r, i, :], R=[h.bufs[i]], W=[hbuf_b[seg][i]])
            else:
                for i in range(NT):
                    r = rows(i)
                    o = 8 * i
                    sumsq_h(i)
                    for q in range(4):
                        S.dma("sp", gfb.t[0:r, :], g_final_d[0][q * 512:(q + 1) * 512].partition_broadcast(r),
                              W=[gfb.b])
                        S.op("dve", lambda: V.scalar_tensor_tensor(out=h.t[0:r, i, q * 512:(q + 1) * 512],
                                                                   in0=h.t[0:r, i, q * 512:(q + 1) * 512],
                                                                   scalar=stat.t[0:r, o + 6:o + 7], in1=gfb.t[0:r, :],
                                                                   op0=ALU.mult, op1=ALU.mult),
                             R=[h.bufs[i], stat.bufs[i], gfb.b], W=[h.bufs[i]])
                    S.dma("sp", yout[seg, tcol(i):tcol(i) + r, :], h.t[0:r, i, :], R=[h.bufs[i]])
    barrier()
    es.close()
    return nc


def _prep_inputs(inp, NTP, DEPTH, ncores):
    f = lambda a: np.ascontiguousarray(np.asarray(a, dtype=np.float32))
    xp_, xs_ = f(inp["x_prompt"]), f(inp["x_sample"])
    pp_, ps_ = f(inp["p_prompt"]), f(inp["p_sample"])
    sst, scv = f(inp["state_ssm"]), f(inp["state_conv"])
    rows_ = []
    for d in range(DEPTH):
        rows_ += [f(inp["g_mix"])[d].reshape(16, 128), f(inp["g_ffn"])[d].reshape(16, 128),
                  f(inp["g_pg"])[d].reshape(16, 128), f(inp["g_ssd"])[d].reshape(16, 128),
                  f(inp["g_cm"])[d].reshape(16, 128), f(inp["conv_w"])[d].reshape(4 * 24, 128),
                  f(inp["conv_b"])[d].reshape(24, 128)]
    rows_.append(f(inp["g_final"]).reshape(16, 128))
    prm = np.ascontiguousarray(np.concatenate(rows_, axis=0))
    smallp = np.ascontiguousarray(np.concatenate([f(inp["dt_bias"]), f(inp["a_log"]), f(inp["d_skip"])], axis=1))
    shared = {"prm": prm, "smallp": smallp, "g_cm": f(inp["g_cm"]), "w_s": f(inp["w_s"]), "b_s": f(inp["b_s"]),
              "w_in": f(inp["w_in"]), "w_out": f(inp["w_out"]), "w_gate": f(inp["w_gate"]), "w_up": f(inp["w_up"]),
              "w_down": f(inp["w_down"]), "w_pg": f(inp["w_pg"]), "w_ple": f(inp["w_ple"]),
              "g_final": f(inp["g_final"]).reshape(1, 2048)}
    maps = []
    for c in range(ncores):
        sq = c % 4
        m = dict(shared)
        xin, pin, ss, sc_ = [], [], [], []
        for seg in range(2):
            sl = slice(seg * NTP, (seg + 1) * NTP)
            b0 = (sq * 2 + seg) * NSS
            bs = slice(b0, b0 + NSS)
            xin.append(np.concatenate([xp_[sq, sl], xs_[bs].reshape(TS, D)], axis=0))
            pin.append(np.concatenate([pp_[:, sq, sl], ps_[:, bs].reshape(DEPTH, TS, DPLE)], axis=1))
            ss.append(sst[:, bs].reshape(DEPTH, NSS, 2048, 128))
            sc_.append(scv[:, bs].reshape(DEPTH, NSS * 3, CONV))
        m["xin"] = np.ascontiguousarray(np.stack(xin, axis=0))
        m["pin"] = np.ascontiguousarray(np.stack(pin, axis=1))
        m["sst"] = np.ascontiguousarray(np.stack(ss, axis=1))
        m["scv"] = np.ascontiguousarray(np.stack(sc_, axis=1))
        maps.append(m)
    return maps


def _assemble(res, NTP, DEPTH, ncores, nb_prompt, nb_sample):
    SEQ = 2 * NTP
    y_p = np.zeros((nb_prompt, SEQ, D), np.float32)
    y_s = np.zeros((nb_sample, 4, D), np.float32)
    ssm_p = np.zeros((DEPTH, nb_prompt, 32, 64, 128), np.float32)
    conv_p = np.zeros((DEPTH, nb_prompt, 3, CONV), np.float32)
    ssm_s = np.zeros((DEPTH, nb_sample, 32, 64, 128), np.float32)
    conv_s = np.zeros((DEPTH, nb_sample, 3, CONV), np.float32)
    v_s = np.zeros((DEPTH, nb_sample, 4, 2048), np.float32)
    for c in range(4):
        r = res[c]
        sq = c
        for seg in range(2):
            b0 = (sq * 2 + seg) * NSS
            bs = slice(b0, b0 + NSS)
            y_p[sq, seg * NTP:(seg + 1) * NTP] = r["yout"][seg, :NTP]
            y_s[bs] = r["yout"][seg, NTP:].reshape(NSS, 4, D)
            ssm_s[:, bs] = r["ssm_s"][:, seg].reshape(DEPTH, NSS, 32, 64, 128)
            conv_s[:, bs] = r["conv_o"][:, seg, 3:].reshape(DEPTH, NSS, 3, CONV)
            v_s[:, bs] = r["v_s"][:, seg].reshape(DEPTH, NSS, 4, 2048)
        ssm_p[:, sq] = r["ssm_p"].reshape(DEPTH, 32, 64, 128)
        conv_p[:, sq] = r["conv_o"][:, 1, 0:3]
    return (y_p, y_s, ssm_p, conv_p, ssm_s, conv_s, v_s)


LAST_RES = None


def kernel(**inputs):
    global LAST_RES
    DEPTH = int(np.asarray(inputs["w_in"]).shape[0])
    nbp, SEQ = np.asarray(inputs["x_prompt"]).shape[:2]
    nbs = np.asarray(inputs["x_sample"]).shape[0]
    ncores = 8
    NTP = SEQ // 2
    nc = build(NTP, DEPTH)
    maps = _prep_inputs(inputs, NTP, DEPTH, ncores)
    res = run_bass_kernel_spmd(nc, maps, core_ids=list(range(ncores)))
    if DBG:
        LAST_RES = res.results
    return _assemble(res.results, NTP, DEPTH, ncores, nbp, nbs)
```

```python
import numpy as np
from contextlib import ExitStack
import concourse.bass as bass
import concourse.mybir as mybir
from concourse.bass_utils import run_bass_kernel_spmd

F32 = mybir.dt.float32
BF16 = mybir.dt.bfloat16
AF = mybir.ActivationFunctionType
ALU = mybir.AluOpType
AX = mybir.AxisListType

D = 2048
DIN = 9248
DFF = 5632
DPLE = 256
CONV = 3072
I1 = 2048
I2 = I1 + 3072
I3 = I2 + 32
I4 = I3 + 2048
EPS = 1e-6
NSS = 16
TS = 64
SAME_ENGINE_SYNC = True
RING_SSD, RING_CM, RING_FFN, RING_PLE = 6, 6, 16, 12


class Buf:
    __slots__ = ("w", "r", "name")

    def __init__(self, name=""):
        self.w = None
        self.r = {}
        self.name = name


class Sched:
    def __init__(self, nc, es, ndma=24):
        self.nc = nc
        self.eng = {"pe": nc.tensor, "dve": nc.vector, "act": nc.scalar, "pool": nc.gpsimd, "sp": nc.sync}
        self.sem = {}
        for k in ["pe", "dve", "act", "pool"]:
            self.sem[k] = es.enter_context(nc.semaphore("s_" + k))
        self.cnt = {k: 0 for k in self.sem}
        self.seen = {e: {} for e in self.eng}
        self.dsem = [es.enter_context(nc.semaphore("s_dma%d" % i)) for i in range(ndma)]
        self.dcnt = [0] * ndma
        self.drr = 0
        self.ndma = ndma

    def _semof(self, key):
        if isinstance(key, tuple):
            return self.dsem[key[1]]
        return self.sem[key]

    def _wait(self, e, deps):
        best = {}
        for d in deps:
            if d is None:
                continue
            key, val = d
            if key == e and not SAME_ENGINE_SYNC:
                continue
            if val > best.get(key, 0):
                best[key] = val
        for key, val in best.items():
            if self.seen[e].get(key, 0) >= val:
                continue
            self.eng[e].wait_ge(self._semof(key), val)
            self.seen[e][key] = val

    def _deps(self, R, W):
        deps = []
        for b in R:
            deps.append(b.w)
        for b in W:
            deps.append(b.w)
            deps.extend(b.r.values())
        return deps

    def _mark(self, tok, R, W):
        key = tok[0]
        for b in R:
            b.r[key] = tok
        for b in W:
            b.w = tok
            b.r = {}

    def op(self, e, fn, R=(), W=()):
        self._wait(e, self._deps(R, W))
        inst = fn()
        self.cnt[e] += 1
        inst.then_inc(self.sem[e], 1)
        self._mark((e, self.cnt[e]), R, W)

    def group(self, e, fns, R=(), W=()):
        self._wait(e, self._deps(R, W))
        inst = None
        for fn in fns:
            inst = fn()
        self.cnt[e] += 1
        inst.then_inc(self.sem[e], 1)
        self._mark((e, self.cnt[e]), R, W)

    def dma(self, q, out, in_, R=(), W=(), **kw):
        i = self.drr
        self.drr = (i + 1) % self.ndma
        deps = self._deps(R, W)
        if self.dcnt[i] > 0:
            deps.append((("dma", i), self.dcnt[i]))
        self._wait(q, deps)
        inst = self.eng[q].dma_start(out=out, in_=in_, **kw)
        self.dcnt[i] += 16
        inst.then_inc(self.dsem[i], 16)
        self._mark((("dma", i), self.dcnt[i]), R, W)

    def drain(self, e="sp"):
        deps = [(("dma", i), self.dcnt[i]) for i in range(self.ndma) if self.dcnt[i] > 0]
        deps += [(k, self.cnt[k]) for k in self.cnt if self.cnt[k] > 0]
        self._wait(e, deps)


def split_cols(c0, n, mx=512):
    out = []
    while n > 0:
        k = min(mx, n)
        out.append((c0, k))
        c0 += k
        n -= k
    return out


DBG = False


def build(NTP, DEPTH, stop_after=None):
    NCH = NTP // 128
    NT = NCH + 1
    T = NTP + TS
    TC = 3 + T
    nc = bass.Bass("TRN2", target_bir_lowering=False)

    def din(name, shape):
        return nc.dram_tensor(name, list(shape), F32, kind="ExternalInput").ap()

    def dout(name, shape):
        return nc.dram_tensor(name, list(shape), F32, kind="ExternalOutput").ap()

    NSEG = 2
    xin = din("xin", [NSEG, T, D])
    pin = din("pin", [DEPTH, NSEG, T, DPLE])
    sst = din("sst", [DEPTH, NSEG, NSS, 2048, 128])
    scv = din("scv", [DEPTH, NSEG, NSS * 3, CONV])
    prm = din("prm", [DEPTH * 200 + 16, 128])
    smallp = din("smallp", [DEPTH, 96])
    g_cm_d = din("g_cm", [DEPTH, 2048])
    w_s_d = din("w_s", [DEPTH, 16, 128, 128])
    b_s_d = din("b_s", [DEPTH, 16, 128])
    w_in = din("w_in", [DEPTH, D, DIN])
    w_out = din("w_out", [DEPTH, 4096, D])
    w_gate = din("w_gate", [DEPTH, D, DFF])
    w_up = din("w_up", [DEPTH, D, DFF])
    w_down = din("w_down", [DEPTH, DFF, D])
    w_pg = din("w_pg", [DEPTH, D, D])
    w_ple = din("w_ple", [DEPTH, DPLE, D])
    g_final_d = din("g_final", [1, 2048])

    yout = dout("yout", [NSEG, T, D])
    ssm_p = dout("ssm_p", [DEPTH, 2048, 128])
    conv_o = dout("conv_o", [DEPTH, NSEG, 3 + NSS * 3, CONV])
    ssm_s = dout("ssm_s", [DEPTH, NSEG, NSS, 2048, 128])
    v_s = dout("v_s", [DEPTH, NSEG, TS, 2048])
    hbuf = nc.dram_tensor("hbuf", [NSEG, T, D], F32, kind="Internal").ap()
    hstate = nc.dram_tensor("hstate", [DEPTH, 4, 128, 512], F32, kind="Internal").ap()
    hbuf_b = [[Buf() for _ in range(NTP // 128 + 1)] for _ in range(NSEG)]
    hstate_b = [[Buf() for _ in range(4)] for _ in range(DEPTH)]

    es = ExitStack()
    S = Sched(nc, es)

    class TL:
        def __init__(self, name, shape, dt, nb=1, psum=False, sc=None):
            f = nc.psum_tensor if psum else nc.sbuf_tensor
            self.t = (sc or es).enter_context(f(name, list(shape), dt))
            self.bufs = [Buf(name + str(i)) for i in range(nb)]
            self.b = self.bufs[0]

    def rows(i):
        return 128 if i < NCH else TS

    def ncol(i):
        return 3 + i * 128

    def tcol(i):
        return i * 128

    h = TL("h", [128, NT, D], F32, nb=NT)
    nT = TL("nT", [128, 16, TC], BF16)
    ring = []
    ring_i = [0]

    def make_ring(sc, n, tag):
        ring[:] = [TL("ring%s_%d" % (tag, k), [128, 2048], BF16, sc=sc) for k in range(n)]
        ring_i[0] = 0
    scr = TL("scr", [128, 512], F32)
    PT = TL("PT", [128, DEPTH * 200 + 16], F32)
    ident = TL("ident", [128, 128], F32)
    tri = TL("tri", [128, 128], F32)
    ustr = TL("ustr", [128, 128], F32)
    ones = TL("ones", [128, 128], F32)
    maskS = TL("maskS", [64, 64], F32)
    blk1 = TL("blk1", [64, 64], F32)
    rmask = TL("rmask", [64, 16], F32)
    maskB3 = TL("maskB3", [128, 16, 64], BF16)
    smb = TL("smb", [128, 96], F32)
    CONSTS = [ident.b, tri.b, ustr.b, ones.b, maskS.b, blk1.b, rmask.b, maskB3.b]

    banks = [TL("bank%d" % i, [128, 512], F32, psum=True) for i in range(8)]
    NROT = 6
    rot = [0]

    def ps():
        b = banks[rot[0]]
        rot[0] = (rot[0] + 1) % NROT
        return b

    accA = banks[6]
    accB = banks[7]

    V = nc.vector
    A = nc.scalar
    PE = nc.tensor

    def const_tri(tl, n, cmp, base=0, mult=1, pat=-1):
        def f():
            nc.gpsimd.memset(tl.t[:], 1.0)
            return nc.gpsimd.affine_select(out=tl.t[:], in_=tl.t[:], pattern=[[pat, n]], compare_op=cmp,
                                           fill=0.0, base=base, channel_multiplier=mult)
        S.group("pool", [f], W=[tl.b])

    const_tri(ident, 128, ALU.is_equal)
    const_tri(tri, 128, ALU.is_ge, pat=1, mult=-1)
    const_tri(ustr, 128, ALU.is_gt, pat=-1, mult=1)
    S.op("pool", lambda: nc.gpsimd.memset(ones.t[:], 1.0), W=[ones.b])
    Rm = TL("Rm", [4, 64], F32)

    def f_rm():
        nc.gpsimd.memset(Rm.t[:], 1.0)
        return nc.gpsimd.affine_select(out=Rm.t[:].rearrange("p (b l) -> p b l", l=4),
                                       in_=Rm.t[:].rearrange("p (b l) -> p b l", l=4), pattern=[[0, 16], [1, 4]],
                                       compare_op=ALU.is_equal, fill=0.0, base=0, channel_multiplier=-1)
    S.group("pool", [f_rm], W=[Rm.b])
    def f_rmask():
        nc.gpsimd.memset(rmask.t[:], 1.0)
        nc.gpsimd.affine_select(out=rmask.t[:], in_=rmask.t[:], pattern=[[-4, 16]], compare_op=ALU.is_ge,
                                fill=0.0, base=0, channel_multiplier=1)
        return nc.gpsimd.affine_select(out=rmask.t[:], in_=rmask.t[:], pattern=[[4, 16]], compare_op=ALU.is_ge,
                                       fill=0.0, base=3, channel_multiplier=-1)
    S.group("pool", [f_rmask], W=[rmask.b])
    with ExitStack() as sc0:
        mb3f = TL("mb3f", [128, 16, 64], F32, sc=sc0)

        def f_mb3():
            nc.gpsimd.memset(mb3f.t[:], 1.0)
            nc.gpsimd.affine_select(out=mb3f.t[:], in_=mb3f.t[:], pattern=[[-4, 16], [1, 64]], compare_op=ALU.is_ge,
                                    fill=0.0, base=0, channel_multiplier=0)
            return nc.gpsimd.affine_select(out=mb3f.t[:], in_=mb3f.t[:], pattern=[[4, 16], [-1, 64]],
                                           compare_op=ALU.is_ge, fill=0.0, base=3, channel_multiplier=0)
        S.group("pool", [f_mb3], W=[mb3f.b])
        S.op("dve", lambda: V.tensor_copy(out=maskB3.t[:, :, :], in_=mb3f.t[:, :, :]), R=[mb3f.b], W=[maskB3.b])
        for e_ in ["pe", "dve", "act", "pool", "sp"]:
            S.drain(e_)
    bk = ps()
    rmT = TL("rmT", [16, 64], F32)
    S.op("pe", lambda: PE.transpose(bk.t[0:16, 0:64], rmask.t[:], ident.t[0:64, 0:64]), R=[rmask.b, ident.b], W=[bk.b])
    S.op("dve", lambda: V.tensor_copy(out=rmT.t[:], in_=bk.t[0:16, 0:64]), R=[bk.b], W=[rmT.b])
    bk2 = ps()
    S.op("pe", lambda: PE.matmul(bk2.t[0:64, 0:64], lhsT=rmT.t[:], rhs=rmT.t[:], start=True, stop=True),
         R=[rmT.b], W=[bk2.b])
    S.op("dve", lambda: V.tensor_copy(out=blk1.t[:], in_=bk2.t[0:64, 0:64]), R=[bk2.b], W=[blk1.b])
    S.op("dve", lambda: V.tensor_tensor(out=maskS.t[:], in0=blk1.t[:], in1=tri.t[0:64, 0:64], op=ALU.mult),
         R=[blk1.b, tri.b], W=[maskS.b])

    NPR = DEPTH * 200 + 16
    O_GMIX, O_GFFN, O_GPG, O_GSSD, O_GCM, O_CW, O_CB = 0, 16, 32, 48, 64, 80, 176
    O_GFIN = DEPTH * 200
    pstg = TL("pstg", [128, 128], F32)
    for r0 in range(0, NPR, 128):
        nr = min(128, NPR - r0)
        S.dma("sp", pstg.t[0:nr, :], prm[r0:r0 + nr, :], W=[pstg.b])
        bkp = ps()
        S.op("pe", lambda: PE.transpose(bkp.t[:, 0:nr], pstg.t[0:nr, :], ident.t[0:nr, 0:nr]),
             R=[pstg.b, ident.b], W=[bkp.b])
        S.op("dve", lambda: V.tensor_copy(out=PT.t[:, r0:r0 + nr], in_=bkp.t[:, 0:nr]), R=[bkp.b], W=[PT.b])

    ntail = TL("ntail", [128, 16, 3], BF16)

    yb = TL("yb", [128, 4, T], BF16)
    dbg_t = {}

    def dump_yb(name):
        if not DBG:
            return
        dd = dout(name, [128, 4 * T])
        for ct in range(4):
            for (c0, n) in split_cols(0, T):
                S.op("dve", lambda: V.tensor_copy(out=scr.t[:, 0:n], in_=yb.t[:, ct, c0:c0 + n]), R=[yb.b], W=[scr.b])
                S.dma("sp", dd[:, ct * T + c0:ct * T + c0 + n], scr.t[:, 0:n], R=[scr.b])
    epsb = TL("epsb", [128, 1], F32)
    S.op("dve", lambda: V.memset(epsb.t[:, :], EPS), W=[epsb.b])
    stat = TL("stat", [128, NT * 8], F32, nb=NT)

    def barrier():
        for e in ["pe", "dve", "act", "pool", "sp"]:
            S.drain(e)

    def wload(src_ap):
        sl = ring[ring_i[0]]
        ring_i[0] = (ring_i[0] + 1) % len(ring)
        a, b = src_ap.shape[1], src_ap.shape[2]
        view = sl.t[:, 0:a * b].rearrange("p (a b) -> p a b", a=a)
        S.dma("pool", view, src_ap, W=[sl.b])
        return view, sl.b

    def wblock(w2d, c0, ncols=512, K=2048, r0=0):
        out = []
        for s in range(K // 512):
            src = w2d[r0 + s * 512:r0 + (s + 1) * 512, c0:c0 + ncols].rearrange("(kc p) n -> p kc n", p=128)
            out.append(wload(src))
        return out

    def mm_fm(slots, ct, c0, n):
        bk = ps()
        KC = 4 * len(slots)
        fns = []
        for kc in range(KC):
            v = slots[kc // 4][0]
            fns.append(lambda kc=kc, v=v: PE.matmul(bk.t[:, 0:n], lhsT=v[:, kc % 4, ct * 128:(ct + 1) * 128],
                                                    rhs=nT.t[:, kc, c0:c0 + n], start=(kc == 0), stop=(kc == KC - 1)))
        S.group("pe", fns, R=[nT.b] + [b for _, b in slots], W=[bk.b])
        return bk

    def gemm_tm(slots, src, src_buf, colof, consumer, ncols=512):
        KC = 4 * len(slots)
        for i in range(NT):
            bk = ps()
            r = rows(i)
            c = colof(i)
            fns = []
            for kc in range(KC):
                v = slots[kc // 4][0]
                fns.append(lambda kc=kc, v=v: PE.matmul(bk.t[0:r, 0:ncols], lhsT=src[:, kc, c:c + r],
                                                        rhs=v[:, kc % 4, 0:ncols], start=(kc == 0),
                                                        stop=(kc == KC - 1)))
            S.group("pe", fns, R=[src_buf] + [b for _, b in slots], W=[bk.b])
            consumer(i, bk)

    def rstd_of(i, n_el):
        r = rows(i)
        o = 8 * i
        S.op("act", lambda: A.activation(out=stat.t[0:r, o + 5:o + 6], in_=stat.t[0:r, o + 4:o + 5], func=AF.Sqrt,
                                         scale=1.0 / n_el, bias=epsb.t[0:r, 0:1]), R=[stat.bufs[i], epsb.b],
             W=[stat.bufs[i]])
        S.op("dve", lambda: V.reciprocal(out=stat.t[0:r, o + 6:o + 7], in_=stat.t[0:r, o + 5:o + 6]),
             R=[stat.bufs[i]], W=[stat.bufs[i]])

    def sumsq_h(i):
        r = rows(i)
        o = 8 * i
        for q in range(4):
            S.op("act", lambda: A.activation(out=scr.t[0:r, :], in_=h.t[0:r, i, q * 512:(q + 1) * 512], func=AF.Square,
                                             accum_out=stat.t[0:r, o + q:o + q + 1]),
                 R=[h.bufs[i]], W=[scr.b, stat.bufs[i]])
        S.op("dve", lambda: V.tensor_reduce(out=stat.t[0:r, o + 4:o + 5], in_=stat.t[0:r, o:o + 4], axis=AX.X,
                                            op=ALU.add), R=[stat.bufs[i]], W=[stat.bufs[i]])
        rstd_of(i, D)

    def norm_to_nT(gcol):
        for i in range(NT):
            r = rows(i)
            o = 8 * i
            sumsq_h(i)
            for q in range(4):
                S.op("dve", lambda: V.tensor_scalar(out=scr.t[0:r, :], in0=h.t[0:r, i, q * 512:(q + 1) * 512],
                                                    scalar1=stat.t[0:r, o + 6:o + 7], scalar2=None, op0=ALU.mult),
                     R=[h.bufs[i], stat.bufs[i]], W=[scr.b])
                bk = ps()
                fns = [lambda kk=kk: PE.transpose(bk.t[:, kk * 128:kk * 128 + r], scr.t[0:r, kk * 128:(kk + 1) * 128],
                                                  ident.t[0:r, 0:r]) for kk in range(4)]
                S.group("pe", fns, R=[scr.b, ident.b], W=[bk.b])
                c = ncol(i)
                S.op("dve", lambda: V.tensor_tensor(
                    out=nT.t[:, q * 4:(q + 1) * 4, c:c + r],
                    in0=bk.t[:, :].rearrange("p (a b) -> p a b", a=4)[:, :, 0:r],
                    in1=PT.t[:, gcol + q * 4:gcol + q * 4 + 4].unsqueeze(2).to_broadcast([128, 4, r]),
                    op=ALU.mult), R=[bk.b, PT.b], W=[nT.b])

    def gelu_psum(bk, r, n, out_ap, out_buf, tmpx, tmpa):
        S.op("act", lambda: A.copy(out=tmpx.t[0:r, 0:n], in_=bk.t[0:r, 0:n]), R=[bk.b], W=[tmpx.b])
        S.op("dve", lambda: V.scalar_tensor_tensor(out=tmpa.t[0:r, 0:n], in0=tmpx.t[0:r, 0:n], scalar=0.044715,
                                                   in1=tmpx.t[0:r, 0:n], op0=ALU.mult, op1=ALU.mult),
             R=[tmpx.b], W=[tmpa.b])
        S.op("dve", lambda: V.scalar_tensor_tensor(out=tmpa.t[0:r, 0:n], in0=tmpa.t[0:r, 0:n], scalar=1.0,
                                                   in1=tmpx.t[0:r, 0:n], op0=ALU.add, op1=ALU.mult),
             R=[tmpx.b, tmpa.b], W=[tmpa.b])
        S.op("act", lambda: A.activation(out=tmpa.t[0:r, 0:n], in_=tmpa.t[0:r, 0:n], func=AF.Sigmoid,
                                         scale=1.5957691216), R=[tmpa.b], W=[tmpa.b])
        S.op("dve", lambda: V.tensor_tensor(out=out_ap, in0=tmpx.t[0:r, 0:n], in1=tmpa.t[0:r, 0:n], op=ALU.mult),
             R=[tmpx.b, tmpa.b], W=[out_buf])

    FMB_H = split_cols(0, TC)
    FMB = split_cols(3, T)

    def outproj(wsrc2d, r0):
        for cb in range(4):
            src = wsrc2d[r0:r0 + 512, cb * 512:(cb + 1) * 512].rearrange("(kc p) n -> p kc n", p=128)
            sl = wload(src)

            def cons(i, bk, cb=cb):
                r = rows(i)
                S.op("dve", lambda: V.tensor_tensor(out=h.t[0:r, i, cb * 512:(cb + 1) * 512],
                                                    in0=h.t[0:r, i, cb * 512:(cb + 1) * 512], in1=bk.t[0:r, :],
                                                    op=ALU.add), R=[bk.b, h.bufs[i]], W=[h.bufs[i]])
            gemm_tm([sl], yb.t, yb.b, tcol, cons)

    def ssd_phase(d, seg, sc):
        pc = d * 200
        L = lambda name, shape, dt, nb=1: TL(name + "_%d_%d" % (d, seg), shape, dt, nb=nb, sc=sc)
        make_ring(sc, RING_SSD, "s%d_%d" % (d, seg))
        Wdt = L("Wdt", [128, 16, 32], BF16)
        dtv = L("dtv", [128, NT, 32], F32)
        dtA = L("dtA", [128, NT, 32], F32)
        eac = L("eac", [128, NT, 32], F32)
        dtdec = L("dtdec", [128, NT, 32], F32)
        cdec = L("cdec", [128, NT, 32], F32)
        tmp32 = L("tmp32", [128, 64], F32)
        xp = L("xp", [128, TC], F32)
        xps = L("xps", [128, NSS, 7], F32)
        hsq = L("hsq", [128, 4, NSS * 3], F32)
        cvst = L("cvst", [48, 512], F32)
        cvo1 = L("cvo1", [128, 3 + NSS * 3], F32)
        cvt1 = L("cvt1", [64, 128], F32)
        cacc = L("cacc", [128, T], F32)
        BTg = L("BTg", [128, T], BF16)
        CTg = L("CTg", [128, T], BF16)
        Btok = L("Btok", [128, NT, 128], BF16)
        xtok = L("xtok", [128, NT, 512], BF16)
        cbm = L("cbm", [128, 128], F32)
        rh = L("rh", [128, 4, 128], F32)
        Mh = L("Mh", [128, 8, 128], BF16)
        xdt = L("xdt", [128, 512], BF16)
        xdts = L("xdts", [128, 512], BF16)
        prev = L("prev", [128, 512], F32)
        prevb = L("prevb", [128, 512], BF16)
        t1 = L("t1", [128, 512], F32)
        sqf = L("sqf", [128, 512], F32)
        Hs = [L("Hs%d" % k, [128, 4, 128], F32) for k in range(2)]
        HTb = L("HTb", [128, 512], BF16)
        Cmb = L("Cmb", [128, 64], BF16)
        Bmb = L("Bmb", [64, 128], BF16)
        dtArep = L("dtArep", [64, 512], F32)
        dAT = L("dAT", [128, 4, NSS], F32)

        norm_to_nT(pc + O_GMIX)
        if seg == 0:
            S.op("dve", lambda: V.memset(nT.t[:, :, 0:3], 0.0), W=[nT.b])
        else:
            S.op("dve", lambda: V.tensor_copy(out=nT.t[:, :, 0:3], in_=ntail.t[:, :, :]), R=[ntail.b], W=[nT.b])
        if seg + 1 < NSEG:
            S.op("dve", lambda: V.tensor_copy(out=ntail.t[:, :, :], in_=nT.t[:, :, NTP:NTP + 3]), R=[nT.b],
                 W=[ntail.b])
        S.dma("sp", smb.t[:, :], smallp[d].partition_broadcast(128), W=[smb.b])
        S.op("act", lambda: A.activation(out=smb.t[:, 32:64], in_=smb.t[:, 32:64], func=AF.Exp), R=[smb.b], W=[smb.b])
        S.op("dve", lambda: V.tensor_scalar(out=smb.t[:, 32:64], in0=smb.t[:, 32:64], scalar1=-1.0, scalar2=None,
                                            op0=ALU.mult), R=[smb.b], W=[smb.b])
        S.dma("pool", Wdt.t[:, :, :], w_in[d][:, I2:I3].rearrange("(kc p) n -> p kc n", p=128), W=[Wdt.b])
        for i in range(NT):
            r = rows(i)
            c = ncol(i)
            bk = ps()
            fns = [lambda kc=kc: PE.matmul(bk.t[0:r, 0:32], lhsT=nT.t[:, kc, c:c + r], rhs=Wdt.t[:, kc, :],
                                           start=(kc == 0), stop=(kc == 15)) for kc in range(16)]
            S.group("pe", fns, R=[nT.b, Wdt.b], W=[bk.b])
            S.op("dve", lambda: V.tensor_tensor(out=tmp32.t[0:r, 0:32], in0=bk.t[0:r, 0:32], in1=smb.t[0:r, 0:32],
                                                op=ALU.add), R=[bk.b, smb.b], W=[tmp32.b])
            S.op("act", lambda: A.activation(out=tmp32.t[0:r, 0:32], in_=tmp32.t[0:r, 0:32], func=AF.Exp),
                 R=[tmp32.b], W=[tmp32.b])
            S.op("act", lambda: A.activation(out=dtv.t[0:r, i, :], in_=tmp32.t[0:r, 0:32], func=AF.Ln, bias=1.0),
                 R=[tmp32.b], W=[dtv.b])
            S.op("dve", lambda: V.tensor_tensor(out=dtA.t[0:r, i, :], in0=dtv.t[0:r, i, :], in1=smb.t[0:r, 32:64],
                                                op=ALU.mult), R=[dtv.b, smb.b], W=[dtA.b])
            bk2 = ps()
            mk = tri if i < NCH else maskS
            on = ones if i < NCH else blk1
            S.group("pe", [lambda: PE.matmul(bk2.t[0:r, 0:32], lhsT=mk.t[0:r, 0:r], rhs=dtA.t[0:r, i, :], start=True,
                                             stop=True),
                           lambda: PE.matmul(bk2.t[0:r, 32:64], lhsT=on.t[0:r, 0:r], rhs=dtA.t[0:r, i, :], start=True,
                                             stop=True)], R=[dtA.b] + CONSTS, W=[bk2.b])
            S.op("act", lambda: A.activation(out=eac.t[0:r, i, :], in_=bk2.t[0:r, 0:32], func=AF.Exp),
                 R=[bk2.b], W=[eac.b])
            S.op("act", lambda: A.activation(out=cdec.t[0:r, i, :], in_=bk2.t[0:r, 32:64], func=AF.Exp),
                 R=[bk2.b], W=[cdec.b])
            S.op("act", lambda: A.copy(out=tmp32.t[0:r, 32:64], in_=bk2.t[0:r, 32:64]), R=[bk2.b], W=[tmp32.b])
            S.op("dve", lambda: V.tensor_tensor(out=tmp32.t[0:r, 32:64], in0=tmp32.t[0:r, 32:64], in1=bk2.t[0:r, 0:32],
                                                op=ALU.subtract), R=[bk2.b, tmp32.b], W=[tmp32.b])
            S.op("act", lambda: A.activation(out=tmp32.t[0:r, 32:64], in_=tmp32.t[0:r, 32:64], func=AF.Exp),
                 R=[tmp32.b], W=[tmp32.b])
            S.op("dve", lambda: V.tensor_tensor(out=dtdec.t[0:r, i, :], in0=tmp32.t[0:r, 32:64], in1=dtv.t[0:r, i, :],
                                                op=ALU.mult), R=[tmp32.b, dtv.b], W=[dtdec.b])

        def prep_hist(ct0, nct):
            S.dma("sp", cvst.t[0:48, 0:nct * 128], scv[d, seg][:, ct0 * 128:(ct0 + nct) * 128], W=[cvst.b])
            for k in range(nct):
                bk = ps()
                S.op("pe", lambda: PE.transpose(bk.t[:, 0:48], cvst.t[0:48, k * 128:(k + 1) * 128], ident.t[0:48, 0:48]),
                     R=[cvst.b, ident.b], W=[bk.b])
                S.op("act", lambda: A.copy(out=hsq.t[:, k, :], in_=bk.t[:, 0:48]), R=[bk.b], W=[hsq.b])

        def conv_tile(ctg, k, kind):
            cw = pc + O_CW
            cb_ = pc + O_CB + ctg
            S.op("dve", lambda: V.tensor_copy(out=xps.t[:, :, 0:3],
                                              in_=hsq.t[:, k, :].rearrange("p (b j) -> p b j", j=3)),
                 R=[hsq.b], W=[xps.b])
            S.op("dve", lambda: V.tensor_copy(out=xps.t[:, :, 3:7],
                                              in_=xp.t[:, 3 + NTP:3 + NTP + TS].rearrange("p (b l) -> p b l", l=4)),
                 R=[xp.b, xps.b], W=[xps.b])
            S.op("act", lambda: A.copy(out=cvo1.t[:, 0:3], in_=xp.t[:, NTP:NTP + 3]), R=[xp.b], W=[cvo1.b])
            S.op("act", lambda: A.copy(out=cvo1.t[:, 3:3 + NSS * 3].rearrange("p (b j) -> p b j", j=3),
                                       in_=xps.t[:, :, 4:7]), R=[xps.b, cvo1.b], W=[cvo1.b])
            bk = ps()
            S.op("pe", lambda: PE.transpose(bk.t[0:51, 0:128], cvo1.t[:, :], ident.t[:, :]), R=[cvo1.b, ident.b],
                 W=[bk.b])
            S.op("act", lambda: A.copy(out=cvt1.t[0:51, :], in_=bk.t[0:51, 0:128]), R=[bk.b], W=[cvt1.b])
            S.dma("sp", conv_o[d, seg][:, ctg * 128:(ctg + 1) * 128], cvt1.t[0:51, :], R=[cvt1.b])
            S.op("dve", lambda: V.tensor_scalar(out=cacc.t[:, 0:NTP], in0=xp.t[:, 0:NTP],
                                                scalar1=PT.t[:, cw + ctg:cw + ctg + 1], scalar2=None, op0=ALU.mult),
                 R=[xp.b, PT.b], W=[cacc.b])
            for j in range(1, 4):
                S.op("dve", lambda: V.scalar_tensor_tensor(out=cacc.t[:, 0:NTP], in0=xp.t[:, j:j + NTP],
                                                           scalar=PT.t[:, cw + j * 24 + ctg:cw + j * 24 + ctg + 1],
                                                           in1=cacc.t[:, 0:NTP], op0=ALU.mult, op1=ALU.add),
                     R=[xp.b, PT.b, cacc.b], W=[cacc.b])
            cs = cacc.t[:, NTP:T].rearrange("p (b l) -> p b l", l=4)
            S.op("dve", lambda: V.tensor_scalar(out=cs, in0=xps.t[:, :, 0:4], scalar1=PT.t[:, cw + ctg:cw + ctg + 1],
                                                scalar2=None, op0=ALU.mult), R=[xps.b, PT.b, cacc.b], W=[cacc.b])
            for j in range(1, 4):
                S.op("dve", lambda: V.scalar_tensor_tensor(out=cs, in0=xps.t[:, :, j:j + 4],
                                                           scalar=PT.t[:, cw + j * 24 + ctg:cw + j * 24 + ctg + 1],
                                                           in1=cs, op0=ALU.mult, op1=ALU.add),
                     R=[xps.b, PT.b, cacc.b], W=[cacc.b])
            if kind == "C":
                S.op("act", lambda: A.activation(out=CTg.t[:, :], in_=cacc.t[:, :], func=AF.Silu,
                                                 bias=PT.t[:, cb_:cb_ + 1]), R=[cacc.b, PT.b], W=[CTg.b])
            else:
                S.op("act", lambda: A.activation(out=cacc.t[:, :], in_=cacc.t[:, :], func=AF.Silu,
                                                 bias=PT.t[:, cb_:cb_ + 1]), R=[cacc.b, PT.b], W=[cacc.b])
            if kind == "B":
                S.op("dve", lambda: V.tensor_copy(out=BTg.t[:, :], in_=cacc.t[:, :]), R=[cacc.b], W=[BTg.b])

        def to_tok(dst, width_off):
            for i0 in range(0, NT, 4):
                bk = ps()
                tiles = list(range(i0, min(NT, i0 + 4)))
                fns = [lambda i=i: PE.transpose(bk.t[0:rows(i), (i - i0) * 128:(i - i0 + 1) * 128],
                                                cacc.t[:, tcol(i):tcol(i) + rows(i)], ident.t[:, :]) for i in tiles]
                S.group("pe", fns, R=[cacc.b, ident.b], W=[bk.b])
                full = [i for i in tiles if rows(i) == 128]
                if full:
                    nf = len(full)
                    S.op("act", lambda: A.copy(out=dst.t[:, full[0]:full[0] + nf, width_off:width_off + 128],
                                               in_=bk.t[:, 0:nf * 128].rearrange("p (a b) -> p a b", a=nf)),
                         R=[bk.b], W=[dst.b])
                for i in tiles:
                    if rows(i) != 128:
                        S.op("act", lambda: A.copy(out=dst.t[0:TS, i, width_off:width_off + 128],
                                                   in_=bk.t[0:TS, (i - i0) * 128:(i - i0 + 1) * 128]),
                             R=[bk.b], W=[dst.b])

        def sample_states(g, bko):
            iS = NCH
            cS = tcol(iS)
            hsl = slice(g * 8, (g + 1) * 8)
            S.op("dve", lambda: V.tensor_copy(out=dtArep.t[:, :].rearrange("p (a b) -> p a b", a=8),
                                              in_=dtA.t[0:TS, iS, hsl].unsqueeze(2).to_broadcast([TS, 8, 64])),
                 R=[dtA.b], W=[dtArep.b])
            bkd = ps()
            fns = [lambda rt=rt: PE.matmul(bkd.t[:, rt * NSS:(rt + 1) * NSS], lhsT=dtArep.t[:, rt * 128:(rt + 1) * 128],
                                           rhs=rmask.t[:, :], start=True, stop=True) for rt in range(4)]
            S.group("pe", fns, R=[dtArep.b] + CONSTS, W=[bkd.b])
            S.op("act", lambda: A.activation(out=dAT.t[:, :, :],
                                             in_=bkd.t[:, 0:4 * NSS].rearrange("p (a b) -> p a b", a=4),
                                             func=AF.Exp), R=[bkd.b], W=[dAT.b])
            for b in range(NSS):
                H = Hs[b % 2]
                S.dma("sp", H.t[:, :, :], sst[d, seg, b][g * 512:(g + 1) * 512, :].rearrange("(rt p) n -> p rt n", p=128),
                      W=[H.b])
                S.op("dve", lambda: V.tensor_tensor(out=Cmb.t[:, :], in0=CTg.t[:, cS:cS + TS], in1=maskB3.t[:, b, :],
                                                    op=ALU.mult), R=[CTg.b] + CONSTS, W=[Cmb.b])
                S.op("dve", lambda: V.tensor_scalar(out=Bmb.t[:, :], in0=Btok.t[0:TS, iS, :],
                                                    scalar1=rmask.t[:, b:b + 1], scalar2=None, op0=ALU.mult),
                     R=[Btok.b] + CONSTS, W=[Bmb.b])
                bkt = ps()
                fns = [lambda rt=rt: PE.transpose(bkt.t[:, rt * 128:(rt + 1) * 128], H.t[:, rt, :], ident.t[:, :])
                       for rt in range(4)]
                S.group("pe", fns, R=[H.b, ident.b], W=[bkt.b])
                S.op("act", lambda: A.copy(out=HTb.t[:, :], in_=bkt.t[:, :]), R=[bkt.b], W=[HTb.b])
                S.op("pe", lambda: PE.matmul(bko.t[0:TS, :], lhsT=Cmb.t[:, :], rhs=HTb.t[:, :], start=(b == 0),
                                             stop=(b == NSS - 1)), R=[Cmb.b, HTb.b], W=[bko.b])
                bku = ps()
                fns = [lambda rt=rt: PE.matmul(bku.t[:, rt * 128:(rt + 1) * 128],
                                               lhsT=xdts.t[0:TS, rt * 128:(rt + 1) * 128], rhs=Bmb.t[:, :], start=True,
                                               stop=True) for rt in range(4)]
                S.group("pe", fns, R=[xdts.b, Bmb.b], W=[bku.b])
                S.op("dve", lambda: V.tensor_tensor(out=H.t[:, :, :], in0=H.t[:, :, :],
                                                    in1=dAT.t[:, :, b:b + 1].to_broadcast([128, 4, 128]), op=ALU.mult),
                     R=[H.b, dAT.b], W=[H.b])
                S.op("dve", lambda: V.tensor_tensor(out=H.t[:, :, :], in0=H.t[:, :, :],
                                                    in1=bku.t[:, :].rearrange("p (a b) -> p a b", a=4), op=ALU.add),
                     R=[H.b, bku.b], W=[H.b])
                S.dma("sp", ssm_s[d, seg, b][g * 512:(g + 1) * 512, :].rearrange("(rt p) n -> p rt n", p=128), H.t[:, :, :],
                      R=[H.b])

        for g in range(4):
            for kind, c0w, ctg in (("B", I1 + 2048 + g * 128, 16 + g), ("C", I1 + 2560 + g * 128, 20 + g)):
                slv, slb = wload(w_in[d][:, c0w:c0w + 128].rearrange("(kc p) n -> p kc n", p=128))
                prep_hist(ctg, 1)
                for (c0, n) in FMB_H:
                    bk = ps()
                    fns = [lambda kc=kc: PE.matmul(bk.t[:, 0:n], lhsT=slv[:, kc, :], rhs=nT.t[:, kc, c0:c0 + n],
                                                   start=(kc == 0), stop=(kc == 15)) for kc in range(16)]
                    S.group("pe", fns, R=[nT.b, slb], W=[bk.b])
                    S.op("act", lambda: A.copy(out=xp.t[:, c0:c0 + n], in_=bk.t[:, 0:n]), R=[bk.b], W=[xp.b])
                conv_tile(ctg, 0, kind)
                if kind == "B":
                    to_tok(Btok, 0)
            slots = wblock(w_in[d], I1 + g * 512)
            prep_hist(g * 4, 4)
            for ct in range(4):
                for (c0, n) in FMB_H:
                    bk = mm_fm(slots, ct, c0, n)
                    S.op("act", lambda: A.copy(out=xp.t[:, c0:c0 + n], in_=bk.t[:, 0:n]), R=[bk.b], W=[xp.b])
                conv_tile(g * 4 + ct, ct, "x")
                to_tok(xtok, ct * 128)

            if seg == 0:
                S.op("dve", lambda: V.memset(prev.t[:, :], 0.0), W=[prev.b])
            else:
                S.dma("sp", prev.t[:, :], hstate[d, g], R=[hstate_b[d][g]], W=[prev.b])
            S.op("act", lambda: A.copy(out=prevb.t[:, :], in_=prev.t[:, :]), R=[prev.b], W=[prevb.b])
            hsl = slice(g * 8, (g + 1) * 8)
            for i in range(NT):
                r = rows(i)
                c = tcol(i)
                mk = tri if i < NCH else maskS
                bkc = ps()
                S.op("pe", lambda: PE.matmul(bkc.t[0:r, 0:r], lhsT=BTg.t[:, c:c + r], rhs=CTg.t[:, c:c + r], start=True,
                                             stop=True), R=[BTg.b, CTg.b], W=[bkc.b])
                S.op("dve", lambda: V.tensor_tensor(out=cbm.t[0:r, 0:r], in0=bkc.t[0:r, 0:r], in1=mk.t[0:r, 0:r],
                                                    op=ALU.mult), R=[bkc.b] + CONSTS, W=[cbm.b])
                for h4 in range(2):
                    bks = ps()
                    for hh in range(4):
                        hd = g * 8 + h4 * 4 + hh
                        S.op("dve", lambda: V.tensor_scalar(out=rh.t[0:r, hh, 0:r], in0=tri.t[0:r, 0:r],
                                                            scalar1=dtA.t[0:r, i, hd:hd + 1], scalar2=None,
                                                            op0=ALU.mult), R=[dtA.b] + CONSTS, W=[rh.b])
                    fns = [lambda hh=hh: PE.matmul(bks.t[0:r, hh * 128:hh * 128 + r], lhsT=ustr.t[0:r, 0:r],
                                                   rhs=rh.t[0:r, hh, 0:r], start=True, stop=True) for hh in range(4)]
                    S.group("pe", fns, R=[rh.b] + CONSTS, W=[bks.b])
                    S.op("act", lambda: A.activation(out=rh.t[0:r, :, 0:r],
                                                     in_=bks.t[0:r, :].rearrange("p (a b) -> p a b", a=4)[:, :, 0:r],
                                                     func=AF.Exp), R=[bks.b, rh.b], W=[rh.b])
                    S.op("dve", lambda: V.tensor_tensor(out=Mh.t[0:r, h4 * 4:h4 * 4 + 4, 0:r], in0=rh.t[0:r, :, 0:r],
                                                        in1=cbm.t[0:r, 0:r].unsqueeze(1).to_broadcast([r, 4, r]),
                                                        op=ALU.mult), R=[rh.b, cbm.b], W=[Mh.b])
                xv = xtok.t[0:r, i, :].rearrange("p (a b) -> p a b", a=8)
                S.op("dve", lambda: V.tensor_tensor(out=xdt.t[0:r, :].rearrange("p (a b) -> p a b", a=8), in0=xv,
                                                    in1=dtv.t[0:r, i, hsl].unsqueeze(2).to_broadcast([r, 8, 64]),
                                                    op=ALU.mult), R=[xtok.b, dtv.b], W=[xdt.b])
                S.op("dve", lambda: V.tensor_tensor(out=xdts.t[0:r, :].rearrange("p (a b) -> p a b", a=8), in0=xv,
                                                    in1=dtdec.t[0:r, i, hsl].unsqueeze(2).to_broadcast([r, 8, 64]),
                                                    op=ALU.mult), R=[xtok.b, dtdec.b], W=[xdts.b])
                bky = ps() if i < NCH else accA
                fns = [lambda hh=hh: PE.matmul(bky.t[0:r, hh * 64:(hh + 1) * 64], lhsT=Mh.t[0:r, hh, 0:r],
                                               rhs=xdt.t[0:r, hh * 64:(hh + 1) * 64], start=True, stop=True)
                       for hh in range(8)]
                S.group("pe", fns, R=[Mh.b, xdt.b], W=[bky.b])
                if i < NCH:
                    bko = ps()
                    S.op("pe", lambda: PE.matmul(bko.t[0:r, :], lhsT=CTg.t[:, c:c + r], rhs=prevb.t[:, :], start=True,
                                                 stop=True), R=[CTg.b, prevb.b], W=[bko.b])
                else:
                    bko = accB
                    sample_states(g, bko)
                S.op("dve", lambda: V.tensor_tensor(out=t1.t[0:r, :].rearrange("p (a b) -> p a b", a=8),
                                                    in0=bko.t[0:r, :].rearrange("p (a b) -> p a b", a=8),
                                                    in1=eac.t[0:r, i, hsl].unsqueeze(2).to_broadcast([r, 8, 64]),
                                                    op=ALU.mult), R=[bko.b, eac.b], W=[t1.b])
                S.op("dve", lambda: V.tensor_tensor(out=t1.t[0:r, :], in0=t1.t[0:r, :], in1=bky.t[0:r, :], op=ALU.add),
                     R=[bky.b, t1.b], W=[t1.b])
                S.op("dve", lambda: V.tensor_tensor(out=sqf.t[0:r, :].rearrange("p (a b) -> p a b", a=8), in0=xv,
                                                    in1=smb.t[0:r, 64 + g * 8:64 + g * 8 + 8].unsqueeze(2).to_broadcast(
                                                        [r, 8, 64]), op=ALU.mult), R=[xtok.b, smb.b], W=[sqf.b])
                S.op("dve", lambda: V.tensor_tensor(out=t1.t[0:r, :], in0=t1.t[0:r, :], in1=sqf.t[0:r, :], op=ALU.add),
                     R=[t1.b, sqf.b], W=[t1.b])
                bkt = ps()
                fns = [lambda k4=k4: PE.transpose(bkt.t[:, k4 * 128:k4 * 128 + r], t1.t[0:r, k4 * 128:(k4 + 1) * 128],
                                                  ident.t[0:r, 0:r]) for k4 in range(4)]
                S.group("pe", fns, R=[t1.b, ident.b], W=[bkt.b])
                S.op("act", lambda: A.copy(out=yb.t[:, :, c:c + r],
                                           in_=bkt.t[:, :].rearrange("p (a b) -> p a b", a=4)[:, :, 0:r]),
                     R=[bkt.b], W=[yb.b])
                if i < NCH:
                    bkS = ps()
                    S.op("pe", lambda: PE.matmul(bkS.t[:, :], lhsT=Btok.t[0:r, i, :], rhs=xdts.t[0:r, :], start=True,
                                                 stop=True), R=[Btok.b, xdts.b], W=[bkS.b])
                    S.op("dve", lambda: V.tensor_tensor(out=prev.t[:, :].rearrange("p (a b) -> p a b", a=8),
                                                        in0=prev.t[:, :].rearrange("p (a b) -> p a b", a=8),
                                                        in1=cdec.t[:, i, hsl].unsqueeze(2).to_broadcast([128, 8, 64]),
                                                        op=ALU.mult), R=[prev.b, cdec.b], W=[prev.b])
                    S.op("dve", lambda: V.tensor_tensor(out=prev.t[:, :], in0=prev.t[:, :], in1=bkS.t[:, :], op=ALU.add),
                         R=[prev.b, bkS.b], W=[prev.b])
                    S.op("act", lambda: A.copy(out=prevb.t[:, :], in_=prev.t[:, :]), R=[prev.b], W=[prevb.b])
            if seg + 1 < NSEG:
                S.dma("sp", hstate[d, g], prev.t[:, :], R=[prev.b], W=[hstate_b[d][g]])
            else:
                bkf = ps()
                fns = [lambda k4=k4: PE.transpose(bkf.t[:, k4 * 128:(k4 + 1) * 128],
                                                  prev.t[:, k4 * 128:(k4 + 1) * 128], ident.t[:, :])
                       for k4 in range(4)]
                S.group("pe", fns, R=[prev.b, ident.b], W=[bkf.b])
                S.op("act", lambda: A.copy(out=t1.t[:, :], in_=bkf.t[:, :]), R=[bkf.b], W=[t1.b])
                S.dma("sp", ssm_p[d][g * 512:(g + 1) * 512, :].rearrange("(rt p) n -> p rt n", p=128),
                      t1.t[:, :].rearrange("p (a b) -> p a b", a=4), R=[t1.b])

            slots = wblock(w_in[d], g * 512)
            for (c0, n) in FMB:
                tc0 = c0 - 3
                for ct in range(4):
                    bk = mm_fm(slots, ct, c0, n)
                    S.op("act", lambda: A.activation(out=sqf.t[:, 0:n], in_=bk.t[:, 0:n], func=AF.Silu),
                         R=[bk.b], W=[sqf.b])
                    S.op("dve", lambda: V.tensor_tensor(out=yb.t[:, ct, tc0:tc0 + n], in0=yb.t[:, ct, tc0:tc0 + n],
                                                        in1=sqf.t[:, 0:n], op=ALU.mult), R=[yb.b, sqf.b], W=[yb.b])
                    S.op("act", lambda: A.activation(out=sqf.t[:, 0:n], in_=yb.t[:, ct, tc0:tc0 + n], func=AF.Square),
                         R=[yb.b, sqf.b], W=[sqf.b])
                    S.op("pe", lambda: PE.matmul(accA.t[:, 0:n], lhsT=ones.t[:, :], rhs=sqf.t[:, 0:n], start=(ct == 0),
                                                 stop=(ct == 3)), R=[sqf.b, ones.b], W=[accA.b])
                S.op("act", lambda: A.activation(out=t1.t[:, 0:n], in_=accA.t[:, 0:n], func=AF.Sqrt, scale=1.0 / 512,
                                                 bias=epsb.t[:, 0:1]), R=[accA.b, t1.b, epsb.b], W=[t1.b])
                S.op("dve", lambda: V.reciprocal(out=t1.t[:, 0:n], in_=t1.t[:, 0:n]), R=[t1.b], W=[t1.b])
                for ct in range(4):
                    gc = pc + O_GSSD + g * 4 + ct
                    S.op("dve", lambda: V.scalar_tensor_tensor(out=yb.t[:, ct, tc0:tc0 + n], in0=yb.t[:, ct, tc0:tc0 + n],
                                                               scalar=PT.t[:, gc:gc + 1], in1=t1.t[:, 0:n],
                                                               op0=ALU.mult, op1=ALU.mult),
                         R=[yb.b, PT.b, t1.b], W=[yb.b])
            if d == 0 and g == 0 and seg == 0:
                dump_yb("dbg_yssd0")
            outproj(w_out[d], g * 512)

    def cm_phase(d, seg, sc):
        pc = d * 200
        L = lambda name, shape, dt, nb=1: TL(name + "_%d_%d" % (d, seg), shape, dt, nb=nb, sc=sc)
        make_ring(sc, RING_CM, "c%d_%d" % (d, seg))
        vg = L("vg", [128, NT, 2048], BF16)
        gx = L("gx", [128, 512], F32)
        ga = L("ga", [128, 512], F32)
        ssq = L("ssq", [128, NT, 4], F32)
        Wn = L("Wn", [128, 4, 128], F32)
        WsT = L("WsT", [128, 4, 128], F32)
        Wr = L("Wr", [128, 128], BF16)
        WsS = L("WsS", [64, 16, 64], F32)
        bsb = L("bsb", [128, 4, 192], F32)
        tmpc = L("tmpc", [128, 128], F32)
        gcb = L("gcb", [64, 512], F32)
        vso = L("vso", [64, 512], F32)
        W4n = L("W4n", [4, 16, 4], F32)
        bs4 = L("bs4", [128, 4, 4], F32)
        o1s = gx
        for j in range(4):
            slots = wblock(w_in[d], I4 + j * 512)

            def cons(i, bk, j=j):
                r = rows(i)
                gelu_psum(bk, r, 512, vg.t[0:r, i, j * 512:(j + 1) * 512], vg.b, gx, ga)
                S.op("act", lambda: A.activation(out=ga.t[0:r, :], in_=vg.t[0:r, i, j * 512:(j + 1) * 512],
                                                 func=AF.Square, accum_out=ssq.t[0:r, i, j:j + 1]),
                     R=[vg.b, ga.b], W=[ga.b, ssq.b])
            gemm_tm(slots, nT.t, nT.b, ncol, cons)
        for i in range(NT):
            r = rows(i)
            S.op("dve", lambda: V.tensor_reduce(out=stat.t[0:r, 8 * i + 4:8 * i + 5], in_=ssq.t[0:r, i, :], axis=AX.X,
                                                op=ALU.add), R=[ssq.b, stat.bufs[i]], W=[stat.bufs[i]])
            rstd_of(i, 2048)
        iS = NCH
        for q in range(4):
            S.dma("sp", gcb.t[:, :], g_cm_d[d][q * 512:(q + 1) * 512].partition_broadcast(64), W=[gcb.b])
            S.op("dve", lambda: V.scalar_tensor_tensor(out=vso.t[:, :], in0=vg.t[0:TS, iS, q * 512:(q + 1) * 512],
                                                       scalar=stat.t[0:TS, 8 * iS + 6:8 * iS + 7], in1=gcb.t[:, :],
                                                       op0=ALU.mult, op1=ALU.mult),
                 R=[vg.b, stat.bufs[iS], gcb.b], W=[vso.b])
            S.dma("sp", v_s[d, seg][:, q * 512:(q + 1) * 512], vso.t[:, :], R=[vso.b])
        S.dma("sp", W4n.t[0:4, :, :], w_s_d[d][:, 0:4, 0:4].rearrange("g t s -> t g s"), W=[W4n.b])
        for hf in range(2):
            bk = ps()
            fns = [lambda g8=g8: PE.matmul(bk.t[0:4, g8 * 64:(g8 + 1) * 64], lhsT=W4n.t[0:4, hf * 8 + g8, :],
                                           rhs=Rm.t[0:4, :], start=True, stop=True) for g8 in range(8)]
            S.group("pe", fns, R=[W4n.b, Rm.b], W=[bk.b])
            S.op("act", lambda: A.copy(out=o1s.t[0:4, :], in_=bk.t[0:4, :]), R=[bk.b], W=[o1s.b])
            bk2 = ps()
            S.op("pe", lambda: PE.matmul(bk2.t[0:64, :], lhsT=Rm.t[0:4, :], rhs=o1s.t[0:4, :], start=True, stop=True),
                 R=[Rm.b, o1s.b], W=[bk2.b])
            S.op("dve", lambda: V.tensor_tensor(out=WsS.t[:, hf * 8:(hf + 1) * 8, :],
                                                in0=bk2.t[0:64, :].rearrange("p (a b) -> p a b", a=8),
                                                in1=maskS.t[:, :].unsqueeze(1).to_broadcast([64, 8, 64]), op=ALU.mult),
                 R=[bk2.b] + CONSTS, W=[WsS.b])
        for j in range(4):
            S.dma("sp", Wn.t[:, :, :], w_s_d[d][j * 4:(j + 1) * 4].rearrange("g t s -> t g s"), W=[Wn.b])
            bk = ps()
            fns = [lambda k=k: PE.transpose(bk.t[:, k * 128:(k + 1) * 128], Wn.t[:, k, :], ident.t[:, :])
                   for k in range(4)]
            S.group("pe", fns, R=[Wn.b, ident.b], W=[bk.b])
            S.op("dve", lambda: V.tensor_tensor(out=WsT.t[:, :, :], in0=bk.t[:, :].rearrange("p (a b) -> p a b", a=4),
                                                in1=tri.t[:, :].unsqueeze(1).to_broadcast([128, 4, 128]), op=ALU.mult),
                 R=[bk.b] + CONSTS, W=[WsT.b])
            S.dma("sp", bsb.t[:, :, 0:128], b_s_d[d][j * 4:(j + 1) * 4, :].partition_broadcast(128), W=[bsb.b])
            S.dma("sp", bs4.t[:, :, :], b_s_d[d][j * 4:(j + 1) * 4, 0:4].partition_broadcast(128), W=[bs4.b])
            S.op("dve", lambda: V.tensor_copy(out=bsb.t[:, :, 128:192].rearrange("p g (b l) -> p g b l", l=4),
                                              in_=bs4.t[:, :, :].unsqueeze(2).to_broadcast([128, 4, NSS, 4])),
                 R=[bs4.b, bsb.b], W=[bsb.b])
            slots = wblock(w_in[d], I3 + j * 512)
            for ct in range(4):
                for (c0, n) in FMB:
                    bk = mm_fm(slots, ct, c0, n)
                    gelu_psum(bk, 128, n, yb.t[:, ct, c0 - 3:c0 - 3 + n], yb.b, gx, ga)
            for ct in range(4):
                gg = j * 4 + ct
                for i in range(NT):
                    r = rows(i)
                    c = tcol(i)
                    rs = stat.t[0:r, 8 * i + 6:8 * i + 7]
                    if i < NCH:
                        S.op("dve", lambda: V.tensor_scalar(out=Wr.t[0:r, 0:r], in0=WsT.t[0:r, ct, 0:r], scalar1=rs,
                                                            scalar2=None, op0=ALU.mult),
                             R=[WsT.b, stat.bufs[i]], W=[Wr.b])
                        boff = 0
                    else:
                        S.op("dve", lambda: V.tensor_scalar(out=Wr.t[0:r, 0:r], in0=WsS.t[0:r, gg, 0:r], scalar1=rs,
                                                            scalar2=None, op0=ALU.mult),
                             R=[WsS.b, stat.bufs[i]], W=[Wr.b])
                        boff = 128
                    bk = ps()
                    S.op("pe", lambda: PE.matmul(bk.t[:, 0:r], lhsT=vg.t[0:r, i, gg * 128:(gg + 1) * 128],
                                                 rhs=Wr.t[0:r, 0:r], start=True, stop=True), R=[vg.b, Wr.b], W=[bk.b])
                    gcol = pc + O_GCM + gg
                    S.op("dve", lambda: V.scalar_tensor_tensor(out=tmpc.t[:, 0:r], in0=bk.t[:, 0:r],
                                                               scalar=PT.t[:, gcol:gcol + 1],
                                                               in1=bsb.t[:, ct, boff:boff + r], op0=ALU.mult,
                                                               op1=ALU.add), R=[bk.b, PT.b, bsb.b], W=[tmpc.b])
                    S.op("dve", lambda: V.tensor_tensor(out=yb.t[:, ct, c:c + r], in0=yb.t[:, ct, c:c + r],
                                                        in1=tmpc.t[:, 0:r], op=ALU.mult), R=[yb.b, tmpc.b], W=[yb.b])
            if d == 0 and j == 0 and seg == 0:
                dump_yb("dbg_ycm0")
            outproj(w_out[d], 2048 + j * 512)

    def ffn_phase(d, seg, sc):
        pc = d * 200
        make_ring(sc, RING_FFN, "f%d_%d" % (d, seg))
        norm_to_nT(pc + O_GFFN)
        for blk in range(DFF // 512):
            slots = wblock(w_gate[d], blk * 512)
            for ct in range(4):
                for (c0, n) in FMB:
                    bk = mm_fm(slots, ct, c0, n)
                    S.op("act", lambda: A.activation(out=yb.t[:, ct, c0 - 3:c0 - 3 + n], in_=bk.t[:, 0:n], func=AF.Silu),
                         R=[bk.b], W=[yb.b])
            slots = wblock(w_up[d], blk * 512)
            for ct in range(4):
                for (c0, n) in FMB:
                    bk = mm_fm(slots, ct, c0, n)
                    S.op("dve", lambda: V.tensor_tensor(out=yb.t[:, ct, c0 - 3:c0 - 3 + n],
                                                        in0=yb.t[:, ct, c0 - 3:c0 - 3 + n], in1=bk.t[:, 0:n],
                                                        op=ALU.mult), R=[bk.b, yb.b], W=[yb.b])
            outproj(w_down[d], blk * 512)

    def ple_phase(d, seg, sc):
        pc = d * 200
        L = lambda name, shape, dt, nb=1: TL(name + "_%d_%d" % (d, seg), shape, dt, nb=nb, sc=sc)
        make_ring(sc, RING_PLE, "p%d_%d" % (d, seg))
        pT = L("pT", [128, 2, T], BF16)
        ptok = L("ptok", [128, DPLE], F32)
        gsig = L("gsig", [128, 512], F32)
        for i in range(NT):
            r = rows(i)
            c = tcol(i)
            S.dma("sp", ptok.t[0:r, :], pin[d, seg, c:c + r, :], W=[ptok.b])
            bk = ps()
            fns = [lambda k=k: PE.transpose(bk.t[:, k * 128:k * 128 + r], ptok.t[0:r, k * 128:(k + 1) * 128],
                                            ident.t[0:r, 0:r]) for k in range(2)]
            S.group("pe", fns, R=[ptok.b, ident.b], W=[bk.b])
            S.op("act", lambda: A.copy(out=pT.t[:, :, c:c + r],
                                       in_=bk.t[:, 0:256].rearrange("p (a b) -> p a b", a=2)[:, :, 0:r]),
                 R=[bk.b], W=[pT.b])
        norm_to_nT(pc + O_GPG)
        wpl = L("wpl", [128, 2, D], BF16)
        S.dma("pool", wpl.t[:, :, :], w_ple[d].rearrange("(kc p) n -> p kc n", p=128), W=[wpl.b])
        for cb in range(4):
            slots = wblock(w_pg[d], cb * 512)

            def cons(i, bk, cb=cb):
                r = rows(i)
                c = tcol(i)
                S.op("act", lambda: A.activation(out=gsig.t[0:r, :], in_=bk.t[0:r, :], func=AF.Sigmoid),
                     R=[bk.b], W=[gsig.b])
                bkp = ps()
                fns = [lambda k=k: PE.matmul(bkp.t[0:r, :], lhsT=pT.t[:, k, c:c + r],
                                             rhs=wpl.t[:, k, cb * 512:(cb + 1) * 512], start=(k == 0), stop=(k == 1))
                       for k in range(2)]
                S.group("pe", fns, R=[pT.b, wpl.b], W=[bkp.b])
                S.op("dve", lambda: V.tensor_tensor(out=gsig.t[0:r, :], in0=gsig.t[0:r, :], in1=bkp.t[0:r, :],
                                                    op=ALU.mult), R=[gsig.b, bkp.b], W=[gsig.b])
                S.op("dve", lambda: V.tensor_tensor(out=h.t[0:r, i, cb * 512:(cb + 1) * 512],
                                                    in0=h.t[0:r, i, cb * 512:(cb + 1) * 512], in1=gsig.t[0:r, :],
                                                    op=ALU.add), R=[gsig.b, h.bufs[i]], W=[h.bufs[i]])
            gemm_tm(slots, nT.t, nT.b, ncol, cons)

    for d in range(DEPTH):
        for seg in range(NSEG):
            for i in range(NT):
                r = rows(i)
                if d == 0:
                    S.dma("sp", h.t[0:r, i, :], xin[seg, tcol(i):tcol(i) + r, :], W=[h.bufs[i]])
                else:
                    S.dma("sp", h.t[0:r, i, :], hbuf[seg, tcol(i):tcol(i) + r, :], R=[hbuf_b[seg][i]], W=[h.bufs[i]])
            for phase in (ssd_phase, cm_phase, ffn_phase, ple_phase):
                with ExitStack() as sc:
                    phase(d, seg, sc)
                    barrier()
            if d + 1 < DEPTH:
                for i in range(NT):
                    r = rows(i)
                    S.dma("sp", hbuf[seg, tcol(i):tcol(i) + r, :], h.t[0:r, i, :], R=[h.bufs[i]], W=[hbuf_b[seg][i]])
            else:
                with ExitStack() as sc_fin:
                    gfb = TL("gfb_%d" % seg, [128, 512], F32, sc=sc_fin)
                    for i in range(NT):
                        r = rows(i)
                        o = 8 * i
                        sumsq_h(i)
                        for q in range(4):
                            S.dma("sp", gfb.t[0:r, :], g_final_d[0][q * 512:(q + 1) * 512].partition_broadcast(r),
                                  W=[gfb.b])
                            S.op("dve", lambda: V.scalar_tensor_tensor(out=h.t[0:r, i, q * 512:(q + 1) * 512],
                                                                       in0=h.t[0:r, i, q * 512:(q + 1) * 512],
                                                                       scalar=stat.t[0:r, o + 6:o + 7], in1=gfb.t[0:r, :],
                                                                       op0=ALU.mult, op1=ALU.mult),
                                 R=[h.bufs[i], stat.bufs[i], gfb.b], W=[h.bufs[i]])
                        S.dma("sp", yout[seg, tcol(i):tcol(i) + r, :], h.t[0:r, i, :], R=[h.bufs[i]])
                    barrier()
    barrier()
    es.close()
    return nc


def _prep_inputs(inp, NTP, DEPTH, ncores):
    f = lambda a: np.ascontiguousarray(np.asarray(a, dtype=np.float32))
    xp_, xs_ = f(inp["x_prompt"]), f(inp["x_sample"])
    pp_, ps_ = f(inp["p_prompt"]), f(inp["p_sample"])
    sst, scv = f(inp["state_ssm"]), f(inp["state_conv"])
    rows_ = []
    for d in range(DEPTH):
        rows_ += [f(inp["g_mix"])[d].reshape(16, 128), f(inp["g_ffn"])[d].reshape(16, 128),
                  f(inp["g_pg"])[d].reshape(16, 128), f(inp["g_ssd"])[d].reshape(16, 128),
                  f(inp["g_cm"])[d].reshape(16, 128), f(inp["conv_w"])[d].reshape(4 * 24, 128),
                  f(inp["conv_b"])[d].reshape(24, 128)]
    rows_.append(f(inp["g_final"]).reshape(16, 128))
    prm = np.ascontiguousarray(np.concatenate(rows_, axis=0))
    smallp = np.ascontiguousarray(np.concatenate([f(inp["dt_bias"]), f(inp["a_log"]), f(inp["d_skip"])], axis=1))
    shared = {"prm": prm, "smallp": smallp, "g_cm": f(inp["g_cm"]), "w_s": f(inp["w_s"]), "b_s": f(inp["b_s"]),
              "w_in": f(inp["w_in"]), "w_out": f(inp["w_out"]), "w_gate": f(inp["w_gate"]), "w_up": f(inp["w_up"]),
              "w_down": f(inp["w_down"]), "w_pg": f(inp["w_pg"]), "w_ple": f(inp["w_ple"]),
              "g_final": f(inp["g_final"]).reshape(1, 2048)}
    maps = []
    for c in range(ncores):
        sq = c % 4
        m = dict(shared)
        xin, pin, ss, sc_ = [], [], [], []
        for seg in range(2):
            sl = slice(seg * NTP, (seg + 1) * NTP)
            b0 = (sq * 2 + seg) * NSS
            bs = slice(b0, b0 + NSS)
            xin.append(np.concatenate([xp_[sq, sl], xs_[bs].reshape(TS, D)], axis=0))
            pin.append(np.concatenate([pp_[:, sq, sl], ps_[:, bs].reshape(DEPTH, TS, DPLE)], axis=1))
            ss.append(sst[:, bs].reshape(DEPTH, NSS, 2048, 128))
            sc_.append(scv[:, bs].reshape(DEPTH, NSS * 3, CONV))
        m["xin"] = np.ascontiguousarray(np.stack(xin, axis=0))
        m["pin"] = np.ascontiguousarray(np.stack(pin, axis=1))
        m["sst"] = np.ascontiguousarray(np.stack(ss, axis=1))
        m["scv"] = np.ascontiguousarray(np.stack(sc_, axis=1))
        maps.append(m)
    return maps


def _assemble(res, NTP, DEPTH, ncores, nb_prompt, nb_sample):
    SEQ = 2 * NTP
    y_p = np.zeros((nb_prompt, SEQ, D), np.float32)
    y_s = np.zeros((nb_sample, 4, D), np.float32)
    ssm_p = np.zeros((DEPTH, nb_prompt, 32, 64, 128), np.float32)
    conv_p = np.zeros((DEPTH, nb_prompt, 3, CONV), np.float32)
    ssm_s = np.zeros((DEPTH, nb_sample, 32, 64, 128), np.float32)
    conv_s = np.zeros((DEPTH, nb_sample, 3, CONV), np.float32)
    v_s = np.zeros((DEPTH, nb_sample, 4, 2048), np.float32)
    for c in range(4):
        r = res[c]
        sq = c
        for seg in range(2):
            b0 = (sq * 2 + seg) * NSS
            bs = slice(b0, b0 + NSS)
            y_p[sq, seg * NTP:(seg + 1) * NTP] = r["yout"][seg, :NTP]
            y_s[bs] = r["yout"][seg, NTP:].reshape(NSS, 4, D)
            ssm_s[:, bs] = r["ssm_s"][:, seg].reshape(DEPTH, NSS, 32, 64, 128)
            conv_s[:, bs] = r["conv_o"][:, seg, 3:].reshape(DEPTH, NSS, 3, CONV)
            v_s[:, bs] = r["v_s"][:, seg].reshape(DEPTH, NSS, 4, 2048)
        ssm_p[:, sq] = r["ssm_p"].reshape(DEPTH, 32, 64, 128)
        conv_p[:, sq] = r["conv_o"][:, 1, 0:3]
    return (y_p, y_s, ssm_p, conv_p, ssm_s, conv_s, v_s)


LAST_RES = None


def kernel(**inputs):
    global LAST_RES
    DEPTH = int(np.asarray(inputs["w_in"]).shape[0])
    nbp, SEQ = np.asarray(inputs["x_prompt"]).shape[:2]
    nbs = np.asarray(inputs["x_sample"]).shape[0]
    ncores = 8
    NTP = SEQ // 2
    nc = build(NTP, DEPTH)
    maps = _prep_inputs(inputs, NTP, DEPTH, ncores)
    res = run_bass_kernel_spmd(nc, maps, core_ids=list(range(ncores)))
    if DBG:
        LAST_RES = res.results
    return _assemble(res.results, NTP, DEPTH, ncores, nbp, nbs)
```

```python
import numpy as np
from contextlib import ExitStack
import concourse.bass as bass
import concourse.mybir as mybir
from concourse.bass_utils import run_bass_kernel_spmd

F32 = mybir.dt.float32
BF16 = mybir.dt.bfloat16
AF = mybir.ActivationFunctionType
ALU = mybir.AluOpType
AX = mybir.AxisListType

D = 2048
DIN = 9248
DFF = 5632
DPLE = 256
CONV = 3072
I1 = 2048
I2 = I1 + 3072
I3 = I2 + 32
I4 = I3 + 2048
EPS = 1e-6
NSS = 16
TS = 64
SAME_ENGINE_SYNC = True
RING_SSD, RING_CM, RING_FFN, RING_PLE = 5, 6, 16, 12


class Buf:
    __slots__ = ("w", "r", "name", "pend")

    def __init__(self, name=""):
        self.w = None
        self.r = {}
        self.name = name
        self.pend = False


class Sched:
    def __init__(self, nc, es, ndma=24):
        self.nc = nc
        self.eng = {"pe": nc.tensor, "dve": nc.vector, "act": nc.scalar, "pool": nc.gpsimd, "sp": nc.sync}
        self.sem = {}
        for k in ["pe", "dve", "act", "pool"]:
            self.sem[k] = es.enter_context(nc.semaphore("s_" + k))
        self.cnt = {k: 0 for k in self.sem}
        self.seen = {e: {} for e in self.eng}
        self.dsem = [es.enter_context(nc.semaphore("s_dma%d" % i)) for i in range(ndma)]
        self.dcnt = [0] * ndma
        self.drr = 0
        self.ndma = ndma

    def _semof(self, key):
        if isinstance(key, tuple):
            return self.dsem[key[1]]
        return self.sem[key]

    def _wait(self, e, deps):
        best = {}
        for d in deps:
            if d is None:
                continue
            key, val = d
            if key == e and not SAME_ENGINE_SYNC:
                continue
            if val > best.get(key, 0):
                best[key] = val
        for key, val in best.items():
            if self.seen[e].get(key, 0) >= val:
                continue
            self.eng[e].wait_ge(self._semof(key), val)
            self.seen[e][key] = val

    def _deps(self, R, W):
        deps = []
        for b in R:
            deps.append(b.w)
        for b in W:
            deps.append(b.w)
            deps.extend(b.r.values())
        return deps

    def _mark(self, tok, R, W):
        key = tok[0]
        for b in R:
            b.pend = False
            b.r[key] = tok
        for b in W:
            b.w = tok
            b.r = {}

    def op(self, e, fn, R=(), W=()):
        self._wait(e, self._deps(R, W))
        inst = fn()
        self.cnt[e] += 1
        inst.then_inc(self.sem[e], 1)
        self._mark((e, self.cnt[e]), R, W)

    def group(self, e, fns, R=(), W=()):
        self._wait(e, self._deps(R, W))
        inst = None
        for fn in fns:
            inst = fn()
        self.cnt[e] += 1
        inst.then_inc(self.sem[e], 1)
        self._mark((e, self.cnt[e]), R, W)

    def dma(self, q, out, in_, R=(), W=(), **kw):
        i = self.drr
        self.drr = (i + 1) % self.ndma
        deps = self._deps(R, W)
        if self.dcnt[i] > 0:
            deps.append((("dma", i), self.dcnt[i]))
        self._wait(q, deps)
        inst = self.eng[q].dma_start(out=out, in_=in_, **kw)
        self.dcnt[i] += 16
        inst.then_inc(self.dsem[i], 16)
        self._mark((("dma", i), self.dcnt[i]), R, W)

    def drain(self, e="sp"):
        deps = [(("dma", i), self.dcnt[i]) for i in range(self.ndma) if self.dcnt[i] > 0]
        deps += [(k, self.cnt[k]) for k in self.cnt if self.cnt[k] > 0]
        self._wait(e, deps)


def split_cols(c0, n, mx=512):
    out = []
    while n > 0:
        k = min(mx, n)
        out.append((c0, k))
        c0 += k
        n -= k
    return out


DBG = False


def build(NTP, DEPTH, stop_after=None):
    NCH = NTP // 128
    NT = NCH + 1
    T = NTP + TS
    TC = 3 + T
    nc = bass.Bass("TRN2", target_bir_lowering=False)

    def din(name, shape):
        return nc.dram_tensor(name, list(shape), F32, kind="ExternalInput").ap()

    def dout(name, shape):
        return nc.dram_tensor(name, list(shape), F32, kind="ExternalOutput").ap()

    NSEG = 2
    xin = din("xin", [NSEG, T, D])
    pin = din("pin", [DEPTH, NSEG, T, DPLE])
    sst = din("sst", [DEPTH, NSEG, NSS, 2048, 128])
    scv = din("scv", [DEPTH, NSEG, NSS * 3, CONV])
    prm = din("prm", [DEPTH * 200 + 16, 128])
    smallp = din("smallp", [DEPTH, 96])
    g_cm_d = din("g_cm", [DEPTH, 2048])
    w_s_d = din("w_s", [DEPTH, 16, 128, 128])
    b_s_d = din("b_s", [DEPTH, 16, 128])
    w_in = din("w_in", [DEPTH, D, DIN])
    w_out = din("w_out", [DEPTH, 4096, D])
    w_gate = din("w_gate", [DEPTH, D, DFF])
    w_up = din("w_up", [DEPTH, D, DFF])
    w_down = din("w_down", [DEPTH, DFF, D])
    w_pg = din("w_pg", [DEPTH, D, D])
    w_ple = din("w_ple", [DEPTH, DPLE, D])
    g_final_d = din("g_final", [1, 2048])

    yout = dout("yout", [NSEG, T, D])
    ssm_p = dout("ssm_p", [DEPTH, 2048, 128])
    conv_o = dout("conv_o", [DEPTH, NSEG, 3 + NSS * 3, CONV])
    ssm_s = dout("ssm_s", [DEPTH, NSEG, NSS, 2048, 128])
    v_s = dout("v_s", [DEPTH, NSEG, TS, 2048])
    hbuf = nc.dram_tensor("hbuf", [NSEG, T, D], F32, kind="Internal").ap()
    hstate = nc.dram_tensor("hstate", [DEPTH, 4, 128, 512], F32, kind="Internal").ap()
    hbuf_b = [[Buf() for _ in range(NTP // 128 + 1)] for _ in range(NSEG)]
    hstate_b = [[Buf() for _ in range(4)] for _ in range(DEPTH)]

    es = ExitStack()
    S = Sched(nc, es)

    class TL:
        def __init__(self, name, shape, dt, nb=1, psum=False, sc=None):
            f = nc.psum_tensor if psum else nc.sbuf_tensor
            self.t = (sc or es).enter_context(f(name, list(shape), dt))
            self.bufs = [Buf(name + str(i)) for i in range(nb)]
            self.b = self.bufs[0]

    def rows(i):
        return 128 if i < NCH else TS

    def ncol(i):
        return 3 + i * 128

    def tcol(i):
        return i * 128

    h = TL("h", [128, NT, D], F32, nb=NT)
    nT = TL("nT", [128, 16, TC], BF16)
    ring = []
    ring_i = [0]

    def make_ring(sc, n, tag):
        ring[:] = [TL("ring%s_%d" % (tag, k), [128, 2048], BF16, sc=sc) for k in range(n)]
        ring_i[0] = 0
    scr = TL("scr", [128, 512], F32)
    PT = TL("PT", [128, DEPTH * 200 + 16], F32)
    ident = TL("ident", [128, 128], F32)
    tri = TL("tri", [128, 128], F32)
    ustr = TL("ustr", [128, 128], F32)
    ones = TL("ones", [128, 128], F32)
    maskS = TL("maskS", [64, 64], F32)
    blk1 = TL("blk1", [64, 64], F32)
    rmask = TL("rmask", [64, 16], F32)
    maskB3 = TL("maskB3", [128, 16, 64], BF16)
    smb = TL("smb", [128, 96], F32)
    CONSTS = [ident.b, tri.b, ustr.b, ones.b, maskS.b, blk1.b, rmask.b, maskB3.b]

    banks = [TL("bank%d" % i, [128, 512], F32, psum=True) for i in range(8)]
    NROT = 5
    rot = [0]

    def ps():
        b = banks[rot[0]]
        rot[0] = (rot[0] + 1) % NROT
        if b.b.pend:
            raise RuntimeError("PSUM rotation hazard: bank reused before its consumer was emitted")
        b.b.pend = True
        return b

    accA = banks[5]
    accB = banks[7]
    bky2 = [banks[5], banks[6]]

    V = nc.vector
    A = nc.scalar
    PE = nc.tensor

    def const_tri(tl, n, cmp, base=0, mult=1, pat=-1):
        def f():
            nc.gpsimd.memset(tl.t[:], 1.0)
            return nc.gpsimd.affine_select(out=tl.t[:], in_=tl.t[:], pattern=[[pat, n]], compare_op=cmp,
                                           fill=0.0, base=base, channel_multiplier=mult)
        S.group("pool", [f], W=[tl.b])

    const_tri(ident, 128, ALU.is_equal)
    const_tri(tri, 128, ALU.is_ge, pat=1, mult=-1)
    const_tri(ustr, 128, ALU.is_gt, pat=-1, mult=1)
    S.op("pool", lambda: nc.gpsimd.memset(ones.t[:], 1.0), W=[ones.b])
    Rm = TL("Rm", [4, 64], F32)

    def f_rm():
        nc.gpsimd.memset(Rm.t[:], 1.0)
        return nc.gpsimd.affine_select(out=Rm.t[:].rearrange("p (b l) -> p b l", l=4),
                                       in_=Rm.t[:].rearrange("p (b l) -> p b l", l=4), pattern=[[0, 16], [1, 4]],
                                       compare_op=ALU.is_equal, fill=0.0, base=0, channel_multiplier=-1)
    S.group("pool", [f_rm], W=[Rm.b])
    def f_rmask():
        nc.gpsimd.memset(rmask.t[:], 1.0)
        nc.gpsimd.affine_select(out=rmask.t[:], in_=rmask.t[:], pattern=[[-4, 16]], compare_op=ALU.is_ge,
                                fill=0.0, base=0, channel_multiplier=1)
        return nc.gpsimd.affine_select(out=rmask.t[:], in_=rmask.t[:], pattern=[[4, 16]], compare_op=ALU.is_ge,
                                       fill=0.0, base=3, channel_multiplier=-1)
    S.group("pool", [f_rmask], W=[rmask.b])
    with ExitStack() as sc0:
        mb3f = TL("mb3f", [128, 16, 64], F32, sc=sc0)

        def f_mb3():
            nc.gpsimd.memset(mb3f.t[:], 1.0)
            nc.gpsimd.affine_select(out=mb3f.t[:], in_=mb3f.t[:], pattern=[[-4, 16], [1, 64]], compare_op=ALU.is_ge,
                                    fill=0.0, base=0, channel_multiplier=0)
            return nc.gpsimd.affine_select(out=mb3f.t[:], in_=mb3f.t[:], pattern=[[4, 16], [-1, 64]],
                                           compare_op=ALU.is_ge, fill=0.0, base=3, channel_multiplier=0)
        S.group("pool", [f_mb3], W=[mb3f.b])
        S.op("dve", lambda: V.tensor_copy(out=maskB3.t[:, :, :], in_=mb3f.t[:, :, :]), R=[mb3f.b], W=[maskB3.b])
        for e_ in ["pe", "dve", "act", "pool", "sp"]:
            S.drain(e_)
    bk = ps()
    rmT = TL("rmT", [16, 64], F32)
    S.op("pe", lambda: PE.transpose(bk.t[0:16, 0:64], rmask.t[:], ident.t[0:64, 0:64]), R=[rmask.b, ident.b], W=[bk.b])
    S.op("dve", lambda: V.tensor_copy(out=rmT.t[:], in_=bk.t[0:16, 0:64]), R=[bk.b], W=[rmT.b])
    bk2 = ps()
    S.op("pe", lambda: PE.matmul(bk2.t[0:64, 0:64], lhsT=rmT.t[:], rhs=rmT.t[:], start=True, stop=True),
         R=[rmT.b], W=[bk2.b])
    S.op("dve", lambda: V.tensor_copy(out=blk1.t[:], in_=bk2.t[0:64, 0:64]), R=[bk2.b], W=[blk1.b])
    S.op("dve", lambda: V.tensor_tensor(out=maskS.t[:], in0=blk1.t[:], in1=tri.t[0:64, 0:64], op=ALU.mult),
         R=[blk1.b, tri.b], W=[maskS.b])

    NPR = DEPTH * 200 + 16
    O_GMIX, O_GFFN, O_GPG, O_GSSD, O_GCM, O_CW, O_CB = 0, 16, 32, 48, 64, 80, 176
    O_GFIN = DEPTH * 200
    pstg = TL("pstg", [128, 128], F32)
    for r0 in range(0, NPR, 128):
        nr = min(128, NPR - r0)
        S.dma("sp", pstg.t[0:nr, :], prm[r0:r0 + nr, :], W=[pstg.b])
        bkp = ps()
        S.op("pe", lambda: PE.transpose(bkp.t[:, 0:nr], pstg.t[0:nr, :], ident.t[0:nr, 0:nr]),
             R=[pstg.b, ident.b], W=[bkp.b])
        S.op("dve", lambda: V.tensor_copy(out=PT.t[:, r0:r0 + nr], in_=bkp.t[:, 0:nr]), R=[bkp.b], W=[PT.b])

    ntail = TL("ntail", [128, 16, 3], BF16)

    yb = TL("yb", [128, 4, T], BF16)
    dbg_t = {}

    def dump_yb(name):
        if not DBG:
            return
        dd = dout(name, [128, 4 * T])
        for ct in range(4):
            for (c0, n) in split_cols(0, T):
                S.op("dve", lambda: V.tensor_copy(out=scr.t[:, 0:n], in_=yb.t[:, ct, c0:c0 + n]), R=[yb.b], W=[scr.b])
                S.dma("sp", dd[:, ct * T + c0:ct * T + c0 + n], scr.t[:, 0:n], R=[scr.b])
    epsb = TL("epsb", [128, 1], F32)
    S.op("dve", lambda: V.memset(epsb.t[:, :], EPS), W=[epsb.b])
    stat = TL("stat", [128, NT * 8], F32, nb=NT)

    def barrier():
        for e in ["pe", "dve", "act", "pool", "sp"]:
            S.drain(e)

    def wload(src_ap):
        sl = ring[ring_i[0]]
        ring_i[0] = (ring_i[0] + 1) % len(ring)
        if sl.b.pend:
            raise RuntimeError("weight ring hazard: slot reloaded before its consumer was emitted")
        sl.b.pend = True
        a, b = src_ap.shape[1], src_ap.shape[2]
        view = sl.t[:, 0:a * b].rearrange("p (a b) -> p a b", a=a)
        S.dma("pool", view, src_ap, W=[sl.b])
        return view, sl.b

    def wblock(w2d, c0, ncols=512, K=2048, r0=0):
        out = []
        for s in range(K // 512):
            src = w2d[r0 + s * 512:r0 + (s + 1) * 512, c0:c0 + ncols].rearrange("(kc p) n -> p kc n", p=128)
            out.append(wload(src))
        return out

    def mm_fm(slots, ct, c0, n):
        bk = ps()
        KC = 4 * len(slots)
        fns = []
        for kc in range(KC):
            v = slots[kc // 4][0]
            fns.append(lambda kc=kc, v=v: PE.matmul(bk.t[:, 0:n], lhsT=v[:, kc % 4, ct * 128:(ct + 1) * 128],
                                                    rhs=nT.t[:, kc, c0:c0 + n], start=(kc == 0), stop=(kc == KC - 1)))
        S.group("pe", fns, R=[nT.b] + [b for _, b in slots], W=[bk.b])
        return bk

    def gemm_tm(slots, src, src_buf, colof, consumer, ncols=512):
        KC = 4 * len(slots)
        for i in range(NT):
            bk = ps()
            r = rows(i)
            c = colof(i)
            fns = []
            for kc in range(KC):
                v = slots[kc // 4][0]
                fns.append(lambda kc=kc, v=v: PE.matmul(bk.t[0:r, 0:ncols], lhsT=src[:, kc, c:c + r],
                                                        rhs=v[:, kc % 4, 0:ncols], start=(kc == 0),
                                                        stop=(kc == KC - 1)))
            S.group("pe", fns, R=[src_buf] + [b for _, b in slots], W=[bk.b])
            consumer(i, bk)

    def rstd_of(i, n_el):
        r = rows(i)
        o = 8 * i
        S.op("act", lambda: A.activation(out=stat.t[0:r, o + 5:o + 6], in_=stat.t[0:r, o + 4:o + 5], func=AF.Sqrt,
                                         scale=1.0 / n_el, bias=epsb.t[0:r, 0:1]), R=[stat.bufs[i], epsb.b],
             W=[stat.bufs[i]])
        S.op("dve", lambda: V.reciprocal(out=stat.t[0:r, o + 6:o + 7], in_=stat.t[0:r, o + 5:o + 6]),
             R=[stat.bufs[i]], W=[stat.bufs[i]])

    def sumsq_h(i):
        r = rows(i)
        o = 8 * i
        for q in range(4):
            S.op("act", lambda: A.activation(out=scr.t[0:r, :], in_=h.t[0:r, i, q * 512:(q + 1) * 512], func=AF.Square,
                                             accum_out=stat.t[0:r, o + q:o + q + 1]),
                 R=[h.bufs[i]], W=[scr.b, stat.bufs[i]])
        S.op("dve", lambda: V.tensor_reduce(out=stat.t[0:r, o + 4:o + 5], in_=stat.t[0:r, o:o + 4], axis=AX.X,
                                            op=ALU.add), R=[stat.bufs[i]], W=[stat.bufs[i]])
        rstd_of(i, D)

    def norm_to_nT(gcol):
        for i in range(NT):
            r = rows(i)
            o = 8 * i
            sumsq_h(i)
            for q in range(4):
                S.op("dve", lambda: V.tensor_scalar(out=scr.t[0:r, :], in0=h.t[0:r, i, q * 512:(q + 1) * 512],
                                                    scalar1=stat.t[0:r, o + 6:o + 7], scalar2=None, op0=ALU.mult),
                     R=[h.bufs[i], stat.bufs[i]], W=[scr.b])
                bk = ps()
                fns = [lambda kk=kk: PE.transpose(bk.t[:, kk * 128:kk * 128 + r], scr.t[0:r, kk * 128:(kk + 1) * 128],
                                                  ident.t[0:r, 0:r]) for kk in range(4)]
                S.group("pe", fns, R=[scr.b, ident.b], W=[bk.b])
                c = ncol(i)
                S.op("dve", lambda: V.tensor_tensor(
                    out=nT.t[:, q * 4:(q + 1) * 4, c:c + r],
                    in0=bk.t[:, :].rearrange("p (a b) -> p a b", a=4)[:, :, 0:r],
                    in1=PT.t[:, gcol + q * 4:gcol + q * 4 + 4].unsqueeze(2).to_broadcast([128, 4, r]),
                    op=ALU.mult), R=[bk.b, PT.b], W=[nT.b])

    def gelu_psum(bk, r, n, out_ap, out_buf, tmpx, tmpa):
        S.op("act", lambda: A.copy(out=tmpx.t[0:r, 0:n], in_=bk.t[0:r, 0:n]), R=[bk.b], W=[tmpx.b])
        S.op("dve", lambda: V.scalar_tensor_tensor(out=tmpa.t[0:r, 0:n], in0=tmpx.t[0:r, 0:n], scalar=0.044715,
                                                   in1=tmpx.t[0:r, 0:n], op0=ALU.mult, op1=ALU.mult),
             R=[tmpx.b], W=[tmpa.b])
        S.op("dve", lambda: V.scalar_tensor_tensor(out=tmpa.t[0:r, 0:n], in0=tmpa.t[0:r, 0:n], scalar=1.0,
                                                   in1=tmpx.t[0:r, 0:n], op0=ALU.add, op1=ALU.mult),
             R=[tmpx.b, tmpa.b], W=[tmpa.b])
        S.op("act", lambda: A.activation(out=tmpa.t[0:r, 0:n], in_=tmpa.t[0:r, 0:n], func=AF.Sigmoid,
                                         scale=1.5957691216), R=[tmpa.b], W=[tmpa.b])
        S.op("dve", lambda: V.tensor_tensor(out=out_ap, in0=tmpx.t[0:r, 0:n], in1=tmpa.t[0:r, 0:n], op=ALU.mult),
             R=[tmpx.b, tmpa.b], W=[out_buf])

    FMB_H = split_cols(0, TC)
    FMB = split_cols(3, T)

    def outproj(wsrc2d, r0):
        for cb in range(4):
            src = wsrc2d[r0:r0 + 512, cb * 512:(cb + 1) * 512].rearrange("(kc p) n -> p kc n", p=128)
            sl = wload(src)

            def cons(i, bk, cb=cb):
                r = rows(i)
                S.op("dve", lambda: V.tensor_tensor(out=h.t[0:r, i, cb * 512:(cb + 1) * 512],
                                                    in0=h.t[0:r, i, cb * 512:(cb + 1) * 512], in1=bk.t[0:r, :],
                                                    op=ALU.add), R=[bk.b, h.bufs[i]], W=[h.bufs[i]])
            gemm_tm([sl], yb.t, yb.b, tcol, cons)

    def ssd_phase(d, seg, sc):
        pc = d * 200
        L = lambda name, shape, dt, nb=1: TL(name + "_%d_%d" % (d, seg), shape, dt, nb=nb, sc=sc)
        make_ring(sc, RING_SSD, "s%d_%d" % (d, seg))
        Wdt = L("Wdt", [128, 16, 32], BF16)
        dtv = L("dtv", [128, NT, 32], F32)
        dtA = L("dtA", [128, NT, 32], F32)
        eac = L("eac", [128, NT, 32], F32)
        dtdec = L("dtdec", [128, NT, 32], F32)
        cdec = L("cdec", [128, NT, 32], F32)
        tmp32 = L("tmp32", [128, 64], F32)
        xp = L("xp", [128, TC], F32)
        xps = L("xps", [128, NSS, 7], F32)
        hsq = L("hsq", [128, 4, NSS * 3], F32)
        cvst = L("cvst", [48, 512], F32)
        cvo1 = L("cvo1", [128, 3 + NSS * 3], F32)
        cvt1 = L("cvt1", [64, 128], F32)
        cacc = L("cacc", [128, T], F32)
        BTg = L("BTg", [128, T], BF16)
        CTg = L("CTg", [128, T], BF16)
        Btok = L("Btok", [128, NT, 128], BF16)
        xtok = L("xtok", [128, NT, 512], BF16)
        cbm2 = [L("cbm%d" % k, [128, 128], F32) for k in range(2)]
        rh2 = [L("rh%d" % k, [128, 4, 128], F32) for k in range(2)]
        Mh2 = [L("Mh%d" % k, [128, 8, 128], BF16) for k in range(2)]
        xdt2 = [L("xdt%d" % k, [128, 512], BF16) for k in range(2)]
        xdts2 = [L("xdts%d" % k, [128, 512], BF16) for k in range(2)]
        prev = L("prev", [128, 512], F32)
        prevb = L("prevb", [128, 512], BF16)
        t1 = L("t1", [128, 512], F32)
        sqf = L("sqf", [128, 512], F32)
        Hs = [L("Hs%d" % k, [128, 4, 128], F32) for k in range(2)]
        HTb = L("HTb", [128, 512], BF16)
        Cmb = L("Cmb", [128, 64], BF16)
        Bmb = L("Bmb", [64, 128], BF16)
        dAT = L("dAT", [128, 4, NSS], F32)
        dtArep = sqf

        norm_to_nT(pc + O_GMIX)
        if seg == 0:
            S.op("dve", lambda: V.memset(nT.t[:, :, 0:3], 0.0), W=[nT.b])
        else:
            S.op("dve", lambda: V.tensor_copy(out=nT.t[:, :, 0:3], in_=ntail.t[:, :, :]), R=[ntail.b], W=[nT.b])
        if seg + 1 < NSEG:
            S.op("dve", lambda: V.tensor_copy(out=ntail.t[:, :, :], in_=nT.t[:, :, NTP:NTP + 3]), R=[nT.b],
                 W=[ntail.b])
        S.dma("sp", smb.t[:, :], smallp[d].partition_broadcast(128), W=[smb.b])
        S.op("act", lambda: A.activation(out=smb.t[:, 32:64], in_=smb.t[:, 32:64], func=AF.Exp), R=[smb.b], W=[smb.b])
        S.op("dve", lambda: V.tensor_scalar(out=smb.t[:, 32:64], in0=smb.t[:, 32:64], scalar1=-1.0, scalar2=None,
                                            op0=ALU.mult), R=[smb.b], W=[smb.b])
        S.dma("pool", Wdt.t[:, :, :], w_in[d][:, I2:I3].rearrange("(kc p) n -> p kc n", p=128), W=[Wdt.b])
        for i in range(NT):
            r = rows(i)
            c = ncol(i)
            bk = ps()
            fns = [lambda kc=kc: PE.matmul(bk.t[0:r, 0:32], lhsT=nT.t[:, kc, c:c + r], rhs=Wdt.t[:, kc, :],
                                           start=(kc == 0), stop=(kc == 15)) for kc in range(16)]
            S.group("pe", fns, R=[nT.b, Wdt.b], W=[bk.b])
            S.op("dve", lambda: V.tensor_tensor(out=tmp32.t[0:r, 0:32], in0=bk.t[0:r, 0:32], in1=smb.t[0:r, 0:32],
                                                op=ALU.add), R=[bk.b, smb.b], W=[tmp32.b])
            S.op("act", lambda: A.activation(out=tmp32.t[0:r, 0:32], in_=tmp32.t[0:r, 0:32], func=AF.Exp),
                 R=[tmp32.b], W=[tmp32.b])
            S.op("act", lambda: A.activation(out=dtv.t[0:r, i, :], in_=tmp32.t[0:r, 0:32], func=AF.Ln, bias=1.0),
                 R=[tmp32.b], W=[dtv.b])
            S.op("dve", lambda: V.tensor_tensor(out=dtA.t[0:r, i, :], in0=dtv.t[0:r, i, :], in1=smb.t[0:r, 32:64],
                                                op=ALU.mult), R=[dtv.b, smb.b], W=[dtA.b])
            bk2 = ps()
            mk = tri if i < NCH else maskS
            on = ones if i < NCH else blk1
            S.group("pe", [lambda: PE.matmul(bk2.t[0:r, 0:32], lhsT=mk.t[0:r, 0:r], rhs=dtA.t[0:r, i, :], start=True,
                                             stop=True),
                           lambda: PE.matmul(bk2.t[0:r, 32:64], lhsT=on.t[0:r, 0:r], rhs=dtA.t[0:r, i, :], start=True,
                                             stop=True)], R=[dtA.b] + CONSTS, W=[bk2.b])
            S.op("act", lambda: A.activation(out=eac.t[0:r, i, :], in_=bk2.t[0:r, 0:32], func=AF.Exp),
                 R=[bk2.b], W=[eac.b])
            S.op("act", lambda: A.activation(out=cdec.t[0:r, i, :], in_=bk2.t[0:r, 32:64], func=AF.Exp),
                 R=[bk2.b], W=[cdec.b])
            S.op("act", lambda: A.copy(out=tmp32.t[0:r, 32:64], in_=bk2.t[0:r, 32:64]), R=[bk2.b], W=[tmp32.b])
            S.op("dve", lambda: V.tensor_tensor(out=tmp32.t[0:r, 32:64], in0=tmp32.t[0:r, 32:64], in1=bk2.t[0:r, 0:32],
                                                op=ALU.subtract), R=[bk2.b, tmp32.b], W=[tmp32.b])
            S.op("act", lambda: A.activation(out=tmp32.t[0:r, 32:64], in_=tmp32.t[0:r, 32:64], func=AF.Exp),
                 R=[tmp32.b], W=[tmp32.b])
            S.op("dve", lambda: V.tensor_tensor(out=dtdec.t[0:r, i, :], in0=tmp32.t[0:r, 32:64], in1=dtv.t[0:r, i, :],
                                                op=ALU.mult), R=[tmp32.b, dtv.b], W=[dtdec.b])

        def prep_hist(ct0, nct):
            S.dma("sp", cvst.t[0:48, 0:nct * 128], scv[d, seg][:, ct0 * 128:(ct0 + nct) * 128], W=[cvst.b])
            for k in range(nct):
                bk = ps()
                S.op("pe", lambda: PE.transpose(bk.t[:, 0:48], cvst.t[0:48, k * 128:(k + 1) * 128], ident.t[0:48, 0:48]),
                     R=[cvst.b, ident.b], W=[bk.b])
                S.op("act", lambda: A.copy(out=hsq.t[:, k, :], in_=bk.t[:, 0:48]), R=[bk.b], W=[hsq.b])

        def conv_tile(ctg, k, kind):
            cw = pc + O_CW
            cb_ = pc + O_CB + ctg
            S.op("dve", lambda: V.tensor_copy(out=xps.t[:, :, 0:3],
                                              in_=hsq.t[:, k, :].rearrange("p (b j) -> p b j", j=3)),
                 R=[hsq.b], W=[xps.b])
            S.op("dve", lambda: V.tensor_copy(out=xps.t[:, :, 3:7],
                                              in_=xp.t[:, 3 + NTP:3 + NTP + TS].rearrange("p (b l) -> p b l", l=4)),
                 R=[xp.b, xps.b], W=[xps.b])
            S.op("act", lambda: A.copy(out=cvo1.t[:, 0:3], in_=xp.t[:, NTP:NTP + 3]), R=[xp.b], W=[cvo1.b])
            S.op("act", lambda: A.copy(out=cvo1.t[:, 3:3 + NSS * 3].rearrange("p (b j) -> p b j", j=3),
                                       in_=xps.t[:, :, 4:7]), R=[xps.b, cvo1.b], W=[cvo1.b])
            bk = ps()
            S.op("pe", lambda: PE.transpose(bk.t[0:51, 0:128], cvo1.t[:, :], ident.t[:, :]), R=[cvo1.b, ident.b],
                 W=[bk.b])
            S.op("act", lambda: A.copy(out=cvt1.t[0:51, :], in_=bk.t[0:51, 0:128]), R=[bk.b], W=[cvt1.b])
            S.dma("sp", conv_o[d, seg][:, ctg * 128:(ctg + 1) * 128], cvt1.t[0:51, :], R=[cvt1.b])
            S.op("dve", lambda: V.tensor_scalar(out=cacc.t[:, 0:NTP], in0=xp.t[:, 0:NTP],
                                                scalar1=PT.t[:, cw + ctg:cw + ctg + 1], scalar2=None, op0=ALU.mult),
                 R=[xp.b, PT.b], W=[cacc.b])
            for j in range(1, 4):
                S.op("dve", lambda: V.scalar_tensor_tensor(out=cacc.t[:, 0:NTP], in0=xp.t[:, j:j + NTP],
                                                           scalar=PT.t[:, cw + j * 24 + ctg:cw + j * 24 + ctg + 1],
                                                           in1=cacc.t[:, 0:NTP], op0=ALU.mult, op1=ALU.add),
                     R=[xp.b, PT.b, cacc.b], W=[cacc.b])
            cs = cacc.t[:, NTP:T].rearrange("p (b l) -> p b l", l=4)
            S.op("dve", lambda: V.tensor_scalar(out=cs, in0=xps.t[:, :, 0:4], scalar1=PT.t[:, cw + ctg:cw + ctg + 1],
                                                scalar2=None, op0=ALU.mult), R=[xps.b, PT.b, cacc.b], W=[cacc.b])
            for j in range(1, 4):
                S.op("dve", lambda: V.scalar_tensor_tensor(out=cs, in0=xps.t[:, :, j:j + 4],
                                                           scalar=PT.t[:, cw + j * 24 + ctg:cw + j * 24 + ctg + 1],
                                                           in1=cs, op0=ALU.mult, op1=ALU.add),
                     R=[xps.b, PT.b, cacc.b], W=[cacc.b])
            if kind == "C":
                S.op("act", lambda: A.activation(out=CTg.t[:, :], in_=cacc.t[:, :], func=AF.Silu,
                                                 bias=PT.t[:, cb_:cb_ + 1]), R=[cacc.b, PT.b], W=[CTg.b])
            else:
                S.op("act", lambda: A.activation(out=cacc.t[:, :], in_=cacc.t[:, :], func=AF.Silu,
                                                 bias=PT.t[:, cb_:cb_ + 1]), R=[cacc.b, PT.b], W=[cacc.b])
            if kind == "B":
                S.op("dve", lambda: V.tensor_copy(out=BTg.t[:, :], in_=cacc.t[:, :]), R=[cacc.b], W=[BTg.b])

        def to_tok(dst, width_off):
            for i0 in range(0, NT, 4):
                bk = ps()
                tiles = list(range(i0, min(NT, i0 + 4)))
                fns = [lambda i=i: PE.transpose(bk.t[0:rows(i), (i - i0) * 128:(i - i0 + 1) * 128],
                                                cacc.t[:, tcol(i):tcol(i) + rows(i)], ident.t[:, :]) for i in tiles]
                S.group("pe", fns, R=[cacc.b, ident.b], W=[bk.b])
                full = [i for i in tiles if rows(i) == 128]
                if full:
                    nf = len(full)
                    S.op("act", lambda: A.copy(out=dst.t[:, full[0]:full[0] + nf, width_off:width_off + 128],
                                               in_=bk.t[:, 0:nf * 128].rearrange("p (a b) -> p a b", a=nf)),
                         R=[bk.b], W=[dst.b])
                for i in tiles:
                    if rows(i) != 128:
                        S.op("act", lambda: A.copy(out=dst.t[0:TS, i, width_off:width_off + 128],
                                                   in_=bk.t[0:TS, (i - i0) * 128:(i - i0 + 1) * 128]),
                             R=[bk.b], W=[dst.b])

        def sample_states(g, bko, xdts):
            iS = NCH
            cS = tcol(iS)
            hsl = slice(g * 8, (g + 1) * 8)
            S.op("dve", lambda: V.tensor_copy(out=dtArep.t[0:TS, :].rearrange("p (a b) -> p a b", a=8),
                                              in_=dtA.t[0:TS, iS, hsl].unsqueeze(2).to_broadcast([TS, 8, 64])),
                 R=[dtA.b], W=[dtArep.b])
            bkd = ps()
            fns = [lambda rt=rt: PE.matmul(bkd.t[:, rt * NSS:(rt + 1) * NSS], lhsT=dtArep.t[0:TS, rt * 128:(rt + 1) * 128],
                                           rhs=rmask.t[:, :], start=True, stop=True) for rt in range(4)]
            S.group("pe", fns, R=[dtArep.b] + CONSTS, W=[bkd.b])
            S.op("act", lambda: A.activation(out=dAT.t[:, :, :],
                                             in_=bkd.t[:, 0:4 * NSS].rearrange("p (a b) -> p a b", a=4),
                                             func=AF.Exp), R=[bkd.b], W=[dAT.b])
            for b in range(NSS):
                H = Hs[b % 2]
                S.dma("sp", H.t[:, :, :], sst[d, seg, b][g * 512:(g + 1) * 512, :].rearrange("(rt p) n -> p rt n", p=128),
                      W=[H.b])
                S.op("dve", lambda: V.tensor_tensor(out=Cmb.t[:, :], in0=CTg.t[:, cS:cS + TS], in1=maskB3.t[:, b, :],
                                                    op=ALU.mult), R=[CTg.b] + CONSTS, W=[Cmb.b])
                S.op("dve", lambda: V.tensor_scalar(out=Bmb.t[:, :], in0=Btok.t[0:TS, iS, :],
                                                    scalar1=rmask.t[:, b:b + 1], scalar2=None, op0=ALU.mult),
                     R=[Btok.b] + CONSTS, W=[Bmb.b])
                bkt = ps()
                fns = [lambda rt=rt: PE.transpose(bkt.t[:, rt * 128:(rt + 1) * 128], H.t[:, rt, :], ident.t[:, :])
                       for rt in range(4)]
                S.group("pe", fns, R=[H.b, ident.b], W=[bkt.b])
                S.op("act", lambda: A.copy(out=HTb.t[:, :], in_=bkt.t[:, :]), R=[bkt.b], W=[HTb.b])
                S.op("pe", lambda: PE.matmul(bko.t[0:TS, :], lhsT=Cmb.t[:, :], rhs=HTb.t[:, :], start=(b == 0),
                                             stop=(b == NSS - 1)), R=[Cmb.b, HTb.b], W=[bko.b])
                bku = ps()
                fns = [lambda rt=rt: PE.matmul(bku.t[:, rt * 128:(rt + 1) * 128],
                                               lhsT=xdts.t[0:TS, rt * 128:(rt + 1) * 128], rhs=Bmb.t[:, :], start=True,
                                               stop=True) for rt in range(4)]
                S.group("pe", fns, R=[xdts.b, Bmb.b], W=[bku.b])
                S.op("dve", lambda: V.tensor_tensor(out=H.t[:, :, :], in0=H.t[:, :, :],
                                                    in1=dAT.t[:, :, b:b + 1].to_broadcast([128, 4, 128]), op=ALU.mult),
                     R=[H.b, dAT.b], W=[H.b])
                S.op("dve", lambda: V.tensor_tensor(out=H.t[:, :, :], in0=H.t[:, :, :],
                                                    in1=bku.t[:, :].rearrange("p (a b) -> p a b", a=4), op=ALU.add),
                     R=[H.b, bku.b], W=[H.b])
                S.dma("sp", ssm_s[d, seg, b][g * 512:(g + 1) * 512, :].rearrange("(rt p) n -> p rt n", p=128), H.t[:, :, :],
                      R=[H.b])

        for g in range(4):
            hsl = slice(g * 8, (g + 1) * 8)
            bslot = wload(w_in[d][:, I1 + 2048 + g * 128:I1 + 2048 + (g + 1) * 128].rearrange("(kc p) n -> p kc n", p=128))
            cslot = wload(w_in[d][:, I1 + 2560 + g * 128:I1 + 2560 + (g + 1) * 128].rearrange("(kc p) n -> p kc n", p=128))
            xslots_box = []
            jobs = [("B", 16 + g, None), ("C", 20 + g, None)] + [("x", g * 4 + ct, ct) for ct in range(4)]

            def job_gemm(job):
                kind, ctg, ct = job
                res_ = []
                for (c0, n) in FMB_H:
                    if kind == "x":
                        if not xslots_box:
                            xslots_box.extend(wblock(w_in[d], I1 + g * 512))
                        bk = mm_fm(xslots_box, ct, c0, n)
                    else:
                        slv, slb = bslot if kind == "B" else cslot
                        bk = ps()
                        fns = [lambda kc=kc: PE.matmul(bk.t[:, 0:n], lhsT=slv[:, kc, :], rhs=nT.t[:, kc, c0:c0 + n],
                                                       start=(kc == 0), stop=(kc == 15)) for kc in range(16)]
                        S.group("pe", fns, R=[nT.b, slb], W=[bk.b])
                    res_.append((c0, n, bk))
                return res_

            for k, job in enumerate(jobs):
                kind, ctg, ct = job
                if kind != "x":
                    prep_hist(ctg, 1)
                    hk = 0
                else:
                    if ct == 0:
                        prep_hist(g * 4, 4)
                    hk = ct
                pend = job_gemm(job)
                for (c0, n, bk) in pend:
                    S.op("act", lambda: A.copy(out=xp.t[:, c0:c0 + n], in_=bk.t[:, 0:n]), R=[bk.b], W=[xp.b])
                conv_tile(ctg, hk, kind)
                if kind == "B":
                    to_tok(Btok, 0)
                elif kind == "x":
                    to_tok(xtok, ct * 128)

            if seg == 0:
                S.op("dve", lambda: V.memset(prev.t[:, :], 0.0), W=[prev.b])
            else:
                S.dma("sp", prev.t[:, :], hstate[d, g], R=[hstate_b[d][g]], W=[prev.b])
            S.op("act", lambda: A.copy(out=prevb.t[:, :], in_=prev.t[:, :]), R=[prev.b], W=[prevb.b])

            def front(i):
                r = rows(i)
                c = tcol(i)
                p = i % 2
                mk = tri if i < NCH else maskS
                cbm_, rh_, Mh_, xdt_, xdts_, bky = cbm2[p], rh2[p], Mh2[p], xdt2[p], xdts2[p], bky2[p]
                bkc = ps()
                S.op("pe", lambda: PE.matmul(bkc.t[0:r, 0:r], lhsT=BTg.t[:, c:c + r], rhs=CTg.t[:, c:c + r], start=True,
                                             stop=True), R=[BTg.b, CTg.b], W=[bkc.b])
                S.op("dve", lambda: V.tensor_tensor(out=cbm_.t[0:r, 0:r], in0=bkc.t[0:r, 0:r], in1=mk.t[0:r, 0:r],
                                                    op=ALU.mult), R=[bkc.b] + CONSTS, W=[cbm_.b])
                for h4 in range(2):
                    bks = ps()
                    for hh in range(4):
                        hd = g * 8 + h4 * 4 + hh
                        S.op("dve", lambda: V.tensor_scalar(out=rh_.t[0:r, hh, 0:r], in0=tri.t[0:r, 0:r],
                                                            scalar1=dtA.t[0:r, i, hd:hd + 1], scalar2=None,
                                                            op0=ALU.mult), R=[dtA.b] + CONSTS, W=[rh_.b])
                    fns = [lambda hh=hh: PE.matmul(bks.t[0:r, hh * 128:hh * 128 + r], lhsT=ustr.t[0:r, 0:r],
                                                   rhs=rh_.t[0:r, hh, 0:r], start=True, stop=True) for hh in range(4)]
                    S.group("pe", fns, R=[rh_.b] + CONSTS, W=[bks.b])
                    S.op("act", lambda: A.activation(out=rh_.t[0:r, :, 0:r],
                                                     in_=bks.t[0:r, :].rearrange("p (a b) -> p a b", a=4)[:, :, 0:r],
                                                     func=AF.Exp), R=[bks.b, rh_.b], W=[rh_.b])
                    S.op("dve", lambda: V.tensor_tensor(out=Mh_.t[0:r, h4 * 4:h4 * 4 + 4, 0:r], in0=rh_.t[0:r, :, 0:r],
                                                        in1=cbm_.t[0:r, 0:r].unsqueeze(1).to_broadcast([r, 4, r]),
                                                        op=ALU.mult), R=[rh_.b, cbm_.b], W=[Mh_.b])
                xv = xtok.t[0:r, i, :].rearrange("p (a b) -> p a b", a=8)
                S.op("dve", lambda: V.tensor_tensor(out=xdt_.t[0:r, :].rearrange("p (a b) -> p a b", a=8), in0=xv,
                                                    in1=dtv.t[0:r, i, hsl].unsqueeze(2).to_broadcast([r, 8, 64]),
                                                    op=ALU.mult), R=[xtok.b, dtv.b], W=[xdt_.b])
                S.op("dve", lambda: V.tensor_tensor(out=xdts_.t[0:r, :].rearrange("p (a b) -> p a b", a=8), in0=xv,
                                                    in1=dtdec.t[0:r, i, hsl].unsqueeze(2).to_broadcast([r, 8, 64]),
                                                    op=ALU.mult), R=[xtok.b, dtdec.b], W=[xdts_.b])
                fns = [lambda hh=hh: PE.matmul(bky.t[0:r, hh * 64:(hh + 1) * 64], lhsT=Mh_.t[0:r, hh, 0:r],
                                               rhs=xdt_.t[0:r, hh * 64:(hh + 1) * 64], start=True, stop=True)
                       for hh in range(8)]
                S.group("pe", fns, R=[Mh_.b, xdt_.b], W=[bky.b])

            def back(i):
                r = rows(i)
                c = tcol(i)
                p = i % 2
                xdts_, bky = xdts2[p], bky2[p]
                xv = xtok.t[0:r, i, :].rearrange("p (a b) -> p a b", a=8)
                if i < NCH:
                    bko = ps()
                    S.op("pe", lambda: PE.matmul(bko.t[0:r, :], lhsT=CTg.t[:, c:c + r], rhs=prevb.t[:, :], start=True,
                                                 stop=True), R=[CTg.b, prevb.b], W=[bko.b])
                else:
                    bko = accB
                    sample_states(g, bko, xdts_)
                S.op("dve", lambda: V.tensor_tensor(out=t1.t[0:r, :].rearrange("p (a b) -> p a b", a=8),
                                                    in0=bko.t[0:r, :].rearrange("p (a b) -> p a b", a=8),
                                                    in1=eac.t[0:r, i, hsl].unsqueeze(2).to_broadcast([r, 8, 64]),
                                                    op=ALU.mult), R=[bko.b, eac.b], W=[t1.b])
                S.op("dve", lambda: V.tensor_tensor(out=t1.t[0:r, :], in0=t1.t[0:r, :], in1=bky.t[0:r, :], op=ALU.add),
                     R=[bky.b, t1.b], W=[t1.b])
                S.op("dve", lambda: V.tensor_tensor(out=sqf.t[0:r, :].rearrange("p (a b) -> p a b", a=8), in0=xv,
                                                    in1=smb.t[0:r, 64 + g * 8:64 + g * 8 + 8].unsqueeze(2).to_broadcast(
                                                        [r, 8, 64]), op=ALU.mult), R=[xtok.b, smb.b], W=[sqf.b])
                S.op("dve", lambda: V.tensor_tensor(out=t1.t[0:r, :], in0=t1.t[0:r, :], in1=sqf.t[0:r, :], op=ALU.add),
                     R=[t1.b, sqf.b], W=[t1.b])
                bkt = ps()
                fns = [lambda k4=k4: PE.transpose(bkt.t[:, k4 * 128:k4 * 128 + r], t1.t[0:r, k4 * 128:(k4 + 1) * 128],
                                                  ident.t[0:r, 0:r]) for k4 in range(4)]
                S.group("pe", fns, R=[t1.b, ident.b], W=[bkt.b])
                S.op("act", lambda: A.copy(out=yb.t[:, :, c:c + r],
                                           in_=bkt.t[:, :].rearrange("p (a b) -> p a b", a=4)[:, :, 0:r]),
                     R=[bkt.b], W=[yb.b])
                if i < NCH:
                    bkS = ps()
                    S.op("pe", lambda: PE.matmul(bkS.t[:, :], lhsT=Btok.t[0:r, i, :], rhs=xdts_.t[0:r, :], start=True,
                                                 stop=True), R=[Btok.b, xdts_.b], W=[bkS.b])
                    S.op("dve", lambda: V.tensor_tensor(out=prev.t[:, :].rearrange("p (a b) -> p a b", a=8),
                                                        in0=prev.t[:, :].rearrange("p (a b) -> p a b", a=8),
                                                        in1=cdec.t[:, i, hsl].unsqueeze(2).to_broadcast([128, 8, 64]),
                                                        op=ALU.mult), R=[prev.b, cdec.b], W=[prev.b])
                    S.op("dve", lambda: V.tensor_tensor(out=prev.t[:, :], in0=prev.t[:, :], in1=bkS.t[:, :], op=ALU.add),
                         R=[prev.b, bkS.b], W=[prev.b])
                    S.op("act", lambda: A.copy(out=prevb.t[:, :], in_=prev.t[:, :]), R=[prev.b], W=[prevb.b])

            for i in range(NT + 1):
                if i < NT:
                    front(i)
                if i >= 1:
                    back(i - 1)
            if seg + 1 < NSEG:
                S.dma("sp", hstate[d, g], prev.t[:, :], R=[prev.b], W=[hstate_b[d][g]])
            else:
                bkf = ps()
                fns = [lambda k4=k4: PE.transpose(bkf.t[:, k4 * 128:(k4 + 1) * 128],
                                                  prev.t[:, k4 * 128:(k4 + 1) * 128], ident.t[:, :])
                       for k4 in range(4)]
                S.group("pe", fns, R=[prev.b, ident.b], W=[bkf.b])
                S.op("act", lambda: A.copy(out=t1.t[:, :], in_=bkf.t[:, :]), R=[bkf.b], W=[t1.b])
                S.dma("sp", ssm_p[d][g * 512:(g + 1) * 512, :].rearrange("(rt p) n -> p rt n", p=128),
                      t1.t[:, :].rearrange("p (a b) -> p a b", a=4), R=[t1.b])


            slots = wblock(w_in[d], g * 512)
            for (c0, n) in FMB:
                tc0 = c0 - 3
                for ct in range(4):
                    bk = mm_fm(slots, ct, c0, n)
                    S.op("act", lambda: A.activation(out=sqf.t[:, 0:n], in_=bk.t[:, 0:n], func=AF.Silu),
                         R=[bk.b], W=[sqf.b])
                    S.op("dve", lambda: V.tensor_tensor(out=yb.t[:, ct, tc0:tc0 + n], in0=yb.t[:, ct, tc0:tc0 + n],
                                                        in1=sqf.t[:, 0:n], op=ALU.mult), R=[yb.b, sqf.b], W=[yb.b])
                    S.op("act", lambda: A.activation(out=sqf.t[:, 0:n], in_=yb.t[:, ct, tc0:tc0 + n], func=AF.Square),
                         R=[yb.b, sqf.b], W=[sqf.b])
                    S.op("pe", lambda: PE.matmul(accA.t[:, 0:n], lhsT=ones.t[:, :], rhs=sqf.t[:, 0:n], start=(ct == 0),
                                                 stop=(ct == 3)), R=[sqf.b, ones.b], W=[accA.b])
                S.op("act", lambda: A.activation(out=t1.t[:, 0:n], in_=accA.t[:, 0:n], func=AF.Sqrt, scale=1.0 / 512,
                                                 bias=epsb.t[:, 0:1]), R=[accA.b, t1.b, epsb.b], W=[t1.b])
                S.op("dve", lambda: V.reciprocal(out=t1.t[:, 0:n], in_=t1.t[:, 0:n]), R=[t1.b], W=[t1.b])
                for ct in range(4):
                    gc = pc + O_GSSD + g * 4 + ct
                    S.op("dve", lambda: V.scalar_tensor_tensor(out=yb.t[:, ct, tc0:tc0 + n], in0=yb.t[:, ct, tc0:tc0 + n],
                                                               scalar=PT.t[:, gc:gc + 1], in1=t1.t[:, 0:n],
                                                               op0=ALU.mult, op1=ALU.mult),
                         R=[yb.b, PT.b, t1.b], W=[yb.b])
            if d == 0 and g == 0 and seg == 0:
                dump_yb("dbg_yssd0")
            outproj(w_out[d], g * 512)

    def cm_phase(d, seg, sc):
        pc = d * 200
        L = lambda name, shape, dt, nb=1: TL(name + "_%d_%d" % (d, seg), shape, dt, nb=nb, sc=sc)
        make_ring(sc, RING_CM, "c%d_%d" % (d, seg))
        vg = L("vg", [128, NT, 2048], BF16)
        gx = L("gx", [128, 512], F32)
        ga = L("ga", [128, 512], F32)
        ssq = L("ssq", [128, NT, 4], F32)
        Wn = L("Wn", [128, 4, 128], F32)
        WsT = L("WsT", [128, 4, 128], F32)
        Wr = L("Wr", [128, 128], BF16)
        WsS = L("WsS", [64, 16, 64], F32)
        bsb = L("bsb", [128, 4, 192], F32)
        tmpc = L("tmpc", [128, 128], F32)
        gcb = L("gcb", [64, 512], F32)
        vso = L("vso", [64, 512], F32)
        W4n = L("W4n", [4, 16, 4], F32)
        bs4 = L("bs4", [128, 4, 4], F32)
        o1s = gx
        for j in range(4):
            slots = wblock(w_in[d], I4 + j * 512)

            def cons(i, bk, j=j):
                r = rows(i)
                gelu_psum(bk, r, 512, vg.t[0:r, i, j * 512:(j + 1) * 512], vg.b, gx, ga)
                S.op("act", lambda: A.activation(out=ga.t[0:r, :], in_=vg.t[0:r, i, j * 512:(j + 1) * 512],
                                                 func=AF.Square, accum_out=ssq.t[0:r, i, j:j + 1]),
                     R=[vg.b, ga.b], W=[ga.b, ssq.b])
            gemm_tm(slots, nT.t, nT.b, ncol, cons)
        for i in range(NT):
            r = rows(i)
            S.op("dve", lambda: V.tensor_reduce(out=stat.t[0:r, 8 * i + 4:8 * i + 5], in_=ssq.t[0:r, i, :], axis=AX.X,
                                                op=ALU.add), R=[ssq.b, stat.bufs[i]], W=[stat.bufs[i]])
            rstd_of(i, 2048)
        iS = NCH
        for q in range(4):
            S.dma("sp", gcb.t[:, :], g_cm_d[d][q * 512:(q + 1) * 512].partition_broadcast(64), W=[gcb.b])
            S.op("dve", lambda: V.scalar_tensor_tensor(out=vso.t[:, :], in0=vg.t[0:TS, iS, q * 512:(q + 1) * 512],
                                                       scalar=stat.t[0:TS, 8 * iS + 6:8 * iS + 7], in1=gcb.t[:, :],
                                                       op0=ALU.mult, op1=ALU.mult),
                 R=[vg.b, stat.bufs[iS], gcb.b], W=[vso.b])
            S.dma("sp", v_s[d, seg][:, q * 512:(q + 1) * 512], vso.t[:, :], R=[vso.b])
        S.dma("sp", W4n.t[0:4, :, :], w_s_d[d][:, 0:4, 0:4].rearrange("g t s -> t g s"), W=[W4n.b])
        for hf in range(2):
            bk = ps()
            fns = [lambda g8=g8: PE.matmul(bk.t[0:4, g8 * 64:(g8 + 1) * 64], lhsT=W4n.t[0:4, hf * 8 + g8, :],
                                           rhs=Rm.t[0:4, :], start=True, stop=True) for g8 in range(8)]
            S.group("pe", fns, R=[W4n.b, Rm.b], W=[bk.b])
            S.op("act", lambda: A.copy(out=o1s.t[0:4, :], in_=bk.t[0:4, :]), R=[bk.b], W=[o1s.b])
            bk2 = ps()
            S.op("pe", lambda: PE.matmul(bk2.t[0:64, :], lhsT=Rm.t[0:4, :], rhs=o1s.t[0:4, :], start=True, stop=True),
                 R=[Rm.b, o1s.b], W=[bk2.b])
            S.op("dve", lambda: V.tensor_tensor(out=WsS.t[:, hf * 8:(hf + 1) * 8, :],
                                                in0=bk2.t[0:64, :].rearrange("p (a b) -> p a b", a=8),
                                                in1=maskS.t[:, :].unsqueeze(1).to_broadcast([64, 8, 64]), op=ALU.mult),
                 R=[bk2.b] + CONSTS, W=[WsS.b])
        for j in range(4):
            S.dma("sp", Wn.t[:, :, :], w_s_d[d][j * 4:(j + 1) * 4].rearrange("g t s -> t g s"), W=[Wn.b])
            bk = ps()
            fns = [lambda k=k: PE.transpose(bk.t[:, k * 128:(k + 1) * 128], Wn.t[:, k, :], ident.t[:, :])
                   for k in range(4)]
            S.group("pe", fns, R=[Wn.b, ident.b], W=[bk.b])
            S.op("dve", lambda: V.tensor_tensor(out=WsT.t[:, :, :], in0=bk.t[:, :].rearrange("p (a b) -> p a b", a=4),
                                                in1=tri.t[:, :].unsqueeze(1).to_broadcast([128, 4, 128]), op=ALU.mult),
                 R=[bk.b] + CONSTS, W=[WsT.b])
            S.dma("sp", bsb.t[:, :, 0:128], b_s_d[d][j * 4:(j + 1) * 4, :].partition_broadcast(128), W=[bsb.b])
            S.dma("sp", bs4.t[:, :, :], b_s_d[d][j * 4:(j + 1) * 4, 0:4].partition_broadcast(128), W=[bs4.b])
            S.op("dve", lambda: V.tensor_copy(out=bsb.t[:, :, 128:192].rearrange("p g (b l) -> p g b l", l=4),
                                              in_=bs4.t[:, :, :].unsqueeze(2).to_broadcast([128, 4, NSS, 4])),
                 R=[bs4.b, bsb.b], W=[bsb.b])
            slots = wblock(w_in[d], I3 + j * 512)
            for ct in range(4):
                for (c0, n) in FMB:
                    bk = mm_fm(slots, ct, c0, n)
                    gelu_psum(bk, 128, n, yb.t[:, ct, c0 - 3:c0 - 3 + n], yb.b, gx, ga)
            for ct in range(4):
                gg = j * 4 + ct
                for i in range(NT):
                    r = rows(i)
                    c = tcol(i)
                    rs = stat.t[0:r, 8 * i + 6:8 * i + 7]
                    if i < NCH:
                        S.op("dve", lambda: V.tensor_scalar(out=Wr.t[0:r, 0:r], in0=WsT.t[0:r, ct, 0:r], scalar1=rs,
                                                            scalar2=None, op0=ALU.mult),
                             R=[WsT.b, stat.bufs[i]], W=[Wr.b])
                        boff = 0
                    else:
                        S.op("dve", lambda: V.tensor_scalar(out=Wr.t[0:r, 0:r], in0=WsS.t[0:r, gg, 0:r], scalar1=rs,
                                                            scalar2=None, op0=ALU.mult),
                             R=[WsS.b, stat.bufs[i]], W=[Wr.b])
                        boff = 128
                    bk = ps()
                    S.op("pe", lambda: PE.matmul(bk.t[:, 0:r], lhsT=vg.t[0:r, i, gg * 128:(gg + 1) * 128],
                                                 rhs=Wr.t[0:r, 0:r], start=True, stop=True), R=[vg.b, Wr.b], W=[bk.b])
                    gcol = pc + O_GCM + gg
                    S.op("dve", lambda: V.scalar_tensor_tensor(out=tmpc.t[:, 0:r], in0=bk.t[:, 0:r],
                                                               scalar=PT.t[:, gcol:gcol + 1],
                                                               in1=bsb.t[:, ct, boff:boff + r], op0=ALU.mult,
                                                               op1=ALU.add), R=[bk.b, PT.b, bsb.b], W=[tmpc.b])
                    S.op("dve", lambda: V.tensor_tensor(out=yb.t[:, ct, c:c + r], in0=yb.t[:, ct, c:c + r],
                                                        in1=tmpc.t[:, 0:r], op=ALU.mult), R=[yb.b, tmpc.b], W=[yb.b])
            if d == 0 and j == 0 and seg == 0:
                dump_yb("dbg_ycm0")
            outproj(w_out[d], 2048 + j * 512)

    def ffn_phase(d, seg, sc):
        pc = d * 200
        make_ring(sc, RING_FFN, "f%d_%d" % (d, seg))
        norm_to_nT(pc + O_GFFN)
        for blk in range(DFF // 512):
            slots = wblock(w_gate[d], blk * 512)
            for ct in range(4):
                for (c0, n) in FMB:
                    bk = mm_fm(slots, ct, c0, n)
                    S.op("act", lambda: A.activation(out=yb.t[:, ct, c0 - 3:c0 - 3 + n], in_=bk.t[:, 0:n], func=AF.Silu),
                         R=[bk.b], W=[yb.b])
            slots = wblock(w_up[d], blk * 512)
            for ct in range(4):
                for (c0, n) in FMB:
                    bk = mm_fm(slots, ct, c0, n)
                    S.op("dve", lambda: V.tensor_tensor(out=yb.t[:, ct, c0 - 3:c0 - 3 + n],
                                                        in0=yb.t[:, ct, c0 - 3:c0 - 3 + n], in1=bk.t[:, 0:n],
                                                        op=ALU.mult), R=[bk.b, yb.b], W=[yb.b])
            outproj(w_down[d], blk * 512)

    def ple_phase(d, seg, sc):
        pc = d * 200
        L = lambda name, shape, dt, nb=1: TL(name + "_%d_%d" % (d, seg), shape, dt, nb=nb, sc=sc)
        make_ring(sc, RING_PLE, "p%d_%d" % (d, seg))
        pT = L("pT", [128, 2, T], BF16)
        ptok = L("ptok", [128, DPLE], F32)
        gsig = L("gsig", [128, 512], F32)
        for i in range(NT):
            r = rows(i)
            c = tcol(i)
            S.dma("sp", ptok.t[0:r, :], pin[d, seg, c:c + r, :], W=[ptok.b])
            bk = ps()
            fns = [lambda k=k: PE.transpose(bk.t[:, k * 128:k * 128 + r], ptok.t[0:r, k * 128:(k + 1) * 128],
                                            ident.t[0:r, 0:r]) for k in range(2)]
            S.group("pe", fns, R=[ptok.b, ident.b], W=[bk.b])
            S.op("act", lambda: A.copy(out=pT.t[:, :, c:c + r],
                                       in_=bk.t[:, 0:256].rearrange("p (a b) -> p a b", a=2)[:, :, 0:r]),
                 R=[bk.b], W=[pT.b])
        norm_to_nT(pc + O_GPG)
        wpl = L("wpl", [128, 2, D], BF16)
        S.dma("pool", wpl.t[:, :, :], w_ple[d].rearrange("(kc p) n -> p kc n", p=128), W=[wpl.b])
        for cb in range(4):
            slots = wblock(w_pg[d], cb * 512)

            def cons(i, bk, cb=cb):
                r = rows(i)
                c = tcol(i)
                S.op("act", lambda: A.activation(out=gsig.t[0:r, :], in_=bk.t[0:r, :], func=AF.Sigmoid),
                     R=[bk.b], W=[gsig.b])
                bkp = ps()
                fns = [lambda k=k: PE.matmul(bkp.t[0:r, :], lhsT=pT.t[:, k, c:c + r],
                                             rhs=wpl.t[:, k, cb * 512:(cb + 1) * 512], start=(k == 0), stop=(k == 1))
                       for k in range(2)]
                S.group("pe", fns, R=[pT.b, wpl.b], W=[bkp.b])
                S.op("dve", lambda: V.tensor_tensor(out=gsig.t[0:r, :], in0=gsig.t[0:r, :], in1=bkp.t[0:r, :],
                                                    op=ALU.mult), R=[gsig.b, bkp.b], W=[gsig.b])
                S.op("dve", lambda: V.tensor_tensor(out=h.t[0:r, i, cb * 512:(cb + 1) * 512],
                                                    in0=h.t[0:r, i, cb * 512:(cb + 1) * 512], in1=gsig.t[0:r, :],
                                                    op=ALU.add), R=[gsig.b, h.bufs[i]], W=[h.bufs[i]])
            gemm_tm(slots, nT.t, nT.b, ncol, cons)

    for d in range(DEPTH):
        for seg in range(NSEG):
            for i in range(NT):
                r = rows(i)
                if d == 0:
                    S.dma("sp", h.t[0:r, i, :], xin[seg, tcol(i):tcol(i) + r, :], W=[h.bufs[i]])
                else:
                    S.dma("sp", h.t[0:r, i, :], hbuf[seg, tcol(i):tcol(i) + r, :], R=[hbuf_b[seg][i]], W=[h.bufs[i]])
            for phase in (ssd_phase, cm_phase, ffn_phase, ple_phase):
                with ExitStack() as sc:
                    phase(d, seg, sc)
                    barrier()
            if d + 1 < DEPTH:
                for i in range(NT):
                    r = rows(i)
                    S.dma("sp", hbuf[seg, tcol(i):tcol(i) + r, :], h.t[0:r, i, :], R=[h.bufs[i]], W=[hbuf_b[seg][i]])
            else:
                with ExitStack() as sc_fin:
                    gfb = TL("gfb_%d" % seg, [128, 512], F32, sc=sc_fin)
                    for i in range(NT):
                        r = rows(i)
                        o = 8 * i
                        sumsq_h(i)
                        for q in range(4):
                            S.dma("sp", gfb.t[0:r, :], g_final_d[0][q * 512:(q + 1) * 512].partition_broadcast(r),
                                  W=[gfb.b])
                            S.op("dve", lambda: V.scalar_tensor_tensor(out=h.t[0:r, i, q * 512:(q + 1) * 512],
                                                                       in0=h.t[0:r, i, q * 512:(q + 1) * 512],
                                                                       scalar=stat.t[0:r, o + 6:o + 7], in1=gfb.t[0:r, :],
                                                                       op0=ALU.mult, op1=ALU.mult),
                                 R=[h.bufs[i], stat.bufs[i], gfb.b], W=[h.bufs[i]])
                        S.dma("sp", yout[seg, tcol(i):tcol(i) + r, :], h.t[0:r, i, :], R=[h.bufs[i]])
                    barrier()
    barrier()
    es.close()
    return nc


def _prep_inputs(inp, NTP, DEPTH, ncores):
    f = lambda a: np.ascontiguousarray(np.asarray(a, dtype=np.float32))
    xp_, xs_ = f(inp["x_prompt"]), f(inp["x_sample"])
    pp_, ps_ = f(inp["p_prompt"]), f(inp["p_sample"])
    sst, scv = f(inp["state_ssm"]), f(inp["state_conv"])
    rows_ = []
    for d in range(DEPTH):
        rows_ += [f(inp["g_mix"])[d].reshape(16, 128), f(inp["g_ffn"])[d].reshape(16, 128),
                  f(inp["g_pg"])[d].reshape(16, 128), f(inp["g_ssd"])[d].reshape(16, 128),
                  f(inp["g_cm"])[d].reshape(16, 128), f(inp["conv_w"])[d].reshape(4 * 24, 128),
                  f(inp["conv_b"])[d].reshape(24, 128)]
    rows_.append(f(inp["g_final"]).reshape(16, 128))
    prm = np.ascontiguousarray(np.concatenate(rows_, axis=0))
    smallp = np.ascontiguousarray(np.concatenate([f(inp["dt_bias"]), f(inp["a_log"]), f(inp["d_skip"])], axis=1))
    shared = {"prm": prm, "smallp": smallp, "g_cm": f(inp["g_cm"]), "w_s": f(inp["w_s"]), "b_s": f(inp["b_s"]),
              "w_in": f(inp["w_in"]), "w_out": f(inp["w_out"]), "w_gate": f(inp["w_gate"]), "w_up": f(inp["w_up"]),
              "w_down": f(inp["w_down"]), "w_pg": f(inp["w_pg"]), "w_ple": f(inp["w_ple"]),
              "g_final": f(inp["g_final"]).reshape(1, 2048)}
    maps = []
    for c in range(ncores):
        sq = c % 4
        m = dict(shared)
        xin, pin, ss, sc_ = [], [], [], []
        for seg in range(2):
            sl = slice(seg * NTP, (seg + 1) * NTP)
            b0 = (sq * 2 + seg) * NSS
            bs = slice(b0, b0 + NSS)
            xin.append(np.concatenate([xp_[sq, sl], xs_[bs].reshape(TS, D)], axis=0))
            pin.append(np.concatenate([pp_[:, sq, sl], ps_[:, bs].reshape(DEPTH, TS, DPLE)], axis=1))
            ss.append(sst[:, bs].reshape(DEPTH, NSS, 2048, 128))
            sc_.append(scv[:, bs].reshape(DEPTH, NSS * 3, CONV))
        m["xin"] = np.ascontiguousarray(np.stack(xin, axis=0))
        m["pin"] = np.ascontiguousarray(np.stack(pin, axis=1))
        m["sst"] = np.ascontiguousarray(np.stack(ss, axis=1))
        m["scv"] = np.ascontiguousarray(np.stack(sc_, axis=1))
        maps.append(m)
    return maps


def _assemble(res, NTP, DEPTH, ncores, nb_prompt, nb_sample):
    SEQ = 2 * NTP
    y_p = np.zeros((nb_prompt, SEQ, D), np.float32)
    y_s = np.zeros((nb_sample, 4, D), np.float32)
    ssm_p = np.zeros((DEPTH, nb_prompt, 32, 64, 128), np.float32)
    conv_p = np.zeros((DEPTH, nb_prompt, 3, CONV), np.float32)
    ssm_s = np.zeros((DEPTH, nb_sample, 32, 64, 128), np.float32)
    conv_s = np.zeros((DEPTH, nb_sample, 3, CONV), np.float32)
    v_s = np.zeros((DEPTH, nb_sample, 4, 2048), np.float32)
    for c in range(4):
        r = res[c]
        sq = c
        for seg in range(2):
            b0 = (sq * 2 + seg) * NSS
            bs = slice(b0, b0 + NSS)
            y_p[sq, seg * NTP:(seg + 1) * NTP] = r["yout"][seg, :NTP]
            y_s[bs] = r["yout"][seg, NTP:].reshape(NSS, 4, D)
            ssm_s[:, bs] = r["ssm_s"][:, seg].reshape(DEPTH, NSS, 32, 64, 128)
            conv_s[:, bs] = r["conv_o"][:, seg, 3:].reshape(DEPTH, NSS, 3, CONV)
            v_s[:, bs] = r["v_s"][:, seg].reshape(DEPTH, NSS, 4, 2048)
        ssm_p[:, sq] = r["ssm_p"].reshape(DEPTH, 32, 64, 128)
        conv_p[:, sq] = r["conv_o"][:, 1, 0:3]
    return (y_p, y_s, ssm_p, conv_p, ssm_s, conv_s, v_s)


LAST_RES = None


def kernel(**inputs):
    global LAST_RES
    DEPTH = int(np.asarray(inputs["w_in"]).shape[0])
    nbp, SEQ = np.asarray(inputs["x_prompt"]).shape[:2]
    nbs = np.asarray(inputs["x_sample"]).shape[0]
    ncores = 8
    NTP = SEQ // 2
    nc = build(NTP, DEPTH)
    maps = _prep_inputs(inputs, NTP, DEPTH, ncores)
    res = run_bass_kernel_spmd(nc, maps, core_ids=list(range(ncores)))
    if DBG:
        LAST_RES = res.results
    return _assemble(res.results, NTP, DEPTH, ncores, nbp, nbs)
```

```python
import numpy as np
from contextlib import ExitStack
import concourse.bass as bass
import concourse.mybir as mybir
from concourse.bass_utils import run_bass_kernel_spmd

F32 = mybir.dt.float32
BF16 = mybir.dt.bfloat16
AF = mybir.ActivationFunctionType
ALU = mybir.AluOpType
AX = mybir.AxisListType

D = 2048
DIN = 9248
DFF = 5632
DPLE = 256
CONV = 3072
I1 = 2048
I2 = I1 + 3072
I3 = I2 + 32
I4 = I3 + 2048
EPS = 1e-6
NSS = 16
TS = 64
SAME_ENGINE_SYNC = True
RING_SSD, RING_CM, RING_FFN, RING_PLE = 5, 6, 16, 12


class Buf:
    __slots__ = ("w", "r", "name", "pend")

    def __init__(self, name=""):
        self.w = None
        self.r = {}
        self.name = name
        self.pend = False


class Sched:
    def __init__(self, nc, es, ndma=24):
        self.nc = nc
        self.eng = {"pe": nc.tensor, "dve": nc.vector, "act": nc.scalar, "pool": nc.gpsimd, "sp": nc.sync}
        self.sem = {}
        for k in ["pe", "dve", "act", "pool"]:
            self.sem[k] = es.enter_context(nc.semaphore("s_" + k))
        self.cnt = {k: 0 for k in self.sem}
        self.seen = {e: {} for e in self.eng}
        self.dsem = [es.enter_context(nc.semaphore("s_dma%d" % i)) for i in range(ndma)]
        self.dcnt = [0] * ndma
        self.drr = 0
        self.ndma = ndma

    def _semof(self, key):
        if isinstance(key, tuple):
            return self.dsem[key[1]]
        return self.sem[key]

    def _wait(self, e, deps):
        best = {}
        for d in deps:
            if d is None:
                continue
            key, val = d
            if key == e and not SAME_ENGINE_SYNC:
                continue
            if val > best.get(key, 0):
                best[key] = val
        for key, val in best.items():
            if self.seen[e].get(key, 0) >= val:
                continue
            self.eng[e].wait_ge(self._semof(key), val)
            self.seen[e][key] = val

    def _deps(self, R, W):
        deps = []
        for b in R:
            deps.append(b.w)
        for b in W:
            deps.append(b.w)
            deps.extend(b.r.values())
        return deps

    def _mark(self, tok, R, W):
        key = tok[0]
        for b in R:
            b.pend = False
            b.r[key] = tok
        for b in W:
            b.w = tok
            b.r = {}

    def op(self, e, fn, R=(), W=()):
        self._wait(e, self._deps(R, W))
        inst = fn()
        self.cnt[e] += 1
        inst.then_inc(self.sem[e], 1)
        self._mark((e, self.cnt[e]), R, W)

    def group(self, e, fns, R=(), W=()):
        self._wait(e, self._deps(R, W))
        inst = None
        for fn in fns:
            inst = fn()
        self.cnt[e] += 1
        inst.then_inc(self.sem[e], 1)
        self._mark((e, self.cnt[e]), R, W)

    def dma(self, q, out, in_, R=(), W=(), **kw):
        i = self.drr
        self.drr = (i + 1) % self.ndma
        deps = self._deps(R, W)
        if self.dcnt[i] > 0:
            deps.append((("dma", i), self.dcnt[i]))
        self._wait(q, deps)
        inst = self.eng[q].dma_start(out=out, in_=in_, **kw)
        self.dcnt[i] += 16
        inst.then_inc(self.dsem[i], 16)
        self._mark((("dma", i), self.dcnt[i]), R, W)

    def drain(self, e="sp"):
        deps = [(("dma", i), self.dcnt[i]) for i in range(self.ndma) if self.dcnt[i] > 0]
        deps += [(k, self.cnt[k]) for k in self.cnt if self.cnt[k] > 0]
        self._wait(e, deps)


def split_cols(c0, n, mx=512):
    out = []
    while n > 0:
        k = min(mx, n)
        out.append((c0, k))
        c0 += k
        n -= k
    return out


DBG = False


def build(NTP, DEPTH, stop_after=None):
    NCH = NTP // 128
    NT = NCH + 1
    T = NTP + TS
    TC = 3 + T
    nc = bass.Bass("TRN2", target_bir_lowering=False)

    def din(name, shape):
        return nc.dram_tensor(name, list(shape), F32, kind="ExternalInput").ap()

    def dout(name, shape):
        return nc.dram_tensor(name, list(shape), F32, kind="ExternalOutput").ap()

    NSEG = 2
    xin = din("xin", [NSEG, T, D])
    pin = din("pin", [DEPTH, NSEG, T, DPLE])
    sst = din("sst", [DEPTH, NSEG, NSS, 2048, 128])
    scv = din("scv", [DEPTH, NSEG, NSS * 3, CONV])
    prm = din("prm", [DEPTH * 200 + 16, 128])
    smallp = din("smallp", [DEPTH, 96])
    g_cm_d = din("g_cm", [DEPTH, 2048])
    w_s_d = din("w_s", [DEPTH, 16, 128, 128])
    b_s_d = din("b_s", [DEPTH, 16, 128])
    w_in = din("w_in", [DEPTH, D, DIN])
    w_out = din("w_out", [DEPTH, 4096, D])
    w_gate = din("w_gate", [DEPTH, D, DFF])
    w_up = din("w_up", [DEPTH, D, DFF])
    w_down = din("w_down", [DEPTH, DFF, D])
    w_pg = din("w_pg", [DEPTH, D, D])
    w_ple = din("w_ple", [DEPTH, DPLE, D])
    g_final_d = din("g_final", [1, 2048])

    yout = dout("yout", [NSEG, T, D])
    ssm_p = dout("ssm_p", [DEPTH, 2048, 128])
    conv_o = dout("conv_o", [DEPTH, NSEG, 3 + NSS * 3, CONV])
    ssm_s = dout("ssm_s", [DEPTH, NSEG, NSS, 2048, 128])
    v_s = dout("v_s", [DEPTH, NSEG, TS, 2048])
    hbuf = nc.dram_tensor("hbuf", [NSEG, T, D], F32, kind="Internal").ap()
    hstate = nc.dram_tensor("hstate", [DEPTH, 4, 128, 512], F32, kind="Internal").ap()
    hbuf_b = [[Buf() for _ in range(NTP // 128 + 1)] for _ in range(NSEG)]
    hstate_b = [[Buf() for _ in range(4)] for _ in range(DEPTH)]

    es = ExitStack()
    S = Sched(nc, es)

    class TL:
        def __init__(self, name, shape, dt, nb=1, psum=False, sc=None):
            f = nc.psum_tensor if psum else nc.sbuf_tensor
            self.t = (sc or es).enter_context(f(name, list(shape), dt))
            self.bufs = [Buf(name + str(i)) for i in range(nb)]
            self.b = self.bufs[0]

    def rows(i):
        return 128 if i < NCH else TS

    def ncol(i):
        return 3 + i * 128

    def tcol(i):
        return i * 128

    h = TL("h", [128, NT, D], F32, nb=NT)
    nT = TL("nT", [128, 16, TC], BF16)
    ring = []
    ring_i = [0]

    def make_ring(sc, n, tag):
        ring[:] = [TL("ring%s_%d" % (tag, k), [128, 2048], BF16, sc=sc) for k in range(n)]
        ring_i[0] = 0
    scr = TL("scr", [128, 512], F32)
    PT = TL("PT", [128, DEPTH * 200 + 16], F32)
    ident = TL("ident", [128, 128], F32)
    tri = TL("tri", [128, 128], F32)
    ustr = TL("ustr", [128, 128], F32)
    ones = TL("ones", [128, 128], F32)
    maskS = TL("maskS", [64, 64], F32)
    blk1 = TL("blk1", [64, 64], F32)
    rmask = TL("rmask", [64, 16], F32)
    maskB3 = TL("maskB3", [128, 16, 64], BF16)
    smb = TL("smb", [128, 96], F32)
    CONSTS = [ident.b, tri.b, ustr.b, ones.b, maskS.b, blk1.b, rmask.b, maskB3.b]

    banks = [TL("bank%d" % i, [128, 512], F32, psum=True) for i in range(8)]
    NROT = 6
    rot = [0]

    def ps():
        b = banks[rot[0]]
        rot[0] = (rot[0] + 1) % NROT
        if b.b.pend:
            raise RuntimeError("PSUM rotation hazard: bank reused before its consumer was emitted")
        b.b.pend = True
        return b

    bky2 = [banks[6], banks[7]]
    accA = banks[6]
    accB = bky2[(NCH + 1) % 2]

    V = nc.vector
    A = nc.scalar
    PE = nc.tensor

    def const_tri(tl, n, cmp, base=0, mult=1, pat=-1):
        def f():
            nc.gpsimd.memset(tl.t[:], 1.0)
            return nc.gpsimd.affine_select(out=tl.t[:], in_=tl.t[:], pattern=[[pat, n]], compare_op=cmp,
                                           fill=0.0, base=base, channel_multiplier=mult)
        S.group("pool", [f], W=[tl.b])

    const_tri(ident, 128, ALU.is_equal)
    const_tri(tri, 128, ALU.is_ge, pat=1, mult=-1)
    const_tri(ustr, 128, ALU.is_gt, pat=-1, mult=1)
    S.op("pool", lambda: nc.gpsimd.memset(ones.t[:], 1.0), W=[ones.b])
    Rm = TL("Rm", [4, 64], F32)

    def f_rm():
        nc.gpsimd.memset(Rm.t[:], 1.0)
        return nc.gpsimd.affine_select(out=Rm.t[:].rearrange("p (b l) -> p b l", l=4),
                                       in_=Rm.t[:].rearrange("p (b l) -> p b l", l=4), pattern=[[0, 16], [1, 4]],
                                       compare_op=ALU.is_equal, fill=0.0, base=0, channel_multiplier=-1)
    S.group("pool", [f_rm], W=[Rm.b])
    def f_rmask():
        nc.gpsimd.memset(rmask.t[:], 1.0)
        nc.gpsimd.affine_select(out=rmask.t[:], in_=rmask.t[:], pattern=[[-4, 16]], compare_op=ALU.is_ge,
                                fill=0.0, base=0, channel_multiplier=1)
        return nc.gpsimd.affine_select(out=rmask.t[:], in_=rmask.t[:], pattern=[[4, 16]], compare_op=ALU.is_ge,
                                       fill=0.0, base=3, channel_multiplier=-1)
    S.group("pool", [f_rmask], W=[rmask.b])
    with ExitStack() as sc0:
        mb3f = TL("mb3f", [128, 16, 64], F32, sc=sc0)

        def f_mb3():
            nc.gpsimd.memset(mb3f.t[:], 1.0)
            nc.gpsimd.affine_select(out=mb3f.t[:], in_=mb3f.t[:], pattern=[[-4, 16], [1, 64]], compare_op=ALU.is_ge,
                                    fill=0.0, base=0, channel_multiplier=0)
            return nc.gpsimd.affine_select(out=mb3f.t[:], in_=mb3f.t[:], pattern=[[4, 16], [-1, 64]],
                                           compare_op=ALU.is_ge, fill=0.0, base=3, channel_multiplier=0)
        S.group("pool", [f_mb3], W=[mb3f.b])
        S.op("dve", lambda: V.tensor_copy(out=maskB3.t[:, :, :], in_=mb3f.t[:, :, :]), R=[mb3f.b], W=[maskB3.b])
        for e_ in ["pe", "dve", "act", "pool", "sp"]:
            S.drain(e_)
    bk = ps()
    rmT = TL("rmT", [16, 64], F32)
    S.op("pe", lambda: PE.transpose(bk.t[0:16, 0:64], rmask.t[:], ident.t[0:64, 0:64]), R=[rmask.b, ident.b], W=[bk.b])
    S.op("dve", lambda: V.tensor_copy(out=rmT.t[:], in_=bk.t[0:16, 0:64]), R=[bk.b], W=[rmT.b])
    bk2 = ps()
    S.op("pe", lambda: PE.matmul(bk2.t[0:64, 0:64], lhsT=rmT.t[:], rhs=rmT.t[:], start=True, stop=True),
         R=[rmT.b], W=[bk2.b])
    S.op("dve", lambda: V.tensor_copy(out=blk1.t[:], in_=bk2.t[0:64, 0:64]), R=[bk2.b], W=[blk1.b])
    S.op("dve", lambda: V.tensor_tensor(out=maskS.t[:], in0=blk1.t[:], in1=tri.t[0:64, 0:64], op=ALU.mult),
         R=[blk1.b, tri.b], W=[maskS.b])

    NPR = DEPTH * 200 + 16
    O_GMIX, O_GFFN, O_GPG, O_GSSD, O_GCM, O_CW, O_CB = 0, 16, 32, 48, 64, 80, 176
    O_GFIN = DEPTH * 200
    pstg = TL("pstg", [128, 128], F32)
    for r0 in range(0, NPR, 128):
        nr = min(128, NPR - r0)
        S.dma("sp", pstg.t[0:nr, :], prm[r0:r0 + nr, :], W=[pstg.b])
        bkp = ps()
        S.op("pe", lambda: PE.transpose(bkp.t[:, 0:nr], pstg.t[0:nr, :], ident.t[0:nr, 0:nr]),
             R=[pstg.b, ident.b], W=[bkp.b])
        S.op("dve", lambda: V.tensor_copy(out=PT.t[:, r0:r0 + nr], in_=bkp.t[:, 0:nr]), R=[bkp.b], W=[PT.b])

    ntail = TL("ntail", [128, 16, 3], BF16)

    yb = TL("yb", [128, 4, T], BF16)
    dbg_t = {}

    def dump_yb(name):
        if not DBG:
            return
        dd = dout(name, [128, 4 * T])
        for ct in range(4):
            for (c0, n) in split_cols(0, T):
                S.op("dve", lambda: V.tensor_copy(out=scr.t[:, 0:n], in_=yb.t[:, ct, c0:c0 + n]), R=[yb.b], W=[scr.b])
                S.dma("sp", dd[:, ct * T + c0:ct * T + c0 + n], scr.t[:, 0:n], R=[scr.b])
    epsb = TL("epsb", [128, 1], F32)
    S.op("dve", lambda: V.memset(epsb.t[:, :], EPS), W=[epsb.b])
    stat = TL("stat", [128, NT * 8], F32, nb=NT)

    def barrier():
        for e in ["pe", "dve", "act", "pool", "sp"]:
            S.drain(e)

    def wload(src_ap):
        sl = ring[ring_i[0]]
        ring_i[0] = (ring_i[0] + 1) % len(ring)
        if sl.b.pend:
            raise RuntimeError("weight ring hazard: slot reloaded before its consumer was emitted")
        sl.b.pend = True
        a, b = src_ap.shape[1], src_ap.shape[2]
        view = sl.t[:, 0:a * b].rearrange("p (a b) -> p a b", a=a)
        S.dma("pool", view, src_ap, W=[sl.b])
        return view, sl.b

    def wblock(w2d, c0, ncols=512, K=2048, r0=0):
        out = []
        for s in range(K // 512):
            src = w2d[r0 + s * 512:r0 + (s + 1) * 512, c0:c0 + ncols].rearrange("(kc p) n -> p kc n", p=128)
            out.append(wload(src))
        return out

    def mm_fm(slots, ct, c0, n):
        bk = ps()
        KC = 4 * len(slots)
        fns = []
        for kc in range(KC):
            v = slots[kc // 4][0]
            fns.append(lambda kc=kc, v=v: PE.matmul(bk.t[:, 0:n], lhsT=v[:, kc % 4, ct * 128:(ct + 1) * 128],
                                                    rhs=nT.t[:, kc, c0:c0 + n], start=(kc == 0), stop=(kc == KC - 1)))
        S.group("pe", fns, R=[nT.b] + [b for _, b in slots], W=[bk.b])
        return bk

    def gemm_tm(slots, src, src_buf, colof, consumer, ncols=512):
        KC = 4 * len(slots)
        for i in range(NT):
            bk = ps()
            r = rows(i)
            c = colof(i)
            fns = []
            for kc in range(KC):
                v = slots[kc // 4][0]
                fns.append(lambda kc=kc, v=v: PE.matmul(bk.t[0:r, 0:ncols], lhsT=src[:, kc, c:c + r],
                                                        rhs=v[:, kc % 4, 0:ncols], start=(kc == 0),
                                                        stop=(kc == KC - 1)))
            S.group("pe", fns, R=[src_buf] + [b for _, b in slots], W=[bk.b])
            consumer(i, bk)

    def rstd_of(i, n_el):
        r = rows(i)
        o = 8 * i
        S.op("act", lambda: A.activation(out=stat.t[0:r, o + 5:o + 6], in_=stat.t[0:r, o + 4:o + 5], func=AF.Sqrt,
                                         scale=1.0 / n_el, bias=epsb.t[0:r, 0:1]), R=[stat.bufs[i], epsb.b],
             W=[stat.bufs[i]])
        S.op("dve", lambda: V.reciprocal(out=stat.t[0:r, o + 6:o + 7], in_=stat.t[0:r, o + 5:o + 6]),
             R=[stat.bufs[i]], W=[stat.bufs[i]])

    def sumsq_h(i):
        r = rows(i)
        o = 8 * i
        for q in range(4):
            S.op("act", lambda: A.activation(out=scr.t[0:r, :], in_=h.t[0:r, i, q * 512:(q + 1) * 512], func=AF.Square,
                                             accum_out=stat.t[0:r, o + q:o + q + 1]),
                 R=[h.bufs[i]], W=[scr.b, stat.bufs[i]])
        S.op("dve", lambda: V.tensor_reduce(out=stat.t[0:r, o + 4:o + 5], in_=stat.t[0:r, o:o + 4], axis=AX.X,
                                            op=ALU.add), R=[stat.bufs[i]], W=[stat.bufs[i]])
        rstd_of(i, D)

    def norm_to_nT(gcol):
        for i in range(NT):
            r = rows(i)
            o = 8 * i
            sumsq_h(i)
            for q in range(4):
                S.op("dve", lambda: V.tensor_scalar(out=scr.t[0:r, :], in0=h.t[0:r, i, q * 512:(q + 1) * 512],
                                                    scalar1=stat.t[0:r, o + 6:o + 7], scalar2=None, op0=ALU.mult),
                     R=[h.bufs[i], stat.bufs[i]], W=[scr.b])
                bk = ps()
                fns = [lambda kk=kk: PE.transpose(bk.t[:, kk * 128:kk * 128 + r], scr.t[0:r, kk * 128:(kk + 1) * 128],
                                                  ident.t[0:r, 0:r]) for kk in range(4)]
                S.group("pe", fns, R=[scr.b, ident.b], W=[bk.b])
                c = ncol(i)
                S.op("dve", lambda: V.tensor_tensor(
                    out=nT.t[:, q * 4:(q + 1) * 4, c:c + r],
                    in0=bk.t[:, :].rearrange("p (a b) -> p a b", a=4)[:, :, 0:r],
                    in1=PT.t[:, gcol + q * 4:gcol + q * 4 + 4].unsqueeze(2).to_broadcast([128, 4, r]),
                    op=ALU.mult), R=[bk.b, PT.b], W=[nT.b])

    def gelu_psum(bk, r, n, out_ap, out_buf, tmpx, tmpa):
        S.op("act", lambda: A.copy(out=tmpx.t[0:r, 0:n], in_=bk.t[0:r, 0:n]), R=[bk.b], W=[tmpx.b])
        S.op("dve", lambda: V.scalar_tensor_tensor(out=tmpa.t[0:r, 0:n], in0=tmpx.t[0:r, 0:n], scalar=0.044715,
                                                   in1=tmpx.t[0:r, 0:n], op0=ALU.mult, op1=ALU.mult),
             R=[tmpx.b], W=[tmpa.b])
        S.op("dve", lambda: V.scalar_tensor_tensor(out=tmpa.t[0:r, 0:n], in0=tmpa.t[0:r, 0:n], scalar=1.0,
                                                   in1=tmpx.t[0:r, 0:n], op0=ALU.add, op1=ALU.mult),
             R=[tmpx.b, tmpa.b], W=[tmpa.b])
        S.op("act", lambda: A.activation(out=tmpa.t[0:r, 0:n], in_=tmpa.t[0:r, 0:n], func=AF.Sigmoid,
                                         scale=1.5957691216), R=[tmpa.b], W=[tmpa.b])
        S.op("dve", lambda: V.tensor_tensor(out=out_ap, in0=tmpx.t[0:r, 0:n], in1=tmpa.t[0:r, 0:n], op=ALU.mult),
             R=[tmpx.b, tmpa.b], W=[out_buf])

    FMB_H = split_cols(0, TC)
    FMB = split_cols(3, T)

    def outproj(wsrc2d, r0):
        for cb in range(4):
            src = wsrc2d[r0:r0 + 512, cb * 512:(cb + 1) * 512].rearrange("(kc p) n -> p kc n", p=128)
            sl = wload(src)

            def cons(i, bk, cb=cb):
                r = rows(i)
                S.op("dve", lambda: V.tensor_tensor(out=h.t[0:r, i, cb * 512:(cb + 1) * 512],
                                                    in0=h.t[0:r, i, cb * 512:(cb + 1) * 512], in1=bk.t[0:r, :],
                                                    op=ALU.add), R=[bk.b, h.bufs[i]], W=[h.bufs[i]])
            gemm_tm([sl], yb.t, yb.b, tcol, cons)

    def ssd_phase(d, seg, sc):
        pc = d * 200
        L = lambda name, shape, dt, nb=1: TL(name + "_%d_%d" % (d, seg), shape, dt, nb=nb, sc=sc)
        make_ring(sc, RING_SSD, "s%d_%d" % (d, seg))
        Wdt = L("Wdt", [128, 16, 32], BF16)
        dtv = L("dtv", [128, NT, 32], F32)
        dtA = L("dtA", [128, NT, 32], F32)
        eac = L("eac", [128, NT, 32], F32)
        dtdec = L("dtdec", [128, NT, 32], F32)
        cdec = L("cdec", [128, NT, 32], F32)
        tmp32 = L("tmp32", [128, 64], F32)
        xp = L("xp", [128, TC], F32)
        xps = L("xps", [128, NSS, 7], F32)
        hsq = L("hsq", [128, 4, NSS * 3], F32)
        cvo1 = L("cvo1", [128, 3 + NSS * 3], F32)
        cvt1 = L("cvt1", [64, 128], F32)
        cacc = L("cacc", [128, T], F32)
        BTg = L("BTg", [128, T], BF16)
        CTg = L("CTg", [128, T], BF16)
        Btok = L("Btok", [128, NT, 128], BF16)
        xtok = L("xtok", [128, NT, 512], BF16)
        cbm2 = [L("cbm%d" % k, [128, 128], F32) for k in range(2)]
        rh2 = [L("rh%d" % k, [128, 4, 128], F32) for k in range(2)]
        Mh2 = [L("Mh%d" % k, [128, 8, 128], BF16) for k in range(2)]
        xdt2 = [L("xdt%d" % k, [128, 512], BF16) for k in range(2)]
        xdts2 = [L("xdts%d" % k, [128, 512], BF16) for k in range(2)]
        prev = L("prev", [128, 512], F32)
        prevb = L("prevb", [128, 512], BF16)
        t1 = L("t1", [128, 512], F32)
        sqf = L("sqf", [128, 512], F32)
        sqf2 = [sqf, L("sqfb", [128, 512], F32)]
        cvst = sqf2[1]
        Hs = [L("Hs%d" % k, [128, 4, 128], F32) for k in range(2)]
        HTb = L("HTb", [128, 512], BF16)
        Cmb = L("Cmb", [128, 64], BF16)
        Bmb = L("Bmb", [64, 128], BF16)
        dAT = L("dAT", [128, 4, NSS], F32)
        dtArep = sqf

        norm_to_nT(pc + O_GMIX)
        if seg == 0:
            S.op("dve", lambda: V.memset(nT.t[:, :, 0:3], 0.0), W=[nT.b])
        else:
            S.op("dve", lambda: V.tensor_copy(out=nT.t[:, :, 0:3], in_=ntail.t[:, :, :]), R=[ntail.b], W=[nT.b])
        if seg + 1 < NSEG:
            S.op("dve", lambda: V.tensor_copy(out=ntail.t[:, :, :], in_=nT.t[:, :, NTP:NTP + 3]), R=[nT.b],
                 W=[ntail.b])
        S.dma("sp", smb.t[:, :], smallp[d].partition_broadcast(128), W=[smb.b])
        S.op("act", lambda: A.activation(out=smb.t[:, 32:64], in_=smb.t[:, 32:64], func=AF.Exp), R=[smb.b], W=[smb.b])
        S.op("dve", lambda: V.tensor_scalar(out=smb.t[:, 32:64], in0=smb.t[:, 32:64], scalar1=-1.0, scalar2=None,
                                            op0=ALU.mult), R=[smb.b], W=[smb.b])
        S.dma("pool", Wdt.t[:, :, :], w_in[d][:, I2:I3].rearrange("(kc p) n -> p kc n", p=128), W=[Wdt.b])
        for i in range(NT):
            r = rows(i)
            c = ncol(i)
            bk = ps()
            fns = [lambda kc=kc: PE.matmul(bk.t[0:r, 0:32], lhsT=nT.t[:, kc, c:c + r], rhs=Wdt.t[:, kc, :],
                                           start=(kc == 0), stop=(kc == 15)) for kc in range(16)]
            S.group("pe", fns, R=[nT.b, Wdt.b], W=[bk.b])
            S.op("dve", lambda: V.tensor_tensor(out=tmp32.t[0:r, 0:32], in0=bk.t[0:r, 0:32], in1=smb.t[0:r, 0:32],
                                                op=ALU.add), R=[bk.b, smb.b], W=[tmp32.b])
            S.op("act", lambda: A.activation(out=tmp32.t[0:r, 0:32], in_=tmp32.t[0:r, 0:32], func=AF.Exp),
                 R=[tmp32.b], W=[tmp32.b])
            S.op("act", lambda: A.activation(out=dtv.t[0:r, i, :], in_=tmp32.t[0:r, 0:32], func=AF.Ln, bias=1.0),
                 R=[tmp32.b], W=[dtv.b])
            S.op("dve", lambda: V.tensor_tensor(out=dtA.t[0:r, i, :], in0=dtv.t[0:r, i, :], in1=smb.t[0:r, 32:64],
                                                op=ALU.mult), R=[dtv.b, smb.b], W=[dtA.b])
            bk2 = ps()
            mk = tri if i < NCH else maskS
            on = ones if i < NCH else blk1
            S.group("pe", [lambda: PE.matmul(bk2.t[0:r, 0:32], lhsT=mk.t[0:r, 0:r], rhs=dtA.t[0:r, i, :], start=True,
                                             stop=True),
                           lambda: PE.matmul(bk2.t[0:r, 32:64], lhsT=on.t[0:r, 0:r], rhs=dtA.t[0:r, i, :], start=True,
                                             stop=True)], R=[dtA.b] + CONSTS, W=[bk2.b])
            S.op("act", lambda: A.activation(out=eac.t[0:r, i, :], in_=bk2.t[0:r, 0:32], func=AF.Exp),
                 R=[bk2.b], W=[eac.b])
            S.op("act", lambda: A.activation(out=cdec.t[0:r, i, :], in_=bk2.t[0:r, 32:64], func=AF.Exp),
                 R=[bk2.b], W=[cdec.b])
            S.op("act", lambda: A.copy(out=tmp32.t[0:r, 32:64], in_=bk2.t[0:r, 32:64]), R=[bk2.b], W=[tmp32.b])
            S.op("dve", lambda: V.tensor_tensor(out=tmp32.t[0:r, 32:64], in0=tmp32.t[0:r, 32:64], in1=bk2.t[0:r, 0:32],
                                                op=ALU.subtract), R=[bk2.b, tmp32.b], W=[tmp32.b])
            S.op("act", lambda: A.activation(out=tmp32.t[0:r, 32:64], in_=tmp32.t[0:r, 32:64], func=AF.Exp),
                 R=[tmp32.b], W=[tmp32.b])
            S.op("dve", lambda: V.tensor_tensor(out=dtdec.t[0:r, i, :], in0=tmp32.t[0:r, 32:64], in1=dtv.t[0:r, i, :],
                                                op=ALU.mult), R=[tmp32.b, dtv.b], W=[dtdec.b])

        def prep_hist(ct0, nct):
            S.dma("sp", cvst.t[0:48, 0:nct * 128], scv[d, seg][:, ct0 * 128:(ct0 + nct) * 128], W=[cvst.b])
            for k in range(nct):
                bk = ps()
                S.op("pe", lambda: PE.transpose(bk.t[:, 0:48], cvst.t[0:48, k * 128:(k + 1) * 128], ident.t[0:48, 0:48]),
                     R=[cvst.b, ident.b], W=[bk.b])
                S.op("act", lambda: A.copy(out=hsq.t[:, k, :], in_=bk.t[:, 0:48]), R=[bk.b], W=[hsq.b])

        def conv_tile(ctg, k, kind):
            cw = pc + O_CW
            cb_ = pc + O_CB + ctg
            S.op("dve", lambda: V.tensor_copy(out=xps.t[:, :, 0:3],
                                              in_=hsq.t[:, k, :].rearrange("p (b j) -> p b j", j=3)),
                 R=[hsq.b], W=[xps.b])
            S.op("dve", lambda: V.tensor_copy(out=xps.t[:, :, 3:7],
                                              in_=xp.t[:, 3 + NTP:3 + NTP + TS].rearrange("p (b l) -> p b l", l=4)),
                 R=[xp.b, xps.b], W=[xps.b])
            S.op("act", lambda: A.copy(out=cvo1.t[:, 0:3], in_=xp.t[:, NTP:NTP + 3]), R=[xp.b], W=[cvo1.b])
            S.op("act", lambda: A.copy(out=cvo1.t[:, 3:3 + NSS * 3].rearrange("p (b j) -> p b j", j=3),
                                       in_=xps.t[:, :, 4:7]), R=[xps.b, cvo1.b], W=[cvo1.b])
            bk = ps()
            S.op("pe", lambda: PE.transpose(bk.t[0:51, 0:128], cvo1.t[:, :], ident.t[:, :]), R=[cvo1.b, ident.b],
                 W=[bk.b])
            S.op("act", lambda: A.copy(out=cvt1.t[0:51, :], in_=bk.t[0:51, 0:128]), R=[bk.b], W=[cvt1.b])
            S.dma("sp", conv_o[d, seg][:, ctg * 128:(ctg + 1) * 128], cvt1.t[0:51, :], R=[cvt1.b])
            S.op("dve", lambda: V.tensor_scalar(out=cacc.t[:, 0:NTP], in0=xp.t[:, 0:NTP],
                                                scalar1=PT.t[:, cw + ctg:cw + ctg + 1], scalar2=None, op0=ALU.mult),
                 R=[xp.b, PT.b], W=[cacc.b])
            for j in range(1, 4):
                S.op("dve", lambda: V.scalar_tensor_tensor(out=cacc.t[:, 0:NTP], in0=xp.t[:, j:j + NTP],
                                                           scalar=PT.t[:, cw + j * 24 + ctg:cw + j * 24 + ctg + 1],
                                                           in1=cacc.t[:, 0:NTP], op0=ALU.mult, op1=ALU.add),
                     R=[xp.b, PT.b, cacc.b], W=[cacc.b])
            cs = cacc.t[:, NTP:T].rearrange("p (b l) -> p b l", l=4)
            S.op("dve", lambda: V.tensor_scalar(out=cs, in0=xps.t[:, :, 0:4], scalar1=PT.t[:, cw + ctg:cw + ctg + 1],
                                                scalar2=None, op0=ALU.mult), R=[xps.b, PT.b, cacc.b], W=[cacc.b])
            for j in range(1, 4):
                S.op("dve", lambda: V.scalar_tensor_tensor(out=cs, in0=xps.t[:, :, j:j + 4],
                                                           scalar=PT.t[:, cw + j * 24 + ctg:cw + j * 24 + ctg + 1],
                                                           in1=cs, op0=ALU.mult, op1=ALU.add),
                     R=[xps.b, PT.b, cacc.b], W=[cacc.b])
            if kind == "C":
                S.op("act", lambda: A.activation(out=CTg.t[:, :], in_=cacc.t[:, :], func=AF.Silu,
                                                 bias=PT.t[:, cb_:cb_ + 1]), R=[cacc.b, PT.b], W=[CTg.b])
            else:
                S.op("act", lambda: A.activation(out=cacc.t[:, :], in_=cacc.t[:, :], func=AF.Silu,
                                                 bias=PT.t[:, cb_:cb_ + 1]), R=[cacc.b, PT.b], W=[cacc.b])
            if kind == "B":
                S.op("dve", lambda: V.tensor_copy(out=BTg.t[:, :], in_=cacc.t[:, :]), R=[cacc.b], W=[BTg.b])

        def to_tok(dst, width_off):
            for i0 in range(0, NT, 4):
                bk = ps()
                tiles = list(range(i0, min(NT, i0 + 4)))
                fns = [lambda i=i: PE.transpose(bk.t[0:rows(i), (i - i0) * 128:(i - i0 + 1) * 128],
                                                cacc.t[:, tcol(i):tcol(i) + rows(i)], ident.t[:, :]) for i in tiles]
                S.group("pe", fns, R=[cacc.b, ident.b], W=[bk.b])
                full = [i for i in tiles if rows(i) == 128]
                if full:
                    nf = len(full)
                    S.op("act", lambda: A.copy(out=dst.t[:, full[0]:full[0] + nf, width_off:width_off + 128],
                                               in_=bk.t[:, 0:nf * 128].rearrange("p (a b) -> p a b", a=nf)),
                         R=[bk.b], W=[dst.b])
                for i in tiles:
                    if rows(i) != 128:
                        S.op("act", lambda: A.copy(out=dst.t[0:TS, i, width_off:width_off + 128],
                                                   in_=bk.t[0:TS, (i - i0) * 128:(i - i0 + 1) * 128]),
                             R=[bk.b], W=[dst.b])

        def sample_states(g, bko, xdts):
            iS = NCH
            cS = tcol(iS)
            hsl = slice(g * 8, (g + 1) * 8)
            S.op("dve", lambda: V.tensor_copy(out=dtArep.t[0:TS, :].rearrange("p (a b) -> p a b", a=8),
                                              in_=dtA.t[0:TS, iS, hsl].unsqueeze(2).to_broadcast([TS, 8, 64])),
                 R=[dtA.b], W=[dtArep.b])
            bkd = ps()
            fns = [lambda rt=rt: PE.matmul(bkd.t[:, rt * NSS:(rt + 1) * NSS], lhsT=dtArep.t[0:TS, rt * 128:(rt + 1) * 128],
                                           rhs=rmask.t[:, :], start=True, stop=True) for rt in range(4)]
            S.group("pe", fns, R=[dtArep.b] + CONSTS, W=[bkd.b])
            S.op("act", lambda: A.activation(out=dAT.t[:, :, :],
                                             in_=bkd.t[:, 0:4 * NSS].rearrange("p (a b) -> p a b", a=4),
                                             func=AF.Exp), R=[bkd.b], W=[dAT.b])
            for b in range(NSS):
                H = Hs[b % 2]
                S.dma("sp", H.t[:, :, :], sst[d, seg, b][g * 512:(g + 1) * 512, :].rearrange("(rt p) n -> p rt n", p=128),
                      W=[H.b])
                S.op("dve", lambda: V.tensor_tensor(out=Cmb.t[:, :], in0=CTg.t[:, cS:cS + TS], in1=maskB3.t[:, b, :],
                                                    op=ALU.mult), R=[CTg.b] + CONSTS, W=[Cmb.b])
                S.op("dve", lambda: V.tensor_scalar(out=Bmb.t[:, :], in0=Btok.t[0:TS, iS, :],
                                                    scalar1=rmask.t[:, b:b + 1], scalar2=None, op0=ALU.mult),
                     R=[Btok.b] + CONSTS, W=[Bmb.b])
                bkt = ps()
                fns = [lambda rt=rt: PE.transpose(bkt.t[:, rt * 128:(rt + 1) * 128], H.t[:, rt, :], ident.t[:, :])
                       for rt in range(4)]
                S.group("pe", fns, R=[H.b, ident.b], W=[bkt.b])
                S.op("act", lambda: A.copy(out=HTb.t[:, :], in_=bkt.t[:, :]), R=[bkt.b], W=[HTb.b])
                S.op("pe", lambda: PE.matmul(bko.t[0:TS, :], lhsT=Cmb.t[:, :], rhs=HTb.t[:, :], start=(b == 0),
                                             stop=(b == NSS - 1)), R=[Cmb.b, HTb.b], W=[bko.b])
                bku = ps()
                fns = [lambda rt=rt: PE.matmul(bku.t[:, rt * 128:(rt + 1) * 128],
                                               lhsT=xdts.t[0:TS, rt * 128:(rt + 1) * 128], rhs=Bmb.t[:, :], start=True,
                                               stop=True) for rt in range(4)]
                S.group("pe", fns, R=[xdts.b, Bmb.b], W=[bku.b])
                S.op("dve", lambda: V.tensor_tensor(out=H.t[:, :, :], in0=H.t[:, :, :],
                                                    in1=dAT.t[:, :, b:b + 1].to_broadcast([128, 4, 128]), op=ALU.mult),
                     R=[H.b, dAT.b], W=[H.b])
                S.op("dve", lambda: V.tensor_tensor(out=H.t[:, :, :], in0=H.t[:, :, :],
                                                    in1=bku.t[:, :].rearrange("p (a b) -> p a b", a=4), op=ALU.add),
                     R=[H.b, bku.b], W=[H.b])
                S.dma("sp", ssm_s[d, seg, b][g * 512:(g + 1) * 512, :].rearrange("(rt p) n -> p rt n", p=128), H.t[:, :, :],
                      R=[H.b])

        for g in range(4):
            hsl = slice(g * 8, (g + 1) * 8)
            bslot = wload(w_in[d][:, I1 + 2048 + g * 128:I1 + 2048 + (g + 1) * 128].rearrange("(kc p) n -> p kc n", p=128))
            cslot = wload(w_in[d][:, I1 + 2560 + g * 128:I1 + 2560 + (g + 1) * 128].rearrange("(kc p) n -> p kc n", p=128))
            xslots_box = []
            jobs = [("B", 16 + g, None), ("C", 20 + g, None)] + [("x", g * 4 + ct, ct) for ct in range(4)]

            def job_gemm(job):
                kind, ctg, ct = job
                res_ = []
                for (c0, n) in FMB_H:
                    if kind == "x":
                        if not xslots_box:
                            xslots_box.extend(wblock(w_in[d], I1 + g * 512))
                        bk = mm_fm(xslots_box, ct, c0, n)
                    else:
                        slv, slb = bslot if kind == "B" else cslot
                        bk = ps()
                        fns = [lambda kc=kc: PE.matmul(bk.t[:, 0:n], lhsT=slv[:, kc, :], rhs=nT.t[:, kc, c0:c0 + n],
                                                       start=(kc == 0), stop=(kc == 15)) for kc in range(16)]
                        S.group("pe", fns, R=[nT.b, slb], W=[bk.b])
                    res_.append((c0, n, bk))
                return res_

            deferred = [None]
            for k, job in enumerate(jobs):
                kind, ctg, ct = job
                if kind != "x":
                    prep_hist(ctg, 1)
                    hk = 0
                else:
                    if ct == 0:
                        prep_hist(g * 4, 4)
                    hk = ct
                pend = job_gemm(job)
                if deferred[0] is not None:
                    to_tok(*deferred[0])
                    deferred[0] = None
                for (c0, n, bk) in pend:
                    S.op("act", lambda: A.copy(out=xp.t[:, c0:c0 + n], in_=bk.t[:, 0:n]), R=[bk.b], W=[xp.b])
                conv_tile(ctg, hk, kind)
                if kind == "B":
                    deferred[0] = (Btok, 0)
                elif kind == "x":
                    deferred[0] = (xtok, ct * 128)
            if deferred[0] is not None:
                to_tok(*deferred[0])
                deferred[0] = None

            if seg == 0:
                S.op("dve", lambda: V.memset(prev.t[:, :], 0.0), W=[prev.b])
            else:
                S.dma("sp", prev.t[:, :], hstate[d, g], R=[hstate_b[d][g]], W=[prev.b])
            S.op("act", lambda: A.copy(out=prevb.t[:, :], in_=prev.t[:, :]), R=[prev.b], W=[prevb.b])

            def front(i):
                r = rows(i)
                c = tcol(i)
                p = i % 2
                mk = tri if i < NCH else maskS
                cbm_, rh_, Mh_, xdt_, xdts_, bky = cbm2[p], rh2[p], Mh2[p], xdt2[p], xdts2[p], bky2[p]
                bkc = ps()
                S.op("pe", lambda: PE.matmul(bkc.t[0:r, 0:r], lhsT=BTg.t[:, c:c + r], rhs=CTg.t[:, c:c + r], start=True,
                                             stop=True), R=[BTg.b, CTg.b], W=[bkc.b])
                S.op("dve", lambda: V.tensor_tensor(out=cbm_.t[0:r, 0:r], in0=bkc.t[0:r, 0:r], in1=mk.t[0:r, 0:r],
                                                    op=ALU.mult), R=[bkc.b] + CONSTS, W=[cbm_.b])
                for h4 in range(2):
                    bks = ps()
                    for hh in range(4):
                        hd = g * 8 + h4 * 4 + hh
                        S.op("dve", lambda: V.tensor_scalar(out=rh_.t[0:r, hh, 0:r], in0=tri.t[0:r, 0:r],
                                                            scalar1=dtA.t[0:r, i, hd:hd + 1], scalar2=None,
                                                            op0=ALU.mult), R=[dtA.b] + CONSTS, W=[rh_.b])
                    fns = [lambda hh=hh: PE.matmul(bks.t[0:r, hh * 128:hh * 128 + r], lhsT=ustr.t[0:r, 0:r],
                                                   rhs=rh_.t[0:r, hh, 0:r], start=True, stop=True) for hh in range(4)]
                    S.group("pe", fns, R=[rh_.b] + CONSTS, W=[bks.b])
                    S.op("act", lambda: A.activation(out=rh_.t[0:r, :, 0:r],
                                                     in_=bks.t[0:r, :].rearrange("p (a b) -> p a b", a=4)[:, :, 0:r],
                                                     func=AF.Exp), R=[bks.b, rh_.b], W=[rh_.b])
                    S.op("dve", lambda: V.tensor_tensor(out=Mh_.t[0:r, h4 * 4:h4 * 4 + 4, 0:r], in0=rh_.t[0:r, :, 0:r],
                                                        in1=cbm_.t[0:r, 0:r].unsqueeze(1).to_broadcast([r, 4, r]),
                                                        op=ALU.mult), R=[rh_.b, cbm_.b], W=[Mh_.b])
                xv = xtok.t[0:r, i, :].rearrange("p (a b) -> p a b", a=8)
                S.op("dve", lambda: V.tensor_tensor(out=xdt_.t[0:r, :].rearrange("p (a b) -> p a b", a=8), in0=xv,
                                                    in1=dtv.t[0:r, i, hsl].unsqueeze(2).to_broadcast([r, 8, 64]),
                                                    op=ALU.mult), R=[xtok.b, dtv.b], W=[xdt_.b])
                S.op("dve", lambda: V.tensor_tensor(out=xdts_.t[0:r, :].rearrange("p (a b) -> p a b", a=8), in0=xv,
                                                    in1=dtdec.t[0:r, i, hsl].unsqueeze(2).to_broadcast([r, 8, 64]),
                                                    op=ALU.mult), R=[xtok.b, dtdec.b], W=[xdts_.b])
                fns = [lambda hh=hh: PE.matmul(bky.t[0:r, hh * 64:(hh + 1) * 64], lhsT=Mh_.t[0:r, hh, 0:r],
                                               rhs=xdt_.t[0:r, hh * 64:(hh + 1) * 64], start=True, stop=True)
                       for hh in range(8)]
                S.group("pe", fns, R=[Mh_.b, xdt_.b], W=[bky.b])

            def back(i):
                r = rows(i)
                c = tcol(i)
                p = i % 2
                xdts_, bky = xdts2[p], bky2[p]
                xv = xtok.t[0:r, i, :].rearrange("p (a b) -> p a b", a=8)
                if i < NCH:
                    bko = ps()
                    S.op("pe", lambda: PE.matmul(bko.t[0:r, :], lhsT=CTg.t[:, c:c + r], rhs=prevb.t[:, :], start=True,
                                                 stop=True), R=[CTg.b, prevb.b], W=[bko.b])
                else:
                    bko = accB
                    sample_states(g, bko, xdts_)
                S.op("dve", lambda: V.tensor_tensor(out=t1.t[0:r, :].rearrange("p (a b) -> p a b", a=8),
                                                    in0=bko.t[0:r, :].rearrange("p (a b) -> p a b", a=8),
                                                    in1=eac.t[0:r, i, hsl].unsqueeze(2).to_broadcast([r, 8, 64]),
                                                    op=ALU.mult), R=[bko.b, eac.b], W=[t1.b])
                S.op("dve", lambda: V.tensor_tensor(out=t1.t[0:r, :], in0=t1.t[0:r, :], in1=bky.t[0:r, :], op=ALU.add),
                     R=[bky.b, t1.b], W=[t1.b])
                S.op("dve", lambda: V.tensor_tensor(out=sqf.t[0:r, :].rearrange("p (a b) -> p a b", a=8), in0=xv,
                                                    in1=smb.t[0:r, 64 + g * 8:64 + g * 8 + 8].unsqueeze(2).to_broadcast(
                                                        [r, 8, 64]), op=ALU.mult), R=[xtok.b, smb.b], W=[sqf.b])
                S.op("dve", lambda: V.tensor_tensor(out=t1.t[0:r, :], in0=t1.t[0:r, :], in1=sqf.t[0:r, :], op=ALU.add),
                     R=[t1.b, sqf.b], W=[t1.b])
                bkt = ps()
                fns = [lambda k4=k4: PE.transpose(bkt.t[:, k4 * 128:k4 * 128 + r], t1.t[0:r, k4 * 128:(k4 + 1) * 128],
                                                  ident.t[0:r, 0:r]) for k4 in range(4)]
                S.group("pe", fns, R=[t1.b, ident.b], W=[bkt.b])
                S.op("act", lambda: A.copy(out=yb.t[:, :, c:c + r],
                                           in_=bkt.t[:, :].rearrange("p (a b) -> p a b", a=4)[:, :, 0:r]),
                     R=[bkt.b], W=[yb.b])
                if i < NCH:
                    bkS = ps()
                    S.op("pe", lambda: PE.matmul(bkS.t[:, :], lhsT=Btok.t[0:r, i, :], rhs=xdts_.t[0:r, :], start=True,
                                                 stop=True), R=[Btok.b, xdts_.b], W=[bkS.b])
                    S.op("dve", lambda: V.tensor_tensor(out=prev.t[:, :].rearrange("p (a b) -> p a b", a=8),
                                                        in0=prev.t[:, :].rearrange("p (a b) -> p a b", a=8),
                                                        in1=cdec.t[:, i, hsl].unsqueeze(2).to_broadcast([128, 8, 64]),
                                                        op=ALU.mult), R=[prev.b, cdec.b], W=[prev.b])
                    S.op("dve", lambda: V.tensor_tensor(out=prev.t[:, :], in0=prev.t[:, :], in1=bkS.t[:, :], op=ALU.add),
                         R=[prev.b, bkS.b], W=[prev.b])
                    S.op("act", lambda: A.copy(out=prevb.t[:, :], in_=prev.t[:, :]), R=[prev.b], W=[prevb.b])

            for i in range(NT + 1):
                if i < NT:
                    front(i)
                if i >= 1:
                    back(i - 1)
            if seg + 1 < NSEG:
                S.dma("sp", hstate[d, g], prev.t[:, :], R=[prev.b], W=[hstate_b[d][g]])
            else:
                bkf = ps()
                fns = [lambda k4=k4: PE.transpose(bkf.t[:, k4 * 128:(k4 + 1) * 128],
                                                  prev.t[:, k4 * 128:(k4 + 1) * 128], ident.t[:, :])
                       for k4 in range(4)]
                S.group("pe", fns, R=[prev.b, ident.b], W=[bkf.b])
                S.op("act", lambda: A.copy(out=t1.t[:, :], in_=bkf.t[:, :]), R=[bkf.b], W=[t1.b])
                S.dma("sp", ssm_p[d][g * 512:(g + 1) * 512, :].rearrange("(rt p) n -> p rt n", p=128),
                      t1.t[:, :].rearrange("p (a b) -> p a b", a=4), R=[t1.b])


            slots = wblock(w_in[d], g * 512)
            for (c0, n) in FMB:
                tc0 = c0 - 3
                pendz = None

                def ones_mm(pz):
                    ctp, sqp = pz
                    S.op("pe", lambda: PE.matmul(accA.t[:, 0:n], lhsT=ones.t[:, :], rhs=sqp.t[:, 0:n], start=(ctp == 0),
                                                 stop=(ctp == 3)), R=[sqp.b, ones.b], W=[accA.b])
                for ct in range(4):
                    bk = mm_fm(slots, ct, c0, n)
                    if pendz is not None:
                        ones_mm(pendz)
                    sq_ = sqf2[ct % 2]
                    S.op("act", lambda: A.activation(out=sq_.t[:, 0:n], in_=bk.t[:, 0:n], func=AF.Silu),
                         R=[bk.b], W=[sq_.b])
                    S.op("dve", lambda: V.tensor_tensor(out=yb.t[:, ct, tc0:tc0 + n], in0=yb.t[:, ct, tc0:tc0 + n],
                                                        in1=sq_.t[:, 0:n], op=ALU.mult), R=[yb.b, sq_.b], W=[yb.b])
                    S.op("act", lambda: A.activation(out=sq_.t[:, 0:n], in_=yb.t[:, ct, tc0:tc0 + n], func=AF.Square),
                         R=[yb.b, sq_.b], W=[sq_.b])
                    pendz = (ct, sq_)
                ones_mm(pendz)
                S.op("act", lambda: A.activation(out=t1.t[:, 0:n], in_=accA.t[:, 0:n], func=AF.Sqrt, scale=1.0 / 512,
                                                 bias=epsb.t[:, 0:1]), R=[accA.b, t1.b, epsb.b], W=[t1.b])
                S.op("dve", lambda: V.reciprocal(out=t1.t[:, 0:n], in_=t1.t[:, 0:n]), R=[t1.b], W=[t1.b])
                for ct in range(4):
                    gc = pc + O_GSSD + g * 4 + ct
                    S.op("dve", lambda: V.scalar_tensor_tensor(out=yb.t[:, ct, tc0:tc0 + n], in0=yb.t[:, ct, tc0:tc0 + n],
                                                               scalar=PT.t[:, gc:gc + 1], in1=t1.t[:, 0:n],
                                                               op0=ALU.mult, op1=ALU.mult),
                         R=[yb.b, PT.b, t1.b], W=[yb.b])
            if d == 0 and g == 0 and seg == 0:
                dump_yb("dbg_yssd0")
            outproj(w_out[d], g * 512)

    def cm_phase(d, seg, sc):
        pc = d * 200
        L = lambda name, shape, dt, nb=1: TL(name + "_%d_%d" % (d, seg), shape, dt, nb=nb, sc=sc)
        make_ring(sc, RING_CM, "c%d_%d" % (d, seg))
        vg = L("vg", [128, NT, 2048], BF16)
        gx = L("gx", [128, 512], F32)
        ga = L("ga", [128, 512], F32)
        ssq = L("ssq", [128, NT, 4], F32)
        Wn = L("Wn", [128, 4, 128], F32)
        WsT = L("WsT", [128, 4, 128], F32)
        Wr = L("Wr", [128, 128], BF16)
        WsS = L("WsS", [64, 16, 64], F32)
        bsb = L("bsb", [128, 4, 192], F32)
        tmpc = L("tmpc", [128, 128], F32)
        gcb = L("gcb", [64, 512], F32)
        vso = L("vso", [64, 512], F32)
        W4n = L("W4n", [4, 16, 4], F32)
        bs4 = L("bs4", [128, 4, 4], F32)
        o1s = gx
        for j in range(4):
            slots = wblock(w_in[d], I4 + j * 512)

            def cons(i, bk, j=j):
                r = rows(i)
                gelu_psum(bk, r, 512, vg.t[0:r, i, j * 512:(j + 1) * 512], vg.b, gx, ga)
                S.op("act", lambda: A.activation(out=ga.t[0:r, :], in_=vg.t[0:r, i, j * 512:(j + 1) * 512],
                                                 func=AF.Square, accum_out=ssq.t[0:r, i, j:j + 1]),
                     R=[vg.b, ga.b], W=[ga.b, ssq.b])
            gemm_tm(slots, nT.t, nT.b, ncol, cons)
        for i in range(NT):
            r = rows(i)
            S.op("dve", lambda: V.tensor_reduce(out=stat.t[0:r, 8 * i + 4:8 * i + 5], in_=ssq.t[0:r, i, :], axis=AX.X,
                                                op=ALU.add), R=[ssq.b, stat.bufs[i]], W=[stat.bufs[i]])
            rstd_of(i, 2048)
        iS = NCH
        for q in range(4):
            S.dma("sp", gcb.t[:, :], g_cm_d[d][q * 512:(q + 1) * 512].partition_broadcast(64), W=[gcb.b])
            S.op("dve", lambda: V.scalar_tensor_tensor(out=vso.t[:, :], in0=vg.t[0:TS, iS, q * 512:(q + 1) * 512],
                                                       scalar=stat.t[0:TS, 8 * iS + 6:8 * iS + 7], in1=gcb.t[:, :],
                                                       op0=ALU.mult, op1=ALU.mult),
                 R=[vg.b, stat.bufs[iS], gcb.b], W=[vso.b])
            S.dma("sp", v_s[d, seg][:, q * 512:(q + 1) * 512], vso.t[:, :], R=[vso.b])
        S.dma("sp", W4n.t[0:4, :, :], w_s_d[d][:, 0:4, 0:4].rearrange("g t s -> t g s"), W=[W4n.b])
        for hf in range(2):
            bk = ps()
            fns = [lambda g8=g8: PE.matmul(bk.t[0:4, g8 * 64:(g8 + 1) * 64], lhsT=W4n.t[0:4, hf * 8 + g8, :],
                                           rhs=Rm.t[0:4, :], start=True, stop=True) for g8 in range(8)]
            S.group("pe", fns, R=[W4n.b, Rm.b], W=[bk.b])
            S.op("act", lambda: A.copy(out=o1s.t[0:4, :], in_=bk.t[0:4, :]), R=[bk.b], W=[o1s.b])
            bk2 = ps()
            S.op("pe", lambda: PE.matmul(bk2.t[0:64, :], lhsT=Rm.t[0:4, :], rhs=o1s.t[0:4, :], start=True, stop=True),
                 R=[Rm.b, o1s.b], W=[bk2.b])
            S.op("dve", lambda: V.tensor_tensor(out=WsS.t[:, hf * 8:(hf + 1) * 8, :],
                                                in0=bk2.t[0:64, :].rearrange("p (a b) -> p a b", a=8),
                                                in1=maskS.t[:, :].unsqueeze(1).to_broadcast([64, 8, 64]), op=ALU.mult),
                 R=[bk2.b] + CONSTS, W=[WsS.b])
        for j in range(4):
            S.dma("sp", Wn.t[:, :, :], w_s_d[d][j * 4:(j + 1) * 4].rearrange("g t s -> t g s"), W=[Wn.b])
            bk = ps()
            fns = [lambda k=k: PE.transpose(bk.t[:, k * 128:(k + 1) * 128], Wn.t[:, k, :], ident.t[:, :])
                   for k in range(4)]
            S.group("pe", fns, R=[Wn.b, ident.b], W=[bk.b])
            S.op("dve", lambda: V.tensor_tensor(out=WsT.t[:, :, :], in0=bk.t[:, :].rearrange("p (a b) -> p a b", a=4),
                                                in1=tri.t[:, :].unsqueeze(1).to_broadcast([128, 4, 128]), op=ALU.mult),
                 R=[bk.b] + CONSTS, W=[WsT.b])
            S.dma("sp", bsb.t[:, :, 0:128], b_s_d[d][j * 4:(j + 1) * 4, :].partition_broadcast(128), W=[bsb.b])
            S.dma("sp", bs4.t[:, :, :], b_s_d[d][j * 4:(j + 1) * 4, 0:4].partition_broadcast(128), W=[bs4.b])
            S.op("dve", lambda: V.tensor_copy(out=bsb.t[:, :, 128:192].rearrange("p g (b l) -> p g b l", l=4),
                                              in_=bs4.t[:, :, :].unsqueeze(2).to_broadcast([128, 4, NSS, 4])),
                 R=[bs4.b, bsb.b], W=[bsb.b])
            slots = wblock(w_in[d], I3 + j * 512)
            for ct in range(4):
                for (c0, n) in FMB:
                    bk = mm_fm(slots, ct, c0, n)
                    gelu_psum(bk, 128, n, yb.t[:, ct, c0 - 3:c0 - 3 + n], yb.b, gx, ga)
            for ct in range(4):
                gg = j * 4 + ct
                for i in range(NT):
                    r = rows(i)
                    c = tcol(i)
                    rs = stat.t[0:r, 8 * i + 6:8 * i + 7]
                    if i < NCH:
                        S.op("dve", lambda: V.tensor_scalar(out=Wr.t[0:r, 0:r], in0=WsT.t[0:r, ct, 0:r], scalar1=rs,
                                                            scalar2=None, op0=ALU.mult),
                             R=[WsT.b, stat.bufs[i]], W=[Wr.b])
                        boff = 0
                    else:
                        S.op("dve", lambda: V.tensor_scalar(out=Wr.t[0:r, 0:r], in0=WsS.t[0:r, gg, 0:r], scalar1=rs,
                                                            scalar2=None, op0=ALU.mult),
                             R=[WsS.b, stat.bufs[i]], W=[Wr.b])
                        boff = 128
                    bk = ps()
                    S.op("pe", lambda: PE.matmul(bk.t[:, 0:r], lhsT=vg.t[0:r, i, gg * 128:(gg + 1) * 128],
                                                 rhs=Wr.t[0:r, 0:r], start=True, stop=True), R=[vg.b, Wr.b], W=[bk.b])
                    gcol = pc + O_GCM + gg
                    S.op("dve", lambda: V.scalar_tensor_tensor(out=tmpc.t[:, 0:r], in0=bk.t[:, 0:r],
                                                               scalar=PT.t[:, gcol:gcol + 1],
                                                               in1=bsb.t[:, ct, boff:boff + r], op0=ALU.mult,
                                                               op1=ALU.add), R=[bk.b, PT.b, bsb.b], W=[tmpc.b])
                    S.op("dve", lambda: V.tensor_tensor(out=yb.t[:, ct, c:c + r], in0=yb.t[:, ct, c:c + r],
                                                        in1=tmpc.t[:, 0:r], op=ALU.mult), R=[yb.b, tmpc.b], W=[yb.b])
            if d == 0 and j == 0 and seg == 0:
                dump_yb("dbg_ycm0")
            outproj(w_out[d], 2048 + j * 512)

    def ffn_phase(d, seg, sc):
        pc = d * 200
        make_ring(sc, RING_FFN, "f%d_%d" % (d, seg))
        norm_to_nT(pc + O_GFFN)
        for blk in range(DFF // 512):
            slots = wblock(w_gate[d], blk * 512)
            for ct in range(4):
                for (c0, n) in FMB:
                    bk = mm_fm(slots, ct, c0, n)
                    S.op("act", lambda: A.activation(out=yb.t[:, ct, c0 - 3:c0 - 3 + n], in_=bk.t[:, 0:n], func=AF.Silu),
                         R=[bk.b], W=[yb.b])
            slots = wblock(w_up[d], blk * 512)
            for ct in range(4):
                for (c0, n) in FMB:
                    bk = mm_fm(slots, ct, c0, n)
                    S.op("dve", lambda: V.tensor_tensor(out=yb.t[:, ct, c0 - 3:c0 - 3 + n],
                                                        in0=yb.t[:, ct, c0 - 3:c0 - 3 + n], in1=bk.t[:, 0:n],
                                                        op=ALU.mult), R=[bk.b, yb.b], W=[yb.b])
            outproj(w_down[d], blk * 512)

    def ple_phase(d, seg, sc):
        pc = d * 200
        L = lambda name, shape, dt, nb=1: TL(name + "_%d_%d" % (d, seg), shape, dt, nb=nb, sc=sc)
        make_ring(sc, RING_PLE, "p%d_%d" % (d, seg))
        pT = L("pT", [128, 2, T], BF16)
        ptok = L("ptok", [128, DPLE], F32)
        gsig = L("gsig", [128, 512], F32)
        for i in range(NT):
            r = rows(i)
            c = tcol(i)
            S.dma("sp", ptok.t[0:r, :], pin[d, seg, c:c + r, :], W=[ptok.b])
            bk = ps()
            fns = [lambda k=k: PE.transpose(bk.t[:, k * 128:k * 128 + r], ptok.t[0:r, k * 128:(k + 1) * 128],
                                            ident.t[0:r, 0:r]) for k in range(2)]
            S.group("pe", fns, R=[ptok.b, ident.b], W=[bk.b])
            S.op("act", lambda: A.copy(out=pT.t[:, :, c:c + r],
                                       in_=bk.t[:, 0:256].rearrange("p (a b) -> p a b", a=2)[:, :, 0:r]),
                 R=[bk.b], W=[pT.b])
        norm_to_nT(pc + O_GPG)
        wpl = L("wpl", [128, 2, D], BF16)
        S.dma("pool", wpl.t[:, :, :], w_ple[d].rearrange("(kc p) n -> p kc n", p=128), W=[wpl.b])
        for cb in range(4):
            slots = wblock(w_pg[d], cb * 512)

            def cons(i, bk, cb=cb):
                r = rows(i)
                c = tcol(i)
                S.op("act", lambda: A.activation(out=gsig.t[0:r, :], in_=bk.t[0:r, :], func=AF.Sigmoid),
                     R=[bk.b], W=[gsig.b])
                bkp = ps()
                fns = [lambda k=k: PE.matmul(bkp.t[0:r, :], lhsT=pT.t[:, k, c:c + r],
                                             rhs=wpl.t[:, k, cb * 512:(cb + 1) * 512], start=(k == 0), stop=(k == 1))
                       for k in range(2)]
                S.group("pe", fns, R=[pT.b, wpl.b], W=[bkp.b])
                S.op("dve", lambda: V.tensor_tensor(out=gsig.t[0:r, :], in0=gsig.t[0:r, :], in1=bkp.t[0:r, :],
                                                    op=ALU.mult), R=[gsig.b, bkp.b], W=[gsig.b])
                S.op("dve", lambda: V.tensor_tensor(out=h.t[0:r, i, cb * 512:(cb + 1) * 512],
                                                    in0=h.t[0:r, i, cb * 512:(cb + 1) * 512], in1=gsig.t[0:r, :],
                                                    op=ALU.add), R=[gsig.b, h.bufs[i]], W=[h.bufs[i]])
            gemm_tm(slots, nT.t, nT.b, ncol, cons)

    for d in range(DEPTH):
        for seg in range(NSEG):
            for i in range(NT):
                r = rows(i)
                if d == 0:
                    S.dma("sp", h.t[0:r, i, :], xin[seg, tcol(i):tcol(i) + r, :], W=[h.bufs[i]])
                else:
                    S.dma("sp", h.t[0:r, i, :], hbuf[seg, tcol(i):tcol(i) + r, :], R=[hbuf_b[seg][i]], W=[h.bufs[i]])
            for phase in (ssd_phase, cm_phase, ffn_phase, ple_phase):
                with ExitStack() as sc:
                    phase(d, seg, sc)
                    barrier()
            if d + 1 < DEPTH:
                for i in range(NT):
                    r = rows(i)
                    S.dma("sp", hbuf[seg, tcol(i):tcol(i) + r, :], h.t[0:r, i, :], R=[h.bufs[i]], W=[hbuf_b[seg][i]])
            else:
                with ExitStack() as sc_fin:
                    gfb = TL("gfb_%d" % seg, [128, 512], F32, sc=sc_fin)
                    for i in range(NT):
                        r = rows(i)
                        o = 8 * i
                        sumsq_h(i)
                        for q in range(4):
                            S.dma("sp", gfb.t[0:r, :], g_final_d[0][q * 512:(q + 1) * 512].partition_broadcast(r),
                                  W=[gfb.b])
                            S.op("dve", lambda: V.scalar_tensor_tensor(out=h.t[0:r, i, q * 512:(q + 1) * 512],
                                                                       in0=h.t[0:r, i, q * 512:(q + 1) * 512],
                                                                       scalar=stat.t[0:r, o + 6:o + 7], in1=gfb.t[0:r, :],
                                                                       op0=ALU.mult, op1=ALU.mult),
                                 R=[h.bufs[i], stat.bufs[i], gfb.b], W=[h.bufs[i]])
                        S.dma("sp", yout[seg, tcol(i):tcol(i) + r, :], h.t[0:r, i, :], R=[h.bufs[i]])
                    barrier()
    barrier()
    es.close()
    return nc


def _prep_inputs(inp, NTP, DEPTH, ncores):
    f = lambda a: np.ascontiguousarray(np.asarray(a, dtype=np.float32))
    xp_, xs_ = f(inp["x_prompt"]), f(inp["x_sample"])
    pp_, ps_ = f(inp["p_prompt"]), f(inp["p_sample"])
    sst, scv = f(inp["state_ssm"]), f(inp["state_conv"])
    rows_ = []
    for d in range(DEPTH):
        rows_ += [f(inp["g_mix"])[d].reshape(16, 128), f(inp["g_ffn"])[d].reshape(16, 128),
                  f(inp["g_pg"])[d].reshape(16, 128), f(inp["g_ssd"])[d].reshape(16, 128),
                  f(inp["g_cm"])[d].reshape(16, 128), f(inp["conv_w"])[d].reshape(4 * 24, 128),
                  f(inp["conv_b"])[d].reshape(24, 128)]
    rows_.append(f(inp["g_final"]).reshape(16, 128))
    prm = np.ascontiguousarray(np.concatenate(rows_, axis=0))
    smallp = np.ascontiguousarray(np.concatenate([f(inp["dt_bias"]), f(inp["a_log"]), f(inp["d_skip"])], axis=1))
    shared = {"prm": prm, "smallp": smallp, "g_cm": f(inp["g_cm"]), "w_s": f(inp["w_s"]), "b_s": f(inp["b_s"]),
              "w_in": f(inp["w_in"]), "w_out": f(inp["w_out"]), "w_gate": f(inp["w_gate"]), "w_up": f(inp["w_up"]),
              "w_down": f(inp["w_down"]), "w_pg": f(inp["w_pg"]), "w_ple": f(inp["w_ple"]),
              "g_final": f(inp["g_final"]).reshape(1, 2048)}
    maps = []
    for c in range(ncores):
        sq = c % 4
        m = dict(shared)
        xin, pin, ss, sc_ = [], [], [], []
        for seg in range(2):
            sl = slice(seg * NTP, (seg + 1) * NTP)
            b0 = (sq * 2 + seg) * NSS
            bs = slice(b0, b0 + NSS)
            xin.append(np.concatenate([xp_[sq, sl], xs_[bs].reshape(TS, D)], axis=0))
            pin.append(np.concatenate([pp_[:, sq, sl], ps_[:, bs].reshape(DEPTH, TS, DPLE)], axis=1))
            ss.append(sst[:, bs].reshape(DEPTH, NSS, 2048, 128))
            sc_.append(scv[:, bs].reshape(DEPTH, NSS * 3, CONV))
        m["xin"] = np.ascontiguousarray(np.stack(xin, axis=0))
        m["pin"] = np.ascontiguousarray(np.stack(pin, axis=1))
        m["sst"] = np.ascontiguousarray(np.stack(ss, axis=1))
        m["scv"] = np.ascontiguousarray(np.stack(sc_, axis=1))
        maps.append(m)
    return maps


def _assemble(res, NTP, DEPTH, ncores, nb_prompt, nb_sample):
    SEQ = 2 * NTP
    y_p = np.zeros((nb_prompt, SEQ, D), np.float32)
    y_s = np.zeros((nb_sample, 4, D), np.float32)
    ssm_p = np.zeros((DEPTH, nb_prompt, 32, 64, 128), np.float32)
    conv_p = np.zeros((DEPTH, nb_prompt, 3, CONV), np.float32)
    ssm_s = np.zeros((DEPTH, nb_sample, 32, 64, 128), np.float32)
    conv_s = np.zeros((DEPTH, nb_sample, 3, CONV), np.float32)
    v_s = np.zeros((DEPTH, nb_sample, 4, 2048), np.float32)
    for c in range(4):
        r = res[c]
        sq = c
        for seg in range(2):
            b0 = (sq * 2 + seg) * NSS
            bs = slice(b0, b0 + NSS)
            y_p[sq, seg * NTP:(seg + 1) * NTP] = r["yout"][seg, :NTP]
            y_s[bs] = r["yout"][seg, NTP:].reshape(NSS, 4, D)
            ssm_s[:, bs] = r["ssm_s"][:, seg].reshape(DEPTH, NSS, 32, 64, 128)
            conv_s[:, bs] = r["conv_o"][:, seg, 3:].reshape(DEPTH, NSS, 3, CONV)
            v_s[:, bs] = r["v_s"][:, seg].reshape(DEPTH, NSS, 4, 2048)
        ssm_p[:, sq] = r["ssm_p"].reshape(DEPTH, 32, 64, 128)
        conv_p[:, sq] = r["conv_o"][:, 1, 0:3]
    return (y_p, y_s, ssm_p, conv_p, ssm_s, conv_s, v_s)


LAST_RES = None


def kernel(**inputs):
    global LAST_RES
    DEPTH = int(np.asarray(inputs["w_in"]).shape[0])
    nbp, SEQ = np.asarray(inputs["x_prompt"]).shape[:2]
    nbs = np.asarray(inputs["x_sample"]).shape[0]
    ncores = 8
    NTP = SEQ // 2
    nc = build(NTP, DEPTH)
    maps = _prep_inputs(inputs, NTP, DEPTH, ncores)
    res = run_bass_kernel_spmd(nc, maps, core_ids=list(range(ncores)))
    if DBG:
        LAST_RES = res.results
    return _assemble(res.results, NTP, DEPTH, ncores, nbp, nbs)
```

```python
import numpy as np
from contextlib import ExitStack
import concourse.bass as bass
import concourse.mybir as mybir
from concourse.bass_utils import run_bass_kernel_spmd

F32 = mybir.dt.float32
BF16 = mybir.dt.bfloat16
AF = mybir.ActivationFunctionType
ALU = mybir.AluOpType
AX = mybir.AxisListType

D = 2048
DIN = 9248
DFF = 5632
DPLE = 256
CONV = 3072
I1 = 2048
I2 = I1 + 3072
I3 = I2 + 32
I4 = I3 + 2048
EPS = 1e-6
NSS = 16
TS = 64
SAME_ENGINE_SYNC = True
RING_SSD, RING_CM, RING_FFN, RING_PLE = 5, 5, 16, 12


class Buf:
    __slots__ = ("w", "r", "name", "pend")

    def __init__(self, name=""):
        self.w = None
        self.r = {}
        self.name = name
        self.pend = False


class Sched:
    def __init__(self, nc, es, ndma=24):
        self.nc = nc
        self.eng = {"pe": nc.tensor, "dve": nc.vector, "act": nc.scalar, "pool": nc.gpsimd, "sp": nc.sync}
        self.sem = {}
        for k in ["pe", "dve", "act", "pool"]:
            self.sem[k] = es.enter_context(nc.semaphore("s_" + k))
        self.cnt = {k: 0 for k in self.sem}
        self.seen = {e: {} for e in self.eng}
        self.dsem = [es.enter_context(nc.semaphore("s_dma%d" % i)) for i in range(ndma)]
        self.dcnt = [0] * ndma
        self.drr = 0
        self.ndma = ndma

    def _semof(self, key):
        if isinstance(key, tuple):
            return self.dsem[key[1]]
        return self.sem[key]

    def _wait(self, e, deps):
        best = {}
        for d in deps:
            if d is None:
                continue
            key, val = d
            if key == e and not SAME_ENGINE_SYNC:
                continue
            if val > best.get(key, 0):
                best[key] = val
        for key, val in best.items():
            if self.seen[e].get(key, 0) >= val:
                continue
            self.eng[e].wait_ge(self._semof(key), val)
            self.seen[e][key] = val

    def _deps(self, R, W):
        deps = []
        for b in R:
            deps.append(b.w)
        for b in W:
            deps.append(b.w)
            deps.extend(b.r.values())
        return deps

    def _mark(self, tok, R, W):
        key = tok[0]
        for b in R:
            b.pend = False
            b.r[key] = tok
        for b in W:
            b.w = tok
            b.r = {}

    def op(self, e, fn, R=(), W=()):
        self._wait(e, self._deps(R, W))
        inst = fn()
        self.cnt[e] += 1
        inst.then_inc(self.sem[e], 1)
        self._mark((e, self.cnt[e]), R, W)

    def group(self, e, fns, R=(), W=()):
        self._wait(e, self._deps(R, W))
        inst = None
        for fn in fns:
            inst = fn()
        self.cnt[e] += 1
        inst.then_inc(self.sem[e], 1)
        self._mark((e, self.cnt[e]), R, W)

    def dma(self, q, out, in_, R=(), W=(), **kw):
        i = self.drr
        self.drr = (i + 1) % self.ndma
        deps = self._deps(R, W)
        if self.dcnt[i] > 0:
            deps.append((("dma", i), self.dcnt[i]))
        self._wait(q, deps)
        inst = self.eng[q].dma_start(out=out, in_=in_, **kw)
        self.dcnt[i] += 16
        inst.then_inc(self.dsem[i], 16)
        self._mark((("dma", i), self.dcnt[i]), R, W)

    def drain(self, e="sp"):
        deps = [(("dma", i), self.dcnt[i]) for i in range(self.ndma) if self.dcnt[i] > 0]
        deps += [(k, self.cnt[k]) for k in self.cnt if self.cnt[k] > 0]
        self._wait(e, deps)


def split_cols(c0, n, mx=512):
    out = []
    while n > 0:
        k = min(mx, n)
        out.append((c0, k))
        c0 += k
        n -= k
    return out


DBG = False


def build(NTP, DEPTH, stop_after=None):
    NCH = NTP // 128
    NT = NCH + 1
    T = NTP + TS
    TC = 3 + T
    nc = bass.Bass("TRN2", target_bir_lowering=False)

    def din(name, shape):
        return nc.dram_tensor(name, list(shape), F32, kind="ExternalInput").ap()

    def dout(name, shape):
        return nc.dram_tensor(name, list(shape), F32, kind="ExternalOutput").ap()

    NSEG = 2
    xin = din("xin", [NSEG, T, D])
    pin = din("pin", [DEPTH, NSEG, T, DPLE])
    sst = din("sst", [DEPTH, NSEG, NSS, 2048, 128])
    scv = din("scv", [DEPTH, NSEG, NSS * 3, CONV])
    prm = din("prm", [DEPTH * 200 + 16, 128])
    smallp = din("smallp", [DEPTH, 96])
    g_cm_d = din("g_cm", [DEPTH, 2048])
    w_s_d = din("w_s", [DEPTH, 16, 128, 128])
    b_s_d = din("b_s", [DEPTH, 16, 128])
    w_in = din("w_in", [DEPTH, D, DIN])
    w_out = din("w_out", [DEPTH, 4096, D])
    w_gate = din("w_gate", [DEPTH, D, DFF])
    w_up = din("w_up", [DEPTH, D, DFF])
    w_down = din("w_down", [DEPTH, DFF, D])
    w_pg = din("w_pg", [DEPTH, D, D])
    w_ple = din("w_ple", [DEPTH, DPLE, D])
    g_final_d = din("g_final", [1, 2048])

    yout = dout("yout", [NSEG, T, D])
    ssm_p = dout("ssm_p", [DEPTH, 2048, 128])
    conv_o = dout("conv_o", [DEPTH, NSEG, 3 + NSS * 3, CONV])
    ssm_s = dout("ssm_s", [DEPTH, NSEG, NSS, 2048, 128])
    v_s = dout("v_s", [DEPTH, NSEG, TS, 2048])
    hbuf = nc.dram_tensor("hbuf", [NSEG, T, D], F32, kind="Internal").ap()
    hstate = nc.dram_tensor("hstate", [DEPTH, 4, 128, 512], F32, kind="Internal").ap()
    hbuf_b = [[Buf() for _ in range(NTP // 128 + 1)] for _ in range(NSEG)]
    hstate_b = [[Buf() for _ in range(4)] for _ in range(DEPTH)]

    es = ExitStack()
    S = Sched(nc, es)

    class TL:
        def __init__(self, name, shape, dt, nb=1, psum=False, sc=None):
            f = nc.psum_tensor if psum else nc.sbuf_tensor
            self.t = (sc or es).enter_context(f(name, list(shape), dt))
            self.bufs = [Buf(name + str(i)) for i in range(nb)]
            self.b = self.bufs[0]

    def rows(i):
        return 128 if i < NCH else TS

    def ncol(i):
        return 3 + i * 128

    def tcol(i):
        return i * 128

    h = TL("h", [128, NT, D], F32, nb=NT)
    nT = TL("nT", [128, 16, TC], BF16)
    ring = []
    ring_i = [0]

    def make_ring(sc, n, tag):
        ring[:] = [TL("ring%s_%d" % (tag, k), [128, 2048], BF16, sc=sc) for k in range(n)]
        ring_i[0] = 0
    scr = TL("scr", [128, 512], F32)
    PT = TL("PT", [128, DEPTH * 200 + 16], F32)
    ident = TL("ident", [128, 128], F32)
    tri = TL("tri", [128, 128], F32)
    ustr = TL("ustr", [128, 128], F32)
    ones = TL("ones", [128, 128], F32)
    maskS = TL("maskS", [64, 64], F32)
    blk1 = TL("blk1", [64, 64], F32)
    rmask = TL("rmask", [64, 16], F32)
    maskB3 = TL("maskB3", [128, 16, 64], BF16)
    smb = TL("smb", [128, 96], F32)
    CONSTS = [ident.b, tri.b, ustr.b, ones.b, maskS.b, blk1.b, rmask.b, maskB3.b]

    banks = [TL("bank%d" % i, [128, 512], F32, psum=True) for i in range(8)]
    NROT = 6
    rot = [0]

    def ps():
        b = banks[rot[0]]
        rot[0] = (rot[0] + 1) % NROT
        if b.b.pend:
            raise RuntimeError("PSUM rotation hazard: bank reused before its consumer was emitted")
        b.b.pend = True
        return b

    bky2 = [banks[6], banks[7]]
    accA = banks[6]
    accB = bky2[(NCH + 1) % 2]

    V = nc.vector
    A = nc.scalar
    PE = nc.tensor

    def const_tri(tl, n, cmp, base=0, mult=1, pat=-1):
        def f():
            nc.gpsimd.memset(tl.t[:], 1.0)
            return nc.gpsimd.affine_select(out=tl.t[:], in_=tl.t[:], pattern=[[pat, n]], compare_op=cmp,
                                           fill=0.0, base=base, channel_multiplier=mult)
        S.group("pool", [f], W=[tl.b])

    const_tri(ident, 128, ALU.is_equal)
    const_tri(tri, 128, ALU.is_ge, pat=1, mult=-1)
    const_tri(ustr, 128, ALU.is_gt, pat=-1, mult=1)
    S.op("pool", lambda: nc.gpsimd.memset(ones.t[:], 1.0), W=[ones.b])
    Rm = TL("Rm", [4, 64], F32)

    def f_rm():
        nc.gpsimd.memset(Rm.t[:], 1.0)
        return nc.gpsimd.affine_select(out=Rm.t[:].rearrange("p (b l) -> p b l", l=4),
                                       in_=Rm.t[:].rearrange("p (b l) -> p b l", l=4), pattern=[[0, 16], [1, 4]],
                                       compare_op=ALU.is_equal, fill=0.0, base=0, channel_multiplier=-1)
    S.group("pool", [f_rm], W=[Rm.b])
    def f_rmask():
        nc.gpsimd.memset(rmask.t[:], 1.0)
        nc.gpsimd.affine_select(out=rmask.t[:], in_=rmask.t[:], pattern=[[-4, 16]], compare_op=ALU.is_ge,
                                fill=0.0, base=0, channel_multiplier=1)
        return nc.gpsimd.affine_select(out=rmask.t[:], in_=rmask.t[:], pattern=[[4, 16]], compare_op=ALU.is_ge,
                                       fill=0.0, base=3, channel_multiplier=-1)
    S.group("pool", [f_rmask], W=[rmask.b])
    with ExitStack() as sc0:
        mb3f = TL("mb3f", [128, 16, 64], F32, sc=sc0)

        def f_mb3():
            nc.gpsimd.memset(mb3f.t[:], 1.0)
            nc.gpsimd.affine_select(out=mb3f.t[:], in_=mb3f.t[:], pattern=[[-4, 16], [1, 64]], compare_op=ALU.is_ge,
                                    fill=0.0, base=0, channel_multiplier=0)
            return nc.gpsimd.affine_select(out=mb3f.t[:], in_=mb3f.t[:], pattern=[[4, 16], [-1, 64]],
                                           compare_op=ALU.is_ge, fill=0.0, base=3, channel_multiplier=0)
        S.group("pool", [f_mb3], W=[mb3f.b])
        S.op("dve", lambda: V.tensor_copy(out=maskB3.t[:, :, :], in_=mb3f.t[:, :, :]), R=[mb3f.b], W=[maskB3.b])
        for e_ in ["pe", "dve", "act", "pool", "sp"]:
            S.drain(e_)
    bk = ps()
    rmT = TL("rmT", [16, 64], F32)
    S.op("pe", lambda: PE.transpose(bk.t[0:16, 0:64], rmask.t[:], ident.t[0:64, 0:64]), R=[rmask.b, ident.b], W=[bk.b])
    S.op("dve", lambda: V.tensor_copy(out=rmT.t[:], in_=bk.t[0:16, 0:64]), R=[bk.b], W=[rmT.b])
    bk2 = ps()
    S.op("pe", lambda: PE.matmul(bk2.t[0:64, 0:64], lhsT=rmT.t[:], rhs=rmT.t[:], start=True, stop=True),
         R=[rmT.b], W=[bk2.b])
    S.op("dve", lambda: V.tensor_copy(out=blk1.t[:], in_=bk2.t[0:64, 0:64]), R=[bk2.b], W=[blk1.b])
    S.op("dve", lambda: V.tensor_tensor(out=maskS.t[:], in0=blk1.t[:], in1=tri.t[0:64, 0:64], op=ALU.mult),
         R=[blk1.b, tri.b], W=[maskS.b])

    NPR = DEPTH * 200 + 16
    O_GMIX, O_GFFN, O_GPG, O_GSSD, O_GCM, O_CW, O_CB = 0, 16, 32, 48, 64, 80, 176
    O_GFIN = DEPTH * 200
    pstg = TL("pstg", [128, 128], F32)
    for r0 in range(0, NPR, 128):
        nr = min(128, NPR - r0)
        S.dma("sp", pstg.t[0:nr, :], prm[r0:r0 + nr, :], W=[pstg.b])
        bkp = ps()
        S.op("pe", lambda: PE.transpose(bkp.t[:, 0:nr], pstg.t[0:nr, :], ident.t[0:nr, 0:nr]),
             R=[pstg.b, ident.b], W=[bkp.b])
        S.op("dve", lambda: V.tensor_copy(out=PT.t[:, r0:r0 + nr], in_=bkp.t[:, 0:nr]), R=[bkp.b], W=[PT.b])

    ntail = TL("ntail", [128, 16, 3], BF16)

    yb = TL("yb", [128, 4, T], BF16)
    dbg_t = {}

    def dump_yb(name):
        if not DBG:
            return
        dd = dout(name, [128, 4 * T])
        for ct in range(4):
            for (c0, n) in split_cols(0, T):
                S.op("dve", lambda: V.tensor_copy(out=scr.t[:, 0:n], in_=yb.t[:, ct, c0:c0 + n]), R=[yb.b], W=[scr.b])
                S.dma("sp", dd[:, ct * T + c0:ct * T + c0 + n], scr.t[:, 0:n], R=[scr.b])
    epsb = TL("epsb", [128, 1], F32)
    S.op("dve", lambda: V.memset(epsb.t[:, :], EPS), W=[epsb.b])
    stat = TL("stat", [128, NT * 8], F32, nb=NT)

    def barrier():
        for e in ["pe", "dve", "act", "pool", "sp"]:
            S.drain(e)

    def wload(src_ap):
        sl = ring[ring_i[0]]
        ring_i[0] = (ring_i[0] + 1) % len(ring)
        if sl.b.pend:
            raise RuntimeError("weight ring hazard: slot reloaded before its consumer was emitted")
        sl.b.pend = True
        a, b = src_ap.shape[1], src_ap.shape[2]
        view = sl.t[:, 0:a * b].rearrange("p (a b) -> p a b", a=a)
        S.dma("pool", view, src_ap, W=[sl.b])
        return view, sl.b

    def wblock(w2d, c0, ncols=512, K=2048, r0=0):
        out = []
        for s in range(K // 512):
            src = w2d[r0 + s * 512:r0 + (s + 1) * 512, c0:c0 + ncols].rearrange("(kc p) n -> p kc n", p=128)
            out.append(wload(src))
        return out

    def mm_fm(slots, ct, c0, n):
        bk = ps()
        KC = 4 * len(slots)
        fns = []
        for kc in range(KC):
            v = slots[kc // 4][0]
            fns.append(lambda kc=kc, v=v: PE.matmul(bk.t[:, 0:n], lhsT=v[:, kc % 4, ct * 128:(ct + 1) * 128],
                                                    rhs=nT.t[:, kc, c0:c0 + n], start=(kc == 0), stop=(kc == KC - 1)))
        S.group("pe", fns, R=[nT.b] + [b for _, b in slots], W=[bk.b])
        return bk

    def gemm_tm(slots, src, src_buf, colof, consumer, ncols=512):
        KC = 4 * len(slots)
        for i in range(NT):
            bk = ps()
            r = rows(i)
            c = colof(i)
            fns = []
            for kc in range(KC):
                v = slots[kc // 4][0]
                fns.append(lambda kc=kc, v=v: PE.matmul(bk.t[0:r, 0:ncols], lhsT=src[:, kc, c:c + r],
                                                        rhs=v[:, kc % 4, 0:ncols], start=(kc == 0),
                                                        stop=(kc == KC - 1)))
            S.group("pe", fns, R=[src_buf] + [b for _, b in slots], W=[bk.b])
            consumer(i, bk)

    def rstd_of(i, n_el):
        r = rows(i)
        o = 8 * i
        S.op("act", lambda: A.activation(out=stat.t[0:r, o + 5:o + 6], in_=stat.t[0:r, o + 4:o + 5], func=AF.Sqrt,
                                         scale=1.0 / n_el, bias=epsb.t[0:r, 0:1]), R=[stat.bufs[i], epsb.b],
             W=[stat.bufs[i]])
        S.op("dve", lambda: V.reciprocal(out=stat.t[0:r, o + 6:o + 7], in_=stat.t[0:r, o + 5:o + 6]),
             R=[stat.bufs[i]], W=[stat.bufs[i]])

    def sumsq_h(i):
        r = rows(i)
        o = 8 * i
        for q in range(4):
            S.op("act", lambda: A.activation(out=scr.t[0:r, :], in_=h.t[0:r, i, q * 512:(q + 1) * 512], func=AF.Square,
                                             accum_out=stat.t[0:r, o + q:o + q + 1]),
                 R=[h.bufs[i]], W=[scr.b, stat.bufs[i]])
        S.op("dve", lambda: V.tensor_reduce(out=stat.t[0:r, o + 4:o + 5], in_=stat.t[0:r, o:o + 4], axis=AX.X,
                                            op=ALU.add), R=[stat.bufs[i]], W=[stat.bufs[i]])
        rstd_of(i, D)

    def norm_to_nT(gcol):
        for i in range(NT):
            r = rows(i)
            o = 8 * i
            sumsq_h(i)
            for q in range(4):
                S.op("dve", lambda: V.tensor_scalar(out=scr.t[0:r, :], in0=h.t[0:r, i, q * 512:(q + 1) * 512],
                                                    scalar1=stat.t[0:r, o + 6:o + 7], scalar2=None, op0=ALU.mult),
                     R=[h.bufs[i], stat.bufs[i]], W=[scr.b])
                bk = ps()
                fns = [lambda kk=kk: PE.transpose(bk.t[:, kk * 128:kk * 128 + r], scr.t[0:r, kk * 128:(kk + 1) * 128],
                                                  ident.t[0:r, 0:r]) for kk in range(4)]
                S.group("pe", fns, R=[scr.b, ident.b], W=[bk.b])
                c = ncol(i)
                S.op("dve", lambda: V.tensor_tensor(
                    out=nT.t[:, q * 4:(q + 1) * 4, c:c + r],
                    in0=bk.t[:, :].rearrange("p (a b) -> p a b", a=4)[:, :, 0:r],
                    in1=PT.t[:, gcol + q * 4:gcol + q * 4 + 4].unsqueeze(2).to_broadcast([128, 4, r]),
                    op=ALU.mult), R=[bk.b, PT.b], W=[nT.b])

    def gelu_psum(bk, r, n, out_ap, out_buf, tmpx, tmpa):
        S.op("act", lambda: A.copy(out=tmpx.t[0:r, 0:n], in_=bk.t[0:r, 0:n]), R=[bk.b], W=[tmpx.b])
        S.op("dve", lambda: V.scalar_tensor_tensor(out=tmpa.t[0:r, 0:n], in0=tmpx.t[0:r, 0:n], scalar=0.044715,
                                                   in1=tmpx.t[0:r, 0:n], op0=ALU.mult, op1=ALU.mult),
             R=[tmpx.b], W=[tmpa.b])
        S.op("dve", lambda: V.scalar_tensor_tensor(out=tmpa.t[0:r, 0:n], in0=tmpa.t[0:r, 0:n], scalar=1.0,
                                                   in1=tmpx.t[0:r, 0:n], op0=ALU.add, op1=ALU.mult),
             R=[tmpx.b, tmpa.b], W=[tmpa.b])
        S.op("act", lambda: A.activation(out=tmpa.t[0:r, 0:n], in_=tmpa.t[0:r, 0:n], func=AF.Sigmoid,
                                         scale=1.5957691216), R=[tmpa.b], W=[tmpa.b])
        S.op("dve", lambda: V.tensor_tensor(out=out_ap, in0=tmpx.t[0:r, 0:n], in1=tmpa.t[0:r, 0:n], op=ALU.mult),
             R=[tmpx.b, tmpa.b], W=[out_buf])

    FMB_H = split_cols(0, TC)
    FMB = split_cols(3, T)

    def outproj(wsrc2d, r0):
        for cb in range(4):
            src = wsrc2d[r0:r0 + 512, cb * 512:(cb + 1) * 512].rearrange("(kc p) n -> p kc n", p=128)
            sl = wload(src)

            def cons(i, bk, cb=cb):
                r = rows(i)
                S.op("dve", lambda: V.tensor_tensor(out=h.t[0:r, i, cb * 512:(cb + 1) * 512],
                                                    in0=h.t[0:r, i, cb * 512:(cb + 1) * 512], in1=bk.t[0:r, :],
                                                    op=ALU.add), R=[bk.b, h.bufs[i]], W=[h.bufs[i]])
            gemm_tm([sl], yb.t, yb.b, tcol, cons)

    def ssd_phase(d, seg, sc):
        pc = d * 200
        L = lambda name, shape, dt, nb=1: TL(name + "_%d_%d" % (d, seg), shape, dt, nb=nb, sc=sc)
        make_ring(sc, RING_SSD, "s%d_%d" % (d, seg))
        Wdt = L("Wdt", [128, 16, 32], BF16)
        dtv = L("dtv", [128, NT, 32], F32)
        dtA = L("dtA", [128, NT, 32], F32)
        eac = L("eac", [128, NT, 32], F32)
        dtdec = L("dtdec", [128, NT, 32], F32)
        cdec = L("cdec", [128, NT, 32], F32)
        tmp32 = L("tmp32", [128, 64], F32)
        xp = L("xp", [128, TC], F32)
        xps = L("xps", [128, NSS, 7], F32)
        hsq = L("hsq", [128, 4, NSS * 3], F32)
        cvo1 = L("cvo1", [128, 3 + NSS * 3], F32)
        cvt1 = L("cvt1", [64, 128], F32)
        cacc = L("cacc", [128, T], F32)
        BTg = L("BTg", [128, T], BF16)
        CTg = L("CTg", [128, T], BF16)
        Btok = L("Btok", [128, NT, 128], BF16)
        xtok = L("xtok", [128, NT, 512], BF16)
        cbm2 = [L("cbm%d" % k, [128, 128], F32) for k in range(2)]
        rh2 = [L("rh%d" % k, [128, 4, 128], F32) for k in range(2)]
        Mh2 = [L("Mh%d" % k, [128, 8, 128], BF16) for k in range(2)]
        xdt2 = [L("xdt%d" % k, [128, 512], BF16) for k in range(2)]
        xdts2 = [L("xdts%d" % k, [128, 512], BF16) for k in range(2)]
        prev = L("prev", [128, 512], F32)
        prevb = L("prevb", [128, 512], BF16)
        t1 = L("t1", [128, 512], F32)
        sqf = L("sqf", [128, 512], F32)
        sqf2 = [sqf, L("sqfb", [128, 512], F32)]
        cvst = sqf2[1]
        Hs = [L("Hs%d" % k, [128, 4, 128], F32) for k in range(2)]
        HTb = L("HTb", [128, 512], BF16)
        Cmb = L("Cmb", [128, 64], BF16)
        Bmb = L("Bmb", [64, 128], BF16)
        dAT = L("dAT", [128, 4, NSS], F32)
        dtArep = sqf

        norm_to_nT(pc + O_GMIX)
        if seg == 0:
            S.op("dve", lambda: V.memset(nT.t[:, :, 0:3], 0.0), W=[nT.b])
        else:
            S.op("dve", lambda: V.tensor_copy(out=nT.t[:, :, 0:3], in_=ntail.t[:, :, :]), R=[ntail.b], W=[nT.b])
        if seg + 1 < NSEG:
            S.op("dve", lambda: V.tensor_copy(out=ntail.t[:, :, :], in_=nT.t[:, :, NTP:NTP + 3]), R=[nT.b],
                 W=[ntail.b])
        S.dma("sp", smb.t[:, :], smallp[d].partition_broadcast(128), W=[smb.b])
        S.op("act", lambda: A.activation(out=smb.t[:, 32:64], in_=smb.t[:, 32:64], func=AF.Exp), R=[smb.b], W=[smb.b])
        S.op("dve", lambda: V.tensor_scalar(out=smb.t[:, 32:64], in0=smb.t[:, 32:64], scalar1=-1.0, scalar2=None,
                                            op0=ALU.mult), R=[smb.b], W=[smb.b])
        S.dma("pool", Wdt.t[:, :, :], w_in[d][:, I2:I3].rearrange("(kc p) n -> p kc n", p=128), W=[Wdt.b])
        for i in range(NT):
            r = rows(i)
            c = ncol(i)
            bk = ps()
            fns = [lambda kc=kc: PE.matmul(bk.t[0:r, 0:32], lhsT=nT.t[:, kc, c:c + r], rhs=Wdt.t[:, kc, :],
                                           start=(kc == 0), stop=(kc == 15)) for kc in range(16)]
            S.group("pe", fns, R=[nT.b, Wdt.b], W=[bk.b])
            S.op("dve", lambda: V.tensor_tensor(out=tmp32.t[0:r, 0:32], in0=bk.t[0:r, 0:32], in1=smb.t[0:r, 0:32],
                                                op=ALU.add), R=[bk.b, smb.b], W=[tmp32.b])
            S.op("act", lambda: A.activation(out=tmp32.t[0:r, 0:32], in_=tmp32.t[0:r, 0:32], func=AF.Exp),
                 R=[tmp32.b], W=[tmp32.b])
            S.op("act", lambda: A.activation(out=dtv.t[0:r, i, :], in_=tmp32.t[0:r, 0:32], func=AF.Ln, bias=1.0),
                 R=[tmp32.b], W=[dtv.b])
            S.op("dve", lambda: V.tensor_tensor(out=dtA.t[0:r, i, :], in0=dtv.t[0:r, i, :], in1=smb.t[0:r, 32:64],
                                                op=ALU.mult), R=[dtv.b, smb.b], W=[dtA.b])
            bk2 = ps()
            mk = tri if i < NCH else maskS
            on = ones if i < NCH else blk1
            S.group("pe", [lambda: PE.matmul(bk2.t[0:r, 0:32], lhsT=mk.t[0:r, 0:r], rhs=dtA.t[0:r, i, :], start=True,
                                             stop=True),
                           lambda: PE.matmul(bk2.t[0:r, 32:64], lhsT=on.t[0:r, 0:r], rhs=dtA.t[0:r, i, :], start=True,
                                             stop=True)], R=[dtA.b] + CONSTS, W=[bk2.b])
            S.op("act", lambda: A.activation(out=eac.t[0:r, i, :], in_=bk2.t[0:r, 0:32], func=AF.Exp),
                 R=[bk2.b], W=[eac.b])
            S.op("act", lambda: A.activation(out=cdec.t[0:r, i, :], in_=bk2.t[0:r, 32:64], func=AF.Exp),
                 R=[bk2.b], W=[cdec.b])
            S.op("act", lambda: A.copy(out=tmp32.t[0:r, 32:64], in_=bk2.t[0:r, 32:64]), R=[bk2.b], W=[tmp32.b])
            S.op("dve", lambda: V.tensor_tensor(out=tmp32.t[0:r, 32:64], in0=tmp32.t[0:r, 32:64], in1=bk2.t[0:r, 0:32],
                                                op=ALU.subtract), R=[bk2.b, tmp32.b], W=[tmp32.b])
            S.op("act", lambda: A.activation(out=tmp32.t[0:r, 32:64], in_=tmp32.t[0:r, 32:64], func=AF.Exp),
                 R=[tmp32.b], W=[tmp32.b])
            S.op("dve", lambda: V.tensor_tensor(out=dtdec.t[0:r, i, :], in0=tmp32.t[0:r, 32:64], in1=dtv.t[0:r, i, :],
                                                op=ALU.mult), R=[tmp32.b, dtv.b], W=[dtdec.b])

        def prep_hist(ct0, nct):
            S.dma("sp", cvst.t[0:48, 0:nct * 128], scv[d, seg][:, ct0 * 128:(ct0 + nct) * 128], W=[cvst.b])
            for k in range(nct):
                bk = ps()
                S.op("pe", lambda: PE.transpose(bk.t[:, 0:48], cvst.t[0:48, k * 128:(k + 1) * 128], ident.t[0:48, 0:48]),
                     R=[cvst.b, ident.b], W=[bk.b])
                S.op("act", lambda: A.copy(out=hsq.t[:, k, :], in_=bk.t[:, 0:48]), R=[bk.b], W=[hsq.b])

        def conv_tile(ctg, k, kind):
            cw = pc + O_CW
            cb_ = pc + O_CB + ctg
            S.op("dve", lambda: V.tensor_copy(out=xps.t[:, :, 0:3],
                                              in_=hsq.t[:, k, :].rearrange("p (b j) -> p b j", j=3)),
                 R=[hsq.b], W=[xps.b])
            S.op("dve", lambda: V.tensor_copy(out=xps.t[:, :, 3:7],
                                              in_=xp.t[:, 3 + NTP:3 + NTP + TS].rearrange("p (b l) -> p b l", l=4)),
                 R=[xp.b, xps.b], W=[xps.b])
            S.op("act", lambda: A.copy(out=cvo1.t[:, 0:3], in_=xp.t[:, NTP:NTP + 3]), R=[xp.b], W=[cvo1.b])
            S.op("act", lambda: A.copy(out=cvo1.t[:, 3:3 + NSS * 3].rearrange("p (b j) -> p b j", j=3),
                                       in_=xps.t[:, :, 4:7]), R=[xps.b, cvo1.b], W=[cvo1.b])
            bk = ps()
            S.op("pe", lambda: PE.transpose(bk.t[0:51, 0:128], cvo1.t[:, :], ident.t[:, :]), R=[cvo1.b, ident.b],
                 W=[bk.b])
            S.op("act", lambda: A.copy(out=cvt1.t[0:51, :], in_=bk.t[0:51, 0:128]), R=[bk.b], W=[cvt1.b])
            S.dma("sp", conv_o[d, seg][:, ctg * 128:(ctg + 1) * 128], cvt1.t[0:51, :], R=[cvt1.b])
            S.op("dve", lambda: V.tensor_scalar(out=cacc.t[:, 0:NTP], in0=xp.t[:, 0:NTP],
                                                scalar1=PT.t[:, cw + ctg:cw + ctg + 1], scalar2=None, op0=ALU.mult),
                 R=[xp.b, PT.b], W=[cacc.b])
            for j in range(1, 4):
                S.op("dve", lambda: V.scalar_tensor_tensor(out=cacc.t[:, 0:NTP], in0=xp.t[:, j:j + NTP],
                                                           scalar=PT.t[:, cw + j * 24 + ctg:cw + j * 24 + ctg + 1],
                                                           in1=cacc.t[:, 0:NTP], op0=ALU.mult, op1=ALU.add),
                     R=[xp.b, PT.b, cacc.b], W=[cacc.b])
            cs = cacc.t[:, NTP:T].rearrange("p (b l) -> p b l", l=4)
            S.op("dve", lambda: V.tensor_scalar(out=cs, in0=xps.t[:, :, 0:4], scalar1=PT.t[:, cw + ctg:cw + ctg + 1],
                                                scalar2=None, op0=ALU.mult), R=[xps.b, PT.b, cacc.b], W=[cacc.b])
            for j in range(1, 4):
                S.op("dve", lambda: V.scalar_tensor_tensor(out=cs, in0=xps.t[:, :, j:j + 4],
                                                           scalar=PT.t[:, cw + j * 24 + ctg:cw + j * 24 + ctg + 1],
                                                           in1=cs, op0=ALU.mult, op1=ALU.add),
                     R=[xps.b, PT.b, cacc.b], W=[cacc.b])
            if kind == "C":
                S.op("act", lambda: A.activation(out=CTg.t[:, :], in_=cacc.t[:, :], func=AF.Silu,
                                                 bias=PT.t[:, cb_:cb_ + 1]), R=[cacc.b, PT.b], W=[CTg.b])
            else:
                S.op("act", lambda: A.activation(out=cacc.t[:, :], in_=cacc.t[:, :], func=AF.Silu,
                                                 bias=PT.t[:, cb_:cb_ + 1]), R=[cacc.b, PT.b], W=[cacc.b])
            if kind == "B":
                S.op("dve", lambda: V.tensor_copy(out=BTg.t[:, :], in_=cacc.t[:, :]), R=[cacc.b], W=[BTg.b])

        def to_tok(dst, width_off):
            for i0 in range(0, NT, 4):
                bk = ps()
                tiles = list(range(i0, min(NT, i0 + 4)))
                fns = [lambda i=i: PE.transpose(bk.t[0:rows(i), (i - i0) * 128:(i - i0 + 1) * 128],
                                                cacc.t[:, tcol(i):tcol(i) + rows(i)], ident.t[:, :]) for i in tiles]
                S.group("pe", fns, R=[cacc.b, ident.b], W=[bk.b])
                full = [i for i in tiles if rows(i) == 128]
                if full:
                    nf = len(full)
                    S.op("act", lambda: A.copy(out=dst.t[:, full[0]:full[0] + nf, width_off:width_off + 128],
                                               in_=bk.t[:, 0:nf * 128].rearrange("p (a b) -> p a b", a=nf)),
                         R=[bk.b], W=[dst.b])
                for i in tiles:
                    if rows(i) != 128:
                        S.op("act", lambda: A.copy(out=dst.t[0:TS, i, width_off:width_off + 128],
                                                   in_=bk.t[0:TS, (i - i0) * 128:(i - i0 + 1) * 128]),
                             R=[bk.b], W=[dst.b])

        def sample_states(g, bko, xdts):
            iS = NCH
            cS = tcol(iS)
            hsl = slice(g * 8, (g + 1) * 8)
            S.op("dve", lambda: V.tensor_copy(out=dtArep.t[0:TS, :].rearrange("p (a b) -> p a b", a=8),
                                              in_=dtA.t[0:TS, iS, hsl].unsqueeze(2).to_broadcast([TS, 8, 64])),
                 R=[dtA.b], W=[dtArep.b])
            bkd = ps()
            fns = [lambda rt=rt: PE.matmul(bkd.t[:, rt * NSS:(rt + 1) * NSS], lhsT=dtArep.t[0:TS, rt * 128:(rt + 1) * 128],
                                           rhs=rmask.t[:, :], start=True, stop=True) for rt in range(4)]
            S.group("pe", fns, R=[dtArep.b] + CONSTS, W=[bkd.b])
            S.op("act", lambda: A.activation(out=dAT.t[:, :, :],
                                             in_=bkd.t[:, 0:4 * NSS].rearrange("p (a b) -> p a b", a=4),
                                             func=AF.Exp), R=[bkd.b], W=[dAT.b])
            def load_H(b):
                Hb = Hs[b % 2]
                S.dma("sp", Hb.t[:, :, :],
                      sst[d, seg, b][g * 512:(g + 1) * 512, :].rearrange("(rt p) n -> p rt n", p=128), W=[Hb.b])
            load_H(0)
            for b in range(NSS):
                H = Hs[b % 2]
                if b + 1 < NSS:
                    load_H(b + 1)
                S.op("dve", lambda: V.tensor_tensor(out=Cmb.t[:, :], in0=CTg.t[:, cS:cS + TS], in1=maskB3.t[:, b, :],
                                                    op=ALU.mult), R=[CTg.b] + CONSTS, W=[Cmb.b])
                S.op("dve", lambda: V.tensor_scalar(out=Bmb.t[:, :], in0=Btok.t[0:TS, iS, :],
                                                    scalar1=rmask.t[:, b:b + 1], scalar2=None, op0=ALU.mult),
                     R=[Btok.b] + CONSTS, W=[Bmb.b])
                bkt = ps()
                fns = [lambda rt=rt: PE.transpose(bkt.t[:, rt * 128:(rt + 1) * 128], H.t[:, rt, :], ident.t[:, :])
                       for rt in range(4)]
                S.group("pe", fns, R=[H.b, ident.b], W=[bkt.b])
                S.op("act", lambda: A.copy(out=HTb.t[:, :], in_=bkt.t[:, :]), R=[bkt.b], W=[HTb.b])
                S.op("pe", lambda: PE.matmul(bko.t[0:TS, :], lhsT=Cmb.t[:, :], rhs=HTb.t[:, :], start=(b == 0),
                                             stop=(b == NSS - 1)), R=[Cmb.b, HTb.b], W=[bko.b])
                bku = ps()
                fns = [lambda rt=rt: PE.matmul(bku.t[:, rt * 128:(rt + 1) * 128],
                                               lhsT=xdts.t[0:TS, rt * 128:(rt + 1) * 128], rhs=Bmb.t[:, :], start=True,
                                               stop=True) for rt in range(4)]
                S.group("pe", fns, R=[xdts.b, Bmb.b], W=[bku.b])
                S.op("dve", lambda: V.tensor_tensor(out=H.t[:, :, :], in0=H.t[:, :, :],
                                                    in1=dAT.t[:, :, b:b + 1].to_broadcast([128, 4, 128]), op=ALU.mult),
                     R=[H.b, dAT.b], W=[H.b])
                S.op("dve", lambda: V.tensor_tensor(out=H.t[:, :, :], in0=H.t[:, :, :],
                                                    in1=bku.t[:, :].rearrange("p (a b) -> p a b", a=4), op=ALU.add),
                     R=[H.b, bku.b], W=[H.b])
                S.dma("pool", ssm_s[d, seg, b][g * 512:(g + 1) * 512, :].rearrange("(rt p) n -> p rt n", p=128),
                      H.t[:, :, :], R=[H.b])

        for g in range(4):
            hsl = slice(g * 8, (g + 1) * 8)
            bslot = wload(w_in[d][:, I1 + 2048 + g * 128:I1 + 2048 + (g + 1) * 128].rearrange("(kc p) n -> p kc n", p=128))
            cslot = wload(w_in[d][:, I1 + 2560 + g * 128:I1 + 2560 + (g + 1) * 128].rearrange("(kc p) n -> p kc n", p=128))
            xslots_box = []
            jobs = [("B", 16 + g, None), ("C", 20 + g, None)] + [("x", g * 4 + ct, ct) for ct in range(4)]

            def job_gemm(job):
                kind, ctg, ct = job
                res_ = []
                for (c0, n) in FMB_H:
                    if kind == "x":
                        if not xslots_box:
                            xslots_box.extend(wblock(w_in[d], I1 + g * 512))
                        bk = mm_fm(xslots_box, ct, c0, n)
                    else:
                        slv, slb = bslot if kind == "B" else cslot
                        bk = ps()
                        fns = [lambda kc=kc: PE.matmul(bk.t[:, 0:n], lhsT=slv[:, kc, :], rhs=nT.t[:, kc, c0:c0 + n],
                                                       start=(kc == 0), stop=(kc == 15)) for kc in range(16)]
                        S.group("pe", fns, R=[nT.b, slb], W=[bk.b])
                    res_.append((c0, n, bk))
                return res_

            deferred = [None]
            for k, job in enumerate(jobs):
                kind, ctg, ct = job
                if kind != "x":
                    prep_hist(ctg, 1)
                    hk = 0
                else:
                    if ct == 0:
                        prep_hist(g * 4, 4)
                    hk = ct
                pend = job_gemm(job)
                if deferred[0] is not None:
                    to_tok(*deferred[0])
                    deferred[0] = None
                for (c0, n, bk) in pend:
                    S.op("act", lambda: A.copy(out=xp.t[:, c0:c0 + n], in_=bk.t[:, 0:n]), R=[bk.b], W=[xp.b])
                conv_tile(ctg, hk, kind)
                if kind == "B":
                    deferred[0] = (Btok, 0)
                elif kind == "x":
                    deferred[0] = (xtok, ct * 128)
            if deferred[0] is not None:
                to_tok(*deferred[0])
                deferred[0] = None

            if seg == 0:
                S.op("dve", lambda: V.memset(prev.t[:, :], 0.0), W=[prev.b])
            else:
                S.dma("sp", prev.t[:, :], hstate[d, g], R=[hstate_b[d][g]], W=[prev.b])
            S.op("act", lambda: A.copy(out=prevb.t[:, :], in_=prev.t[:, :]), R=[prev.b], W=[prevb.b])

            def front(i):
                r = rows(i)
                c = tcol(i)
                p = i % 2
                mk = tri if i < NCH else maskS
                cbm_, rh_, Mh_, xdt_, xdts_, bky = cbm2[p], rh2[p], Mh2[p], xdt2[p], xdts2[p], bky2[p]
                bkc = ps()
                S.op("pe", lambda: PE.matmul(bkc.t[0:r, 0:r], lhsT=BTg.t[:, c:c + r], rhs=CTg.t[:, c:c + r], start=True,
                                             stop=True), R=[BTg.b, CTg.b], W=[bkc.b])
                S.op("dve", lambda: V.tensor_tensor(out=cbm_.t[0:r, 0:r], in0=bkc.t[0:r, 0:r], in1=mk.t[0:r, 0:r],
                                                    op=ALU.mult), R=[bkc.b] + CONSTS, W=[cbm_.b])
                for h4 in range(2):
                    bks = ps()
                    for hh in range(4):
                        hd = g * 8 + h4 * 4 + hh
                        S.op("dve", lambda: V.tensor_scalar(out=rh_.t[0:r, hh, 0:r], in0=tri.t[0:r, 0:r],
                                                            scalar1=dtA.t[0:r, i, hd:hd + 1], scalar2=None,
                                                            op0=ALU.mult), R=[dtA.b] + CONSTS, W=[rh_.b])
                    fns = [lambda hh=hh: PE.matmul(bks.t[0:r, hh * 128:hh * 128 + r], lhsT=ustr.t[0:r, 0:r],
                                                   rhs=rh_.t[0:r, hh, 0:r], start=True, stop=True) for hh in range(4)]
                    S.group("pe", fns, R=[rh_.b] + CONSTS, W=[bks.b])
                    S.op("act", lambda: A.activation(out=rh_.t[0:r, :, 0:r],
                                                     in_=bks.t[0:r, :].rearrange("p (a b) -> p a b", a=4)[:, :, 0:r],
                                                     func=AF.Exp), R=[bks.b, rh_.b], W=[rh_.b])
                    S.op("dve", lambda: V.tensor_tensor(out=Mh_.t[0:r, h4 * 4:h4 * 4 + 4, 0:r], in0=rh_.t[0:r, :, 0:r],
                                                        in1=cbm_.t[0:r, 0:r].unsqueeze(1).to_broadcast([r, 4, r]),
                                                        op=ALU.mult), R=[rh_.b, cbm_.b], W=[Mh_.b])
                xv = xtok.t[0:r, i, :].rearrange("p (a b) -> p a b", a=8)
                S.op("dve", lambda: V.tensor_tensor(out=xdt_.t[0:r, :].rearrange("p (a b) -> p a b", a=8), in0=xv,
                                                    in1=dtv.t[0:r, i, hsl].unsqueeze(2).to_broadcast([r, 8, 64]),
                                                    op=ALU.mult), R=[xtok.b, dtv.b], W=[xdt_.b])
                S.op("dve", lambda: V.tensor_tensor(out=xdts_.t[0:r, :].rearrange("p (a b) -> p a b", a=8), in0=xv,
                                                    in1=dtdec.t[0:r, i, hsl].unsqueeze(2).to_broadcast([r, 8, 64]),
                                                    op=ALU.mult), R=[xtok.b, dtdec.b], W=[xdts_.b])
                fns = [lambda hh=hh: PE.matmul(bky.t[0:r, hh * 64:(hh + 1) * 64], lhsT=Mh_.t[0:r, hh, 0:r],
                                               rhs=xdt_.t[0:r, hh * 64:(hh + 1) * 64], start=True, stop=True)
                       for hh in range(8)]
                S.group("pe", fns, R=[Mh_.b, xdt_.b], W=[bky.b])

            def back(i):
                r = rows(i)
                c = tcol(i)
                p = i % 2
                xdts_, bky = xdts2[p], bky2[p]
                xv = xtok.t[0:r, i, :].rearrange("p (a b) -> p a b", a=8)
                if i < NCH:
                    bko = ps()
                    S.op("pe", lambda: PE.matmul(bko.t[0:r, :], lhsT=CTg.t[:, c:c + r], rhs=prevb.t[:, :], start=True,
                                                 stop=True), R=[CTg.b, prevb.b], W=[bko.b])
                else:
                    bko = accB
                    sample_states(g, bko, xdts_)
                S.op("dve", lambda: V.tensor_tensor(out=t1.t[0:r, :].rearrange("p (a b) -> p a b", a=8),
                                                    in0=bko.t[0:r, :].rearrange("p (a b) -> p a b", a=8),
                                                    in1=eac.t[0:r, i, hsl].unsqueeze(2).to_broadcast([r, 8, 64]),
                                                    op=ALU.mult), R=[bko.b, eac.b], W=[t1.b])
                S.op("dve", lambda: V.tensor_tensor(out=t1.t[0:r, :], in0=t1.t[0:r, :], in1=bky.t[0:r, :], op=ALU.add),
                     R=[bky.b, t1.b], W=[t1.b])
                S.op("dve", lambda: V.tensor_tensor(out=sqf.t[0:r, :].rearrange("p (a b) -> p a b", a=8), in0=xv,
                                                    in1=smb.t[0:r, 64 + g * 8:64 + g * 8 + 8].unsqueeze(2).to_broadcast(
                                                        [r, 8, 64]), op=ALU.mult), R=[xtok.b, smb.b], W=[sqf.b])
                S.op("dve", lambda: V.tensor_tensor(out=t1.t[0:r, :], in0=t1.t[0:r, :], in1=sqf.t[0:r, :], op=ALU.add),
                     R=[t1.b, sqf.b], W=[t1.b])
                bkt = ps()
                fns = [lambda k4=k4: PE.transpose(bkt.t[:, k4 * 128:k4 * 128 + r], t1.t[0:r, k4 * 128:(k4 + 1) * 128],
                                                  ident.t[0:r, 0:r]) for k4 in range(4)]
                S.group("pe", fns, R=[t1.b, ident.b], W=[bkt.b])
                S.op("act", lambda: A.copy(out=yb.t[:, :, c:c + r],
                                           in_=bkt.t[:, :].rearrange("p (a b) -> p a b", a=4)[:, :, 0:r]),
                     R=[bkt.b], W=[yb.b])
                if i < NCH:
                    bkS = ps()
                    S.op("pe", lambda: PE.matmul(bkS.t[:, :], lhsT=Btok.t[0:r, i, :], rhs=xdts_.t[0:r, :], start=True,
                                                 stop=True), R=[Btok.b, xdts_.b], W=[bkS.b])
                    S.op("dve", lambda: V.tensor_tensor(out=prev.t[:, :].rearrange("p (a b) -> p a b", a=8),
                                                        in0=prev.t[:, :].rearrange("p (a b) -> p a b", a=8),
                                                        in1=cdec.t[:, i, hsl].unsqueeze(2).to_broadcast([128, 8, 64]),
                                                        op=ALU.mult), R=[prev.b, cdec.b], W=[prev.b])
                    S.op("dve", lambda: V.tensor_tensor(out=prev.t[:, :], in0=prev.t[:, :], in1=bkS.t[:, :], op=ALU.add),
                         R=[prev.b, bkS.b], W=[prev.b])
                    S.op("act", lambda: A.copy(out=prevb.t[:, :], in_=prev.t[:, :]), R=[prev.b], W=[prevb.b])

            for i in range(NT + 1):
                if i < NT:
                    front(i)
                if i >= 1:
                    back(i - 1)
            if seg + 1 < NSEG:
                S.dma("sp", hstate[d, g], prev.t[:, :], R=[prev.b], W=[hstate_b[d][g]])
            else:
                bkf = ps()
                fns = [lambda k4=k4: PE.transpose(bkf.t[:, k4 * 128:(k4 + 1) * 128],
                                                  prev.t[:, k4 * 128:(k4 + 1) * 128], ident.t[:, :])
                       for k4 in range(4)]
                S.group("pe", fns, R=[prev.b, ident.b], W=[bkf.b])
                S.op("act", lambda: A.copy(out=t1.t[:, :], in_=bkf.t[:, :]), R=[bkf.b], W=[t1.b])
                S.dma("sp", ssm_p[d][g * 512:(g + 1) * 512, :].rearrange("(rt p) n -> p rt n", p=128),
                      t1.t[:, :].rearrange("p (a b) -> p a b", a=4), R=[t1.b])


            slots = wblock(w_in[d], g * 512)
            for (c0, n) in FMB:
                tc0 = c0 - 3
                pendz = None

                def ones_mm(pz):
                    ctp, sqp = pz
                    S.op("pe", lambda: PE.matmul(accA.t[:, 0:n], lhsT=ones.t[:, :], rhs=sqp.t[:, 0:n], start=(ctp == 0),
                                                 stop=(ctp == 3)), R=[sqp.b, ones.b], W=[accA.b])
                for ct in range(4):
                    bk = mm_fm(slots, ct, c0, n)
                    if pendz is not None:
                        ones_mm(pendz)
                    sq_ = sqf2[ct % 2]
                    S.op("act", lambda: A.activation(out=sq_.t[:, 0:n], in_=bk.t[:, 0:n], func=AF.Silu),
                         R=[bk.b], W=[sq_.b])
                    S.op("dve", lambda: V.tensor_tensor(out=yb.t[:, ct, tc0:tc0 + n], in0=yb.t[:, ct, tc0:tc0 + n],
                                                        in1=sq_.t[:, 0:n], op=ALU.mult), R=[yb.b, sq_.b], W=[yb.b])
                    S.op("act", lambda: A.activation(out=sq_.t[:, 0:n], in_=yb.t[:, ct, tc0:tc0 + n], func=AF.Square),
                         R=[yb.b, sq_.b], W=[sq_.b])
                    pendz = (ct, sq_)
                ones_mm(pendz)
                S.op("act", lambda: A.activation(out=t1.t[:, 0:n], in_=accA.t[:, 0:n], func=AF.Sqrt, scale=1.0 / 512,
                                                 bias=epsb.t[:, 0:1]), R=[accA.b, t1.b, epsb.b], W=[t1.b])
                S.op("dve", lambda: V.reciprocal(out=t1.t[:, 0:n], in_=t1.t[:, 0:n]), R=[t1.b], W=[t1.b])
                for ct in range(4):
                    gc = pc + O_GSSD + g * 4 + ct
                    S.op("dve", lambda: V.scalar_tensor_tensor(out=yb.t[:, ct, tc0:tc0 + n], in0=yb.t[:, ct, tc0:tc0 + n],
                                                               scalar=PT.t[:, gc:gc + 1], in1=t1.t[:, 0:n],
                                                               op0=ALU.mult, op1=ALU.mult),
                         R=[yb.b, PT.b, t1.b], W=[yb.b])
            if d == 0 and g == 0 and seg == 0:
                dump_yb("dbg_yssd0")
            outproj(w_out[d], g * 512)

    def cm_phase(d, seg, sc):
        pc = d * 200
        L = lambda name, shape, dt, nb=1: TL(name + "_%d_%d" % (d, seg), shape, dt, nb=nb, sc=sc)
        make_ring(sc, RING_CM, "c%d_%d" % (d, seg))
        vg = L("vg", [128, NT, 2048], BF16)
        gx2 = [L("gx%d" % k, [128, 512], F32) for k in range(2)]
        ga2 = [L("ga%d" % k, [128, 512], F32) for k in range(2)]
        gx = gx2[0]
        gcnt = [0]
        ssq = L("ssq", [128, NT, 4], F32)
        Wn = L("Wn", [128, 4, 128], F32)
        WsT = L("WsT", [128, 4, 128], F32)
        Wr2 = [L("Wr%d" % k, [128, 128], BF16) for k in range(2)]
        WsS = L("WsS", [64, 16, 64], F32)
        bsb = L("bsb", [128, 4, 192], F32)
        tmpc2 = [L("tmpc%d" % k, [128, 128], F32) for k in range(2)]
        gcb = L("gcb", [64, 512], F32)
        vso = L("vso", [64, 512], F32)
        W4n = L("W4n", [4, 16, 4], F32)
        bs4 = L("bs4", [128, 4, 4], F32)
        o1s = gx
        for j in range(4):
            slots = wblock(w_in[d], I4 + j * 512)

            def cons(i, bk, j=j):
                r = rows(i)
                gx_, ga_ = gx2[gcnt[0] % 2], ga2[gcnt[0] % 2]
                gcnt[0] += 1
                gelu_psum(bk, r, 512, vg.t[0:r, i, j * 512:(j + 1) * 512], vg.b, gx_, ga_)
                S.op("act", lambda: A.activation(out=ga_.t[0:r, :], in_=vg.t[0:r, i, j * 512:(j + 1) * 512],
                                                 func=AF.Square, accum_out=ssq.t[0:r, i, j:j + 1]),
                     R=[vg.b, ga_.b], W=[ga_.b, ssq.b])
            gemm_tm(slots, nT.t, nT.b, ncol, cons)
        for i in range(NT):
            r = rows(i)
            S.op("dve", lambda: V.tensor_reduce(out=stat.t[0:r, 8 * i + 4:8 * i + 5], in_=ssq.t[0:r, i, :], axis=AX.X,
                                                op=ALU.add), R=[ssq.b, stat.bufs[i]], W=[stat.bufs[i]])
            rstd_of(i, 2048)
        iS = NCH
        for q in range(4):
            S.dma("sp", gcb.t[:, :], g_cm_d[d][q * 512:(q + 1) * 512].partition_broadcast(64), W=[gcb.b])
            S.op("dve", lambda: V.scalar_tensor_tensor(out=vso.t[:, :], in0=vg.t[0:TS, iS, q * 512:(q + 1) * 512],
                                                       scalar=stat.t[0:TS, 8 * iS + 6:8 * iS + 7], in1=gcb.t[:, :],
                                                       op0=ALU.mult, op1=ALU.mult),
                 R=[vg.b, stat.bufs[iS], gcb.b], W=[vso.b])
            S.dma("sp", v_s[d, seg][:, q * 512:(q + 1) * 512], vso.t[:, :], R=[vso.b])
        S.dma("sp", W4n.t[0:4, :, :], w_s_d[d][:, 0:4, 0:4].rearrange("g t s -> t g s"), W=[W4n.b])
        for hf in range(2):
            bk = ps()
            fns = [lambda g8=g8: PE.matmul(bk.t[0:4, g8 * 64:(g8 + 1) * 64], lhsT=W4n.t[0:4, hf * 8 + g8, :],
                                           rhs=Rm.t[0:4, :], start=True, stop=True) for g8 in range(8)]
            S.group("pe", fns, R=[W4n.b, Rm.b], W=[bk.b])
            S.op("act", lambda: A.copy(out=o1s.t[0:4, :], in_=bk.t[0:4, :]), R=[bk.b], W=[o1s.b])
            bk2 = ps()
            S.op("pe", lambda: PE.matmul(bk2.t[0:64, :], lhsT=Rm.t[0:4, :], rhs=o1s.t[0:4, :], start=True, stop=True),
                 R=[Rm.b, o1s.b], W=[bk2.b])
            S.op("dve", lambda: V.tensor_tensor(out=WsS.t[:, hf * 8:(hf + 1) * 8, :],
                                                in0=bk2.t[0:64, :].rearrange("p (a b) -> p a b", a=8),
                                                in1=maskS.t[:, :].unsqueeze(1).to_broadcast([64, 8, 64]), op=ALU.mult),
                 R=[bk2.b] + CONSTS, W=[WsS.b])
        for j in range(4):
            S.dma("sp", Wn.t[:, :, :], w_s_d[d][j * 4:(j + 1) * 4].rearrange("g t s -> t g s"), W=[Wn.b])
            bk = ps()
            fns = [lambda k=k: PE.transpose(bk.t[:, k * 128:(k + 1) * 128], Wn.t[:, k, :], ident.t[:, :])
                   for k in range(4)]
            S.group("pe", fns, R=[Wn.b, ident.b], W=[bk.b])
            S.op("dve", lambda: V.tensor_tensor(out=WsT.t[:, :, :], in0=bk.t[:, :].rearrange("p (a b) -> p a b", a=4),
                                                in1=tri.t[:, :].unsqueeze(1).to_broadcast([128, 4, 128]), op=ALU.mult),
                 R=[bk.b] + CONSTS, W=[WsT.b])
            S.dma("sp", bsb.t[:, :, 0:128], b_s_d[d][j * 4:(j + 1) * 4, :].partition_broadcast(128), W=[bsb.b])
            S.dma("sp", bs4.t[:, :, :], b_s_d[d][j * 4:(j + 1) * 4, 0:4].partition_broadcast(128), W=[bs4.b])
            S.op("dve", lambda: V.tensor_copy(out=bsb.t[:, :, 128:192].rearrange("p g (b l) -> p g b l", l=4),
                                              in_=bs4.t[:, :, :].unsqueeze(2).to_broadcast([128, 4, NSS, 4])),
                 R=[bs4.b, bsb.b], W=[bsb.b])
            slots = wblock(w_in[d], I3 + j * 512)
            for ct in range(4):
                for (c0, n) in FMB:
                    bk = mm_fm(slots, ct, c0, n)
                    gx_, ga_ = gx2[gcnt[0] % 2], ga2[gcnt[0] % 2]
                    gcnt[0] += 1
                    gelu_psum(bk, 128, n, yb.t[:, ct, c0 - 3:c0 - 3 + n], yb.b, gx_, ga_)
            for ct in range(4):
                gg = j * 4 + ct
                for i in range(NT):
                    r = rows(i)
                    c = tcol(i)
                    rs = stat.t[0:r, 8 * i + 6:8 * i + 7]
                    Wr, tmpc = Wr2[gcnt[0] % 2], tmpc2[gcnt[0] % 2]
                    gcnt[0] += 1
                    if i < NCH:
                        S.op("dve", lambda: V.tensor_scalar(out=Wr.t[0:r, 0:r], in0=WsT.t[0:r, ct, 0:r], scalar1=rs,
                                                            scalar2=None, op0=ALU.mult),
                             R=[WsT.b, stat.bufs[i]], W=[Wr.b])
                        boff = 0
                    else:
                        S.op("dve", lambda: V.tensor_scalar(out=Wr.t[0:r, 0:r], in0=WsS.t[0:r, gg, 0:r], scalar1=rs,
                                                            scalar2=None, op0=ALU.mult),
                             R=[WsS.b, stat.bufs[i]], W=[Wr.b])
                        boff = 128
                    bk = ps()
                    S.op("pe", lambda: PE.matmul(bk.t[:, 0:r], lhsT=vg.t[0:r, i, gg * 128:(gg + 1) * 128],
                                                 rhs=Wr.t[0:r, 0:r], start=True, stop=True), R=[vg.b, Wr.b], W=[bk.b])
                    gcol = pc + O_GCM + gg
                    S.op("dve", lambda: V.scalar_tensor_tensor(out=tmpc.t[:, 0:r], in0=bk.t[:, 0:r],
                                                               scalar=PT.t[:, gcol:gcol + 1],
                                                               in1=bsb.t[:, ct, boff:boff + r], op0=ALU.mult,
                                                               op1=ALU.add), R=[bk.b, PT.b, bsb.b], W=[tmpc.b])
                    S.op("dve", lambda: V.tensor_tensor(out=yb.t[:, ct, c:c + r], in0=yb.t[:, ct, c:c + r],
                                                        in1=tmpc.t[:, 0:r], op=ALU.mult), R=[yb.b, tmpc.b], W=[yb.b])
            if d == 0 and j == 0 and seg == 0:
                dump_yb("dbg_ycm0")
            outproj(w_out[d], 2048 + j * 512)

    def ffn_phase(d, seg, sc):
        pc = d * 200
        make_ring(sc, RING_FFN, "f%d_%d" % (d, seg))
        norm_to_nT(pc + O_GFFN)
        for blk in range(DFF // 512):
            slots = wblock(w_gate[d], blk * 512)
            for ct in range(4):
                for (c0, n) in FMB:
                    bk = mm_fm(slots, ct, c0, n)
                    S.op("act", lambda: A.activation(out=yb.t[:, ct, c0 - 3:c0 - 3 + n], in_=bk.t[:, 0:n], func=AF.Silu),
                         R=[bk.b], W=[yb.b])
            slots = wblock(w_up[d], blk * 512)
            for ct in range(4):
                for (c0, n) in FMB:
                    bk = mm_fm(slots, ct, c0, n)
                    S.op("dve", lambda: V.tensor_tensor(out=yb.t[:, ct, c0 - 3:c0 - 3 + n],
                                                        in0=yb.t[:, ct, c0 - 3:c0 - 3 + n], in1=bk.t[:, 0:n],
                                                        op=ALU.mult), R=[bk.b, yb.b], W=[yb.b])
            outproj(w_down[d], blk * 512)

    def ple_phase(d, seg, sc):
        pc = d * 200
        L = lambda name, shape, dt, nb=1: TL(name + "_%d_%d" % (d, seg), shape, dt, nb=nb, sc=sc)
        make_ring(sc, RING_PLE, "p%d_%d" % (d, seg))
        pT = L("pT", [128, 2, T], BF16)
        ptok = L("ptok", [128, DPLE], F32)
        gsig = L("gsig", [128, 512], F32)
        for i in range(NT):
            r = rows(i)
            c = tcol(i)
            S.dma("sp", ptok.t[0:r, :], pin[d, seg, c:c + r, :], W=[ptok.b])
            bk = ps()
            fns = [lambda k=k: PE.transpose(bk.t[:, k * 128:k * 128 + r], ptok.t[0:r, k * 128:(k + 1) * 128],
                                            ident.t[0:r, 0:r]) for k in range(2)]
            S.group("pe", fns, R=[ptok.b, ident.b], W=[bk.b])
            S.op("act", lambda: A.copy(out=pT.t[:, :, c:c + r],
                                       in_=bk.t[:, 0:256].rearrange("p (a b) -> p a b", a=2)[:, :, 0:r]),
                 R=[bk.b], W=[pT.b])
        norm_to_nT(pc + O_GPG)
        wpl = L("wpl", [128, 2, D], BF16)
        S.dma("pool", wpl.t[:, :, :], w_ple[d].rearrange("(kc p) n -> p kc n", p=128), W=[wpl.b])
        for cb in range(4):
            slots = wblock(w_pg[d], cb * 512)

            def cons(i, bk, cb=cb):
                r = rows(i)
                c = tcol(i)
                S.op("act", lambda: A.activation(out=gsig.t[0:r, :], in_=bk.t[0:r, :], func=AF.Sigmoid),
                     R=[bk.b], W=[gsig.b])
                bkp = ps()
                fns = [lambda k=k: PE.matmul(bkp.t[0:r, :], lhsT=pT.t[:, k, c:c + r],
                                             rhs=wpl.t[:, k, cb * 512:(cb + 1) * 512], start=(k == 0), stop=(k == 1))
                       for k in range(2)]
                S.group("pe", fns, R=[pT.b, wpl.b], W=[bkp.b])
                S.op("dve", lambda: V.tensor_tensor(out=gsig.t[0:r, :], in0=gsig.t[0:r, :], in1=bkp.t[0:r, :],
                                                    op=ALU.mult), R=[gsig.b, bkp.b], W=[gsig.b])
                S.op("dve", lambda: V.tensor_tensor(out=h.t[0:r, i, cb * 512:(cb + 1) * 512],
                                                    in0=h.t[0:r, i, cb * 512:(cb + 1) * 512], in1=gsig.t[0:r, :],
                                                    op=ALU.add), R=[gsig.b, h.bufs[i]], W=[h.bufs[i]])
            gemm_tm(slots, nT.t, nT.b, ncol, cons)

    for d in range(DEPTH):
        for seg in range(NSEG):
            for i in range(NT):
                r = rows(i)
                if d == 0:
                    S.dma("sp", h.t[0:r, i, :], xin[seg, tcol(i):tcol(i) + r, :], W=[h.bufs[i]])
                else:
                    S.dma("sp", h.t[0:r, i, :], hbuf[seg, tcol(i):tcol(i) + r, :], R=[hbuf_b[seg][i]], W=[h.bufs[i]])
            for phase in (ssd_phase, cm_phase, ffn_phase, ple_phase):
                with ExitStack() as sc:
                    phase(d, seg, sc)
                    barrier()
            if d + 1 < DEPTH:
                for i in range(NT):
                    r = rows(i)
                    S.dma("sp", hbuf[seg, tcol(i):tcol(i) + r, :], h.t[0:r, i, :], R=[h.bufs[i]], W=[hbuf_b[seg][i]])
            else:
                with ExitStack() as sc_fin:
                    gfb = TL("gfb_%d" % seg, [128, 512], F32, sc=sc_fin)
                    for i in range(NT):
                        r = rows(i)
                        o = 8 * i
                        sumsq_h(i)
                        for q in range(4):
                            S.dma("sp", gfb.t[0:r, :], g_final_d[0][q * 512:(q + 1) * 512].partition_broadcast(r),
                                  W=[gfb.b])
                            S.op("dve", lambda: V.scalar_tensor_tensor(out=h.t[0:r, i, q * 512:(q + 1) * 512],
                                                                       in0=h.t[0:r, i, q * 512:(q + 1) * 512],
                                                                       scalar=stat.t[0:r, o + 6:o + 7], in1=gfb.t[0:r, :],
                                                                       op0=ALU.mult, op1=ALU.mult),
                                 R=[h.bufs[i], stat.bufs[i], gfb.b], W=[h.bufs[i]])
                        S.dma("sp", yout[seg, tcol(i):tcol(i) + r, :], h.t[0:r, i, :], R=[h.bufs[i]])
                    barrier()
    barrier()
    es.close()
    return nc


def _prep_inputs(inp, NTP, DEPTH, ncores):
    f = lambda a: np.ascontiguousarray(np.asarray(a, dtype=np.float32))
    xp_, xs_ = f(inp["x_prompt"]), f(inp["x_sample"])
    pp_, ps_ = f(inp["p_prompt"]), f(inp["p_sample"])
    sst, scv = f(inp["state_ssm"]), f(inp["state_conv"])
    rows_ = []
    for d in range(DEPTH):
        rows_ += [f(inp["g_mix"])[d].reshape(16, 128), f(inp["g_ffn"])[d].reshape(16, 128),
                  f(inp["g_pg"])[d].reshape(16, 128), f(inp["g_ssd"])[d].reshape(16, 128),
                  f(inp["g_cm"])[d].reshape(16, 128), f(inp["conv_w"])[d].reshape(4 * 24, 128),
                  f(inp["conv_b"])[d].reshape(24, 128)]
    rows_.append(f(inp["g_final"]).reshape(16, 128))
    prm = np.ascontiguousarray(np.concatenate(rows_, axis=0))
    smallp = np.ascontiguousarray(np.concatenate([f(inp["dt_bias"]), f(inp["a_log"]), f(inp["d_skip"])], axis=1))
    shared = {"prm": prm, "smallp": smallp, "g_cm": f(inp["g_cm"]), "w_s": f(inp["w_s"]), "b_s": f(inp["b_s"]),
              "w_in": f(inp["w_in"]), "w_out": f(inp["w_out"]), "w_gate": f(inp["w_gate"]), "w_up": f(inp["w_up"]),
              "w_down": f(inp["w_down"]), "w_pg": f(inp["w_pg"]), "w_ple": f(inp["w_ple"]),
              "g_final": f(inp["g_final"]).reshape(1, 2048)}
    maps = []
    for c in range(ncores):
        sq = c % 4
        m = dict(shared)
        xin, pin, ss, sc_ = [], [], [], []
        for seg in range(2):
            sl = slice(seg * NTP, (seg + 1) * NTP)
            b0 = (sq * 2 + seg) * NSS
            bs = slice(b0, b0 + NSS)
            xin.append(np.concatenate([xp_[sq, sl], xs_[bs].reshape(TS, D)], axis=0))
            pin.append(np.concatenate([pp_[:, sq, sl], ps_[:, bs].reshape(DEPTH, TS, DPLE)], axis=1))
            ss.append(sst[:, bs].reshape(DEPTH, NSS, 2048, 128))
            sc_.append(scv[:, bs].reshape(DEPTH, NSS * 3, CONV))
        m["xin"] = np.ascontiguousarray(np.stack(xin, axis=0))
        m["pin"] = np.ascontiguousarray(np.stack(pin, axis=1))
        m["sst"] = np.ascontiguousarray(np.stack(ss, axis=1))
        m["scv"] = np.ascontiguousarray(np.stack(sc_, axis=1))
        maps.append(m)
    return maps


def _assemble(res, NTP, DEPTH, ncores, nb_prompt, nb_sample):
    SEQ = 2 * NTP
    y_p = np.zeros((nb_prompt, SEQ, D), np.float32)
    y_s = np.zeros((nb_sample, 4, D), np.float32)
    ssm_p = np.zeros((DEPTH, nb_prompt, 32, 64, 128), np.float32)
    conv_p = np.zeros((DEPTH, nb_prompt, 3, CONV), np.float32)
    ssm_s = np.zeros((DEPTH, nb_sample, 32, 64, 128), np.float32)
    conv_s = np.zeros((DEPTH, nb_sample, 3, CONV), np.float32)
    v_s = np.zeros((DEPTH, nb_sample, 4, 2048), np.float32)
    for c in range(4):
        r = res[c]
        sq = c
        for seg in range(2):
            b0 = (sq * 2 + seg) * NSS
            bs = slice(b0, b0 + NSS)
            y_p[sq, seg * NTP:(seg + 1) * NTP] = r["yout"][seg, :NTP]
            y_s[bs] = r["yout"][seg, NTP:].reshape(NSS, 4, D)
            ssm_s[:, bs] = r["ssm_s"][:, seg].reshape(DEPTH, NSS, 32, 64, 128)
            conv_s[:, bs] = r["conv_o"][:, seg, 3:].reshape(DEPTH, NSS, 3, CONV)
            v_s[:, bs] = r["v_s"][:, seg].reshape(DEPTH, NSS, 4, 2048)
        ssm_p[:, sq] = r["ssm_p"].reshape(DEPTH, 32, 64, 128)
        conv_p[:, sq] = r["conv_o"][:, 1, 0:3]
    return (y_p, y_s, ssm_p, conv_p, ssm_s, conv_s, v_s)


LAST_RES = None


def kernel(**inputs):
    global LAST_RES
    DEPTH = int(np.asarray(inputs["w_in"]).shape[0])
    nbp, SEQ = np.asarray(inputs["x_prompt"]).shape[:2]
    nbs = np.asarray(inputs["x_sample"]).shape[0]
    ncores = 8
    NTP = SEQ // 2
    nc = build(NTP, DEPTH)
    maps = _prep_inputs(inputs, NTP, DEPTH, ncores)
    res = run_bass_kernel_spmd(nc, maps, core_ids=list(range(ncores)))
    if DBG:
        LAST_RES = res.results
    return _assemble(res.results, NTP, DEPTH, ncores, nbp, nbs)
```

```python
import numpy as np
from contextlib import ExitStack
import concourse.bass as bass
import concourse.mybir as mybir
from concourse.bass_utils import run_bass_kernel_spmd

F32 = mybir.dt.float32
BF16 = mybir.dt.bfloat16
AF = mybir.ActivationFunctionType
ALU = mybir.AluOpType
AX = mybir.AxisListType

D = 2048
DIN = 9248
DFF = 5632
DPLE = 256
CONV = 3072
I1 = 2048
I2 = I1 + 3072
I3 = I2 + 32
I4 = I3 + 2048
EPS = 1e-6
NSS = 16
TS = 64
SAME_ENGINE_SYNC = True
RING_SSD, RING_CM, RING_FFN, RING_PLE = 5, 5, 16, 12


class Buf:
    __slots__ = ("w", "r", "name", "pend")

    def __init__(self, name=""):
        self.w = None
        self.r = {}
        self.name = name
        self.pend = False


class Sched:
    def __init__(self, nc, es, ndma=24):
        self.nc = nc
        self.eng = {"pe": nc.tensor, "dve": nc.vector, "act": nc.scalar, "pool": nc.gpsimd, "sp": nc.sync}
        self.sem = {}
        for k in ["pe", "dve", "act", "pool"]:
            self.sem[k] = es.enter_context(nc.semaphore("s_" + k))
        self.cnt = {k: 0 for k in self.sem}
        self.seen = {e: {} for e in self.eng}
        self.dsem = [es.enter_context(nc.semaphore("s_dma%d" % i)) for i in range(ndma)]
        self.dcnt = [0] * ndma
        self.drr = 0
        self.ndma = ndma

    def _semof(self, key):
        if isinstance(key, tuple):
            return self.dsem[key[1]]
        return self.sem[key]

    def _wait(self, e, deps):
        best = {}
        for d in deps:
            if d is None:
                continue
            key, val = d
            if key == e and not SAME_ENGINE_SYNC:
                continue
            if val > best.get(key, 0):
                best[key] = val
        for key, val in best.items():
            if self.seen[e].get(key, 0) >= val:
                continue
            self.eng[e].wait_ge(self._semof(key), val)
            self.seen[e][key] = val

    def _deps(self, R, W):
        deps = []
        for b in R:
            deps.append(b.w)
        for b in W:
            deps.append(b.w)
            deps.extend(b.r.values())
        return deps

    def _mark(self, tok, R, W):
        key = tok[0]
        for b in R:
            b.pend = False
            b.r[key] = tok
        for b in W:
            b.w = tok
            b.r = {}

    def op(self, e, fn, R=(), W=()):
        self._wait(e, self._deps(R, W))
        inst = fn()
        self.cnt[e] += 1
        inst.then_inc(self.sem[e], 1)
        self._mark((e, self.cnt[e]), R, W)

    def group(self, e, fns, R=(), W=()):
        self._wait(e, self._deps(R, W))
        inst = None
        for fn in fns:
            inst = fn()
        self.cnt[e] += 1
        inst.then_inc(self.sem[e], 1)
        self._mark((e, self.cnt[e]), R, W)

    def dma(self, q, out, in_, R=(), W=(), **kw):
        i = self.drr
        self.drr = (i + 1) % self.ndma
        deps = self._deps(R, W)
        if self.dcnt[i] > 0:
            deps.append((("dma", i), self.dcnt[i]))
        self._wait(q, deps)
        inst = self.eng[q].dma_start(out=out, in_=in_, **kw)
        self.dcnt[i] += 16
        inst.then_inc(self.dsem[i], 16)
        self._mark((("dma", i), self.dcnt[i]), R, W)

    def drain(self, e="sp"):
        deps = [(("dma", i), self.dcnt[i]) for i in range(self.ndma) if self.dcnt[i] > 0]
        deps += [(k, self.cnt[k]) for k in self.cnt if self.cnt[k] > 0]
        self._wait(e, deps)


def split_cols(c0, n, mx=512):
    out = []
    while n > 0:
        k = min(mx, n)
        out.append((c0, k))
        c0 += k
        n -= k
    return out


DBG = False


def build(NTP, DEPTH, stop_after=None):
    NCH = NTP // 128
    NT = NCH + 1
    T = NTP + TS
    TC = 3 + T
    nc = bass.Bass("TRN2", target_bir_lowering=False)

    def din(name, shape):
        return nc.dram_tensor(name, list(shape), F32, kind="ExternalInput").ap()

    def dout(name, shape):
        return nc.dram_tensor(name, list(shape), F32, kind="ExternalOutput").ap()

    NSEG = 2
    xin = din("xin", [NSEG, T, D])
    pin = din("pin", [DEPTH, NSEG, T, DPLE])
    sst = din("sst", [DEPTH, NSEG, NSS, 2048, 128])
    scv = din("scv", [DEPTH, NSEG, NSS * 3, CONV])
    prm = din("prm", [DEPTH * 200 + 16, 128])
    smallp = din("smallp", [DEPTH, 96])
    g_cm_d = din("g_cm", [DEPTH, 2048])
    w_s_d = din("w_s", [DEPTH, 16, 128, 128])
    b_s_d = din("b_s", [DEPTH, 16, 128])
    w_in = din("w_in", [DEPTH, D, DIN])
    w_out = din("w_out", [DEPTH, 4096, D])
    w_gate = din("w_gate", [DEPTH, D, DFF])
    w_up = din("w_up", [DEPTH, D, DFF])
    w_down = din("w_down", [DEPTH, DFF, D])
    w_pg = din("w_pg", [DEPTH, D, D])
    w_ple = din("w_ple", [DEPTH, DPLE, D])
    g_final_d = din("g_final", [1, 2048])

    yout = dout("yout", [NSEG, T, D])
    ssm_p = dout("ssm_p", [DEPTH, 2048, 128])
    conv_o = dout("conv_o", [DEPTH, NSEG, 3 + NSS * 3, CONV])
    ssm_s = dout("ssm_s", [DEPTH, NSEG, NSS, 2048, 128])
    v_s = dout("v_s", [DEPTH, NSEG, TS, 2048])
    hbuf = nc.dram_tensor("hbuf", [NSEG, T, D], F32, kind="Internal").ap()
    hstate = nc.dram_tensor("hstate", [DEPTH, 4, 128, 512], F32, kind="Internal").ap()
    hbuf_b = [[Buf() for _ in range(NTP // 128 + 1)] for _ in range(NSEG)]
    hstate_b = [[Buf() for _ in range(4)] for _ in range(DEPTH)]

    es = ExitStack()
    S = Sched(nc, es)

    class TL:
        def __init__(self, name, shape, dt, nb=1, psum=False, sc=None):
            f = nc.psum_tensor if psum else nc.sbuf_tensor
            self.t = (sc or es).enter_context(f(name, list(shape), dt))
            self.bufs = [Buf(name + str(i)) for i in range(nb)]
            self.b = self.bufs[0]

    def rows(i):
        return 128 if i < NCH else TS

    def ncol(i):
        return 3 + i * 128

    def tcol(i):
        return i * 128

    h = TL("h", [128, NT, D], F32, nb=NT)
    nT = TL("nT", [128, 16, TC], BF16)
    ring = []
    ring_i = [0]

    def make_ring(sc, n, tag):
        ring[:] = [TL("ring%s_%d" % (tag, k), [128, 2048], BF16, sc=sc) for k in range(n)]
        ring_i[0] = 0
    scr = TL("scr", [128, 512], F32)
    PT = TL("PT", [128, DEPTH * 200 + 16], F32)
    ident = TL("ident", [128, 128], F32)
    tri = TL("tri", [128, 128], F32)
    ustr = TL("ustr", [128, 128], F32)
    ones = TL("ones", [128, 128], F32)
    maskS = TL("maskS", [64, 64], F32)
    blk1 = TL("blk1", [64, 64], F32)
    rmask = TL("rmask", [64, 16], F32)
    maskB3 = TL("maskB3", [128, 16, 64], BF16)
    smb = TL("smb", [128, 96], F32)
    CONSTS = [ident.b, tri.b, ustr.b, ones.b, maskS.b, blk1.b, rmask.b, maskB3.b]

    banks = [TL("bank%d" % i, [128, 512], F32, psum=True) for i in range(8)]
    NROT = 6
    rot = [0]

    def ps():
        b = banks[rot[0]]
        rot[0] = (rot[0] + 1) % NROT
        if b.b.pend:
            raise RuntimeError("PSUM rotation hazard: bank reused before its consumer was emitted")
        b.b.pend = True
        return b

    bky2 = [banks[6], banks[7]]
    accA = banks[6]
    accB = bky2[(NCH + 1) % 2]

    V = nc.vector
    A = nc.scalar
    PE = nc.tensor

    def const_tri(tl, n, cmp, base=0, mult=1, pat=-1):
        def f():
            nc.gpsimd.memset(tl.t[:], 1.0)
            return nc.gpsimd.affine_select(out=tl.t[:], in_=tl.t[:], pattern=[[pat, n]], compare_op=cmp,
                                           fill=0.0, base=base, channel_multiplier=mult)
        S.group("pool", [f], W=[tl.b])

    const_tri(ident, 128, ALU.is_equal)
    const_tri(tri, 128, ALU.is_ge, pat=1, mult=-1)
    const_tri(ustr, 128, ALU.is_gt, pat=-1, mult=1)
    S.op("pool", lambda: nc.gpsimd.memset(ones.t[:], 1.0), W=[ones.b])
    Rm = TL("Rm", [4, 64], F32)

    def f_rm():
        nc.gpsimd.memset(Rm.t[:], 1.0)
        return nc.gpsimd.affine_select(out=Rm.t[:].rearrange("p (b l) -> p b l", l=4),
                                       in_=Rm.t[:].rearrange("p (b l) -> p b l", l=4), pattern=[[0, 16], [1, 4]],
                                       compare_op=ALU.is_equal, fill=0.0, base=0, channel_multiplier=-1)
    S.group("pool", [f_rm], W=[Rm.b])
    def f_rmask():
        nc.gpsimd.memset(rmask.t[:], 1.0)
        nc.gpsimd.affine_select(out=rmask.t[:], in_=rmask.t[:], pattern=[[-4, 16]], compare_op=ALU.is_ge,
                                fill=0.0, base=0, channel_multiplier=1)
        return nc.gpsimd.affine_select(out=rmask.t[:], in_=rmask.t[:], pattern=[[4, 16]], compare_op=ALU.is_ge,
                                       fill=0.0, base=3, channel_multiplier=-1)
    S.group("pool", [f_rmask], W=[rmask.b])
    with ExitStack() as sc0:
        mb3f = TL("mb3f", [128, 16, 64], F32, sc=sc0)

        def f_mb3():
            nc.gpsimd.memset(mb3f.t[:], 1.0)
            nc.gpsimd.affine_select(out=mb3f.t[:], in_=mb3f.t[:], pattern=[[-4, 16], [1, 64]], compare_op=ALU.is_ge,
                                    fill=0.0, base=0, channel_multiplier=0)
            return nc.gpsimd.affine_select(out=mb3f.t[:], in_=mb3f.t[:], pattern=[[4, 16], [-1, 64]],
                                           compare_op=ALU.is_ge, fill=0.0, base=3, channel_multiplier=0)
        S.group("pool", [f_mb3], W=[mb3f.b])
        S.op("dve", lambda: V.tensor_copy(out=maskB3.t[:, :, :], in_=mb3f.t[:, :, :]), R=[mb3f.b], W=[maskB3.b])
        for e_ in ["pe", "dve", "act", "pool", "sp"]:
            S.drain(e_)
    bk = ps()
    rmT = TL("rmT", [16, 64], F32)
    S.op("pe", lambda: PE.transpose(bk.t[0:16, 0:64], rmask.t[:], ident.t[0:64, 0:64]), R=[rmask.b, ident.b], W=[bk.b])
    S.op("dve", lambda: V.tensor_copy(out=rmT.t[:], in_=bk.t[0:16, 0:64]), R=[bk.b], W=[rmT.b])
    bk2 = ps()
    S.op("pe", lambda: PE.matmul(bk2.t[0:64, 0:64], lhsT=rmT.t[:], rhs=rmT.t[:], start=True, stop=True),
         R=[rmT.b], W=[bk2.b])
    S.op("dve", lambda: V.tensor_copy(out=blk1.t[:], in_=bk2.t[0:64, 0:64]), R=[bk2.b], W=[blk1.b])
    S.op("dve", lambda: V.tensor_tensor(out=maskS.t[:], in0=blk1.t[:], in1=tri.t[0:64, 0:64], op=ALU.mult),
         R=[blk1.b, tri.b], W=[maskS.b])

    NPR = DEPTH * 200 + 16
    O_GMIX, O_GFFN, O_GPG, O_GSSD, O_GCM, O_CW, O_CB = 0, 16, 32, 48, 64, 80, 176
    O_GFIN = DEPTH * 200
    pstg = TL("pstg", [128, 128], F32)
    for r0 in range(0, NPR, 128):
        nr = min(128, NPR - r0)
        S.dma("sp", pstg.t[0:nr, :], prm[r0:r0 + nr, :], W=[pstg.b])
        bkp = ps()
        S.op("pe", lambda: PE.transpose(bkp.t[:, 0:nr], pstg.t[0:nr, :], ident.t[0:nr, 0:nr]),
             R=[pstg.b, ident.b], W=[bkp.b])
        S.op("dve", lambda: V.tensor_copy(out=PT.t[:, r0:r0 + nr], in_=bkp.t[:, 0:nr]), R=[bkp.b], W=[PT.b])

    ntail = TL("ntail", [128, 16, 3], BF16)

    yb = TL("yb", [128, 4, T], BF16)
    dbg_t = {}

    def dump_yb(name):
        if not DBG:
            return
        dd = dout(name, [128, 4 * T])
        for ct in range(4):
            for (c0, n) in split_cols(0, T):
                S.op("dve", lambda: V.tensor_copy(out=scr.t[:, 0:n], in_=yb.t[:, ct, c0:c0 + n]), R=[yb.b], W=[scr.b])
                S.dma("sp", dd[:, ct * T + c0:ct * T + c0 + n], scr.t[:, 0:n], R=[scr.b])
    epsb = TL("epsb", [128, 1], F32)
    S.op("dve", lambda: V.memset(epsb.t[:, :], EPS), W=[epsb.b])
    stat = TL("stat", [128, NT * 8], F32, nb=NT)

    def barrier():
        for e in ["pe", "dve", "act", "pool", "sp"]:
            S.drain(e)

    def wload(src_ap):
        sl = ring[ring_i[0]]
        ring_i[0] = (ring_i[0] + 1) % len(ring)
        if sl.b.pend:
            raise RuntimeError("weight ring hazard: slot reloaded before its consumer was emitted")
        sl.b.pend = True
        a, b = src_ap.shape[1], src_ap.shape[2]
        view = sl.t[:, 0:a * b].rearrange("p (a b) -> p a b", a=a)
        S.dma("pool", view, src_ap, W=[sl.b])
        return view, sl.b

    def wblock(w2d, c0, ncols=512, K=2048, r0=0):
        out = []
        for s in range(K // 512):
            src = w2d[r0 + s * 512:r0 + (s + 1) * 512, c0:c0 + ncols].rearrange("(kc p) n -> p kc n", p=128)
            out.append(wload(src))
        return out

    def mm_fm(slots, ct, c0, n):
        bk = ps()
        KC = 4 * len(slots)
        fns = []
        for kc in range(KC):
            v = slots[kc // 4][0]
            fns.append(lambda kc=kc, v=v: PE.matmul(bk.t[:, 0:n], lhsT=v[:, kc % 4, ct * 128:(ct + 1) * 128],
                                                    rhs=nT.t[:, kc, c0:c0 + n], start=(kc == 0), stop=(kc == KC - 1)))
        S.group("pe", fns, R=[nT.b] + [b for _, b in slots], W=[bk.b])
        return bk

    def gemm_tm(slots, src, src_buf, colof, consumer, ncols=512):
        KC = 4 * len(slots)
        for i in range(NT):
            bk = ps()
            r = rows(i)
            c = colof(i)
            fns = []
            for kc in range(KC):
                v = slots[kc // 4][0]
                fns.append(lambda kc=kc, v=v: PE.matmul(bk.t[0:r, 0:ncols], lhsT=src[:, kc, c:c + r],
                                                        rhs=v[:, kc % 4, 0:ncols], start=(kc == 0),
                                                        stop=(kc == KC - 1)))
            S.group("pe", fns, R=[src_buf] + [b for _, b in slots], W=[bk.b])
            consumer(i, bk)

    def rstd_of(i, n_el):
        r = rows(i)
        o = 8 * i
        S.op("act", lambda: A.activation(out=stat.t[0:r, o + 5:o + 6], in_=stat.t[0:r, o + 4:o + 5], func=AF.Sqrt,
                                         scale=1.0 / n_el, bias=epsb.t[0:r, 0:1]), R=[stat.bufs[i], epsb.b],
             W=[stat.bufs[i]])
        S.op("dve", lambda: V.reciprocal(out=stat.t[0:r, o + 6:o + 7], in_=stat.t[0:r, o + 5:o + 6]),
             R=[stat.bufs[i]], W=[stat.bufs[i]])

    def sumsq_h(i, with_rstd=True):
        r = rows(i)
        o = 8 * i
        for q in range(4):
            S.op("act", lambda: A.activation(out=scr.t[0:r, :], in_=h.t[0:r, i, q * 512:(q + 1) * 512], func=AF.Square,
                                             accum_out=stat.t[0:r, o + q:o + q + 1]),
                 R=[h.bufs[i]], W=[scr.b, stat.bufs[i]])
        S.op("dve", lambda: V.tensor_reduce(out=stat.t[0:r, o + 4:o + 5], in_=stat.t[0:r, o:o + 4], axis=AX.X,
                                            op=ALU.add), R=[stat.bufs[i]], W=[stat.bufs[i]])
        if with_rstd:
            rstd_of(i, D)

    def norm_to_nT(gcol):
        for i in range(NT):
            sumsq_h(i, with_rstd=False)
        for i in range(NT):
            rstd_of(i, D)
        for i in range(NT):
            r = rows(i)
            o = 8 * i
            for q in range(4):
                S.op("dve", lambda: V.tensor_scalar(out=scr.t[0:r, :], in0=h.t[0:r, i, q * 512:(q + 1) * 512],
                                                    scalar1=stat.t[0:r, o + 6:o + 7], scalar2=None, op0=ALU.mult),
                     R=[h.bufs[i], stat.bufs[i]], W=[scr.b])
                bk = ps()
                fns = [lambda kk=kk: PE.transpose(bk.t[:, kk * 128:kk * 128 + r], scr.t[0:r, kk * 128:(kk + 1) * 128],
                                                  ident.t[0:r, 0:r]) for kk in range(4)]
                S.group("pe", fns, R=[scr.b, ident.b], W=[bk.b])
                c = ncol(i)
                S.op("dve", lambda: V.tensor_tensor(
                    out=nT.t[:, q * 4:(q + 1) * 4, c:c + r],
                    in0=bk.t[:, :].rearrange("p (a b) -> p a b", a=4)[:, :, 0:r],
                    in1=PT.t[:, gcol + q * 4:gcol + q * 4 + 4].unsqueeze(2).to_broadcast([128, 4, r]),
                    op=ALU.mult), R=[bk.b, PT.b], W=[nT.b])

    def gelu_psum(bk, r, n, out_ap, out_buf, tmpx, tmpa):
        S.op("act", lambda: A.copy(out=tmpx.t[0:r, 0:n], in_=bk.t[0:r, 0:n]), R=[bk.b], W=[tmpx.b])
        S.op("dve", lambda: V.scalar_tensor_tensor(out=tmpa.t[0:r, 0:n], in0=tmpx.t[0:r, 0:n], scalar=0.044715,
                                                   in1=tmpx.t[0:r, 0:n], op0=ALU.mult, op1=ALU.mult),
             R=[tmpx.b], W=[tmpa.b])
        S.op("dve", lambda: V.scalar_tensor_tensor(out=tmpa.t[0:r, 0:n], in0=tmpa.t[0:r, 0:n], scalar=1.0,
                                                   in1=tmpx.t[0:r, 0:n], op0=ALU.add, op1=ALU.mult),
             R=[tmpx.b, tmpa.b], W=[tmpa.b])
        S.op("act", lambda: A.activation(out=tmpa.t[0:r, 0:n], in_=tmpa.t[0:r, 0:n], func=AF.Sigmoid,
                                         scale=1.5957691216), R=[tmpa.b], W=[tmpa.b])
        S.op("dve", lambda: V.tensor_tensor(out=out_ap, in0=tmpx.t[0:r, 0:n], in1=tmpa.t[0:r, 0:n], op=ALU.mult),
             R=[tmpx.b, tmpa.b], W=[out_buf])

    FMB_H = split_cols(0, TC)
    FMB = split_cols(3, T)

    def outproj(wsrc2d, r0):
        for cb in range(4):
            src = wsrc2d[r0:r0 + 512, cb * 512:(cb + 1) * 512].rearrange("(kc p) n -> p kc n", p=128)
            sl = wload(src)

            def cons(i, bk, cb=cb):
                r = rows(i)
                S.op("dve", lambda: V.tensor_tensor(out=h.t[0:r, i, cb * 512:(cb + 1) * 512],
                                                    in0=h.t[0:r, i, cb * 512:(cb + 1) * 512], in1=bk.t[0:r, :],
                                                    op=ALU.add), R=[bk.b, h.bufs[i]], W=[h.bufs[i]])
            gemm_tm([sl], yb.t, yb.b, tcol, cons)

    def ssd_phase(d, seg, sc):
        pc = d * 200
        L = lambda name, shape, dt, nb=1: TL(name + "_%d_%d" % (d, seg), shape, dt, nb=nb, sc=sc)
        make_ring(sc, RING_SSD, "s%d_%d" % (d, seg))
        Wdt = L("Wdt", [128, 16, 32], BF16)
        dtv = L("dtv", [128, NT, 32], F32)
        dtA = L("dtA", [128, NT, 32], F32)
        eac = L("eac", [128, NT, 32], F32)
        dtdec = L("dtdec", [128, NT, 32], F32)
        cdec = L("cdec", [128, NT, 32], F32)
        tmp32 = L("tmp32", [128, 64], F32)
        xp = L("xp", [128, TC], F32)
        xps = L("xps", [128, NSS, 7], F32)
        hsq = L("hsq", [128, 4, NSS * 3], F32)
        cvo1 = L("cvo1", [128, 3 + NSS * 3], F32)
        cvt1 = L("cvt1", [64, 128], F32)
        cacc = L("cacc", [128, T], F32)
        BTg = L("BTg", [128, T], BF16)
        CTg = L("CTg", [128, T], BF16)
        Btok = L("Btok", [128, NT, 128], BF16)
        xtok = L("xtok", [128, NT, 512], BF16)
        cbm2 = [L("cbm%d" % k, [128, 128], F32) for k in range(2)]
        rh2 = [L("rh%d" % k, [128, 4, 128], F32) for k in range(2)]
        Mh2 = [L("Mh%d" % k, [128, 8, 128], BF16) for k in range(2)]
        xdt2 = [L("xdt%d" % k, [128, 512], BF16) for k in range(2)]
        xdts2 = [L("xdts%d" % k, [128, 512], BF16) for k in range(2)]
        prev = L("prev", [128, 512], F32)
        prevb = L("prevb", [128, 512], BF16)
        t1 = L("t1", [128, 512], F32)
        sqf = L("sqf", [128, 512], F32)
        sqf2 = [sqf, L("sqfb", [128, 512], F32)]
        cvst = sqf2[1]
        Hs = [L("Hs%d" % k, [128, 4, 128], F32) for k in range(2)]
        HTb = L("HTb", [128, 512], BF16)
        Cmb = L("Cmb", [128, 64], BF16)
        Bmb = L("Bmb", [64, 128], BF16)
        dAT = L("dAT", [128, 4, NSS], F32)
        dtArep = sqf

        norm_to_nT(pc + O_GMIX)
        if seg == 0:
            S.op("dve", lambda: V.memset(nT.t[:, :, 0:3], 0.0), W=[nT.b])
        else:
            S.op("dve", lambda: V.tensor_copy(out=nT.t[:, :, 0:3], in_=ntail.t[:, :, :]), R=[ntail.b], W=[nT.b])
        if seg + 1 < NSEG:
            S.op("dve", lambda: V.tensor_copy(out=ntail.t[:, :, :], in_=nT.t[:, :, NTP:NTP + 3]), R=[nT.b],
                 W=[ntail.b])
        S.dma("sp", smb.t[:, :], smallp[d].partition_broadcast(128), W=[smb.b])
        S.op("act", lambda: A.activation(out=smb.t[:, 32:64], in_=smb.t[:, 32:64], func=AF.Exp), R=[smb.b], W=[smb.b])
        S.op("dve", lambda: V.tensor_scalar(out=smb.t[:, 32:64], in0=smb.t[:, 32:64], scalar1=-1.0, scalar2=None,
                                            op0=ALU.mult), R=[smb.b], W=[smb.b])
        S.dma("pool", Wdt.t[:, :, :], w_in[d][:, I2:I3].rearrange("(kc p) n -> p kc n", p=128), W=[Wdt.b])
        for i in range(NT):
            r = rows(i)
            c = ncol(i)
            bk = ps()
            fns = [lambda kc=kc: PE.matmul(bk.t[0:r, 0:32], lhsT=nT.t[:, kc, c:c + r], rhs=Wdt.t[:, kc, :],
                                           start=(kc == 0), stop=(kc == 15)) for kc in range(16)]
            S.group("pe", fns, R=[nT.b, Wdt.b], W=[bk.b])
            S.op("dve", lambda: V.tensor_tensor(out=tmp32.t[0:r, 0:32], in0=bk.t[0:r, 0:32], in1=smb.t[0:r, 0:32],
                                                op=ALU.add), R=[bk.b, smb.b], W=[tmp32.b])
            S.op("act", lambda: A.activation(out=tmp32.t[0:r, 0:32], in_=tmp32.t[0:r, 0:32], func=AF.Exp),
                 R=[tmp32.b], W=[tmp32.b])
            S.op("act", lambda: A.activation(out=dtv.t[0:r, i, :], in_=tmp32.t[0:r, 0:32], func=AF.Ln, bias=1.0),
                 R=[tmp32.b], W=[dtv.b])
            S.op("dve", lambda: V.tensor_tensor(out=dtA.t[0:r, i, :], in0=dtv.t[0:r, i, :], in1=smb.t[0:r, 32:64],
                                                op=ALU.mult), R=[dtv.b, smb.b], W=[dtA.b])
            bk2 = ps()
            mk = tri if i < NCH else maskS
            on = ones if i < NCH else blk1
            S.group("pe", [lambda: PE.matmul(bk2.t[0:r, 0:32], lhsT=mk.t[0:r, 0:r], rhs=dtA.t[0:r, i, :], start=True,
                                             stop=True),
                           lambda: PE.matmul(bk2.t[0:r, 32:64], lhsT=on.t[0:r, 0:r], rhs=dtA.t[0:r, i, :], start=True,
                                             stop=True)], R=[dtA.b] + CONSTS, W=[bk2.b])
            S.op("act", lambda: A.activation(out=eac.t[0:r, i, :], in_=bk2.t[0:r, 0:32], func=AF.Exp),
                 R=[bk2.b], W=[eac.b])
            S.op("act", lambda: A.activation(out=cdec.t[0:r, i, :], in_=bk2.t[0:r, 32:64], func=AF.Exp),
                 R=[bk2.b], W=[cdec.b])
            S.op("act", lambda: A.copy(out=tmp32.t[0:r, 32:64], in_=bk2.t[0:r, 32:64]), R=[bk2.b], W=[tmp32.b])
            S.op("dve", lambda: V.tensor_tensor(out=tmp32.t[0:r, 32:64], in0=tmp32.t[0:r, 32:64], in1=bk2.t[0:r, 0:32],
                                                op=ALU.subtract), R=[bk2.b, tmp32.b], W=[tmp32.b])
            S.op("act", lambda: A.activation(out=tmp32.t[0:r, 32:64], in_=tmp32.t[0:r, 32:64], func=AF.Exp),
                 R=[tmp32.b], W=[tmp32.b])
            S.op("dve", lambda: V.tensor_tensor(out=dtdec.t[0:r, i, :], in0=tmp32.t[0:r, 32:64], in1=dtv.t[0:r, i, :],
                                                op=ALU.mult), R=[tmp32.b, dtv.b], W=[dtdec.b])

        def prep_hist(ct0, nct):
            S.dma("sp", cvst.t[0:48, 0:nct * 128], scv[d, seg][:, ct0 * 128:(ct0 + nct) * 128], W=[cvst.b])
            for k in range(nct):
                bk = ps()
                S.op("pe", lambda: PE.transpose(bk.t[:, 0:48], cvst.t[0:48, k * 128:(k + 1) * 128], ident.t[0:48, 0:48]),
                     R=[cvst.b, ident.b], W=[bk.b])
                S.op("act", lambda: A.copy(out=hsq.t[:, k, :], in_=bk.t[:, 0:48]), R=[bk.b], W=[hsq.b])

        def conv_tile(ctg, k, kind):
            cw = pc + O_CW
            cb_ = pc + O_CB + ctg
            S.op("dve", lambda: V.tensor_copy(out=xps.t[:, :, 0:3],
                                              in_=hsq.t[:, k, :].rearrange("p (b j) -> p b j", j=3)),
                 R=[hsq.b], W=[xps.b])
            S.op("dve", lambda: V.tensor_copy(out=xps.t[:, :, 3:7],
                                              in_=xp.t[:, 3 + NTP:3 + NTP + TS].rearrange("p (b l) -> p b l", l=4)),
                 R=[xp.b, xps.b], W=[xps.b])
            S.op("act", lambda: A.copy(out=cvo1.t[:, 0:3], in_=xp.t[:, NTP:NTP + 3]), R=[xp.b], W=[cvo1.b])
            S.op("act", lambda: A.copy(out=cvo1.t[:, 3:3 + NSS * 3].rearrange("p (b j) -> p b j", j=3),
                                       in_=xps.t[:, :, 4:7]), R=[xps.b, cvo1.b], W=[cvo1.b])
            bk = ps()
            S.op("pe", lambda: PE.transpose(bk.t[0:51, 0:128], cvo1.t[:, :], ident.t[:, :]), R=[cvo1.b, ident.b],
                 W=[bk.b])
            S.op("act", lambda: A.copy(out=cvt1.t[0:51, :], in_=bk.t[0:51, 0:128]), R=[bk.b], W=[cvt1.b])
            S.dma("sp", conv_o[d, seg][:, ctg * 128:(ctg + 1) * 128], cvt1.t[0:51, :], R=[cvt1.b])
            S.op("dve", lambda: V.tensor_scalar(out=cacc.t[:, 0:NTP], in0=xp.t[:, 0:NTP],
                                                scalar1=PT.t[:, cw + ctg:cw + ctg + 1], scalar2=None, op0=ALU.mult),
                 R=[xp.b, PT.b], W=[cacc.b])
            for j in range(1, 4):
                S.op("dve", lambda: V.scalar_tensor_tensor(out=cacc.t[:, 0:NTP], in0=xp.t[:, j:j + NTP],
                                                           scalar=PT.t[:, cw + j * 24 + ctg:cw + j * 24 + ctg + 1],
                                                           in1=cacc.t[:, 0:NTP], op0=ALU.mult, op1=ALU.add),
                     R=[xp.b, PT.b, cacc.b], W=[cacc.b])
            cs = cacc.t[:, NTP:T].rearrange("p (b l) -> p b l", l=4)
            S.op("dve", lambda: V.tensor_scalar(out=cs, in0=xps.t[:, :, 0:4], scalar1=PT.t[:, cw + ctg:cw + ctg + 1],
                                                scalar2=None, op0=ALU.mult), R=[xps.b, PT.b, cacc.b], W=[cacc.b])
            for j in range(1, 4):
                S.op("dve", lambda: V.scalar_tensor_tensor(out=cs, in0=xps.t[:, :, j:j + 4],
                                                           scalar=PT.t[:, cw + j * 24 + ctg:cw + j * 24 + ctg + 1],
                                                           in1=cs, op0=ALU.mult, op1=ALU.add),
                     R=[xps.b, PT.b, cacc.b], W=[cacc.b])
            if kind == "C":
                S.op("act", lambda: A.activation(out=CTg.t[:, :], in_=cacc.t[:, :], func=AF.Silu,
                                                 bias=PT.t[:, cb_:cb_ + 1]), R=[cacc.b, PT.b], W=[CTg.b])
            else:
                S.op("act", lambda: A.activation(out=cacc.t[:, :], in_=cacc.t[:, :], func=AF.Silu,
                                                 bias=PT.t[:, cb_:cb_ + 1]), R=[cacc.b, PT.b], W=[cacc.b])
            if kind == "B":
                S.op("dve", lambda: V.tensor_copy(out=BTg.t[:, :], in_=cacc.t[:, :]), R=[cacc.b], W=[BTg.b])

        def to_tok(dst, width_off):
            for i0 in range(0, NT, 4):
                bk = ps()
                tiles = list(range(i0, min(NT, i0 + 4)))
                fns = [lambda i=i: PE.transpose(bk.t[0:rows(i), (i - i0) * 128:(i - i0 + 1) * 128],
                                                cacc.t[:, tcol(i):tcol(i) + rows(i)], ident.t[:, :]) for i in tiles]
                S.group("pe", fns, R=[cacc.b, ident.b], W=[bk.b])
                full = [i for i in tiles if rows(i) == 128]
                if full:
                    nf = len(full)
                    S.op("act", lambda: A.copy(out=dst.t[:, full[0]:full[0] + nf, width_off:width_off + 128],
                                               in_=bk.t[:, 0:nf * 128].rearrange("p (a b) -> p a b", a=nf)),
                         R=[bk.b], W=[dst.b])
                for i in tiles:
                    if rows(i) != 128:
                        S.op("act", lambda: A.copy(out=dst.t[0:TS, i, width_off:width_off + 128],
                                                   in_=bk.t[0:TS, (i - i0) * 128:(i - i0 + 1) * 128]),
                             R=[bk.b], W=[dst.b])

        def sample_states(g, bko, xdts):
            iS = NCH
            cS = tcol(iS)
            hsl = slice(g * 8, (g + 1) * 8)
            S.op("dve", lambda: V.tensor_copy(out=dtArep.t[0:TS, :].rearrange("p (a b) -> p a b", a=8),
                                              in_=dtA.t[0:TS, iS, hsl].unsqueeze(2).to_broadcast([TS, 8, 64])),
                 R=[dtA.b], W=[dtArep.b])
            bkd = ps()
            fns = [lambda rt=rt: PE.matmul(bkd.t[:, rt * NSS:(rt + 1) * NSS], lhsT=dtArep.t[0:TS, rt * 128:(rt + 1) * 128],
                                           rhs=rmask.t[:, :], start=True, stop=True) for rt in range(4)]
            S.group("pe", fns, R=[dtArep.b] + CONSTS, W=[bkd.b])
            S.op("act", lambda: A.activation(out=dAT.t[:, :, :],
                                             in_=bkd.t[:, 0:4 * NSS].rearrange("p (a b) -> p a b", a=4),
                                             func=AF.Exp), R=[bkd.b], W=[dAT.b])
            def load_H(b):
                Hb = Hs[b % 2]
                S.dma("sp", Hb.t[:, :, :],
                      sst[d, seg, b][g * 512:(g + 1) * 512, :].rearrange("(rt p) n -> p rt n", p=128), W=[Hb.b])
            load_H(0)
            for b in range(NSS):
                H = Hs[b % 2]
                if b + 1 < NSS:
                    load_H(b + 1)
                S.op("dve", lambda: V.tensor_tensor(out=Cmb.t[:, :], in0=CTg.t[:, cS:cS + TS], in1=maskB3.t[:, b, :],
                                                    op=ALU.mult), R=[CTg.b] + CONSTS, W=[Cmb.b])
                S.op("dve", lambda: V.tensor_scalar(out=Bmb.t[:, :], in0=Btok.t[0:TS, iS, :],
                                                    scalar1=rmask.t[:, b:b + 1], scalar2=None, op0=ALU.mult),
                     R=[Btok.b] + CONSTS, W=[Bmb.b])
                bkt = ps()
                fns = [lambda rt=rt: PE.transpose(bkt.t[:, rt * 128:(rt + 1) * 128], H.t[:, rt, :], ident.t[:, :])
                       for rt in range(4)]
                S.group("pe", fns, R=[H.b, ident.b], W=[bkt.b])
                S.op("act", lambda: A.copy(out=HTb.t[:, :], in_=bkt.t[:, :]), R=[bkt.b], W=[HTb.b])
                S.op("pe", lambda: PE.matmul(bko.t[0:TS, :], lhsT=Cmb.t[:, :], rhs=HTb.t[:, :], start=(b == 0),
                                             stop=(b == NSS - 1)), R=[Cmb.b, HTb.b], W=[bko.b])
                bku = ps()
                fns = [lambda rt=rt: PE.matmul(bku.t[:, rt * 128:(rt + 1) * 128],
                                               lhsT=xdts.t[0:TS, rt * 128:(rt + 1) * 128], rhs=Bmb.t[:, :], start=True,
                                               stop=True) for rt in range(4)]
                S.group("pe", fns, R=[xdts.b, Bmb.b], W=[bku.b])
                S.op("dve", lambda: V.tensor_tensor(out=H.t[:, :, :], in0=H.t[:, :, :],
                                                    in1=dAT.t[:, :, b:b + 1].to_broadcast([128, 4, 128]), op=ALU.mult),
                     R=[H.b, dAT.b], W=[H.b])
                S.op("dve", lambda: V.tensor_tensor(out=H.t[:, :, :], in0=H.t[:, :, :],
                                                    in1=bku.t[:, :].rearrange("p (a b) -> p a b", a=4), op=ALU.add),
                     R=[H.b, bku.b], W=[H.b])
                S.dma("pool", ssm_s[d, seg, b][g * 512:(g + 1) * 512, :].rearrange("(rt p) n -> p rt n", p=128),
                      H.t[:, :, :], R=[H.b])

        for g in range(4):
            hsl = slice(g * 8, (g + 1) * 8)
            bslot = wload(w_in[d][:, I1 + 2048 + g * 128:I1 + 2048 + (g + 1) * 128].rearrange("(kc p) n -> p kc n", p=128))
            cslot = wload(w_in[d][:, I1 + 2560 + g * 128:I1 + 2560 + (g + 1) * 128].rearrange("(kc p) n -> p kc n", p=128))
            xslots_box = []
            jobs = [("B", 16 + g, None), ("C", 20 + g, None)] + [("x", g * 4 + ct, ct) for ct in range(4)]

            def job_gemm(job):
                kind, ctg, ct = job
                res_ = []
                for (c0, n) in FMB_H:
                    if kind == "x":
                        if not xslots_box:
                            xslots_box.extend(wblock(w_in[d], I1 + g * 512))
                        bk = mm_fm(xslots_box, ct, c0, n)
                    else:
                        slv, slb = bslot if kind == "B" else cslot
                        bk = ps()
                        fns = [lambda kc=kc: PE.matmul(bk.t[:, 0:n], lhsT=slv[:, kc, :], rhs=nT.t[:, kc, c0:c0 + n],
                                                       start=(kc == 0), stop=(kc == 15)) for kc in range(16)]
                        S.group("pe", fns, R=[nT.b, slb], W=[bk.b])
                    res_.append((c0, n, bk))
                return res_

            deferred = [None]
            for k, job in enumerate(jobs):
                kind, ctg, ct = job
                if kind != "x":
                    prep_hist(ctg, 1)
                    hk = 0
                else:
                    if ct == 0:
                        prep_hist(g * 4, 4)
                    hk = ct
                pend = job_gemm(job)
                if deferred[0] is not None:
                    to_tok(*deferred[0])
                    deferred[0] = None
                for (c0, n, bk) in pend:
                    S.op("act", lambda: A.copy(out=xp.t[:, c0:c0 + n], in_=bk.t[:, 0:n]), R=[bk.b], W=[xp.b])
                conv_tile(ctg, hk, kind)
                if kind == "B":
                    deferred[0] = (Btok, 0)
                elif kind == "x":
                    deferred[0] = (xtok, ct * 128)
            if deferred[0] is not None:
                to_tok(*deferred[0])
                deferred[0] = None

            if seg == 0:
                S.op("dve", lambda: V.memset(prev.t[:, :], 0.0), W=[prev.b])
            else:
                S.dma("sp", prev.t[:, :], hstate[d, g], R=[hstate_b[d][g]], W=[prev.b])
            S.op("act", lambda: A.copy(out=prevb.t[:, :], in_=prev.t[:, :]), R=[prev.b], W=[prevb.b])

            def front(i):
                r = rows(i)
                c = tcol(i)
                p = i % 2
                mk = tri if i < NCH else maskS
                cbm_, rh_, Mh_, xdt_, xdts_, bky = cbm2[p], rh2[p], Mh2[p], xdt2[p], xdts2[p], bky2[p]
                bkc = ps()
                S.op("pe", lambda: PE.matmul(bkc.t[0:r, 0:r], lhsT=BTg.t[:, c:c + r], rhs=CTg.t[:, c:c + r], start=True,
                                             stop=True), R=[BTg.b, CTg.b], W=[bkc.b])
                S.op("dve", lambda: V.tensor_tensor(out=cbm_.t[0:r, 0:r], in0=bkc.t[0:r, 0:r], in1=mk.t[0:r, 0:r],
                                                    op=ALU.mult), R=[bkc.b] + CONSTS, W=[cbm_.b])
                for h4 in range(2):
                    bks = ps()
                    for hh in range(4):
                        hd = g * 8 + h4 * 4 + hh
                        S.op("dve", lambda: V.tensor_scalar(out=rh_.t[0:r, hh, 0:r], in0=tri.t[0:r, 0:r],
                                                            scalar1=dtA.t[0:r, i, hd:hd + 1], scalar2=None,
                                                            op0=ALU.mult), R=[dtA.b] + CONSTS, W=[rh_.b])
                    fns = [lambda hh=hh: PE.matmul(bks.t[0:r, hh * 128:hh * 128 + r], lhsT=ustr.t[0:r, 0:r],
                                                   rhs=rh_.t[0:r, hh, 0:r], start=True, stop=True) for hh in range(4)]
                    S.group("pe", fns, R=[rh_.b] + CONSTS, W=[bks.b])
                    S.op("act", lambda: A.activation(out=rh_.t[0:r, :, 0:r],
                                                     in_=bks.t[0:r, :].rearrange("p (a b) -> p a b", a=4)[:, :, 0:r],
                                                     func=AF.Exp), R=[bks.b, rh_.b], W=[rh_.b])
                    S.op("dve", lambda: V.tensor_tensor(out=Mh_.t[0:r, h4 * 4:h4 * 4 + 4, 0:r], in0=rh_.t[0:r, :, 0:r],
                                                        in1=cbm_.t[0:r, 0:r].unsqueeze(1).to_broadcast([r, 4, r]),
                                                        op=ALU.mult), R=[rh_.b, cbm_.b], W=[Mh_.b])
                xv = xtok.t[0:r, i, :].rearrange("p (a b) -> p a b", a=8)
                S.op("dve", lambda: V.tensor_tensor(out=xdt_.t[0:r, :].rearrange("p (a b) -> p a b", a=8), in0=xv,
                                                    in1=dtv.t[0:r, i, hsl].unsqueeze(2).to_broadcast([r, 8, 64]),
                                                    op=ALU.mult), R=[xtok.b, dtv.b], W=[xdt_.b])
                S.op("dve", lambda: V.tensor_tensor(out=xdts_.t[0:r, :].rearrange("p (a b) -> p a b", a=8), in0=xv,
                                                    in1=dtdec.t[0:r, i, hsl].unsqueeze(2).to_broadcast([r, 8, 64]),
                                                    op=ALU.mult), R=[xtok.b, dtdec.b], W=[xdts_.b])
                fns = [lambda hh=hh: PE.matmul(bky.t[0:r, hh * 64:(hh + 1) * 64], lhsT=Mh_.t[0:r, hh, 0:r],
                                               rhs=xdt_.t[0:r, hh * 64:(hh + 1) * 64], start=True, stop=True)
                       for hh in range(8)]
                S.group("pe", fns, R=[Mh_.b, xdt_.b], W=[bky.b])

            def back(i):
                r = rows(i)
                c = tcol(i)
                p = i % 2
                xdts_, bky = xdts2[p], bky2[p]
                xv = xtok.t[0:r, i, :].rearrange("p (a b) -> p a b", a=8)
                if i < NCH:
                    bko = ps()
                    S.op("pe", lambda: PE.matmul(bko.t[0:r, :], lhsT=CTg.t[:, c:c + r], rhs=prevb.t[:, :], start=True,
                                                 stop=True), R=[CTg.b, prevb.b], W=[bko.b])
                else:
                    bko = accB
                    sample_states(g, bko, xdts_)
                S.op("dve", lambda: V.tensor_tensor(out=t1.t[0:r, :].rearrange("p (a b) -> p a b", a=8),
                                                    in0=bko.t[0:r, :].rearrange("p (a b) -> p a b", a=8),
                                                    in1=eac.t[0:r, i, hsl].unsqueeze(2).to_broadcast([r, 8, 64]),
                                                    op=ALU.mult), R=[bko.b, eac.b], W=[t1.b])
                S.op("dve", lambda: V.tensor_tensor(out=t1.t[0:r, :], in0=t1.t[0:r, :], in1=bky.t[0:r, :], op=ALU.add),
                     R=[bky.b, t1.b], W=[t1.b])
                S.op("dve", lambda: V.tensor_tensor(out=sqf.t[0:r, :].rearrange("p (a b) -> p a b", a=8), in0=xv,
                                                    in1=smb.t[0:r, 64 + g * 8:64 + g * 8 + 8].unsqueeze(2).to_broadcast(
                                                        [r, 8, 64]), op=ALU.mult), R=[xtok.b, smb.b], W=[sqf.b])
                S.op("dve", lambda: V.tensor_tensor(out=t1.t[0:r, :], in0=t1.t[0:r, :], in1=sqf.t[0:r, :], op=ALU.add),
                     R=[t1.b, sqf.b], W=[t1.b])
                bkt = ps()
                fns = [lambda k4=k4: PE.transpose(bkt.t[:, k4 * 128:k4 * 128 + r], t1.t[0:r, k4 * 128:(k4 + 1) * 128],
                                                  ident.t[0:r, 0:r]) for k4 in range(4)]
                S.group("pe", fns, R=[t1.b, ident.b], W=[bkt.b])
                S.op("act", lambda: A.copy(out=yb.t[:, :, c:c + r],
                                           in_=bkt.t[:, :].rearrange("p (a b) -> p a b", a=4)[:, :, 0:r]),
                     R=[bkt.b], W=[yb.b])
                if i < NCH:
                    bkS = ps()
                    S.op("pe", lambda: PE.matmul(bkS.t[:, :], lhsT=Btok.t[0:r, i, :], rhs=xdts_.t[0:r, :], start=True,
                                                 stop=True), R=[Btok.b, xdts_.b], W=[bkS.b])
                    S.op("dve", lambda: V.tensor_tensor(out=prev.t[:, :].rearrange("p (a b) -> p a b", a=8),
                                                        in0=prev.t[:, :].rearrange("p (a b) -> p a b", a=8),
                                                        in1=cdec.t[:, i, hsl].unsqueeze(2).to_broadcast([128, 8, 64]),
                                                        op=ALU.mult), R=[prev.b, cdec.b], W=[prev.b])
                    S.op("dve", lambda: V.tensor_tensor(out=prev.t[:, :], in0=prev.t[:, :], in1=bkS.t[:, :], op=ALU.add),
                         R=[prev.b, bkS.b], W=[prev.b])
                    S.op("act", lambda: A.copy(out=prevb.t[:, :], in_=prev.t[:, :]), R=[prev.b], W=[prevb.b])

            for i in range(NT + 1):
                if i < NT:
                    front(i)
                if i >= 1:
                    back(i - 1)
            if seg + 1 < NSEG:
                S.dma("sp", hstate[d, g], prev.t[:, :], R=[prev.b], W=[hstate_b[d][g]])
            else:
                bkf = ps()
                fns = [lambda k4=k4: PE.transpose(bkf.t[:, k4 * 128:(k4 + 1) * 128],
                                                  prev.t[:, k4 * 128:(k4 + 1) * 128], ident.t[:, :])
                       for k4 in range(4)]
                S.group("pe", fns, R=[prev.b, ident.b], W=[bkf.b])
                S.op("act", lambda: A.copy(out=t1.t[:, :], in_=bkf.t[:, :]), R=[bkf.b], W=[t1.b])
                S.dma("sp", ssm_p[d][g * 512:(g + 1) * 512, :].rearrange("(rt p) n -> p rt n", p=128),
                      t1.t[:, :].rearrange("p (a b) -> p a b", a=4), R=[t1.b])


            slots = wblock(w_in[d], g * 512)
            for (c0, n) in FMB:
                tc0 = c0 - 3
                pendz = None

                def ones_mm(pz):
                    ctp, sqp = pz
                    S.op("pe", lambda: PE.matmul(accA.t[:, 0:n], lhsT=ones.t[:, :], rhs=sqp.t[:, 0:n], start=(ctp == 0),
                                                 stop=(ctp == 3)), R=[sqp.b, ones.b], W=[accA.b])
                for ct in range(4):
                    bk = mm_fm(slots, ct, c0, n)
                    if pendz is not None:
                        ones_mm(pendz)
                    sq_ = sqf2[ct % 2]
                    S.op("act", lambda: A.activation(out=sq_.t[:, 0:n], in_=bk.t[:, 0:n], func=AF.Silu),
                         R=[bk.b], W=[sq_.b])
                    S.op("dve", lambda: V.tensor_tensor(out=yb.t[:, ct, tc0:tc0 + n], in0=yb.t[:, ct, tc0:tc0 + n],
                                                        in1=sq_.t[:, 0:n], op=ALU.mult), R=[yb.b, sq_.b], W=[yb.b])
                    S.op("act", lambda: A.activation(out=sq_.t[:, 0:n], in_=yb.t[:, ct, tc0:tc0 + n], func=AF.Square),
                         R=[yb.b, sq_.b], W=[sq_.b])
                    pendz = (ct, sq_)
                ones_mm(pendz)
                S.op("act", lambda: A.activation(out=t1.t[:, 0:n], in_=accA.t[:, 0:n], func=AF.Sqrt, scale=1.0 / 512,
                                                 bias=epsb.t[:, 0:1]), R=[accA.b, t1.b, epsb.b], W=[t1.b])
                S.op("dve", lambda: V.reciprocal(out=t1.t[:, 0:n], in_=t1.t[:, 0:n]), R=[t1.b], W=[t1.b])
                for ct in range(4):
                    gc = pc + O_GSSD + g * 4 + ct
                    S.op("dve", lambda: V.scalar_tensor_tensor(out=yb.t[:, ct, tc0:tc0 + n], in0=yb.t[:, ct, tc0:tc0 + n],
                                                               scalar=PT.t[:, gc:gc + 1], in1=t1.t[:, 0:n],
                                                               op0=ALU.mult, op1=ALU.mult),
                         R=[yb.b, PT.b, t1.b], W=[yb.b])
            if d == 0 and g == 0 and seg == 0:
                dump_yb("dbg_yssd0")
            outproj(w_out[d], g * 512)

    def cm_phase(d, seg, sc):
        pc = d * 200
        L = lambda name, shape, dt, nb=1: TL(name + "_%d_%d" % (d, seg), shape, dt, nb=nb, sc=sc)
        make_ring(sc, RING_CM, "c%d_%d" % (d, seg))
        vg = L("vg", [128, NT, 2048], BF16)
        gx2 = [L("gx%d" % k, [128, 512], F32) for k in range(2)]
        ga2 = [L("ga%d" % k, [128, 512], F32) for k in range(2)]
        gx = gx2[0]
        gcnt = [0]
        ssq = L("ssq", [128, NT, 4], F32)
        Wn = L("Wn", [128, 4, 128], F32)
        WsT = L("WsT", [128, 4, 128], F32)
        Wr2 = [L("Wr%d" % k, [128, 128], BF16) for k in range(2)]
        WsS = L("WsS", [64, 16, 64], F32)
        bsb = L("bsb", [128, 4, 192], F32)
        tmpc2 = [L("tmpc%d" % k, [128, 128], F32) for k in range(2)]
        gcb = L("gcb", [64, 512], F32)
        vso = L("vso", [64, 512], F32)
        W4n = L("W4n", [4, 16, 4], F32)
        bs4 = L("bs4", [128, 4, 4], F32)
        o1s = gx
        for j in range(4):
            slots = wblock(w_in[d], I4 + j * 512)

            def cons(i, bk, j=j):
                r = rows(i)
                gx_, ga_ = gx2[gcnt[0] % 2], ga2[gcnt[0] % 2]
                gcnt[0] += 1
                gelu_psum(bk, r, 512, vg.t[0:r, i, j * 512:(j + 1) * 512], vg.b, gx_, ga_)
                S.op("act", lambda: A.activation(out=ga_.t[0:r, :], in_=vg.t[0:r, i, j * 512:(j + 1) * 512],
                                                 func=AF.Square, accum_out=ssq.t[0:r, i, j:j + 1]),
                     R=[vg.b, ga_.b], W=[ga_.b, ssq.b])
            gemm_tm(slots, nT.t, nT.b, ncol, cons)
        for i in range(NT):
            r = rows(i)
            S.op("dve", lambda: V.tensor_reduce(out=stat.t[0:r, 8 * i + 4:8 * i + 5], in_=ssq.t[0:r, i, :], axis=AX.X,
                                                op=ALU.add), R=[ssq.b, stat.bufs[i]], W=[stat.bufs[i]])
            rstd_of(i, 2048)
        iS = NCH
        for q in range(4):
            S.dma("sp", gcb.t[:, :], g_cm_d[d][q * 512:(q + 1) * 512].partition_broadcast(64), W=[gcb.b])
            S.op("dve", lambda: V.scalar_tensor_tensor(out=vso.t[:, :], in0=vg.t[0:TS, iS, q * 512:(q + 1) * 512],
                                                       scalar=stat.t[0:TS, 8 * iS + 6:8 * iS + 7], in1=gcb.t[:, :],
                                                       op0=ALU.mult, op1=ALU.mult),
                 R=[vg.b, stat.bufs[iS], gcb.b], W=[vso.b])
            S.dma("sp", v_s[d, seg][:, q * 512:(q + 1) * 512], vso.t[:, :], R=[vso.b])
        S.dma("sp", W4n.t[0:4, :, :], w_s_d[d][:, 0:4, 0:4].rearrange("g t s -> t g s"), W=[W4n.b])
        for hf in range(2):
            bk = ps()
            fns = [lambda g8=g8: PE.matmul(bk.t[0:4, g8 * 64:(g8 + 1) * 64], lhsT=W4n.t[0:4, hf * 8 + g8, :],
                                           rhs=Rm.t[0:4, :], start=True, stop=True) for g8 in range(8)]
            S.group("pe", fns, R=[W4n.b, Rm.b], W=[bk.b])
            S.op("act", lambda: A.copy(out=o1s.t[0:4, :], in_=bk.t[0:4, :]), R=[bk.b], W=[o1s.b])
            bk2 = ps()
            S.op("pe", lambda: PE.matmul(bk2.t[0:64, :], lhsT=Rm.t[0:4, :], rhs=o1s.t[0:4, :], start=True, stop=True),
                 R=[Rm.b, o1s.b], W=[bk2.b])
            S.op("dve", lambda: V.tensor_tensor(out=WsS.t[:, hf * 8:(hf + 1) * 8, :],
                                                in0=bk2.t[0:64, :].rearrange("p (a b) -> p a b", a=8),
                                                in1=maskS.t[:, :].unsqueeze(1).to_broadcast([64, 8, 64]), op=ALU.mult),
                 R=[bk2.b] + CONSTS, W=[WsS.b])
        for j in range(4):
            S.dma("sp", Wn.t[:, :, :], w_s_d[d][j * 4:(j + 1) * 4].rearrange("g t s -> t g s"), W=[Wn.b])
            bk = ps()
            fns = [lambda k=k: PE.transpose(bk.t[:, k * 128:(k + 1) * 128], Wn.t[:, k, :], ident.t[:, :])
                   for k in range(4)]
            S.group("pe", fns, R=[Wn.b, ident.b], W=[bk.b])
            S.op("dve", lambda: V.tensor_tensor(out=WsT.t[:, :, :], in0=bk.t[:, :].rearrange("p (a b) -> p a b", a=4),
                                                in1=tri.t[:, :].unsqueeze(1).to_broadcast([128, 4, 128]), op=ALU.mult),
                 R=[bk.b] + CONSTS, W=[WsT.b])
            S.dma("sp", bsb.t[:, :, 0:128], b_s_d[d][j * 4:(j + 1) * 4, :].partition_broadcast(128), W=[bsb.b])
            S.dma("sp", bs4.t[:, :, :], b_s_d[d][j * 4:(j + 1) * 4, 0:4].partition_broadcast(128), W=[bs4.b])
            S.op("dve", lambda: V.tensor_copy(out=bsb.t[:, :, 128:192].rearrange("p g (b l) -> p g b l", l=4),
                                              in_=bs4.t[:, :, :].unsqueeze(2).to_broadcast([128, 4, NSS, 4])),
                 R=[bs4.b, bsb.b], W=[bsb.b])
            slots = wblock(w_in[d], I3 + j * 512)
            for ct in range(4):
                for (c0, n) in FMB:
                    bk = mm_fm(slots, ct, c0, n)
                    gx_, ga_ = gx2[gcnt[0] % 2], ga2[gcnt[0] % 2]
                    gcnt[0] += 1
                    gelu_psum(bk, 128, n, yb.t[:, ct, c0 - 3:c0 - 3 + n], yb.b, gx_, ga_)
            for ct in range(4):
                gg = j * 4 + ct
                for i in range(NT):
                    r = rows(i)
                    c = tcol(i)
                    rs = stat.t[0:r, 8 * i + 6:8 * i + 7]
                    Wr, tmpc = Wr2[gcnt[0] % 2], tmpc2[gcnt[0] % 2]
                    gcnt[0] += 1
                    if i < NCH:
                        S.op("dve", lambda: V.tensor_scalar(out=Wr.t[0:r, 0:r], in0=WsT.t[0:r, ct, 0:r], scalar1=rs,
                                                            scalar2=None, op0=ALU.mult),
                             R=[WsT.b, stat.bufs[i]], W=[Wr.b])
                        boff = 0
                    else:
                        S.op("dve", lambda: V.tensor_scalar(out=Wr.t[0:r, 0:r], in0=WsS.t[0:r, gg, 0:r], scalar1=rs,
                                                            scalar2=None, op0=ALU.mult),
                             R=[WsS.b, stat.bufs[i]], W=[Wr.b])
                        boff = 128
                    bk = ps()
                    S.op("pe", lambda: PE.matmul(bk.t[:, 0:r], lhsT=vg.t[0:r, i, gg * 128:(gg + 1) * 128],
                                                 rhs=Wr.t[0:r, 0:r], start=True, stop=True), R=[vg.b, Wr.b], W=[bk.b])
                    gcol = pc + O_GCM + gg
                    S.op("dve", lambda: V.scalar_tensor_tensor(out=tmpc.t[:, 0:r], in0=bk.t[:, 0:r],
                                                               scalar=PT.t[:, gcol:gcol + 1],
                                                               in1=bsb.t[:, ct, boff:boff + r], op0=ALU.mult,
                                                               op1=ALU.add), R=[bk.b, PT.b, bsb.b], W=[tmpc.b])
                    S.op("dve", lambda: V.tensor_tensor(out=yb.t[:, ct, c:c + r], in0=yb.t[:, ct, c:c + r],
                                                        in1=tmpc.t[:, 0:r], op=ALU.mult), R=[yb.b, tmpc.b], W=[yb.b])
            if d == 0 and j == 0 and seg == 0:
                dump_yb("dbg_ycm0")
            outproj(w_out[d], 2048 + j * 512)

    def ffn_phase(d, seg, sc):
        pc = d * 200
        make_ring(sc, RING_FFN, "f%d_%d" % (d, seg))
        norm_to_nT(pc + O_GFFN)
        for blk in range(DFF // 512):
            slots = wblock(w_gate[d], blk * 512)
            for ct in range(4):
                for (c0, n) in FMB:
                    bk = mm_fm(slots, ct, c0, n)
                    S.op("act", lambda: A.activation(out=yb.t[:, ct, c0 - 3:c0 - 3 + n], in_=bk.t[:, 0:n], func=AF.Silu),
                         R=[bk.b], W=[yb.b])
            slots = wblock(w_up[d], blk * 512)
            for ct in range(4):
                for (c0, n) in FMB:
                    bk = mm_fm(slots, ct, c0, n)
                    S.op("dve", lambda: V.tensor_tensor(out=yb.t[:, ct, c0 - 3:c0 - 3 + n],
                                                        in0=yb.t[:, ct, c0 - 3:c0 - 3 + n], in1=bk.t[:, 0:n],
                                                        op=ALU.mult), R=[bk.b, yb.b], W=[yb.b])
            outproj(w_down[d], blk * 512)

    def ple_phase(d, seg, sc):
        pc = d * 200
        L = lambda name, shape, dt, nb=1: TL(name + "_%d_%d" % (d, seg), shape, dt, nb=nb, sc=sc)
        make_ring(sc, RING_PLE, "p%d_%d" % (d, seg))
        pT = L("pT", [128, 2, T], BF16)
        ptok = L("ptok", [128, DPLE], F32)
        gsig = L("gsig", [128, 512], F32)
        for i in range(NT):
            r = rows(i)
            c = tcol(i)
            S.dma("sp", ptok.t[0:r, :], pin[d, seg, c:c + r, :], W=[ptok.b])
            bk = ps()
            fns = [lambda k=k: PE.transpose(bk.t[:, k * 128:k * 128 + r], ptok.t[0:r, k * 128:(k + 1) * 128],
                                            ident.t[0:r, 0:r]) for k in range(2)]
            S.group("pe", fns, R=[ptok.b, ident.b], W=[bk.b])
            S.op("act", lambda: A.copy(out=pT.t[:, :, c:c + r],
                                       in_=bk.t[:, 0:256].rearrange("p (a b) -> p a b", a=2)[:, :, 0:r]),
                 R=[bk.b], W=[pT.b])
        norm_to_nT(pc + O_GPG)
        wpl = L("wpl", [128, 2, D], BF16)
        S.dma("pool", wpl.t[:, :, :], w_ple[d].rearrange("(kc p) n -> p kc n", p=128), W=[wpl.b])
        for cb in range(4):
            slots = wblock(w_pg[d], cb * 512)

            def cons(i, bk, cb=cb):
                r = rows(i)
                c = tcol(i)
                S.op("act", lambda: A.activation(out=gsig.t[0:r, :], in_=bk.t[0:r, :], func=AF.Sigmoid),
                     R=[bk.b], W=[gsig.b])
                bkp = ps()
                fns = [lambda k=k: PE.matmul(bkp.t[0:r, :], lhsT=pT.t[:, k, c:c + r],
                                             rhs=wpl.t[:, k, cb * 512:(cb + 1) * 512], start=(k == 0), stop=(k == 1))
                       for k in range(2)]
                S.group("pe", fns, R=[pT.b, wpl.b], W=[bkp.b])
                S.op("dve", lambda: V.tensor_tensor(out=gsig.t[0:r, :], in0=gsig.t[0:r, :], in1=bkp.t[0:r, :],
                                                    op=ALU.mult), R=[gsig.b, bkp.b], W=[gsig.b])
                S.op("dve", lambda: V.tensor_tensor(out=h.t[0:r, i, cb * 512:(cb + 1) * 512],
                                                    in0=h.t[0:r, i, cb * 512:(cb + 1) * 512], in1=gsig.t[0:r, :],
                                                    op=ALU.add), R=[gsig.b, h.bufs[i]], W=[h.bufs[i]])
            gemm_tm(slots, nT.t, nT.b, ncol, cons)

    for d in range(DEPTH):
        for seg in range(NSEG):
            for i in range(NT):
                r = rows(i)
                if d == 0:
                    S.dma("sp", h.t[0:r, i, :], xin[seg, tcol(i):tcol(i) + r, :], W=[h.bufs[i]])
                else:
                    S.dma("sp", h.t[0:r, i, :], hbuf[seg, tcol(i):tcol(i) + r, :], R=[hbuf_b[seg][i]], W=[h.bufs[i]])
            for phase in (ssd_phase, cm_phase, ffn_phase, ple_phase):
                with ExitStack() as sc:
                    phase(d, seg, sc)
                    barrier()
            if d + 1 < DEPTH:
                for i in range(NT):
                    r = rows(i)
                    S.dma("sp", hbuf[seg, tcol(i):tcol(i) + r, :], h.t[0:r, i, :], R=[h.bufs[i]], W=[hbuf_b[seg][i]])
            else:
                with ExitStack() as sc_fin:
                    gfb = TL("gfb_%d" % seg, [128, 512], F32, sc=sc_fin)
                    for i in range(NT):
                        r = rows(i)
                        o = 8 * i
                        sumsq_h(i)
                        for q in range(4):
                            S.dma("sp", gfb.t[0:r, :], g_final_d[0][q * 512:(q + 1) * 512].partition_broadcast(r),
                                  W=[gfb.b])
                            S.op("dve", lambda: V.scalar_tensor_tensor(out=h.t[0:r, i, q * 512:(q + 1) * 512],
                                                                       in0=h.t[0:r, i, q * 512:(q + 1) * 512],
                                                                       scalar=stat.t[0:r, o + 6:o + 7], in1=gfb.t[0:r, :],
                                                                       op0=ALU.mult, op1=ALU.mult),
                                 R=[h.bufs[i], stat.bufs[i], gfb.b], W=[h.bufs[i]])
                        S.dma("sp", yout[seg, tcol(i):tcol(i) + r, :], h.t[0:r, i, :], R=[h.bufs[i]])
                    barrier()
    barrier()
    es.close()
    return nc


def _prep_inputs(inp, NTP, DEPTH, ncores):
    f = lambda a: np.ascontiguousarray(np.asarray(a, dtype=np.float32))
    xp_, xs_ = f(inp["x_prompt"]), f(inp["x_sample"])
    pp_, ps_ = f(inp["p_prompt"]), f(inp["p_sample"])
    sst, scv = f(inp["state_ssm"]), f(inp["state_conv"])
    rows_ = []
    for d in range(DEPTH):
        rows_ += [f(inp["g_mix"])[d].reshape(16, 128), f(inp["g_ffn"])[d].reshape(16, 128),
                  f(inp["g_pg"])[d].reshape(16, 128), f(inp["g_ssd"])[d].reshape(16, 128),
                  f(inp["g_cm"])[d].reshape(16, 128), f(inp["conv_w"])[d].reshape(4 * 24, 128),
                  f(inp["conv_b"])[d].reshape(24, 128)]
    rows_.append(f(inp["g_final"]).reshape(16, 128))
    prm = np.ascontiguousarray(np.concatenate(rows_, axis=0))
    smallp = np.ascontiguousarray(np.concatenate([f(inp["dt_bias"]), f(inp["a_log"]), f(inp["d_skip"])], axis=1))
    shared = {"prm": prm, "smallp": smallp, "g_cm": f(inp["g_cm"]), "w_s": f(inp["w_s"]), "b_s": f(inp["b_s"]),
              "w_in": f(inp["w_in"]), "w_out": f(inp["w_out"]), "w_gate": f(inp["w_gate"]), "w_up": f(inp["w_up"]),
              "w_down": f(inp["w_down"]), "w_pg": f(inp["w_pg"]), "w_ple": f(inp["w_ple"]),
              "g_final": f(inp["g_final"]).reshape(1, 2048)}
    maps = []
    for c in range(ncores):
        sq = c % 4
        m = dict(shared)
        xin, pin, ss, sc_ = [], [], [], []
        for seg in range(2):
            sl = slice(seg * NTP, (seg + 1) * NTP)
            b0 = (sq * 2 + seg) * NSS
            bs = slice(b0, b0 + NSS)
            xin.append(np.concatenate([xp_[sq, sl], xs_[bs].reshape(TS, D)], axis=0))
            pin.append(np.concatenate([pp_[:, sq, sl], ps_[:, bs].reshape(DEPTH, TS, DPLE)], axis=1))
            ss.append(sst[:, bs].reshape(DEPTH, NSS, 2048, 128))
            sc_.append(scv[:, bs].reshape(DEPTH, NSS * 3, CONV))
        m["xin"] = np.ascontiguousarray(np.stack(xin, axis=0))
        m["pin"] = np.ascontiguousarray(np.stack(pin, axis=1))
        m["sst"] = np.ascontiguousarray(np.stack(ss, axis=1))
        m["scv"] = np.ascontiguousarray(np.stack(sc_, axis=1))
        maps.append(m)
    return maps


def _assemble(res, NTP, DEPTH, ncores, nb_prompt, nb_sample):
    SEQ = 2 * NTP
    y_p = np.zeros((nb_prompt, SEQ, D), np.float32)
    y_s = np.zeros((nb_sample, 4, D), np.float32)
    ssm_p = np.zeros((DEPTH, nb_prompt, 32, 64, 128), np.float32)
    conv_p = np.zeros((DEPTH, nb_prompt, 3, CONV), np.float32)
    ssm_s = np.zeros((DEPTH, nb_sample, 32, 64, 128), np.float32)
    conv_s = np.zeros((DEPTH, nb_sample, 3, CONV), np.float32)
    v_s = np.zeros((DEPTH, nb_sample, 4, 2048), np.float32)
    for c in range(4):
        r = res[c]
        sq = c
        for seg in range(2):
            b0 = (sq * 2 + seg) * NSS
            bs = slice(b0, b0 + NSS)
            y_p[sq, seg * NTP:(seg + 1) * NTP] = r["yout"][seg, :NTP]
            y_s[bs] = r["yout"][seg, NTP:].reshape(NSS, 4, D)
            ssm_s[:, bs] = r["ssm_s"][:, seg].reshape(DEPTH, NSS, 32, 64, 128)
            conv_s[:, bs] = r["conv_o"][:, seg, 3:].reshape(DEPTH, NSS, 3, CONV)
            v_s[:, bs] = r["v_s"][:, seg].reshape(DEPTH, NSS, 4, 2048)
        ssm_p[:, sq] = r["ssm_p"].reshape(DEPTH, 32, 64, 128)
        conv_p[:, sq] = r["conv_o"][:, 1, 0:3]
    return (y_p, y_s, ssm_p, conv_p, ssm_s, conv_s, v_s)


LAST_RES = None


def kernel(**inputs):
    global LAST_RES
    DEPTH = int(np.asarray(inputs["w_in"]).shape[0])
    nbp, SEQ = np.asarray(inputs["x_prompt"]).shape[:2]
    nbs = np.asarray(inputs["x_sample"]).shape[0]
    ncores = 8
    NTP = SEQ // 2
    nc = build(NTP, DEPTH)
    maps = _prep_inputs(inputs, NTP, DEPTH, ncores)
    res = run_bass_kernel_spmd(nc, maps, core_ids=list(range(ncores)))
    if DBG:
        LAST_RES = res.results
    return _assemble(res.results, NTP, DEPTH, ncores, nbp, nbs)
```
